# Optimizing a Trainium2 kernel written in Bass

```python
import jax
import jax.numpy as jnp
from jax import lax
import numpy as np

D_MODEL = 2048
BATCH = 2
SEQ = 4096
DEPTH = 1
DEC_BATCH = 128
DEC_SEQ = 8
PAST_LEN = 2048
PAGE_SIZE = 128

NSA_HEADS = 16
NSA_KV_HEADS = 4
NSA_GROUP = NSA_HEADS // NSA_KV_HEADS
HEAD_DIM = 64
NSA_WIDTH = NSA_HEADS * HEAD_DIM
L_CMP = 32
D_CMP = 16
CMP_HID = HEAD_DIM
L_SEL = 64
N_SEL_TOP = 16
WINDOW = 512
Q_BLOCK = 128
ROPE_THETA = 10000.0
HGRN_HEADS = 8
HGRN_DK = 128
HGRN_DV = 128
HGRN_KW = HGRN_HEADS * HGRN_DK
HGRN_VW = HGRN_HEADS * HGRN_DV
HGRN_CHUNK = 64
D_FF = 4 * D_MODEL
EPS = 1e-6
NEG = -1e30
BIG = 1e9
_SIZES = (NSA_WIDTH, 6 * NSA_KV_HEADS * HEAD_DIM, 3 * NSA_HEADS, HGRN_KW, HGRN_KW, HGRN_VW, HGRN_VW, D_MODEL, D_MODEL)
_SPLITS = tuple(int(v) for v in np.cumsum(_SIZES)[:-1])
N_IN = sum(_SIZES)
F32 = jnp.float32

kernel_name = 'nsa_hgrn2_gated_hybrid_step'


def _rmsnorm(x, g):
    xf = x.astype(F32)
    y = xf * lax.rsqrt(jnp.mean(xf * xf, axis=-1, keepdims=True) + EPS)
    return (y * g.astype(F32)).astype(x.dtype)


def _rope(x, pos):
    hd = x.shape[-1]
    half = hd // 2
    inv = ROPE_THETA ** (-jnp.arange(half, dtype=F32) * 2.0 / hd)
    ang = pos.astype(F32)[:, None] * inv[None, :]
    shape = (ang.shape[0],) + (1,) * (x.ndim - 3) + (half,)
    cos = jnp.cos(ang).reshape(shape)
    sin = jnp.sin(ang).reshape(shape)
    xf = x.astype(F32)
    x1, x2 = xf[..., :half], xf[..., half:]
    return jnp.concatenate([x1 * cos - x2 * sin, x2 * cos + x1 * sin], axis=-1).astype(x.dtype)


def _masked_softmax(s, valid):
    s = jnp.where(valid, s, NEG)
    m = jnp.max(s, axis=-1, keepdims=True)
    e = jnp.exp(s - m) * valid
    return e / jnp.maximum(jnp.sum(e, axis=-1, keepdims=True), 1.0)


def _overlap(nc, ns):
    cs = np.arange(nc) * D_CMP
    ce = cs + L_CMP
    ss = np.arange(ns) * L_SEL
    se = ss + L_SEL
    ov = np.clip(np.minimum(ce[:, None], se[None, :]) - np.maximum(cs[:, None], ss[None, :]), 0, None)
    return jnp.asarray((ov / D_CMP).astype(np.float32))


def _compress(rows, pe, w1, w2):
    B, T, G, hd = rows.shape
    n_ch = T // D_CMP
    m = L_CMP // D_CMP
    nc = n_ch - m + 1
    ch = rows[:, :n_ch * D_CMP].reshape(B, n_ch, D_CMP, G, hd).transpose(0, 1, 3, 2, 4).reshape(B, n_ch, G, D_CMP * hd)
    w1p = w1.reshape(m, D_CMP * hd, CMP_HID)
    h = pe.reshape(-1) @ w1
    for i in range(m):
        h = h + ch[:, i:i + nc] @ w1p[i]
    return jax.nn.silu(h) @ w2


def _selected_attn(q, qpos, idx, ks_t, vs_t):
    B, Q, G, R, hd = q.shape
    scale = hd ** -0.5
    bi = jnp.arange(B)[:, None, None]
    gi = jnp.arange(G)[None, None, :]

    def step(carry, j):
        m, l, acc = carry
        kb = ks_t[bi, gi, j]
        vb = vs_t[bi, gi, j]
        s = jnp.einsum('bqgrd,bqgld->bqgrl', q, kb) * scale
        kpos = j[..., None] * L_SEL + jnp.arange(L_SEL)
        valid = (kpos <= qpos[None, :, None, None])[:, :, :, None, :]
        s = jnp.where(valid, s, NEG)
        m_new = jnp.maximum(m, jnp.max(s, axis=-1))
        corr = jnp.exp(m - m_new)
        p = jnp.exp(s - m_new[..., None]) * valid
        l = l * corr + jnp.sum(p, axis=-1)
        acc = acc * corr[..., None] + jnp.einsum('bqgrl,bqgld->bqgrd', p, vb)
        return (m_new, l, acc), None

    init = (jnp.full((B, Q, G, R), NEG, F32), jnp.zeros((B, Q, G, R), F32), jnp.zeros((B, Q, G, R, hd), F32))
    (m, l, acc), _ = lax.scan(step, init, jnp.moveaxis(idx, -1, 0))
    return acc / l[..., None]


def _cmp_sel(q, qpos, kc, vc, ks_t, vs_t):
    hd = q.shape[-1]
    nc = kc.shape[1]
    ns = ks_t.shape[2]
    s = jnp.einsum('bqgrd,bcgd->bqgrc', q, kc) * (hd ** -0.5)
    c_end = jnp.arange(nc) * D_CMP + (L_CMP - 1)
    cvalid = (c_end[None, :] <= qpos[:, None])[None, :, None, None, :]
    p_cmp = _masked_softmax(s, cvalid)
    o_cmp = jnp.einsum('bqgrc,bcgd->bqgrd', p_cmp, vc)
    imp = jnp.einsum('bqgrc,cn->bqgn', p_cmp, _overlap(nc, ns))
    blk = jnp.arange(ns)[None, :]
    cur = (qpos // L_SEL)[:, None]
    visible = blk * L_SEL <= qpos[:, None]
    forced = (blk == 0) | (blk == cur) | (blk == cur - 1)
    score = jnp.where(visible[None, :, None, :], imp + jnp.where(forced, BIG, 0.0)[None, :, None, :], -BIG)
    _, idx = lax.top_k(score, min(N_SEL_TOP, ns))
    o_sel = _selected_attn(q, qpos, idx, ks_t, vs_t)
    return o_cmp, o_sel


def _window_attn(q, qpos, k, v, kpos):
    s = jnp.einsum('bqgrd,bkgd->bqgrk', q, k) * (q.shape[-1] ** -0.5)
    valid = (kpos[None, :] <= qpos[:, None]) & (kpos[None, :] > qpos[:, None] - WINDOW) & (kpos[None, :] >= 0)
    p = _masked_softmax(s, valid[None, :, None, None, :])
    return jnp.einsum('bqgrk,bkgd->bqgrd', p, v)


def _gate_branches(br_gate, o_c, o_s, o_w):
    B, T = o_c.shape[:2]
    o = br_gate[:, :, 0, :, :, None] * o_c + br_gate[:, :, 1, :, :, None] * o_s + br_gate[:, :, 2, :, :, None] * o_w
    return o.reshape(B, T, NSA_WIDTH)


def _hgrn2(q, k, v, log_f, s0, chunk):
    B, T, H, DK = q.shape
    DV = v.shape[-1]
    nck = T // chunk

    def blocks(a):
        return a.reshape(B, nck, chunk, H, a.shape[-1]).transpose(1, 0, 3, 2, 4)

    causal = jnp.tril(jnp.ones((chunk, chunk), bool))[:, :, None]

    def step(S, inp):
        qc, kc, vc, ac = inp
        A = jnp.cumsum(ac, axis=2)
        o = jnp.einsum('bhtk,bhkv->bhtv', qc * jnp.exp(A), S)
        decay = jnp.exp(jnp.where(causal, A[:, :, :, None, :] - A[:, :, None, :, :], -jnp.inf))
        att = jnp.einsum('bhtk,bhsk,bhtsk->bhts', qc, kc, decay)
        o = o + jnp.einsum('bhts,bhsv->bhtv', att, vc)
        a_last = A[:, :, -1:, :]
        S = jnp.exp(a_last[:, :, 0, :])[..., None] * S + jnp.einsum('bhsk,bhsv->bhkv', kc * jnp.exp(a_last - A), vc)
        return S, o

    S, o = lax.scan(step, s0, (blocks(q), blocks(k), blocks(v), blocks(log_f)))
    return o.transpose(1, 0, 3, 2, 4).reshape(B, T, H, DV), S


def _project(x, pos, lb, norm1_g, w_in, q_norm_g, k_norm_g):
    B, T, _ = x.shape
    z = _rmsnorm(x, norm1_g) @ w_in
    zq, zkv, zga, zhq, zhf, zhi, zhg, gate_a, gate_b = jnp.split(z, _SPLITS, axis=-1)
    q = _rope(_rmsnorm(zq.reshape(B, T, NSA_HEADS, HEAD_DIM), q_norm_g), pos)
    q = q.astype(F32).reshape(B, T, NSA_KV_HEADS, NSA_GROUP, HEAD_DIM)
    kv = zkv.reshape(B, T, 3, 2, NSA_KV_HEADS, HEAD_DIM)
    k = _rope(_rmsnorm(kv[:, :, :, 0], k_norm_g[:, None, :]), pos)
    rows = jnp.stack([k, kv[:, :, :, 1]], axis=3)
    br_gate = jax.nn.sigmoid(zga.astype(F32)).reshape(B, T, 3, NSA_KV_HEADS, NSA_GROUP)
    hq = jax.nn.silu(zhq.astype(F32)).reshape(B, T, HGRN_HEADS, HGRN_DK)
    zf = zhf.astype(F32).reshape(B, T, HGRN_HEADS, HGRN_DK)
    log_f = jnp.logaddexp(jnp.log(lb), jnp.log1p(-lb) + jax.nn.log_sigmoid(zf))
    hk = (1.0 - lb) * jax.nn.sigmoid(-zf)
    hv = zhi.astype(F32).reshape(B, T, HGRN_HEADS, HGRN_DV)
    return q, rows, br_gate, hq, hk, hv, log_f, zhg, gate_a, gate_b


def _finish(x, oa, ob, zhg, gate_a, gate_b, hgrn_norm_g, w_proj_a, w_proj_b, w_out, norm2_g, w_ff1, w_ff2):
    dt = x.dtype
    B, T, _ = x.shape
    og = jax.nn.silu(zhg.astype(F32)).reshape(B, T, HGRN_HEADS, HGRN_DV)
    ob = _rmsnorm(ob, hgrn_norm_g) * og
    ya = oa.astype(dt) @ w_proj_a
    yb = ob.reshape(B, T, HGRN_VW).astype(dt) @ w_proj_b
    mix = jax.nn.sigmoid(gate_a) * ya + jax.nn.sigmoid(gate_b) * yb
    x = x + mix @ w_out
    hf = _rmsnorm(x, norm2_g) @ w_ff1
    return x + jnp.square(jax.nn.relu(hf)) @ w_ff2


def _layer_prompt(x, lb, lw):
    (norm1_g, w_in, q_norm_g, k_norm_g, cmp_pe, cmp_w1, cmp_w2, w_proj_a,
     hgrn_norm_g, w_proj_b, w_out, norm2_g, w_ff1, w_ff2) = lw
    B, T, _ = x.shape
    pos = jnp.arange(T)
    q, rows, br_gate, hq, hk, hv, log_f, zhg, gate_a, gate_b = _project(x, pos, lb, norm1_g, w_in, q_norm_g, k_norm_g)
    rows_f = rows.astype(F32)
    kc = _compress(rows_f[:, :, 0, 0], cmp_pe[0], cmp_w1[0], cmp_w2[0])
    vc = _compress(rows_f[:, :, 0, 1], cmp_pe[1], cmp_w1[1], cmp_w2[1])
    ns = T // L_SEL
    sel_blk = rows_f[:, :, 1].reshape(B, ns, L_SEL, 2, NSA_KV_HEADS, HEAD_DIM).transpose(3, 0, 4, 1, 2, 5)
    win_pad = jnp.pad(rows_f[:, :, 2], ((0, 0), (WINDOW, 0), (0, 0), (0, 0), (0, 0)))

    def q_block(b):
        s0 = b * Q_BLOCK
        qb = lax.dynamic_slice_in_dim(q, s0, Q_BLOCK, axis=1)
        qpos = s0 + jnp.arange(Q_BLOCK)
        o_c, o_s = _cmp_sel(qb, qpos, kc, vc, sel_blk[0], sel_blk[1])
        kw = lax.dynamic_slice_in_dim(win_pad, s0, WINDOW + Q_BLOCK, axis=1)
        o_w = _window_attn(qb, qpos, kw[:, :, 0], kw[:, :, 1], s0 - WINDOW + jnp.arange(WINDOW + Q_BLOCK))
        return o_c, o_s, o_w

    outs = lax.map(q_block, jnp.arange(T // Q_BLOCK))
    o_c, o_s, o_w = [jnp.moveaxis(o, 0, 1).reshape(B, T, NSA_KV_HEADS, NSA_GROUP, HEAD_DIM) for o in outs]
    oa = _gate_branches(br_gate, o_c, o_s, o_w)
    ob, s_fin = _hgrn2(hq, hk, hv, log_f, jnp.zeros((B, HGRN_HEADS, HGRN_DK, HGRN_DV), F32), HGRN_CHUNK)
    y = _finish(x, oa, ob, zhg, gate_a, gate_b, hgrn_norm_g, w_proj_a, w_proj_b, w_out, norm2_g, w_ff1, w_ff2)
    wp = min(WINDOW, T)
    return y, (rows[:, :, 0], rows[:, :, 1], rows[:, T - wp:, 2], s_fin)


def _layer_sample(x, cache_cmp, cache_sel, win_buf, s_past, page_table, lb, lw):
    (norm1_g, w_in, q_norm_g, k_norm_g, cmp_pe, cmp_w1, cmp_w2, w_proj_a,
     hgrn_norm_g, w_proj_b, w_out, norm2_g, w_ff1, w_ff2) = lw
    DB, S, _ = x.shape
    P = page_table.shape[1] * cache_cmp.shape[1]
    WB = win_buf.shape[1]
    pos = P + jnp.arange(S)
    q, rows, br_gate, hq, hk, hv, log_f, zhg, gate_a, gate_b = _project(x, pos, lb, norm1_g, w_in, q_norm_g, k_norm_g)
    rows_f = rows.astype(F32)
    kv_shape = (DB, P, 2, NSA_KV_HEADS, HEAD_DIM)
    cmp_all = jnp.concatenate([cache_cmp[page_table].reshape(kv_shape).astype(F32), rows_f[:, :, 0]], axis=1)
    sel_all = jnp.concatenate([cache_sel[page_table].reshape(kv_shape).astype(F32), rows_f[:, :, 1]], axis=1)
    kc = _compress(cmp_all[:, :, 0], cmp_pe[0], cmp_w1[0], cmp_w2[0])
    vc = _compress(cmp_all[:, :, 1], cmp_pe[1], cmp_w1[1], cmp_w2[1])
    t_all = P + S
    ns = -(-t_all // L_SEL)
    sel_all = jnp.pad(sel_all, ((0, 0), (0, ns * L_SEL - t_all), (0, 0), (0, 0), (0, 0)))
    sel_blk = sel_all.reshape(DB, ns, L_SEL, 2, NSA_KV_HEADS, HEAD_DIM).transpose(3, 0, 4, 1, 2, 5)
    o_c, o_s = _cmp_sel(q, pos, kc, vc, sel_blk[0], sel_blk[1])
    win_all = jnp.concatenate([win_buf.astype(F32), rows_f[:, :, 2]], axis=1)
    o_w = _window_attn(q, pos, win_all[:, :, 0], win_all[:, :, 1], P - WB + jnp.arange(WB + S))
    oa = _gate_branches(br_gate, o_c, o_s, o_w)
    ob, s_new = _hgrn2(hq, hk, hv, log_f, s_past.astype(F32), S)
    y = _finish(x, oa, ob, zhg, gate_a, gate_b, hgrn_norm_g, w_proj_a, w_proj_b, w_out, norm2_g, w_ff1, w_ff2)
    return y, (rows[:, :, 0], rows[:, :, 1], win_all[:, S:].astype(rows.dtype), s_new)


def setup_inputs(seed: int = 0) -> dict:
    key = jax.random.key(seed)
    ks = jax.random.split(key, 24)

    def nrm(k, shape, scale):
        return scale * jax.random.normal(k, shape, F32)

    n_pages = PAST_LEN // PAGE_SIZE
    n_used = DEC_BATCH * n_pages
    n_pool = n_used + max(1, n_used // 4)
    win_buf = min(WINDOW, PAST_LEN)
    page_table = jax.random.permutation(ks[6], n_pool)[:n_used].reshape(DEC_BATCH, n_pages).astype(jnp.int32)
    kv_shape = (DEPTH, n_pool, PAGE_SIZE, 2, NSA_KV_HEADS, HEAD_DIM)
    return {
        'x_prompt': nrm(ks[0], (BATCH, SEQ, D_MODEL), 1.0),
        'x_sample': nrm(ks[1], (DEC_BATCH, DEC_SEQ, D_MODEL), 1.0),
        'cache_cmp_kv': nrm(ks[2], kv_shape, 1.0),
        'cache_sel_kv': nrm(ks[3], kv_shape, 1.0),
        'cache_win_kv': nrm(ks[4], (DEPTH, DEC_BATCH, win_buf, 2, NSA_KV_HEADS, HEAD_DIM), 1.0),
        'state_hgrn': nrm(ks[5], (DEPTH, DEC_BATCH, HGRN_HEADS, HGRN_DK, HGRN_DV), 0.3),
        'page_table': page_table,
        'norm1_g': 1.0 + nrm(ks[7], (DEPTH, D_MODEL), 0.02),
        'w_in': nrm(ks[8], (DEPTH, D_MODEL, N_IN), D_MODEL ** -0.5),
        'q_norm_g': 1.0 + nrm(ks[9], (DEPTH, HEAD_DIM), 0.02),
        'k_norm_g': 1.0 + nrm(ks[10], (DEPTH, 3, HEAD_DIM), 0.02),
        'cmp_pe': nrm(ks[11], (DEPTH, 2, L_CMP, HEAD_DIM), 0.1),
        'cmp_w1': nrm(ks[12], (DEPTH, 2, L_CMP * HEAD_DIM, CMP_HID), (L_CMP * HEAD_DIM) ** -0.5),
        'cmp_w2': nrm(ks[13], (DEPTH, 2, CMP_HID, HEAD_DIM), CMP_HID ** -0.5),
        'w_proj_a': nrm(ks[14], (DEPTH, NSA_WIDTH, D_MODEL), NSA_WIDTH ** -0.5),
        'hgrn_lb_logits': nrm(ks[15], (DEPTH + 1, HGRN_KW), 1.0),
        'hgrn_norm_g': 1.0 + nrm(ks[16], (DEPTH, HGRN_DV), 0.02),
        'w_proj_b': nrm(ks[17], (DEPTH, HGRN_VW, D_MODEL), HGRN_VW ** -0.5),
        'w_out': nrm(ks[18], (DEPTH, D_MODEL, D_MODEL), D_MODEL ** -0.5),
        'norm2_g': 1.0 + nrm(ks[19], (DEPTH, D_MODEL), 0.02),
        'w_ff1': nrm(ks[20], (DEPTH, D_MODEL, D_FF), D_MODEL ** -0.5),
        'w_ff2': nrm(ks[21], (DEPTH, D_FF, D_MODEL), D_FF ** -0.5),
    }


def reference(x_prompt, x_sample, cache_cmp_kv, cache_sel_kv, cache_win_kv, state_hgrn, page_table,
              norm1_g, w_in, q_norm_g, k_norm_g, cmp_pe, cmp_w1, cmp_w2, w_proj_a, hgrn_lb_logits,
              hgrn_norm_g, w_proj_b, w_out, norm2_g, w_ff1, w_ff2):
    lb_all = jnp.cumsum(jax.nn.softmax(hgrn_lb_logits.astype(F32), axis=0), axis=0).reshape(DEPTH + 1, HGRN_HEADS, HGRN_DK)
    xp, xs = x_prompt, x_sample
    new_p, new_s = [], []
    for l in range(DEPTH):
        lw = (norm1_g[l], w_in[l], q_norm_g[l], k_norm_g[l], cmp_pe[l], cmp_w1[l], cmp_w2[l], w_proj_a[l],
              hgrn_norm_g[l], w_proj_b[l], w_out[l], norm2_g[l], w_ff1[l], w_ff2[l])
        xp, st_p = _layer_prompt(xp, lb_all[l], lw)
        xs, st_s = _layer_sample(xs, cache_cmp_kv[l], cache_sel_kv[l], cache_win_kv[l], state_hgrn[l], page_table, lb_all[l], lw)
        new_p.append(st_p)
        new_s.append(st_s)
    p_cmp, p_sel, p_win, p_hgrn = [jnp.stack([st[i] for st in new_p]) for i in range(4)]
    s_cmp, s_sel, s_win, s_hgrn = [jnp.stack([st[i] for st in new_s]) for i in range(4)]
    return (xp, xs, p_cmp, p_sel, p_win, p_hgrn, s_cmp, s_sel, s_win, s_hgrn)
```

```python
import numpy as np
from contextlib import ExitStack
import concourse.bass as bass
import concourse.mybir as mybir
from concourse.bass_utils import run_bass_kernel_spmd

F32 = mybir.dt.float32
BF16 = mybir.dt.bfloat16
I32 = mybir.dt.int32
ALU = mybir.AluOpType
AF = mybir.ActivationFunctionType
AX = mybir.AxisListType

ENGS = ['tensor', 'vector', 'scalar', 'gpsimd', 'sync']
EPOCH = 24000

D = 2048
KC = 16
N_IN = 10800
NT = 9
NSEQ = 32
EPS = 1e-6
NPOOL = 2560
SIZES = (1024, 1536, 48, 1024, 1024, 1024, 1024, 2048, 2048)
OFFS = [0]
for _s in SIZES:
    OFFS.append(OFFS[-1] + _s)
O_Q, O_KV, O_GA, O_HQ, O_HF, O_HI, O_HG, O_A, O_B = OFFS[:9]


class Sched:
    def __init__(self, nc, stack):
        self.nc = nc
        self.stack = stack
        self.streams = {e: [] for e in ENGS}
        self.cnt = {e: 0 for e in ENGS}
        self.epoch = {e: 0 for e in ENGS}
        self.semh = {}
        self.seen = {e: {} for e in ENGS}
        self.lastw = {}
        self.readers = {}
        self.dma_tot = {}
        self.nsem = 0
        for e in ENGS:
            self._sem((e, 0))

    def _sem(self, key):
        if key not in self.semh:
            self.nsem += 1
            self.semh[key] = self.stack.enter_context(self.nc.semaphore("s%d" % self.nsem))
        return self.semh[key]

    def _deps(self, eng, reads, writes):
        evs = []
        for k in reads:
            if k in self.lastw:
                evs.append(self.lastw[k])
        for k in writes:
            if k in self.lastw:
                evs.append(self.lastw[k])
            evs.extend(self.readers.get(k, ()))
        need = {}
        for (sk, v) in evs:
            if sk[0] == 'd':
                v = self.dma_tot[sk]
            if self.seen[eng].get(sk, 0) >= v:
                continue
            if need.get(sk, 0) < v:
                need[sk] = v
        for sk, v in need.items():
            self.seen[eng][sk] = v
        return [(self._sem(sk), v) for sk, v in need.items()]

    def _record(self, ev, reads, writes):
        for k in writes:
            self.lastw[k] = ev
            self.readers[k] = []
        for k in reads:
            self.readers.setdefault(k, []).append(ev)

    def op(self, eng, fn, reads=(), writes=()):
        waits = self._deps(eng, reads, writes)
        if self.cnt[eng] >= EPOCH:
            self.epoch[eng] += 1
            self.cnt[eng] = 0
        sk = (eng, self.epoch[eng])
        self.cnt[eng] += 1
        ev = (sk, self.cnt[eng])
        if eng == 'tensor':
            self.seen[eng][sk] = self.cnt[eng]
        self.streams[eng].append((waits, fn, self._sem(sk), 1))
        self._record(ev, reads, writes)
        return ev

    def do(self, eng, method, reads=(), writes=(), **kw):
        return self.op(eng, lambda e: getattr(e, method)(**kw), reads, writes)

    def dd(self, q, out, in_, reads=(), writes=(), sem='d0'):
        return self.dma(q, lambda e: e.dma_start(out=out, in_=in_), reads, writes, sem)

    def dma(self, q, fn, reads=(), writes=(), sem='d0'):
        waits = self._deps(q, reads, writes)
        sk = ('d', sem)
        self.dma_tot[sk] = self.dma_tot.get(sk, 0) + 16
        ev = (sk, self.dma_tot[sk])
        self.streams[q].append((waits, fn, self._sem(sk), 16))
        self._record(ev, reads, writes)
        return ev

    def barrier(self):
        targets = []
        for e in ENGS:
            for ep in range(self.epoch[e] + 1):
                sk = (e, ep)
                v = self.cnt[e] if ep == self.epoch[e] else EPOCH
                if v > 0:
                    targets.append((sk, v))
        for sk, v in self.dma_tot.items():
            targets.append((sk, v))
        for e in ENGS:
            waits = []
            for sk, v in targets:
                if self.seen[e].get(sk, 0) < v:
                    self.seen[e][sk] = v
                    waits.append((self._sem(sk), v))
            if waits:
                self.streams[e].append((waits, None, None, 0))

    def emit(self):
        nc = self.nc
        with nc.Block() as block:
            def mk(ename):
                def body(eng):
                    for waits, fn, sem, inc in self.streams[ename]:
                        for (s, v) in waits:
                            eng.wait_ge(s, v)
                        if fn is not None:
                            fn(eng).then_inc(sem, inc)
                return body
            block.tensor(mk('tensor'))
            block.vector(mk('vector'))
            block.scalar(mk('scalar'))
            block.gpsimd(mk('gpsimd'))
            block.sync(mk('sync'))


class Ctx:
    pass


def chunk_list():
    ch = []
    def seg(o, n, f):
        c = 0
        while c < n:
            w = min(512, n - c)
            ch.append((o + c, w, f))
            c += w
    seg(O_Q, 1024, AF.Copy)
    seg(O_KV, 1536, AF.Copy)
    seg(O_GA, 48, AF.Sigmoid)
    seg(O_HQ, 1024, AF.Silu)
    seg(O_HF, 1024, AF.Sigmoid)
    seg(O_HI, 1024, AF.Copy)
    seg(O_HG, 1024, AF.Silu)
    seg(O_A, 2048, AF.Sigmoid)
    seg(O_B, 2048, AF.Sigmoid)
    return ch


def build(phases=('own', 'seq', 'hgrn', 'att', 'attp', 'atts', 'fin'), debug=False):
    nc = bass.Bass("TRN2", target_bir_lowering=False)
    C = Ctx()
    C.debug = debug
    din = lambda n, s, dt=F32: nc.dram_tensor(n, list(s), dt, kind="ExternalInput").ap()
    dout = lambda n, s, dt=F32: nc.dram_tensor(n, list(s), dt, kind="ExternalOutput").ap()
    dscr = lambda n, s, dt=F32: nc.dram_tensor(n, list(s), dt, kind="Internal").ap()
    x_own = din("x_own", [NT, 128, D])
    cs_own = din("cs_own", [NT, 128, 64])
    x_seq = din("x_seq", [NSEQ, 128, D])
    cs_seq = din("cs_seq", [NSEQ, 128, 64])
    w_in = din("w_in", [D, N_IN])
    g1 = din("g1", [128, KC])
    qkg = din("qkg", [4, 64])
    lbl = din("lbl", [2, 1024])
    jsel = din("jsel", [128, 4])
    cU64 = din("cU64", [128, 128])
    cG2 = din("cG2", [128, 2])
    cDm = din("cDm", [2, 128, 128])
    cPm = din("cPm", [2, 128, 16])
    cMT = din("cMT", [2, 128, 128])
    cU8 = din("cU8", [128, 128])
    cG16 = din("cG16", [128, 16])
    st_in = din("st_in", [16, 8, 128, 128])
    hng = din("hng", [128, 1])
    rows_own = dout("rows_own", [NT, 128, 3, 512])
    s_state = dout("s_state", [16, 8, 128, 128])
    obT_scr = dscr("obT_scr", [NT, 128, 8, 128], BF16)
    oaT_scr = dscr("oaT_scr", [NT, 128, 8, 128], BF16)
    mix_scr = dscr("mix_scr", [NT, 128, D])
    w_pa = din("w_pa", [1024, D]); w_pb = din("w_pb", [1024, D]); w_o = din("w_o", [D, D])
    g2 = din("g2", [128, KC]); w_f1 = din("w_f1", [D, 4 * D]); w_f2 = din("w_f2", [4 * D, D])
    y_own = dout("y_own", [NT, 128, D])
    cE = din("cE", [64, 4096], BF16)
    cselb = din("cselb", [4, 128, 512], BF16)
    cwinb = din("cwinb", [8, 128, 512], BF16)
    ccmpb = din("ccmpb", [8, 2, 128, 512], BF16)
    cvisadd = din("cvisadd", [NT, 128, 2, 64])
    covp = din("covp", [2, 128, 64], BF16)
    covs = din("covs", [128, 64], BF16)
    cOH = din("cOH", [17, 16, 128], BF16)
    cseqb = din("cseqb", [17, 512], BF16)
    cbdc = din("cbdc", [128, 512], BF16)
    cwb0 = din("cwb0", [128, 512], BF16)
    cw1 = din("cw1", [2, 2048, 64]); cw2 = din("cw2", [2, 64, 64]); cpe = din("cpe", [2, 32, 64])
    if 'atts' in phases:
        pt_in = din("pt_in", [1, 256], I32)
        cache_cmp = din("cache_cmp", [NPOOL * 128, 512])
        cache_sel = din("cache_sel", [NPOOL * 128, 512])
        win_in = din("win_in", [16, 512, 512])
        s_win = dout("s_win", [16, 512, 512])
    p_state = dout("p_state", [8, 128, 128])
    zs = dscr("zs", [NT, 128, N_IN])
    zseq = dscr("zseq", [NSEQ, 128, 3584])
    qT_scr = dscr("qT_scr", [NT, 64, 2048], BF16)
    kT_scr = dscr("kT_scr", [4, 64, 4, 4096], BF16)
    v_scr = dscr("v_scr", [NSEQ, 128, 2, 4, 65], BF16)
    sown_scr = dscr("sown_scr", [128, 16, 1024], BF16)

    with ExitStack() as st:
        S = Sched(nc, st)
        sb = lambda name, shape, dt=F32: st.enter_context(nc.sbuf_tensor(name, list(shape), dt))
        ps = lambda name, shape, dt=F32: st.enter_context(nc.psum_tensor(name, list(shape), dt))
        pbank = [ps("pb%d" % i, [128, 512]) for i in range(8)]

        def dbg(name, ap, shape, dt, reads):
            if not C.debug:
                return
            o = nc.dram_tensor("dbg_" + name, list(shape), dt, kind="ExternalOutput").ap()
            S.dd('sync', o, ap, reads=reads, sem='dbg')

        C.pb_i = 0
        C.pb_n = 8

        def next_bank():
            i = C.pb_i % C.pb_n
            C.pb_i = (i + 1) % C.pb_n
            return pbank[i], 'pb%d' % i

        ident = sb("ident", [128, 128])
        S.do('gpsimd', 'memset', writes=['ident'], ap=ident[:], constant=1.0)
        S.do('gpsimd', 'affine_select', reads=['ident'], writes=['ident'], out=ident[:], in_=ident[:], pattern=[[-1, 128]],
             compare_op=ALU.is_equal, fill=0.0, base=0, channel_multiplier=1)
        g1t = sb("g1t", [128, KC])
        S.dd('sync', g1t[:], g1, writes=['g1t'], sem='c')
        qkg_t = sb("qkg_t", [128, 4, 64])
        S.dd('sync', qkg_t[:], qkg.rearrange("(o a) d -> o a d", o=1).to_broadcast([128, 4, 64]), writes=['qkg_t'], sem='c')
        jsel_t = sb("jsel_t", [128, 4])
        S.dd('sync', jsel_t[:], jsel, writes=['jsel_t'], sem='c')
        U64 = sb("U64", [128, 128])
        S.dd('sync', U64[:], cU64, writes=['U64'], sem='c')
        G2 = sb("G2", [128, 2])
        S.dd('sync', G2[:], cG2, writes=['G2'], sem='c')
        C.gb_n = 0

        def alloc_gemm(stack):
            C.gb_n += 1
            u = "_%d" % C.gb_n
            al = lambda name, shape, dt=F32: stack.enter_context(nc.sbuf_tensor(name + u, list(shape), dt))
            C.wst = [al("wst%d" % i, [128, 4, 512]) for i in range(3)]
            C.wbf = [al("wbf%d" % i, [128, KC, 512], BF16) for i in range(2)]
            C.zst = [al("zst%d" % i, [128, 512]) for i in range(4)]
            C.wst_i = 0
            C.wbf_i = 0
            C.zst_i = 0

        gst = ExitStack()
        alloc_gemm(gst)

        def make_xT(x_dram, t0, nt, xT, ssq, rstd, pfx, xst, junk):
            S.do('vector', 'memset', writes=[pfx + 'ssq'], ap=ssq[:, 0:nt], constant=0.0)
            for t in range(nt):
                xb = xst[t % 2]
                xk = 'xst%d' % (t % 2)
                S.dd('sync', xb[:], x_dram[t0 + t], writes=[xk], sem='x%d' % (t % 2))
                S.do('scalar', 'activation', reads=[xk, pfx + 'ssq'], writes=['junk', pfx + 'ssq'], out=junk[:], in_=xb[:], func=AF.Square, accum_out=ssq[:, t:t + 1])
                for k4 in range(4):
                    pb, pk = next_bank()
                    for kk in range(4):
                        kc = k4 * 4 + kk
                        S.do('tensor', 'transpose', reads=[xk, 'ident'], writes=[pk], out=pb[:, kk * 128:(kk + 1) * 128], in_=xb[:, kc * 128:(kc + 1) * 128], identity=ident[:])
                    S.do('vector', 'tensor_copy', reads=[pk], writes=[pfx + 'xT%d' % t], out=xT[:, t, k4 * 4:(k4 + 1) * 4, :], in_=pb[:].rearrange("p (a b) -> p a b", a=4))
            S.do('scalar', 'activation', reads=[pfx + 'ssq'], writes=[pfx + 'rstd'], out=rstd[:, 0:nt], in_=ssq[:, 0:nt], func=AF.Sqrt, scale=1.0 / D, bias=EPS)
            S.do('vector', 'reciprocal', reads=[pfx + 'rstd'], writes=[pfx + 'rstd'], out=rstd[:, 0:nt], in_=rstd[:, 0:nt])

        def gemm(lhs, lhs_key, nt, nkc, w_v, chunks, gain, gain_key, epi):
            def load_chunk(ci):
                c0, w, _ = chunks[ci]
                bi = C.wbf_i
                C.wbf_i = (C.wbf_i + 1) % 2
                wb = C.wbf[bi]
                wk = 'wbf%d' % bi
                for q in range(nkc // 4):
                    si = C.wst_i
                    C.wst_i = (C.wst_i + 1) % 3
                    ws = C.wst[si]
                    S.dd('sync', ws[:, :, 0:w], w_v[:, q * 4:(q + 1) * 4, c0:c0 + w], writes=['wst%d' % si], sem='w%d' % si)
                    if gain is not None:
                        S.do('gpsimd', 'tensor_tensor', reads=['wst%d' % si, gain_key], writes=[wk], out=wb[:, q * 4:(q + 1) * 4, 0:w], in0=ws[:, :, 0:w],
                             in1=gain[:, q * 4:(q + 1) * 4].rearrange("p (a o) -> p a o", o=1).to_broadcast([128, 4, w]), op=ALU.mult)
                    else:
                        S.do('gpsimd', 'tensor_copy', reads=['wst%d' % si], writes=[wk], out=wb[:, q * 4:(q + 1) * 4, 0:w], in_=ws[:, :, 0:w])
                return wb, wk
            nxt = load_chunk(0)
            for ci in range(len(chunks)):
                wb, wk = nxt
                if ci + 1 < len(chunks):
                    nxt = load_chunk(ci + 1)
                c0, w, tag = chunks[ci]
                for t in range(nt):
                    pb, pk = next_bank()
                    for kc in range(nkc):
                        S.do('tensor', 'matmul', reads=[lhs_key(t), wk], writes=[pk], out=pb[:, 0:w], lhsT=lhs(t, kc), rhs=wb[:, kc, 0:w], start=(kc == 0), stop=(kc == nkc - 1))
                    epi(ci, t, pb, pk, c0, w, tag)


        def epi_act_store(dst, rstd, rkey, dkey):
            def epi(ci, t, pb, pk, c0, w, tag):
                func, d0 = tag
                zi = C.zst_i
                C.zst_i = (C.zst_i + 1) % 4
                zb = C.zst[zi]
                S.do('scalar', 'activation', reads=[pk, rkey], writes=['zst%d' % zi], out=zb[:, 0:w], in_=pb[:, 0:w], func=func, scale=rstd[:, t:t + 1])
                S.dd('sync', dst(t)[:, d0:d0 + w], zb[:, 0:w], reads=['zst%d' % zi], writes=[(dkey, t, d0)], sem='zo%d' % zi)
            return epi

        lbst = ExitStack()
        lb_bc = lbst.enter_context(nc.sbuf_tensor("lb_bc", [128, 1024], F32))
        oml_bc = lbst.enter_context(nc.sbuf_tensor("oml_bc", [128, 1024], F32))
        with ExitStack() as ph:
            lraw = ph.enter_context(nc.sbuf_tensor("lraw", [128, 2, 1024], F32))
            S.dd('sync', lraw[:], lbl.rearrange("(o a) d -> o a d", o=1).to_broadcast([128, 2, 1024]), writes=['lraw'], sem='c')
            S.do('vector', 'tensor_tensor', reads=['lraw'], writes=['lb_bc'], out=lb_bc[:], in0=lraw[:, 0, :], in1=lraw[:, 1, :], op=ALU.subtract)
            S.do('scalar', 'activation', reads=['lb_bc'], writes=['lb_bc'], out=lb_bc[:], in_=lb_bc[:], func=AF.Sigmoid)
            S.do('vector', 'tensor_scalar', reads=['lb_bc'], writes=['oml_bc'], out=oml_bc[:], in0=lb_bc[:], scalar1=-1.0, scalar2=1.0, op0=ALU.mult, op1=ALU.add)
            S.barrier()

        w_v = w_in.rearrange("(kc p) n -> p kc n", p=128)

        C.nr_n = 0

        def alloc_nr(psb):
            C.nr_n += 1
            u = "_%d" % C.nr_n
            C.tmpa = psb("tmpa" + u, [128, 1024]); C.tmpb = psb("tmpb" + u, [128, 1024])
            C.hss = psb("hss" + u, [128, 32]); C.hrs = psb("hrs" + u, [128, 32])
            C.cst = [psb("cst%d" % i + u, [128, 64]) for i in range(2)]
            C.tab = [psb("tab%d" % i + u, [128, 4, 4, 32]) for i in range(2)]
            C.rowb = [psb("rowb%d" % i + u, [128, 3, 512]) for i in range(2)]

        def norm_rope(src, dst, H, tb, tk, br, scale, rk, wk):
            tmpa, tmpb, hss, hrs = C.tmpa, C.tmpb, C.hss, C.hrs
            n = H * 64
            sq = tmpa[:, 0:n].rearrange("p (h d) -> p h d", d=64)
            S.do('vector', 'tensor_tensor', reads=rk, writes=['tmpa'], out=sq, in0=src, in1=src, op=ALU.mult)
            S.do('vector', 'tensor_reduce', reads=['tmpa'], writes=['hss'], out=hss[:, 0:H], in_=sq, axis=AX.X, op=ALU.add)
            S.do('scalar', 'activation', reads=['hss'], writes=['hrs'], out=hrs[:, 0:H], in_=hss[:, 0:H], func=AF.Sqrt, scale=1.0 / 64, bias=EPS)
            S.do('vector', 'reciprocal', reads=['hrs'], writes=['hrs'], out=hrs[:, 0:H], in_=hrs[:, 0:H])
            if scale != 1.0:
                S.do('vector', 'tensor_scalar', reads=['hrs'], writes=['hrs'], out=hrs[:, 0:H], in0=hrs[:, 0:H], scalar1=scale, scalar2=None, op0=ALU.mult)
            x1 = src[:, :, 0:32]
            x2 = src[:, :, 32:64]
            T = lambda k: tb[:, br, k, :].rearrange("p (o d) -> p o d", o=1).to_broadcast([128, H, 32])
            a = tmpa[:, 0:H * 32].rearrange("p (h d) -> p h d", d=32)
            b = tmpb[:, 0:H * 32].rearrange("p (h d) -> p h d", d=32)
            S.do('vector', 'tensor_tensor', reads=rk + [tk], writes=['tmpa'], out=a, in0=x1, in1=T(0), op=ALU.mult)
            S.do('vector', 'tensor_tensor', reads=rk + [tk], writes=['tmpb'], out=b, in0=x2, in1=T(1), op=ALU.mult)
            S.do('vector', 'tensor_tensor', reads=['tmpa', 'tmpb'], writes=wk, out=dst[:, :, 0:32], in0=a, in1=b, op=ALU.subtract)
            S.do('vector', 'tensor_tensor', reads=rk + [tk], writes=['tmpa'], out=a, in0=x2, in1=T(2), op=ALU.mult)
            S.do('vector', 'tensor_tensor', reads=rk + [tk], writes=['tmpb'], out=b, in0=x1, in1=T(3), op=ALU.mult)
            S.do('vector', 'tensor_tensor', reads=['tmpa', 'tmpb'], writes=wk, out=dst[:, :, 32:64], in0=a, in1=b, op=ALU.add)
            S.do('vector', 'tensor_tensor', reads=wk + ['hrs'], writes=wk, out=dst, in0=dst, in1=hrs[:, 0:H].rearrange("p (h o) -> p h o", o=1).to_broadcast([128, H, 64]), op=ALU.mult)

        def make_tables(cs_dram, T, par):
            cb = C.cst[par]
            tb = C.tab[par]
            S.dd('sync', cb[:], cs_dram[T], writes=['cst%d' % par], sem='cs%d' % par)
            cosb = cb[:, 0:32].rearrange("p (o d) -> p o d", o=1).to_broadcast([128, 4, 32])
            sinb = cb[:, 32:64].rearrange("p (o d) -> p o d", o=1).to_broadcast([128, 4, 32])
            for k, (tr, gs) in enumerate([(cosb, 0), (sinb, 32), (cosb, 32), (sinb, 0)]):
                S.do('gpsimd', 'tensor_tensor', reads=['cst%d' % par, 'qkg_t'], writes=['tab%d' % par], out=tb[:, :, k, :], in0=qkg_t[:, :, gs:gs + 32], in1=tr, op=ALU.mult)
            return tb, 'tab%d' % par

        def kv_rows(zb, zk, base, tb, tk, rb, rbk):
            for br in range(3):
                o = base + br * 512
                norm_rope(zb[:, o:o + 256].rearrange("p (h d) -> p h d", d=64), rb[:, br, 0:256].rearrange("p (h d) -> p h d", d=64), 4, tb, tk, 1 + br, 1.0, [zk], [rbk])
                S.do('gpsimd', 'tensor_copy', reads=[zk], writes=[rbk], out=rb[:, br, 256:512], in_=zb[:, o + 256:o + 512])

        if 'own' in phases:
            with ExitStack() as ph:
                psb = lambda name, shape, dt=F32: ph.enter_context(nc.sbuf_tensor(name, list(shape), dt))
                xT = psb("xT", [128, NT, KC, 128], BF16)
                ssq = psb("ssq", [128, NT])
                rstd = psb("rstd", [128, NT])
                junk = psb("junk", [128, D])
                xst = [psb("xst%d" % i, [128, D]) for i in range(2)]
                make_xT(x_own, 0, NT, xT, ssq, rstd, 'o', xst, junk)
                chunks = [(c0, w, (f, c0)) for (c0, w, f) in chunk_list()]
                gemm(lambda t, kc: xT[:, t, kc, :], lambda t: 'oxT%d' % t, NT, KC, w_v, chunks, g1t, 'g1t',
                     epi_act_store(lambda t: zs[t], rstd, 'orstd', 'zs'))
                S.barrier()
            with ExitStack() as ph:
                psb = lambda name, shape, dt=F32: ph.enter_context(nc.sbuf_tensor(name, list(shape), dt))
                alloc_nr(psb)
                zq = [psb("zq%d" % i, [128, 2560]) for i in range(2)]
                qf = psb("qf", [128, 1024])
                qTb = [psb("qTb%d" % i, [64, 2048], BF16) for i in range(2)]
                zs_keys = [('zs', None, c[0]) for c in chunk_list()[0:5]]
                for t in range(NT):
                    par = t % 2
                    zb, zk = zq[par], 'zq%d' % par
                    rb, rbk = C.rowb[par], 'rowb%d' % par
                    qb, qbk = qTb[par], 'qTb%d' % par
                    S.dd('sync', zb[:], zs[t, :, 0:2560], reads=[('zs', t, k[2]) for k in zs_keys], writes=[zk], sem='zq%d' % par)
                    tb, tk = make_tables(cs_own, t, par)
                    norm_rope(zb[:, 0:1024].rearrange("p (h d) -> p h d", d=64), qf[:].rearrange("p (h d) -> p h d", d=64), 16, tb, tk, 0, 0.125, [zk], ['qf'])
                    for h4 in range(4):
                        pb, pk = next_bank()
                        for hh in range(4):
                            h = h4 * 4 + hh
                            S.do('tensor', 'transpose', reads=['qf', 'ident'], writes=[pk], out=pb[0:64, hh * 128:(hh + 1) * 128], in_=qf[:, h * 64:(h + 1) * 64], identity=ident[:])
                        S.do('scalar', 'activation', reads=[pk], writes=[qbk], out=qb[:, h4 * 512:(h4 + 1) * 512], in_=pb[0:64, :], func=AF.Copy)
                    S.dd('sync', qT_scr[t], qb[:], reads=[qbk], writes=[('qT', t)], sem='qo')
                    kv_rows(zb, zk, 1024, tb, tk, rb, rbk)
                    S.dd('sync', rows_own[t], rb[:], reads=[rbk], writes=[('rows_own', t)], sem='ro')
                S.barrier()

        if 'seq' in phases:
            seq_chunks = []
            for (c0, w, f) in chunk_list():
                if O_KV <= c0 < O_KV + 1536:
                    seq_chunks.append((c0, w, (f, c0 - O_KV)))
                elif O_HF <= c0 < O_HF + 1024:
                    seq_chunks.append((c0, w, (f, 1536 + c0 - O_HF)))
                elif O_HI <= c0 < O_HI + 1024:
                    seq_chunks.append((c0, w, (f, 2560 + c0 - O_HI)))
            with ExitStack() as ph:
                psb = lambda name, shape, dt=F32: ph.enter_context(nc.sbuf_tensor(name, list(shape), dt))
                xT = psb("sxT", [128, 16, KC, 128], BF16)
                ssq = psb("sssq", [128, 16])
                rstd = psb("srstd", [128, 16])
                junk = psb("sjunk", [128, D])
                xst = [psb("sxst%d" % i, [128, D]) for i in range(2)]
                for half in range(2):
                    make_xT(x_seq, half * 16, 16, xT, ssq, rstd, 's', xst, junk)
                    gemm(lambda t, kc: xT[:, t, kc, :], lambda t: 'sxT%d' % t, 16, KC, w_v, seq_chunks, g1t, 'g1t',
                         epi_act_store(lambda t, half=half: zseq[half * 16 + t], rstd, 'srstd', 'zseq%d' % half))
                S.barrier()
            with ExitStack() as ph:
                psb = lambda name, shape, dt=F32: ph.enter_context(nc.sbuf_tensor(name, list(shape), dt))
                alloc_nr(psb)
                zsq = [psb("zsq%d" % i, [128, 3584]) for i in range(2)]
                ktb = [psb("ktb%d" % i, [64, 4, 4, 128], BF16) for i in range(2)]
                vab = [psb("vab%d" % i, [128, 2, 4, 65], BF16) for i in range(2)]
                Sst = psb("Sst", [128, 8, 128])
                Sacc = [psb("Sacc%d" % i, [128, 2, 1024], BF16) for i in range(2)]
                ff = psb("ff", [128, 1024])
                logf = psb("logf", [128, 1024])
                eR = psb("eR", [128, 1024])
                khat = psb("khat", [128, 1024], BF16)
                hvb = psb("hvb", [128, 1024], BF16)
                ea = psb("ea", [128, 8, 2])
                stmp = psb("stmp", [128, 1024])
                S.do('vector', 'memset', writes=['Sst'], ap=Sst[:], constant=0.0)
                for i in range(2):
                    S.do('gpsimd', 'memset', writes=['vab%d' % i], ap=vab[i][:], constant=1.0)
                for T in range(NSEQ):
                    par = T % 2
                    zb, zk = zsq[par], 'zsq%d' % par
                    rb, rbk = C.rowb[par], 'rowb%d' % par
                    S.dd('sync', zb[:], zseq[T], reads=[('zseq%d' % (T // 16), T % 16, d0) for (_, _, (_, d0)) in seq_chunks], writes=[zk], sem='zq%d' % par)
                    tb, tk = make_tables(cs_seq, T, par)
                    kv_rows(zb, zk, 0, tb, tk, rb, rbk)
                    kb, kbk = ktb[par], 'ktb%d' % par
                    for slot, (br, src_o) in enumerate(((0, 0), (0, 256), (1, 0), (2, 0))):
                        pb, pk = next_bank()
                        for g in range(4):
                            S.do('tensor', 'transpose', reads=[rbk, 'ident'], writes=[pk], out=pb[0:64, g * 128:(g + 1) * 128], in_=rb[:, br, src_o + g * 64:src_o + (g + 1) * 64], identity=ident[:])
                        S.do('scalar', 'activation', reads=[pk], writes=[kbk], out=kb[:, slot, :, :], in_=pb[0:64, :].rearrange("p (g k) -> p g k", g=4), func=AF.Copy)
                    for slot in range(4):
                        S.dd('sync', kT_scr[slot, :, :, T * 128:(T + 1) * 128], kb[:, slot, :, :], reads=[kbk], writes=[('kT', T, slot)], sem='ko')
                    vb, vbk = vab[par], 'vab%d' % par
                    for br in (1, 2):
                        S.do('gpsimd', 'tensor_copy', reads=[rbk], writes=[vbk], out=vb[:, br - 1, :, 0:64], in_=rb[:, br, 256:512].rearrange("p (g d) -> p g d", g=4))
                    S.dd('sync', v_scr[T], vb[:], reads=[vbk], writes=[('v', T)], sem='vo')
                    S.do('vector', 'tensor_tensor', reads=[zk, 'oml_bc'], writes=['ff'], out=ff[:], in0=zb[:, 1536:2560], in1=oml_bc[:], op=ALU.mult)
                    S.do('vector', 'tensor_tensor', reads=['ff', 'lb_bc'], writes=['ff'], out=ff[:], in0=ff[:], in1=lb_bc[:], op=ALU.add)
                    S.do('scalar', 'activation', reads=['ff'], writes=['logf'], out=logf[:], in_=ff[:], func=AF.Ln)
                    S.do('gpsimd', 'tensor_scalar', reads=['ff'], writes=['ff'], out=ff[:], in0=ff[:], scalar1=-1.0, scalar2=1.0, op0=ALU.mult, op1=ALU.add)
                    for hh in range(2):
                        pb, pk = next_bank()
                        S.do('tensor', 'matmul', reads=['U64', 'logf'], writes=[pk], out=pb[:], lhsT=U64[:], rhs=logf[:, hh * 512:(hh + 1) * 512], start=True, stop=True)
                        S.do('scalar', 'activation', reads=[pk], writes=['eR'], out=eR[:, hh * 512:(hh + 1) * 512], in_=pb[:], func=AF.Exp)
                    S.do('vector', 'tensor_tensor', reads=['ff', 'eR'], writes=['khat'], out=khat[:], in0=ff[:], in1=eR[:], op=ALU.mult)
                    S.do('gpsimd', 'tensor_copy', reads=[zk], writes=['hvb'], out=hvb[:], in_=zb[:, 2560:3584])
                    pb, pk = next_bank()
                    for h in range(8):
                        S.do('tensor', 'matmul', reads=['logf', 'G2'], writes=[pk], out=pb[:, h * 2:h * 2 + 2], lhsT=logf[:, h * 128:(h + 1) * 128], rhs=G2[:], start=True, stop=True)
                    S.do('scalar', 'activation', reads=[pk], writes=['ea'], out=ea[:].rearrange("p h c -> p (h c)"), in_=pb[:, 0:16], func=AF.Exp)
                    for cc in range(2):
                        i_own, o = T // 4, T % 4
                        sa, sak = Sacc[i_own % 2], 'Sacc%d' % (i_own % 2)
                        if o == 0:
                            S.do('vector', 'tensor_scalar', reads=['Sst', 'jsel_t'], writes=[sak], out=sa[:, cc, :], in0=Sst[:].rearrange("p h d -> p (h d)"), scalar1=jsel_t[:, o:o + 1], scalar2=None, op0=ALU.mult)
                        else:
                            S.do('vector', 'tensor_scalar', reads=['Sst', 'jsel_t'], writes=['stmp'], out=stmp[:], in0=Sst[:].rearrange("p h d -> p (h d)"), scalar1=jsel_t[:, o:o + 1], scalar2=None, op0=ALU.mult)
                            S.do('vector', 'tensor_tensor', reads=['stmp', sak], writes=[sak], out=sa[:, cc, :], in0=sa[:, cc, :], in1=stmp[:], op=ALU.add)
                        for h4 in range(2):
                            pb, pk = next_bank()
                            for hh in range(4):
                                h = h4 * 4 + hh
                                S.do('tensor', 'matmul', reads=['khat', 'hvb'], writes=[pk], out=pb[:, hh * 128:(hh + 1) * 128], lhsT=khat[cc * 64:(cc + 1) * 64, h * 128:(h + 1) * 128],
                                     rhs=hvb[cc * 64:(cc + 1) * 64, h * 128:(h + 1) * 128], start=True, stop=True)
                            for hh in range(4):
                                h = h4 * 4 + hh
                                S.do('vector', 'scalar_tensor_tensor', reads=['Sst', 'ea', pk], writes=['Sst'], out=Sst[:, h, :], in0=Sst[:, h, :], scalar=ea[:, h, cc:cc + 1], in1=pb[:, hh * 128:(hh + 1) * 128], op0=ALU.mult, op1=ALU.add)
                    if T % 4 == 3:
                        sa, sak = Sacc[(T // 4) % 2], 'Sacc%d' % ((T // 4) % 2)
                        S.dd('sync', sown_scr[:, 2 * (T // 4):2 * (T // 4) + 2, :], sa[:], reads=[sak], writes=[('sown', T // 4)], sem='so')
                S.dd('sync', p_state.rearrange("h k v -> k h v"), Sst[:], reads=['Sst'], writes=['p_state'], sem='po')
                S.barrier()


        if 'hgrn' in phases:
            with ExitStack() as ph:
                psb = lambda name, shape, dt=F32: ph.enter_context(nc.sbuf_tensor(name, list(shape), dt))
                Dm = psb("Dm", [128, 2, 128]); Pm = psb("Pm", [128, 2, 16]); MT = psb("MT", [128, 2, 128])
                U8 = psb("U8", [128, 128]); G16 = psb("G16", [128, 16]); hng_t = psb("hng_t", [128, 1])
                ones_f = psb("ones_f", [128, 128])
                S.dd('sync', Dm[:], cDm.rearrange("a p t -> p a t"), writes=['Dm'], sem='c')
                S.dd('sync', Pm[:], cPm.rearrange("a p t -> p a t"), writes=['Pm'], sem='c')
                S.dd('sync', MT[:], cMT.rearrange("a p t -> p a t"), writes=['MT'], sem='c')
                S.dd('sync', U8[:], cU8, writes=['U8'], sem='c')
                S.dd('sync', G16[:], cG16, writes=['G16'], sem='c')
                S.dd('sync', hng_t[:], hng, writes=['hng_t'], sem='c')
                S.do('gpsimd', 'memset', writes=['ones_f'], ap=ones_f[:], constant=1.0)
                zh = psb("zh", [128, 4096])
                logf = psb("hlogf", [128, 1024]); hk = psb("hhk", [128, 1024])
                ex = psb("hex", [128, 1024]); qh = psb("hqh", [128, 1024]); kh = psb("hkh", [128, 1024])
                qhT = psb("qhT", [128, 8, 128], BF16); khT = psb("khT", [128, 8, 128], BF16)
                vb = psb("hvb2", [128, 1024], BF16)
                attT = psb("attT", [128, 8, 128], BF16)
                ogT = psb("ogT", [128, 8, 128])
                Sp = psb("Sp", [128, 4, 8, 128], BF16)
                S0 = psb("S0", [128, 4, 8, 128])
                em = psb("hem", [128, 8, 16])
                oTs = psb("oTs", [128, 512]); sqT = psb("sqT", [128, 512]); rbc = psb("rbc", [128, 512])
                obT = psb("obT", [128, 8, 128], BF16)
                kmask = psb("kmask", [128, 1024], BF16)
                zh_keys = [c0 for (c0, w, f) in chunk_list() if O_HQ <= c0 < O_A]
                C.pb_n = 6
                for t in range(NT):
                    kind = 0 if t < 8 else 1
                    G = 2 if kind == 0 else 16
                    L = 128 // G
                    S.dd('sync', zh[:], zs[t, :, O_HQ:O_A], reads=[('zs', t, c0) for c0 in zh_keys], writes=['zh'], sem='zh')
                    hq = zh[:, 0:1024]; sf = zh[:, 1024:2048]; hv = zh[:, 2048:3072]; og = zh[:, 3072:4096]
                    S.do('vector', 'tensor_tensor', reads=['zh', 'oml_bc'], writes=['hk'], out=hk[:], in0=sf, in1=oml_bc[:], op=ALU.mult)
                    S.do('vector', 'tensor_tensor', reads=['hk', 'lb_bc'], writes=['hk'], out=hk[:], in0=hk[:], in1=lb_bc[:], op=ALU.add)
                    S.do('scalar', 'activation', reads=['hk'], writes=['hlogf'], out=logf[:], in_=hk[:], func=AF.Ln)
                    S.do('gpsimd', 'tensor_scalar', reads=['hk'], writes=['hk'], out=hk[:], in0=hk[:], scalar1=-1.0, scalar2=1.0, op0=ALU.mult, op1=ALU.add)
                    S.do('gpsimd', 'tensor_copy', reads=['zh'], writes=['hvb2'], out=vb[:], in_=hv)
                    for hh in range(2):
                        pb, pk = next_bank()
                        S.do('tensor', 'matmul', reads=['Dm', 'hlogf'], writes=[pk], out=pb[:], lhsT=Dm[:, kind, :], rhs=logf[:, hh * 512:(hh + 1) * 512], start=True, stop=True)
                        S.do('vector', 'tensor_scalar', reads=[pk], writes=['hex'], out=ex[:, hh * 512:(hh + 1) * 512], in0=pb[:], scalar1=40.0, scalar2=None, op0=ALU.min)
                        S.do('scalar', 'activation', reads=['hex'], writes=['hex'], out=ex[:, hh * 512:(hh + 1) * 512], in_=ex[:, hh * 512:(hh + 1) * 512], func=AF.Exp)
                        S.do('vector', 'tensor_tensor', reads=['hex', 'zh'], writes=['hqh'], out=qh[:, hh * 512:(hh + 1) * 512], in0=ex[:, hh * 512:(hh + 1) * 512], in1=hq[:, hh * 512:(hh + 1) * 512], op=ALU.mult)
                        S.do('vector', 'tensor_scalar', reads=[pk, 'hqh'], writes=['hex'], out=ex[:, hh * 512:(hh + 1) * 512], in0=pb[:], scalar1=-1.0, scalar2=40.0, op0=ALU.mult, op1=ALU.min)
                        S.do('scalar', 'activation', reads=['hex'], writes=['hex'], out=ex[:, hh * 512:(hh + 1) * 512], in_=ex[:, hh * 512:(hh + 1) * 512], func=AF.Exp)
                        S.do('vector', 'tensor_tensor', reads=['hex', 'hk'], writes=['hkh'], out=kh[:, hh * 512:(hh + 1) * 512], in0=ex[:, hh * 512:(hh + 1) * 512], in1=hk[:, hh * 512:(hh + 1) * 512], op=ALU.mult)
                    for (src, sk_, dst, dk_) in ((qh, 'hqh', qhT, 'qhT'), (kh, 'hkh', khT, 'khT'), (None, 'zh', ogT, 'ogT')):
                        for h4 in range(2):
                            pb, pk = next_bank()
                            for hh in range(4):
                                h = h4 * 4 + hh
                                in_ap = og[:, h * 128:(h + 1) * 128] if src is None else src[:, h * 128:(h + 1) * 128]
                                S.do('tensor', 'transpose', reads=[sk_, 'ident'], writes=[pk], out=pb[:, hh * 128:(hh + 1) * 128], in_=in_ap, identity=ident[:])
                            S.do('scalar', 'activation', reads=[pk], writes=[dk_], out=dst[:, h4 * 4:(h4 + 1) * 4, :], in_=pb[:].rearrange("p (a b) -> p a b", a=4), func=AF.Copy)
                    for h4 in range(2):
                        pb, pk = next_bank()
                        for hh in range(4):
                            h = h4 * 4 + hh
                            S.do('tensor', 'matmul', reads=['khT', 'qhT'], writes=[pk], out=pb[:, hh * 128:(hh + 1) * 128], lhsT=khT[:, h, :], rhs=qhT[:, h, :], start=True, stop=True)
                        S.do('vector', 'tensor_tensor', reads=[pk, 'MT'], writes=['attT'], out=attT[:, h4 * 4:(h4 + 1) * 4, :], in0=pb[:].rearrange("p (a b) -> p a b", a=4),
                             in1=MT[:, kind, :].rearrange("p (o t) -> p o t", o=1).to_broadcast([128, 4, 128]), op=ALU.mult)
                    pb, pk = next_bank()
                    for h in range(8):
                        S.do('tensor', 'matmul', reads=['hlogf', 'Pm'], writes=[pk], out=pb[:, h * 16:(h + 1) * 16], lhsT=logf[:, h * 128:(h + 1) * 128], rhs=Pm[:, kind, :], start=True, stop=True)
                    S.do('scalar', 'activation', reads=[pk], writes=['hem'], out=em[:].rearrange("p h g -> p (h g)"), in_=pb[:, 0:128], func=AF.Exp)
                    if kind == 1:
                        for hh in range(2):
                            pb, pk = next_bank()
                            S.do('tensor', 'matmul', reads=['U8', 'hlogf'], writes=[pk], out=pb[:], lhsT=U8[:], rhs=logf[:, hh * 512:(hh + 1) * 512], start=True, stop=True)
                            S.do('scalar', 'activation', reads=[pk], writes=['hex'], out=ex[:, hh * 512:(hh + 1) * 512], in_=pb[:], func=AF.Exp)
                        S.do('vector', 'tensor_tensor', reads=['hex', 'hk'], writes=['hkh'], out=kh[:], in0=ex[:], in1=hk[:], op=ALU.mult)
                        pb, pk = next_bank()
                        for h in range(8):
                            S.do('tensor', 'matmul', reads=['hlogf', 'G16'], writes=[pk], out=pb[:, h * 16:(h + 1) * 16], lhsT=logf[:, h * 128:(h + 1) * 128], rhs=G16[:], start=True, stop=True)
                        S.do('scalar', 'activation', reads=[pk], writes=['hem'], out=em[:].rearrange("p h g -> p (h g)"), in_=pb[:, 0:128], func=AF.Exp)
                    nb = 1 if kind == 0 else 4
                    gb = G // nb
                    oT_banks = [(pbank[6], 'pb6'), (pbank[7], 'pb7')]
                    for h4 in range(2):
                        pbo, pko = oT_banks[h4]
                        for hh in range(4):
                            h = h4 * 4 + hh
                            S.do('tensor', 'matmul', reads=['hvb2', 'attT'], writes=[pko], out=pbo[:, hh * 128:(hh + 1) * 128], lhsT=vb[:, h * 128:(h + 1) * 128], rhs=attT[:, h, :], start=(hh == 0), stop=False)
                    for b_ in range(nb):
                        if kind == 0:
                            S.dd('sync', Sp[:, 0:2, :, :].rearrange("p g h d -> p g (h d)"), sown_scr[:, 2 * t:2 * t + 2, :], reads=[('sown', t)], writes=['Sp'], sem='sp')
                            for g in range(2):
                                for h in range(8):
                                    S.do('vector', 'tensor_scalar', reads=['Sp', 'hem'], writes=['Sp'], out=Sp[:, g, h, :], in0=Sp[:, g, h, :], scalar1=em[:, h, g:g + 1], scalar2=None, op0=ALU.mult)
                        else:
                            S.dd('sync', S0[:].rearrange("p g h d -> p (g h) d"), st_in[b_ * 4:(b_ + 1) * 4].rearrange("g h k d -> k (g h) d"), reads=[], writes=['S0'], sem='sp')
                            S.do('gpsimd', 'tensor_copy', reads=['S0'], writes=['Sp'], out=Sp[:], in_=S0[:])
                        for gl in range(gb):
                            g = b_ * gb + gl
                            for h in range(8):
                                pbo, pko = oT_banks[h // 4]
                                hh = h % 4
                                S.do('tensor', 'matmul', reads=['Sp', 'qhT'], writes=[pko], out=pbo[:, hh * 128 + g * L:hh * 128 + (g + 1) * L], lhsT=Sp[:, gl, h, :], rhs=qhT[:, h, g * L:(g + 1) * L],
                                     start=False, stop=(b_ == nb - 1 and gl == gb - 1))
                        if kind == 1:
                            for gl in range(gb):
                                g = b_ * gb + gl
                                S.do('vector', 'tensor_scalar', reads=['hkh', 'G16'], writes=['kmask'], out=kmask[:], in0=kh[:], scalar1=G16[:, g:g + 1], scalar2=None, op0=ALU.mult)
                                for h4 in range(2):
                                    pb, pk = next_bank()
                                    for hh in range(4):
                                        h = h4 * 4 + hh
                                        S.do('tensor', 'matmul', reads=['kmask', 'hvb2'], writes=[pk], out=pb[:, hh * 128:(hh + 1) * 128], lhsT=kmask[:, h * 128:(h + 1) * 128], rhs=vb[:, h * 128:(h + 1) * 128], start=True, stop=True)
                                    for hh in range(4):
                                        h = h4 * 4 + hh
                                        S.do('vector', 'scalar_tensor_tensor', reads=['S0', 'hem', pk], writes=['S0'], out=S0[:, gl, h, :], in0=S0[:, gl, h, :], scalar=em[:, h, g:g + 1], in1=pb[:, hh * 128:(hh + 1) * 128], op0=ALU.mult, op1=ALU.add)
                            S.dd('sync', s_state[b_ * 4:(b_ + 1) * 4].rearrange("g h k d -> k (g h) d"), S0[:].rearrange("p g h d -> p (g h) d"), reads=['S0'], writes=[('s_state', b_)], sem='sso')
                    for h4 in range(2):
                        pbo, pko = oT_banks[h4]
                        S.do('scalar', 'activation', reads=[pko], writes=['oTs'], out=oTs[:], in_=pbo[:], func=AF.Copy)
                        S.do('vector', 'tensor_tensor', reads=['oTs'], writes=['sqT'], out=sqT[:], in0=oTs[:], in1=oTs[:], op=ALU.mult)
                        pb, pk = next_bank()
                        S.do('tensor', 'matmul', reads=['ones_f', 'sqT'], writes=[pk], out=pb[:], lhsT=ones_f[:], rhs=sqT[:], start=True, stop=True)
                        S.do('scalar', 'activation', reads=[pk], writes=['rbc'], out=rbc[:], in_=pb[:], func=AF.Sqrt, scale=1.0 / 128, bias=EPS)
                        S.do('vector', 'reciprocal', reads=['rbc'], writes=['rbc'], out=rbc[:], in_=rbc[:])
                        S.do('vector', 'tensor_tensor', reads=['rbc', 'oTs'], writes=['oTs'], out=oTs[:], in0=oTs[:], in1=rbc[:], op=ALU.mult)
                        S.do('vector', 'scalar_tensor_tensor', reads=['oTs', 'hng_t', 'ogT'], writes=['obT'], out=obT[:, h4 * 4:(h4 + 1) * 4, :].rearrange("p a b -> p (a b)"), in0=oTs[:], scalar=hng_t[:, 0:1],
                             in1=ogT[:, h4 * 4:(h4 + 1) * 4, :].rearrange("p a b -> p (a b)"), op0=ALU.mult, op1=ALU.mult)
                    S.dd('sync', obT_scr[t], obT[:], reads=['obT'], writes=[('obT', t)], sem='obo')
                    if t in (0, 3, 8):
                        dbg("obT%d" % t, obT[:], [128, 8, 128], BF16, ['obT'])
                S.barrier()
                C.pb_n = 8


        S.barrier()
        lbst.close()
        gst.close()
        if 'att' in phases:
            NEGB = -30000.0
            with ExitStack() as ph:
                psb = lambda name, shape, dt=F32: ph.enter_context(nc.sbuf_tensor(name, list(shape), dt))
                C.pb_n = 3
                ACC = [(pbank[3 + i], 'pb%d' % (3 + i)) for i in range(4)]
                MISC = (pbank[7], 'pb7')
                identb = psb("identb", [128, 128], BF16)
                S.do('vector', 'tensor_copy', reads=['ident'], writes=['identb'], out=identb[:], in_=ident[:])
                E = psb("E", [64, 4096], BF16)
                S.dd('sync', E[:], cE, writes=['E'], sem='c')
                selb = psb("selb", [128, 4, 512], BF16); winb = psb("winb", [128, 8, 512], BF16)
                S.dd('sync', selb[:], cselb.rearrange("a p c -> p a c"), writes=['selb'], sem='c')
                S.dd('sync', winb[:], cwinb.rearrange("a p c -> p a c"), writes=['winb'], sem='c')
                ovp = psb("ovp", [128, 2, 64], BF16); ovs = psb("ovs", [128, 64], BF16)
                S.dd('sync', ovp[:], covp.rearrange("a p c -> p a c"), writes=['ovp'], sem='c')
                S.dd('sync', ovs[:], covs, writes=['ovs'], sem='c')
                OH = psb("OH", [17, 16, 128], BF16); seqb = psb("seqb", [17, 512], BF16)
                S.dd('sync', OH[:], cOH, writes=['OH'], sem='c')
                S.dd('sync', seqb[:], cseqb, writes=['seqb'], sem='c')
                bdc = psb("bdc", [128, 512], BF16); wb0 = psb("wb0", [128, 512], BF16)
                S.dd('sync', bdc[:], cbdc, writes=['bdc'], sem='c')
                S.dd('sync', wb0[:], cwb0, writes=['wb0'], sem='c')
                w1b = psb("w1b", [64, 2, 32, 64], BF16); w2b = psb("w2b", [64, 2, 64], BF16); peb = psb("peb", [64, 2])
                with ExitStack() as ph2:
                    psb2 = lambda name, shape, dt=F32: ph2.enter_context(nc.sbuf_tensor(name, list(shape), dt))
                    w1f = psb2("w1f", [64, 2, 32, 64]); w2f = psb2("w2f", [64, 2, 64]); pef = psb2("pef", [64, 2, 32]); pebf = psb2("pebf", [64, 2, 32], BF16)
                    for kv in range(2):
                        S.dd('sync', w1f[:, kv], cw1[kv].rearrange("(l d) h -> d l h", d=64), writes=['w1f'], sem='c')
                        S.dma('sync', lambda e, kv=kv: e.dma_start(out=pef[:, kv], in_=cpe[kv].rearrange("l d -> d l"), allow_slow_non_contiguous=True), writes=['pef'], sem='c')
                    S.dd('sync', w2f[:], cw2.rearrange("a h d -> h a d"), writes=['w2f'], sem='c')
                    S.do('vector', 'tensor_copy', reads=['w1f'], writes=['w1b'], out=w1b[:], in_=w1f[:])
                    S.do('vector', 'tensor_copy', reads=['w2f'], writes=['w2b'], out=w2b[:], in_=w2f[:])
                    S.do('vector', 'tensor_copy', reads=['pef'], writes=['pebf'], out=pebf[:], in_=pef[:])
                    for kv in range(2):
                        pb, pk = next_bank()
                        for l in range(32):
                            S.do('tensor', 'matmul', reads=['w1b', 'pebf'], writes=[pk], out=pb[0:64, 0:1], lhsT=w1b[:, kv, l, :], rhs=pebf[:, kv, l:l + 1], start=(l == 0), stop=(l == 31))
                        S.do('vector', 'tensor_copy', reads=[pk], writes=['peb'], out=peb[:, kv:kv + 1], in_=pb[0:64, 0:1])
                    S.barrier()

                hsil = psb("hsil", [64, 256], BF16)

                def compress(XT, xkeys, n, kcT_dst, kkey, vc_dst, vkey):
                    for kv in range(2):
                        for g in range(4):
                            pb, pk = next_bank()
                            xt = XT(kv, g)
                            for l in range(32):
                                S.do('tensor', 'matmul', reads=xkeys + ['w1b'], writes=[pk], out=pb[0:64, 0:n], lhsT=w1b[:, kv, l, :], rhs=xt[:, l:l + 16 * (n - 1) + 1:16], start=(l == 0), stop=(l == 31))
                            S.do('scalar', 'activation', reads=[pk, 'peb'], writes=['hsil'], out=hsil[:, 0:n], in_=pb[0:64, 0:n], func=AF.Silu, bias=peb[:, kv:kv + 1])
                            pb2, pk2 = next_bank()
                            if kv == 0:
                                S.do('tensor', 'matmul', reads=['hsil', 'w2b'], writes=[pk2], out=pb2[0:64, 0:n], lhsT=w2b[:, 0, :], rhs=hsil[:, 0:n], start=True, stop=True)
                                S.do('vector', 'tensor_copy', reads=[pk2], writes=[kkey], out=kcT_dst(g), in_=pb2[0:64, 0:n])
                            else:
                                for ct in range((n + 127) // 128):
                                    m = min(128, n - ct * 128)
                                    S.do('tensor', 'matmul', reads=['hsil', 'w2b'], writes=[pk2], out=pb2[0:m, ct * 64:(ct + 1) * 64], lhsT=hsil[:, ct * 128:ct * 128 + m], rhs=w2b[:, 1, :], start=True, stop=True)
                                    S.do('vector', 'tensor_copy', reads=[pk2], writes=[vkey], out=vc_dst(g, ct, m), in_=pb2[0:m, ct * 64:(ct + 1) * 64])

                ptb = [psb("ptb%d" % i, [128, 512], BF16) for i in range(3)]
                C.pt_i = 0

                def attend_g(qT, qkey, jobs, naccs, gs):
                    started = set()
                    last = {}
                    for ji, jb in enumerate(jobs):
                        for g in gs:
                            for (aid, _, _) in jb['V'](g):
                                last[(aid, g)] = ji
                    for ji, jb in enumerate(jobs):
                        nk = jb['nk']
                        if 'prep' in jb:
                            jb['prep']()
                        for g in gs:
                            pb, pk = next_bank()
                            bl = jb['bias'](g)
                            S.do('tensor', 'matmul', reads=jb['keys'] + [qkey], writes=[pk], out=pb[0:nk, :], lhsT=jb['KT'](g), rhs=qT[:, g * 512:(g + 1) * 512], start=True, stop=(len(bl) == 0))
                            for bi, (bl_l, bl_r, bkeys) in enumerate(bl):
                                S.do('tensor', 'matmul', reads=bkeys, writes=[pk], out=pb[0:nk, :], lhsT=bl_l, rhs=bl_r, start=False, stop=(bi == len(bl) - 1))
                            pi = C.pt_i
                            C.pt_i = (C.pt_i + 1) % 3
                            pt, ptk = ptb[pi], 'ptb%d' % pi
                            S.do('scalar', 'activation', reads=[pk], writes=[ptk], out=pt[0:nk, :], in_=pb[0:nk, :], func=AF.Exp)
                            for (aid, vap, nV) in jb['V'](g):
                                ab, abk = naccs[aid][g]
                                for r in range(4):
                                    S.do('tensor', 'matmul', reads=[ptk] + jb['keys'], writes=[abk], out=ab[:, r * nV:(r + 1) * nV], lhsT=pt[0:nk, r * 128:(r + 1) * 128], rhs=vap,
                                         start=((aid, g) not in started), stop=(last[(aid, g)] == ji and r == 3))
                                    started.add((aid, g))

                qTt = [psb("qTt%d" % i, [64, 2048], BF16) for i in range(2)]
                gates = psb("gates", [128, 48])
                visadd = psb("visadd", [128, 2, 64])
                oa = psb("oa", [128, 1024])
                linv = psb("linv", [128, 4])
                imp = psb("imp", [128, 64]); imt = psb("imt", [128, 64])
                m8 = psb("m8", [128, 8]); sc2 = psb("sc2", [128, 64]); thr = psb("thr", [128, 1])
                selbT = [psb("selbT%d" % g, [64, 512], BF16) for g in range(4)]
                osc = psb("osc", [128, 4, 64])
                oaT = psb("oaT_a", [128, 8, 128], BF16)
                cmpb_t = psb("cmpb_t", [128, 2, 512], BF16)

                def topk_bias(t, g, imp_ap):
                    S.do('vector', 'tensor_tensor', reads=['imp', 'visadd'], writes=['imt'], out=imt[:], in0=imp_ap, in1=visadd[:, 0, :], op=ALU.mult)
                    S.do('vector', 'tensor_tensor', reads=['imt', 'visadd'], writes=['imt'], out=imt[:], in0=imt[:], in1=visadd[:, 1, :], op=ALU.add)
                    S.do('vector', 'max', reads=['imt'], writes=['m8'], out=m8[:], in_=imt[:])
                    S.do('vector', 'match_replace', reads=['imt', 'm8'], writes=['sc2'], out=sc2[:], in_to_replace=m8[:], in_values=imt[:], imm_value=-3.0e9)
                    S.do('vector', 'max', reads=['sc2'], writes=['m8'], out=m8[:], in_=sc2[:])
                    S.do('vector', 'tensor_reduce', reads=['m8'], writes=['thr'], out=thr[:], in_=m8[:], axis=AX.X, op=ALU.min)
                    S.do('vector', 'tensor_scalar', reads=['imt', 'thr'], writes=['sc2'], out=sc2[:], in0=imt[:], scalar1=thr[:, 0:1], scalar2=None, op0=ALU.is_ge)
                    S.do('vector', 'tensor_tensor', reads=['sc2', 'visadd'], writes=['sc2'], out=sc2[:], in0=sc2[:], in1=visadd[:, 0, :], op=ALU.mult)
                    S.do('vector', 'tensor_scalar', reads=['sc2'], writes=['sc2'], out=sc2[:], in0=sc2[:], scalar1=-NEGB, scalar2=NEGB, op0=ALU.mult, op1=ALU.add)
                    mb, mk = MISC
                    S.do('tensor', 'transpose', reads=['sc2', 'ident'], writes=[mk], out=mb[0:64, 0:128], in_=sc2[:], identity=ident[:])
                    for r in range(4):
                        S.do('scalar', 'activation', reads=[mk], writes=['selbT%d' % g], out=selbT[g][:, r * 128:(r + 1) * 128], in_=mb[0:64, 0:128], func=AF.Copy)

                def normalize(acc, g, br, nV):
                    ab, abk = acc
                    av = ab[:, 0:4 * nV].rearrange("p (r v) -> p r v", r=4)
                    S.do('vector', 'tensor_scalar', reads=[abk], writes=['linv'], out=linv[:], in0=av[:, :, 64], scalar1=1e-30, scalar2=None, op0=ALU.max)
                    S.do('vector', 'reciprocal', reads=['linv'], writes=['linv'], out=linv[:], in_=linv[:])
                    S.do('vector', 'tensor_tensor', reads=['linv', 'gates'], writes=['linv'], out=linv[:], in0=linv[:], in1=gates[:, br * 16 + g * 4:br * 16 + g * 4 + 4], op=ALU.mult)
                    dst = oa[:, g * 256:(g + 1) * 256].rearrange("p (r d) -> p r d", r=4)
                    lb_ = linv[:].rearrange("p (r o) -> p r o", o=1).to_broadcast([128, 4, 64])
                    if br == 0:
                        S.do('vector', 'tensor_tensor', reads=[abk, 'linv'], writes=['oa'], out=dst, in0=av[:, :, 0:64], in1=lb_, op=ALU.mult)
                    else:
                        S.do('vector', 'tensor_tensor', reads=[abk, 'linv'], writes=['osc'], out=osc[:], in0=av[:, :, 0:64], in1=lb_, op=ALU.mult)
                        S.do('vector', 'tensor_tensor', reads=['osc', 'oa'], writes=['oa'], out=dst, in0=dst, in1=osc[:], op=ALU.add)

                def cmp_finish_g(t, naccs, gs):
                    accs1, accs2 = naccs[0], naccs[1]
                    for g in gs:
                        ab, abk = accs1[g]
                        av = ab[:, 0:260].rearrange("p (r v) -> p r v", r=4)
                        S.do('vector', 'tensor_scalar', reads=[abk], writes=['linv'], out=linv[:], in0=av[:, :, 64], scalar1=1e-30, scalar2=None, op0=ALU.max)
                        S.do('vector', 'reciprocal', reads=['linv'], writes=['linv'], out=linv[:], in_=linv[:])
                        a2, a2k = accs2[g]
                        a2v = a2[:, 0:256].rearrange("p (r v) -> p r v", r=4)
                        S.do('vector', 'tensor_scalar', reads=[a2k, 'linv'], writes=['imp'], out=imp[:], in0=a2v[:, 0, :], scalar1=linv[:, 0:1], scalar2=None, op0=ALU.mult)
                        for r in range(1, 4):
                            S.do('vector', 'scalar_tensor_tensor', reads=[a2k, 'linv', 'imp'], writes=['imp'], out=imp[:], in0=a2v[:, r, :], scalar=linv[:, r:r + 1], in1=imp[:], op0=ALU.mult, op1=ALU.add)
                        topk_bias(t, g, imp[:])
                        normalize(accs1[g], g, 0, 65)

                def finish_tile(t):
                    for k2 in range(2):
                        pb, pk = next_bank()
                        for kk in range(4):
                            kc = k2 * 4 + kk
                            S.do('tensor', 'transpose', reads=['oa', 'ident'], writes=[pk], out=pb[:, kk * 128:(kk + 1) * 128], in_=oa[:, kc * 128:(kc + 1) * 128], identity=ident[:])
                        S.do('scalar', 'activation', reads=[pk], writes=['oaT_a'], out=oaT[:, k2 * 4:(k2 + 1) * 4, :], in_=pb[:].rearrange("p (a b) -> p a b", a=4), func=AF.Copy)
                    S.dd('sync', oaT_scr[t], oaT[:], reads=['oaT_a'], writes=[('oaT', t)], sem='oao')

                def load_tile(t):
                    qb, qk = qTt[t % 2], 'qTt%d' % (t % 2)
                    S.dd('sync', qb[:], qT_scr[t], reads=[('qT', t)], writes=[qk], sem='qt%d' % (t % 2))
                    S.dd('sync', gates[:], zs[t, :, O_GA:O_GA + 48], reads=[('zs', t, O_GA)], writes=['gates'], sem='c')
                    S.dd('sync', visadd[:], cvisadd[t], writes=['visadd'], sem='c')
                    return qb, qk

                if 'attp' in phases:
                  with ExitStack() as ph2:
                    psb2 = lambda name, shape, dt=F32: ph2.enter_context(nc.sbuf_tensor(name, list(shape), dt))
                    kcT = psb2("kcT", [64, 4, 256], BF16)
                    vca = psb2("vca", [128, 2, 4, 65], BF16)
                    S.do('vector', 'memset', writes=['kcT'], ap=kcT[:], constant=0.0)
                    S.do('vector', 'memset', writes=['vca'], ap=vca[:], constant=0.0)
                    S.do('vector', 'memset', writes=['vca'], ap=vca[:, :, :, 64:65], constant=1.0)
                    with ExitStack() as ph3:
                        XT = ph3.enter_context(nc.sbuf_tensor("XTp", [64, 2, 4, 4096], BF16))
                        for kv in range(2):
                            S.dd('sync', XT[:, kv], kT_scr[kv], reads=[('kT', T, kv) for T in range(NSEQ)], writes=['XTp'], sem='c')
                        compress(lambda kv, g: XT[:, kv, g, :], ['XTp'], 255, lambda g: kcT[:, g, 0:255], 'kcT', lambda g, ct, m: vca[0:m, ct, g, 0:64], 'vca')
                        dbg("kcT", kcT[:], [64, 4, 256], BF16, ['kcT'])
                        dbg("vca", vca[:], [128, 2, 4, 65], BF16, ['vca'])
                        S.barrier()
                    KTs = psb2("KTs", [64, 4, 4096], BF16); KTw = psb2("KTw", [64, 4, 4096], BF16)
                    Vsw = psb2("Vsw", [128, NSEQ, 2, 4, 65], BF16)
                    S.dd('sync', KTs[:], kT_scr[2], reads=[('kT', T, 2) for T in range(NSEQ)], writes=['KTs'], sem='c')
                    S.dd('sync', KTw[:], kT_scr[3], reads=[('kT', T, 3) for T in range(NSEQ)], writes=['KTw'], sem='c')
                    for T in range(NSEQ):
                        S.dd('sync', Vsw[:, T].rearrange("p a g d -> p (a g d)"), v_scr[T].rearrange("p a g d -> p (a g d)"), reads=[('v', T)], writes=['Vsw'], sem='c')
                    for t in range(8):
                        qb, qk = load_tile(t)
                        S.dd('sync', cmpb_t[:], ccmpb[t].rearrange("a p c -> p a c"), writes=['cmpb_t'], sem='c')
                        for gp in range(2):
                            gs = (2 * gp, 2 * gp + 1)
                            jobs = []
                            for ct in range(2):
                                jobs.append(dict(nk=128, keys=['kcT', 'vca', 'ovp', 'cmpb_t', 'identb'],
                                                 KT=lambda g, ct=ct: kcT[:, g, ct * 128:(ct + 1) * 128],
                                                 V=lambda g, ct=ct: [(0, vca[:, ct, g, :], 65), (1, ovp[:, ct, :], 64)],
                                                 bias=lambda g, ct=ct: [(identb[:], cmpb_t[:, ct, :], ['identb', 'cmpb_t'])]))
                            naccs = {0: {gs[0]: ACC[0], gs[1]: ACC[1]}, 1: {gs[0]: ACC[2], gs[1]: ACC[3]}}
                            attend_g(qb, qk, jobs, naccs, gs)
                            cmp_finish_g(t, naccs, gs)
                        if t in (0, 3, 7):
                            dbg("oc%d" % t, oa[:], [128, 1024], F32, ['oa'])
                            dbg("selb%d" % t, selbT[1][:], [64, 512], BF16, ['selbT1'])
                        i = t
                        jobs = []
                        for kt in range(4 * i + 4):
                            def bias(g, kt=kt, i=i):
                                bl = [(E[:, kt * 128:(kt + 1) * 128], selbT[g][:], ['E', 'selbT%d' % g])]
                                if kt >= 4 * i:
                                    bl.append((identb[:], selb[:, kt - 4 * i, :], ['identb', 'selb']))
                                return bl
                            jobs.append(dict(nk=128, keys=['KTs', 'Vsw'], KT=lambda g, kt=kt: KTs[:, g, kt * 128:(kt + 1) * 128],
                                             V=lambda g, kt=kt: [(0, Vsw[:, kt, 0, g, :], 65)], bias=bias))
                        naccs = {0: {g: ACC[g] for g in range(4)}}
                        attend_g(qb, qk, jobs, naccs, (0, 1, 2, 3))
                        for g in range(4):
                            normalize(ACC[g], g, 1, 65)
                        jobs = []
                        for o in range(-4, 4):
                            kt = 4 * i + o
                            if kt < 0:
                                continue
                            jobs.append(dict(nk=128, keys=['KTw', 'Vsw'], KT=lambda g, kt=kt: KTw[:, g, kt * 128:(kt + 1) * 128],
                                             V=lambda g, kt=kt: [(0, Vsw[:, kt, 1, g, :], 65)],
                                             bias=lambda g, o=o: [(identb[:], winb[:, o + 4, :], ['identb', 'winb'])]))
                        attend_g(qb, qk, jobs, naccs, (0, 1, 2, 3))
                        for g in range(4):
                            normalize(ACC[g], g, 2, 65)
                        if t in (0, 3, 7):
                            dbg("oa%d" % t, oa[:], [128, 1024], F32, ['oa'])
                        finish_tile(t)
                    S.barrier()

                if 'atts' in phases:
                  with ExitStack() as ph2:
                    psb2 = lambda name, shape, dt=F32: ph2.enter_context(nc.sbuf_tensor(name, list(shape), dt))
                    t = 8
                    for s_ in range(16):
                        for q4 in range(4):
                            S.dd('sync', s_win[s_, q4 * 126:(q4 + 1) * 126, :], win_in[s_, 8 + q4 * 126:8 + (q4 + 1) * 126, :], writes=[('s_win_a', s_, q4)], sem='swo')
                    pti = psb2("pti", [128, 256], I32); ptf = psb2("ptf", [128, 256]); idx = psb2("idx", [128, 256], I32)
                    pio = psb2("pio", [128, 1], I32); piof = psb2("piof", [128, 1])
                    S.dd('sync', pti[:], pt_in.to_broadcast([128, 256]), writes=['pti'], sem='c')
                    S.do('gpsimd', 'iota', writes=['pio'], out=pio[:], pattern=[[0, 1]], base=0, channel_multiplier=1)
                    S.do('vector', 'tensor_copy', reads=['pio'], writes=['piof'], out=piof[:], in_=pio[:])
                    S.do('vector', 'tensor_copy', reads=['pti'], writes=['ptf'], out=ptf[:], in_=pti[:])
                    S.do('vector', 'tensor_scalar', reads=['ptf', 'piof'], writes=['ptf'], out=ptf[:], in0=ptf[:], scalar1=128.0, scalar2=piof[:, 0:1], op0=ALU.mult, op1=ALU.add)
                    S.do('vector', 'tensor_copy', reads=['ptf'], writes=['idx'], out=idx[:], in_=ptf[:])
                    pgb = [psb2("pgb%d" % i, [128, 512]) for i in range(3)]
                    ktp = [psb2("ktp%d" % i, [64, 4, 128], BF16) for i in range(3)]
                    vap = [psb2("vap%d" % i, [128, 4, 65], BF16) for i in range(3)]
                    for i in range(3):
                        S.do('gpsimd', 'memset', writes=['vap%d' % i], ap=vap[i][:], constant=1.0)
                    C.pg_i = 0

                    def fetch_page(cache, s, p):
                        bi = C.pg_i
                        C.pg_i = (bi + 1) % 3
                        pg, pgk = pgb[bi], 'pgb%d' % bi
                        col = s * 16 + p
                        S.dma('gpsimd', lambda e: e.indirect_dma_start(out=pg[:, :], out_offset=None, in_=cache[:, :],
                                                                      in_offset=bass.IndirectOffsetOnAxis(ap=idx[:, col:col + 1], axis=0)),
                              reads=['idx'], writes=[pgk], sem='pg%d' % bi)
                        return bi, pg, pgk

                    def kv_prep(bi, pg, pgk):
                        pb, pk = next_bank()
                        for g in range(4):
                            S.do('tensor', 'transpose', reads=[pgk, 'ident'], writes=[pk], out=pb[0:64, g * 128:(g + 1) * 128], in_=pg[:, g * 64:(g + 1) * 64], identity=ident[:])
                        S.do('scalar', 'activation', reads=[pk], writes=['ktp%d' % bi], out=ktp[bi][:], in_=pb[0:64, :].rearrange("p (g k) -> p g k", g=4), func=AF.Copy)
                        S.do('gpsimd', 'tensor_copy', reads=[pgk], writes=['vap%d' % bi], out=vap[bi][:, :, 0:64], in_=pg[:, 256:512].rearrange("p (g d) -> p g d", g=4))

                    kcA = psb2("kcA", [64, 16, 4, 128], BF16)
                    vcA = psb2("vcA", [128, 16, 4, 65], BF16)
                    S.do('vector', 'memset', writes=['kcA'], ap=kcA[:], constant=0.0)
                    S.do('vector', 'memset', writes=['vcA'], ap=vcA[:], constant=0.0)
                    S.do('vector', 'memset', writes=['vcA'], ap=vcA[:, :, :, 64:65], constant=1.0)
                    with ExitStack() as ph3:
                        XTs = ph3.enter_context(nc.sbuf_tensor("XTs", [64, 2, 4, 2048], BF16))
                        for s_ in range(16):
                            for p in range(16):
                                bi, pg, pgk = fetch_page(cache_cmp, s_, p)
                                for kv in range(2):
                                    pb, pk = next_bank()
                                    for g in range(4):
                                        S.do('tensor', 'transpose', reads=[pgk, 'ident'], writes=[pk], out=pb[0:64, g * 128:(g + 1) * 128], in_=pg[:, kv * 256 + g * 64:kv * 256 + (g + 1) * 64], identity=ident[:])
                                    S.do('scalar', 'activation', reads=[pk], writes=['XTs'], out=XTs[:, kv, :, p * 128:(p + 1) * 128], in_=pb[0:64, :].rearrange("p (g k) -> p g k", g=4), func=AF.Copy)
                            compress(lambda kv, g: XTs[:, kv, g, :], ['XTs'], 127, lambda g, s_=s_: kcA[:, s_, g, 0:127], 'kcA', lambda g, ct, m, s_=s_: vcA[0:m, s_, g, 0:64], 'vcA')
                        S.barrier()
                    qb, qk = load_tile(t)
                    for gp in range(2):
                        gs = (2 * gp, 2 * gp + 1)
                        jobs = []
                        for s_ in range(16):
                            jobs.append(dict(nk=128, keys=['kcA', 'vcA', 'ovs', 'OH', 'seqb'],
                                             KT=lambda g, s_=s_: kcA[:, s_, g, :],
                                             V=lambda g, s_=s_: [(0, vcA[:, s_, g, :], 65), (1, ovs[:], 64)],
                                             bias=lambda g, s_=s_: [(OH[0:17, s_, :], seqb[0:17, :], ['OH', 'seqb'])]))
                        naccs = {0: {gs[0]: ACC[0], gs[1]: ACC[1]}, 1: {gs[0]: ACC[2], gs[1]: ACC[3]}}
                        attend_g(qb, qk, jobs, naccs, gs)
                        cmp_finish_g(t, naccs, gs)
                    rnew = psb2("rnew", [128, 3, 512])
                    ktn = psb2("ktn", [64, 2, 4, 128], BF16); van = psb2("van", [128, 2, 4, 65], BF16)
                    S.dd('sync', rnew[:], rows_own[t], reads=[('rows_own', t)], writes=['rnew'], sem='c')
                    S.do('gpsimd', 'memset', writes=['van'], ap=van[:], constant=1.0)
                    for bi_, br in enumerate((1, 2)):
                        pb, pk = next_bank()
                        for g in range(4):
                            S.do('tensor', 'transpose', reads=['rnew', 'ident'], writes=[pk], out=pb[0:64, g * 128:(g + 1) * 128], in_=rnew[:, br, g * 64:(g + 1) * 64], identity=ident[:])
                        S.do('scalar', 'activation', reads=[pk], writes=['ktn'], out=ktn[:, bi_], in_=pb[0:64, :].rearrange("p (g k) -> p g k", g=4), func=AF.Copy)
                        S.do('gpsimd', 'tensor_copy', reads=['rnew'], writes=['van'], out=van[:, bi_, :, 0:64], in_=rnew[:, br, 256:512].rearrange("p (g d) -> p g d", g=4))
                    S.dd('sync', s_win[:, 504:512, :], rnew[:, 2, :], reads=['rnew'], writes=['s_win_b'], sem='swo')
                    naccs = {0: {g: ACC[g] for g in range(4)}}
                    jobs = []
                    for s_ in range(16):
                        for p in range(16):
                            jb = dict(nk=128)

                            def prep(jb=jb, s_=s_, p=p):
                                bi, pg, pgk = fetch_page(cache_sel, s_, p)
                                kv_prep(bi, pg, pgk)
                                jb['keys'] = ['ktp%d' % bi, 'vap%d' % bi]
                                jb['KT'] = lambda g, bi=bi: ktp[bi][:, g, :]
                                jb['V'] = lambda g, bi=bi: [(0, vap[bi][:, g, :], 65)]
                            jb['prep'] = prep
                            jb['bias'] = lambda g, s_=s_, p=p: [(E[:, p * 128:(p + 1) * 128], selbT[g][:], ['E', 'selbT%d' % g]), (OH[0:16, s_, :], seqb[0:16, :], ['OH', 'seqb'])]
                            jb['V'] = lambda g: [(0, None, 65)]
                            jobs.append(jb)
                    jobs.append(dict(nk=128, keys=['ktn', 'van'], KT=lambda g: ktn[:, 0, g, :], V=lambda g: [(0, van[:, 0, g, :], 65)],
                                     bias=lambda g: [(identb[:], bdc[:], ['identb', 'bdc'])]))
                    attend_g(qb, qk, jobs, naccs, (0, 1, 2, 3))
                    for g in range(4):
                        normalize(ACC[g], g, 1, 65)
                    jobs = []
                    for s_ in range(16):
                        for w_ in range(4):
                            jb = dict(nk=128)

                            def prep(jb=jb, s_=s_, w_=w_):
                                bi = C.pg_i
                                C.pg_i = (bi + 1) % 3
                                pg, pgk = pgb[bi], 'pgb%d' % bi
                                S.dd('sync', pg[:], win_in[s_, w_ * 128:(w_ + 1) * 128, :], writes=[pgk], sem='pg%d' % bi)
                                kv_prep(bi, pg, pgk)
                                jb['keys'] = ['ktp%d' % bi, 'vap%d' % bi]
                                jb['KT'] = lambda g, bi=bi: ktp[bi][:, g, :]
                                jb['V'] = lambda g, bi=bi: [(0, vap[bi][:, g, :], 65)]
                            jb['prep'] = prep

                            def bias(g, s_=s_, w_=w_):
                                bl = [(OH[0:16, s_, :], seqb[0:16, :], ['OH', 'seqb'])]
                                if w_ == 0:
                                    bl.append((identb[:], wb0[:], ['identb', 'wb0']))
                                return bl
                            jb['bias'] = bias
                            jb['V'] = lambda g: [(0, None, 65)]
                            jobs.append(jb)
                    jobs.append(dict(nk=128, keys=['ktn', 'van'], KT=lambda g: ktn[:, 1, g, :], V=lambda g: [(0, van[:, 1, g, :], 65)],
                                     bias=lambda g: [(identb[:], bdc[:], ['identb', 'bdc'])]))
                    attend_g(qb, qk, jobs, naccs, (0, 1, 2, 3))
                    for g in range(4):
                        normalize(ACC[g], g, 2, 65)
                    dbg("oa8", oa[:], [128, 1024], F32, ['oa'])
                    finish_tile(t)
                    S.barrier()
                C.pb_n = 8
                S.barrier()

        gst2 = ExitStack()
        alloc_gemm(gst2)
        if 'fin' in phases:
            with ExitStack() as ph:
                psb = lambda name, shape, dt=F32: ph.enter_context(nc.sbuf_tensor(name, list(shape), dt))
                g2t = psb("g2t", [128, KC])
                S.dd('sync', g2t[:], g2, writes=['g2t'], sem='c')
                ldb = [psb("ldb%d" % i, [128, 512]) for i in range(4)]
                C.ld_i = 0

                def ld(src, reads):
                    i = C.ld_i
                    C.ld_i = (C.ld_i + 1) % 4
                    S.dd('sync', ldb[i][:], src, reads=reads, writes=['ldb%d' % i], sem='ld%d' % i)
                    return ldb[i], 'ldb%d' % i

                wpa_v = w_pa.rearrange("(kc p) n -> p kc n", p=128)
                wpb_v = w_pb.rearrange("(kc p) n -> p kc n", p=128)
                wo_v = w_o.rearrange("(kc p) n -> p kc n", p=128)
                wf1_v = w_f1.rearrange("(kc p) n -> p kc n", p=128)
                ch4 = [(c * 512, 512, None) for c in range(4)]
                x1T = psb("x1T", [128, NT, KC, 128], BF16)
                ssq2 = psb("ssq2", [128, NT, 4])
                rstd2 = psb("rstd2", [128, NT])
                x1_scr = dscr("x1_scr", [NT, 128, D])
                with ExitStack() as ph1:
                    psb1 = lambda name, shape, dt=F32: ph1.enter_context(nc.sbuf_tensor(name, list(shape), dt))
                    mixT = psb1("mixT", [128, NT, KC, 128], BF16)
                    with ExitStack() as ph2:
                        psb2 = lambda name, shape, dt=F32: ph2.enter_context(nc.sbuf_tensor(name, list(shape), dt))
                        oaT = psb2("oaT", [128, NT, 8, 128], BF16)
                        obT2 = psb2("obT2", [128, NT, 8, 128], BF16)
                        mxs = psb2("mxs", [128, 512])
                        for t in range(NT):
                            S.dd('sync', oaT[:, t], oaT_scr[t], reads=[('oaT', t)], writes=['oaT_sb%d' % t], sem='c')
                            S.dd('sync', obT2[:, t], obT_scr[t], reads=[('obT', t)], writes=['obT_sb%d' % t], sem='c')

                        def epi_a(ci, t, pb, pk, c0, w, tag):
                            gb_, gk = ld(zs[t, :, O_A + c0:O_A + c0 + w], [('zs', t, O_A + c0)])
                            zi = C.zst_i
                            C.zst_i = (C.zst_i + 1) % 4
                            S.do('vector', 'tensor_tensor', reads=[pk, gk], writes=['zst%d' % zi], out=C.zst[zi][:], in0=pb[:], in1=gb_[:], op=ALU.mult)
                            S.dd('sync', mix_scr[t, :, c0:c0 + w], C.zst[zi][:], reads=['zst%d' % zi], writes=[('mix', t, c0)], sem='zo%d' % zi)
                        gemm(lambda t, kc: oaT[:, t, kc, :], lambda t: 'oaT_sb%d' % t, NT, 8, wpa_v, ch4, None, None, epi_a)

                        def epi_b(ci, t, pb, pk, c0, w, tag):
                            gb_, gk = ld(zs[t, :, O_B + c0:O_B + c0 + w], [('zs', t, O_B + c0)])
                            pa_, pak = ld(mix_scr[t, :, c0:c0 + w], [('mix', t, c0)])
                            S.do('vector', 'tensor_tensor', reads=[pk, gk], writes=['mxs'], out=mxs[:], in0=pb[:], in1=gb_[:], op=ALU.mult)
                            S.do('vector', 'tensor_tensor', reads=['mxs', pak], writes=['mxs'], out=mxs[:], in0=mxs[:], in1=pa_[:], op=ALU.add)
                            pb2, pk2 = next_bank()
                            for kk in range(4):
                                S.do('tensor', 'transpose', reads=['mxs', 'ident'], writes=[pk2], out=pb2[:, kk * 128:(kk + 1) * 128], in_=mxs[:, kk * 128:(kk + 1) * 128], identity=ident[:])
                            S.do('scalar', 'activation', reads=[pk2], writes=['mixT%d' % t], out=mixT[:, t, ci * 4:(ci + 1) * 4, :], in_=pb2[:].rearrange("p (a b) -> p a b", a=4), func=AF.Copy)
                        gemm(lambda t, kc: obT2[:, t, kc, :], lambda t: 'obT_sb%d' % t, NT, 8, wpb_v, ch4, None, None, epi_b)
                        S.barrier()
                    junk2 = psb1("junk2", [128, 512])
                    x1c = [psb1("x1c%d" % i, [128, 512]) for i in range(2)]
                    C.x1c_i = 0
                    S.do('vector', 'memset', writes=['ssq2'], ap=ssq2[:], constant=0.0)

                    def epi_o(ci, t, pb, pk, c0, w, tag):
                        xb_, xk = ld(x_own[t, :, c0:c0 + w], [])
                        i = C.x1c_i
                        C.x1c_i = (C.x1c_i + 1) % 2
                        xc, yk = x1c[i], 'x1c%d' % i
                        S.do('vector', 'tensor_tensor', reads=[pk, xk], writes=[yk], out=xc[:], in0=pb[:], in1=xb_[:], op=ALU.add)
                        S.dd('sync', x1_scr[t, :, c0:c0 + w], xc[:], reads=[yk], writes=[('x1', t, ci)], sem='x1o%d' % i)
                        S.do('scalar', 'activation', reads=[yk, 'ssq2'], writes=['junk2', 'ssq2'], out=junk2[:], in_=xc[:], func=AF.Square, accum_out=ssq2[:, t, ci:ci + 1])
                        pb2, pk2 = next_bank()
                        for kk in range(4):
                            S.do('tensor', 'transpose', reads=[yk, 'ident'], writes=[pk2], out=pb2[:, kk * 128:(kk + 1) * 128], in_=xc[:, kk * 128:(kk + 1) * 128], identity=ident[:])
                        S.do('scalar', 'activation', reads=[pk2], writes=['x1T%d' % t], out=x1T[:, t, ci * 4:(ci + 1) * 4, :], in_=pb2[:].rearrange("p (a b) -> p a b", a=4), func=AF.Copy)
                    gemm(lambda t, kc: mixT[:, t, kc, :], lambda t: 'mixT%d' % t, NT, KC, wo_v, ch4, None, None, epi_o)
                    S.do('vector', 'tensor_reduce', reads=['ssq2'], writes=['rstd2'], out=rstd2[:], in_=ssq2[:], axis=AX.X, op=ALU.add)
                    S.do('scalar', 'activation', reads=['rstd2'], writes=['rstd2'], out=rstd2[:], in_=rstd2[:], func=AF.Sqrt, scale=1.0 / D, bias=EPS)
                    S.do('vector', 'reciprocal', reads=['rstd2'], writes=['rstd2'], out=rstd2[:], in_=rstd2[:])
                    S.barrier()
                yacc = psb("yacc", [128, NT, D])
                for t in range(NT):
                    for ci in range(4):
                        S.dd('sync', yacc[:, t, ci * 512:(ci + 1) * 512], x1_scr[t, :, ci * 512:(ci + 1) * 512], reads=[('x1', t, ci)], writes=['yacc%d_%d' % (t, ci)], sem='c')
                hT = [psb("hT%d" % i, [128, NT, 4, 128], BF16) for i in range(2)]
                hsb = psb("hsb", [128, 512])
                for fc in range(16):
                    hb, hbk = hT[fc % 2], 'hT%d' % (fc % 2)

                    def epi_h(ci, t, pb, pk, c0, w, tag, hb=hb, hbk=hbk):
                        S.do('scalar', 'activation', reads=[pk, 'rstd2'], writes=['hsb'], out=hsb[:], in_=pb[:], func=AF.Relu, scale=rstd2[:, t:t + 1])
                        S.do('vector', 'tensor_tensor', reads=['hsb'], writes=['hsb'], out=hsb[:], in0=hsb[:], in1=hsb[:], op=ALU.mult)
                        pb2, pk2 = next_bank()
                        for kk in range(4):
                            S.do('tensor', 'transpose', reads=['hsb', 'ident'], writes=[pk2], out=pb2[:, kk * 128:(kk + 1) * 128], in_=hsb[:, kk * 128:(kk + 1) * 128], identity=ident[:])
                        S.do('scalar', 'activation', reads=[pk2], writes=[hbk + '_%d' % t], out=hb[:, t, :, :], in_=pb2[:].rearrange("p (a b) -> p a b", a=4), func=AF.Copy)
                    gemm(lambda t, kc: x1T[:, t, kc, :], lambda t: 'x1T%d' % t, NT, KC, wf1_v, [(fc * 512, 512, None)], g2t, 'g2t', epi_h)
                    wf2_v = w_f2[fc * 512:(fc + 1) * 512, :].rearrange("(kc p) n -> p kc n", p=128)

                    def epi_y(ci, t, pb, pk, c0, w, tag):
                        yk = 'yacc%d_%d' % (t, ci)
                        S.do('vector', 'tensor_tensor', reads=[pk, yk], writes=[yk], out=yacc[:, t, c0:c0 + w], in0=yacc[:, t, c0:c0 + w], in1=pb[:], op=ALU.add)
                    gemm(lambda t, kc, hb=hb: hb[:, t, kc, :], lambda t, hbk=hbk: hbk + '_%d' % t, NT, 4, wf2_v, ch4, None, None, epi_y)
                for t in range(NT):
                    S.dd('sync', y_own[t], yacc[:, t, :], reads=['yacc%d_%d' % (t, ci) for ci in range(4)], writes=[('y', t)], sem='yo')
                S.barrier()

        gst2.close()
        S.barrier()
        S.emit()
    return nc


def _rope_tables(pos):
    half = 32
    inv = (10000.0 ** (-(np.arange(half, dtype=np.float32)) * 2.0 / 64)).astype(np.float32)
    ang = pos.astype(np.float32)[:, None] * inv[None, :]
    return np.concatenate([np.cos(ang), np.sin(ang)], axis=1).astype(np.float32)


def make_in_maps(inp, with_samp=True):
    maps = []
    xp = np.asarray(inp['x_prompt'])
    xs = np.asarray(inp['x_sample'])
    for c in range(8):
        b, j = c // 4, c % 4
        tiles = [xp[b, (4 * i + j) * 128:(4 * i + j + 1) * 128] for i in range(8)]
        tiles.append(xs[16 * c:16 * c + 16].reshape(128, D))
        x_own = np.ascontiguousarray(np.stack(tiles))
        pos = [np.arange((4 * i + j) * 128, (4 * i + j + 1) * 128) for i in range(8)]
        pos.append(np.tile(2048 + np.arange(8), 16))
        cs_own = np.stack([_rope_tables(p) for p in pos])
        U = np.zeros((128, 128), np.float32)
        for s_ in range(128):
            for t_ in range(128):
                if s_ > t_ and s_ // 64 == t_ // 64:
                    U[s_, t_] = 1.0
        G2 = np.zeros((128, 2), np.float32)
        G2[:64, 0] = 1.0
        G2[64:, 1] = 1.0
        js = np.zeros((128, 4), np.float32)
        js[:, j] = 1.0
        Dm = np.zeros((2, 128, 128), np.float32); Pm = np.zeros((2, 128, 16), np.float32); MT = np.zeros((2, 128, 128), np.float32)
        U8 = np.zeros((128, 128), np.float32); G16 = np.zeros((128, 16), np.float32)
        for s_ in range(128):
            G16[s_, s_ // 8] = 1.0
            if s_ % 64 <= 31:
                Pm[0, s_, s_ // 64] = 1.0
            for t_ in range(128):
                if s_ // 64 == t_ // 64:
                    Tst = 1.0 if s_ <= t_ else 0.0
                    Pst = 1.0 if (s_ % 64) <= 31 else 0.0
                    Dm[0, s_, t_] = Tst - Pst
                    MT[0, s_, t_] = Tst
                if s_ // 8 == t_ // 8:
                    Dm[1, s_, t_] = 1.0 if s_ <= t_ else 0.0
                    MT[1, s_, t_] = 1.0 if s_ <= t_ else 0.0
                    U8[s_, t_] = 1.0 if s_ > t_ else 0.0
        import ml_dtypes
        bf = ml_dtypes.bfloat16
        NEGB = -30000.0
        kk = np.arange(128)[:, None]; qq = np.arange(128)[None, :]
        rep4 = lambda a: np.ascontiguousarray(np.tile(a, (1, 4)))
        cE = (np.arange(4096)[None, :] // 64 == np.arange(64)[:, None]).astype(np.float32)
        selb = np.stack([rep4(np.where((o - j) * 128 + kk <= qq, 0.0, NEGB)) for o in range(4)])
        winb = np.stack([rep4(np.where(((o - j) * 128 + kk <= qq) & ((o - j) * 128 + kk > qq - 512), 0.0, NEGB)) for o in range(-4, 4)])
        cmpb = np.zeros((8, 2, 128, 512), np.float32)
        for i in range(8):
            for ct in range(2):
                cidx = ct * 128 + kk
                cmpb[i, ct] = rep4(np.where((16 * cidx + 31 <= (4 * i + j) * 128 + qq) & (cidx < 255), 0.0, NEGB))
        BIG = 1e9
        visadd = np.zeros((NT, 128, 2, 64), np.float32)
        nn = np.arange(64)[None, :]
        for t in range(NT):
            if t < 8:
                qpos = ((4 * t + j) * 128 + np.arange(128))[:, None]
                visible = nn * 64 <= qpos
            else:
                qpos = (2048 + np.arange(128) % 8)[:, None]
                visible = (nn * 64 <= qpos) & (nn < 33)
            cur = qpos // 64
            forced = (nn == 0) | (nn == cur) | (nn == cur - 1)
            visadd[t, :, 0, :] = visible.astype(np.float32)
            visadd[t, :, 1, :] = np.where(visible, np.where(forced, BIG, 0.0), -BIG)
        def overlap(ncc, nss):
            cs = np.arange(ncc) * 16; ce = cs + 32; ss = np.arange(nss) * 64; se = ss + 64
            ov = np.clip(np.minimum(ce[:, None], se[None, :]) - np.maximum(cs[:, None], ss[None, :]), 0, None)
            return (ov / 16).astype(np.float32)
        ovp = np.zeros((256, 64), np.float32); ovp[:255] = overlap(255, 64)
        ovs = np.zeros((128, 64), np.float32); ovs[:127, :33] = overlap(127, 33)
        OH = np.zeros((17, 16, 128), np.float32)
        for s_ in range(16):
            OH[s_, s_, :] = 1.0
        OH[16, :, 127] = 1.0
        seqb = np.full((17, 512), NEGB, np.float32)
        for s_ in range(16):
            seqb[s_] = np.tile(np.where(np.arange(128) // 8 == s_, 0.0, NEGB), 4)
        bdc = rep4(np.where((kk // 8 == qq // 8) & (kk % 8 <= qq % 8), 0.0, NEGB))
        wb0 = rep4(np.where(kk > (qq % 8), 0.0, NEGB))
        m_s = {
            'pt_in': np.ascontiguousarray(np.asarray(inp['page_table'])[16 * c:16 * c + 16].reshape(1, 256).astype(np.int32)),
            'cache_cmp': np.asarray(inp['cache_cmp_kv'])[0].reshape(NPOOL * 128, 512),
            'cache_sel': np.asarray(inp['cache_sel_kv'])[0].reshape(NPOOL * 128, 512),
            'win_in': np.ascontiguousarray(np.asarray(inp['cache_win_kv'])[0, 16 * c:16 * c + 16].reshape(16, 512, 512)),
        }
        m = {
            'cE': cE.astype(bf), 'cselb': selb.astype(bf), 'cwinb': winb.astype(bf), 'ccmpb': cmpb.astype(bf), 'cvisadd': visadd,
            'covp': ovp.reshape(2, 128, 64).astype(bf), 'covs': ovs.astype(bf), 'cOH': OH.astype(bf), 'cseqb': seqb.astype(bf),
            'cbdc': bdc.astype(bf), 'cwb0': wb0.astype(bf),
            'cw1': np.ascontiguousarray(np.asarray(inp['cmp_w1'])[0]), 'cw2': np.ascontiguousarray(np.asarray(inp['cmp_w2'])[0]),
            'cpe': np.ascontiguousarray(np.asarray(inp['cmp_pe'])[0]),
            'w_pa': np.ascontiguousarray(np.asarray(inp['w_proj_a'])[0]), 'w_pb': np.ascontiguousarray(np.asarray(inp['w_proj_b'])[0]),
            'w_o': np.ascontiguousarray(np.asarray(inp['w_out'])[0]), 'w_f1': np.ascontiguousarray(np.asarray(inp['w_ff1'])[0]), 'w_f2': np.ascontiguousarray(np.asarray(inp['w_ff2'])[0]),
            'g2': np.ascontiguousarray(np.asarray(inp['norm2_g'])[0].reshape(KC, 128).T),
            'cDm': Dm, 'cPm': Pm, 'cMT': MT, 'cU8': U8, 'cG16': G16,
            'st_in': np.ascontiguousarray(np.asarray(inp['state_hgrn'])[0, 16 * c:16 * c + 16]),
            'hng': np.ascontiguousarray(np.asarray(inp['hgrn_norm_g'])[0].reshape(128, 1)),
            'x_seq': np.ascontiguousarray(xp[b].reshape(NSEQ, 128, D)),
            'cs_seq': np.ascontiguousarray(_rope_tables(np.arange(4096)).reshape(NSEQ, 128, 64)),
            'lbl': np.ascontiguousarray(np.asarray(inp['hgrn_lb_logits'])),
            'jsel': js, 'cU64': U, 'cG2': G2,
            'x_own': x_own,
            'cs_own': np.ascontiguousarray(cs_own),
            'w_in': np.ascontiguousarray(np.asarray(inp['w_in'])[0]),
            'g1': np.ascontiguousarray(np.asarray(inp['norm1_g'])[0].reshape(KC, 128).T),
            'qkg': np.ascontiguousarray(np.concatenate([np.asarray(inp['q_norm_g']), np.asarray(inp['k_norm_g'])[0]], axis=0)),
        }
        if with_samp:
            m.update(m_s)
        maps.append(m)
    return maps


_NC_CACHE = {}


def run(inp, phases=('own', 'seq', 'hgrn', 'att', 'attp', 'atts', 'fin')):
    key = tuple(phases)
    if key not in _NC_CACHE:
        _NC_CACHE[key] = build(phases)
    nc = _NC_CACHE[key]
    maps = make_in_maps(inp, with_samp=('atts' in phases))
    res = run_bass_kernel_spmd(nc, maps, core_ids=list(range(8)))
    return res.results


def assemble(res):
    B, T = 2, 4096
    p_cmp = np.zeros((1, B, T, 2, 4, 64), np.float32)
    p_sel = np.zeros((1, B, T, 2, 4, 64), np.float32)
    p_win = np.zeros((1, B, 512, 2, 4, 64), np.float32)
    s_cmp = np.zeros((1, 128, 8, 2, 4, 64), np.float32)
    s_sel = np.zeros((1, 128, 8, 2, 4, 64), np.float32)
    y_p = np.zeros((B, T, D), np.float32)
    y_s = np.zeros((128, 8, D), np.float32)
    p_st = np.zeros((1, B, 8, 128, 128), np.float32)
    s_win = np.zeros((1, 128, 512, 2, 4, 64), np.float32)
    s_st = np.zeros((1, 128, 8, 128, 128), np.float32)
    for c in range(8):
        b, j = c // 4, c % 4
        r = res[c]
        rows = r['rows_own']
        for i in range(8):
            J = 4 * i + j
            p_cmp[0, b, J * 128:(J + 1) * 128] = rows[i, :, 0].reshape(128, 2, 4, 64)
            p_sel[0, b, J * 128:(J + 1) * 128] = rows[i, :, 1].reshape(128, 2, 4, 64)
            if J >= 28:
                p_win[0, b, (J - 28) * 128:(J - 27) * 128] = rows[i, :, 2].reshape(128, 2, 4, 64)
        s_cmp[0, 16 * c:16 * c + 16] = rows[8, :, 0].reshape(16, 8, 2, 4, 64)
        s_sel[0, 16 * c:16 * c + 16] = rows[8, :, 1].reshape(16, 8, 2, 4, 64)
        if j == 0 and 'p_state' in r:
            p_st[0, b] = r['p_state']
        if 'y_own' in r:
            for i in range(8):
                J = 4 * i + j
                y_p[b, J * 128:(J + 1) * 128] = r['y_own'][i]
            y_s[16 * c:16 * c + 16] = r['y_own'][8].reshape(16, 8, D)
        if 's_win' in r:
            s_win[0, 16 * c:16 * c + 16] = r['s_win'].reshape(16, 512, 2, 4, 64)
        if 's_state' in r:
            s_st[0, 16 * c:16 * c + 16] = r['s_state']
    return (y_p, y_s, p_cmp, p_sel, p_win, p_st, s_cmp, s_sel, s_win, s_st)


def kernel(**inp):
    res = run(inp)
    return assemble(res)
```

```python
import numpy as np
from contextlib import ExitStack
import concourse.bass as bass
import concourse.mybir as mybir
from concourse.bass_utils import run_bass_kernel_spmd

F32 = mybir.dt.float32
BF16 = mybir.dt.bfloat16
I32 = mybir.dt.int32
ALU = mybir.AluOpType
AF = mybir.ActivationFunctionType
AX = mybir.AxisListType

ENGS = ['tensor', 'vector', 'scalar', 'gpsimd', 'sync']
EPOCH = 24000

D = 2048
KC = 16
N_IN = 10800
NT = 9
NSEQ = 32
EPS = 1e-6
NPOOL = 2560
SIZES = (1024, 1536, 48, 1024, 1024, 1024, 1024, 2048, 2048)
OFFS = [0]
for _s in SIZES:
    OFFS.append(OFFS[-1] + _s)
O_Q, O_KV, O_GA, O_HQ, O_HF, O_HI, O_HG, O_A, O_B = OFFS[:9]


class Sched:
    def __init__(self, nc, stack):
        self.nc = nc
        self.stack = stack
        self.streams = {e: [] for e in ENGS}
        self.cnt = {e: 0 for e in ENGS}
        self.epoch = {e: 0 for e in ENGS}
        self.semh = {}
        self.seen = {e: {} for e in ENGS}
        self.lastw = {}
        self.readers = {}
        self.dma_tot = {}
        self.nsem = 0
        for e in ENGS:
            self._sem((e, 0))

    def _sem(self, key):
        if key not in self.semh:
            self.nsem += 1
            self.semh[key] = self.stack.enter_context(self.nc.semaphore("s%d" % self.nsem))
        return self.semh[key]

    def _deps(self, eng, reads, writes):
        evs = []
        for k in reads:
            if k in self.lastw:
                evs.append(self.lastw[k])
        for k in writes:
            if k in self.lastw:
                evs.append(self.lastw[k])
            evs.extend(self.readers.get(k, ()))
        need = {}
        for (sk, v) in evs:
            if sk[0] == 'd':
                v = self.dma_tot[sk]
            if self.seen[eng].get(sk, 0) >= v:
                continue
            if need.get(sk, 0) < v:
                need[sk] = v
        for sk, v in need.items():
            self.seen[eng][sk] = v
        return [(self._sem(sk), v) for sk, v in need.items()]

    def _record(self, ev, reads, writes):
        for k in writes:
            self.lastw[k] = ev
            self.readers[k] = []
        for k in reads:
            self.readers.setdefault(k, []).append(ev)

    def op(self, eng, fn, reads=(), writes=()):
        waits = self._deps(eng, reads, writes)
        if self.cnt[eng] >= EPOCH:
            self.epoch[eng] += 1
            self.cnt[eng] = 0
        sk = (eng, self.epoch[eng])
        self.cnt[eng] += 1
        ev = (sk, self.cnt[eng])
        if eng == 'tensor':
            self.seen[eng][sk] = self.cnt[eng]
        self.streams[eng].append((waits, fn, self._sem(sk), 1))
        self._record(ev, reads, writes)
        return ev

    def do(self, eng, method, reads=(), writes=(), **kw):
        return self.op(eng, lambda e: getattr(e, method)(**kw), reads, writes)

    def dd(self, q, out, in_, reads=(), writes=(), sem='d0'):
        return self.dma(q, lambda e: e.dma_start(out=out, in_=in_), reads, writes, sem)

    def dma(self, q, fn, reads=(), writes=(), sem='d0'):
        waits = self._deps(q, reads, writes)
        sk = ('d', sem)
        self.dma_tot[sk] = self.dma_tot.get(sk, 0) + 16
        ev = (sk, self.dma_tot[sk])
        self.streams[q].append((waits, fn, self._sem(sk), 16))
        self._record(ev, reads, writes)
        return ev

    def barrier(self):
        targets = []
        for e in ENGS:
            for ep in range(self.epoch[e] + 1):
                sk = (e, ep)
                v = self.cnt[e] if ep == self.epoch[e] else EPOCH
                if v > 0:
                    targets.append((sk, v))
        for sk, v in self.dma_tot.items():
            targets.append((sk, v))
        for e in ENGS:
            waits = []
            for sk, v in targets:
                if self.seen[e].get(sk, 0) < v:
                    self.seen[e][sk] = v
                    waits.append((self._sem(sk), v))
            if waits:
                self.streams[e].append((waits, None, None, 0))

    def emit(self):
        nc = self.nc
        with nc.Block() as block:
            def mk(ename):
                def body(eng):
                    for waits, fn, sem, inc in self.streams[ename]:
                        for (s, v) in waits:
                            eng.wait_ge(s, v)
                        if fn is not None:
                            fn(eng).then_inc(sem, inc)
                return body
            block.tensor(mk('tensor'))
            block.vector(mk('vector'))
            block.scalar(mk('scalar'))
            block.gpsimd(mk('gpsimd'))
            block.sync(mk('sync'))


class Ctx:
    pass


def chunk_list():
    ch = []
    def seg(o, n, f):
        c = 0
        while c < n:
            w = min(512, n - c)
            ch.append((o + c, w, f))
            c += w
    seg(O_Q, 1024, AF.Copy)
    seg(O_KV, 1536, AF.Copy)
    seg(O_GA, 48, AF.Sigmoid)
    seg(O_HQ, 1024, AF.Silu)
    seg(O_HF, 1024, AF.Sigmoid)
    seg(O_HI, 1024, AF.Copy)
    seg(O_HG, 1024, AF.Silu)
    seg(O_A, 2048, AF.Sigmoid)
    seg(O_B, 2048, AF.Sigmoid)
    return ch


def build(phases=('own', 'seq', 'hgrn', 'att', 'attp', 'atts', 'fin'), debug=False):
    nc = bass.Bass("TRN2", target_bir_lowering=False)
    C = Ctx()
    C.debug = debug
    din = lambda n, s, dt=F32: nc.dram_tensor(n, list(s), dt, kind="ExternalInput").ap()
    dout = lambda n, s, dt=F32: nc.dram_tensor(n, list(s), dt, kind="ExternalOutput").ap()
    dscr = lambda n, s, dt=F32: nc.dram_tensor(n, list(s), dt, kind="Internal").ap()
    x_own = din("x_own", [NT, 128, D])
    cs_own = din("cs_own", [NT, 128, 64])
    x_seq = din("x_seq", [NSEQ, 128, D])
    cs_seq = din("cs_seq", [NSEQ, 128, 64])
    w_in = din("w_in", [D, N_IN])
    g1 = din("g1", [128, KC])
    qkg = din("qkg", [4, 64])
    lbl = din("lbl", [2, 1024])
    jsel = din("jsel", [128, 4])
    cU64 = din("cU64", [128, 128])
    cG2 = din("cG2", [128, 2])
    cDm = din("cDm", [2, 128, 128])
    cPm = din("cPm", [2, 128, 16])
    cMT = din("cMT", [2, 128, 128])
    cU8 = din("cU8", [128, 128])
    cG16 = din("cG16", [128, 16])
    st_in = din("st_in", [16, 8, 128, 128])
    hng = din("hng", [128, 1])
    rows_own = dout("rows_own", [NT, 128, 3, 512])
    s_state = dout("s_state", [16, 8, 128, 128])
    obT_scr = dscr("obT_scr", [NT, 128, 8, 128], BF16)
    oaT_scr = dscr("oaT_scr", [NT, 128, 8, 128], BF16)
    mix_scr = dscr("mix_scr", [NT, 128, D])
    w_pa = din("w_pa", [1024, D]); w_pb = din("w_pb", [1024, D]); w_o = din("w_o", [D, D])
    g2 = din("g2", [128, KC]); w_f1 = din("w_f1", [D, 4 * D]); w_f2 = din("w_f2", [4 * D, D])
    y_own = dout("y_own", [NT, 128, D])
    cE = din("cE", [64, 4096], BF16)
    cselb = din("cselb", [4, 128, 512], BF16)
    cwinb = din("cwinb", [8, 128, 512], BF16)
    ccmpb = din("ccmpb", [8, 2, 128, 512], BF16)
    cvisadd = din("cvisadd", [NT, 128, 2, 64])
    covp = din("covp", [2, 128, 64], BF16)
    covs = din("covs", [128, 64], BF16)
    cOH = din("cOH", [17, 16, 128], BF16)
    cseqb = din("cseqb", [17, 512], BF16)
    cbdc = din("cbdc", [128, 512], BF16)
    cwb0 = din("cwb0", [128, 512], BF16)
    cw1 = din("cw1", [2, 2048, 64]); cw2 = din("cw2", [2, 64, 64]); cpe = din("cpe", [2, 32, 64])
    if 'atts' in phases:
        pt_in = din("pt_in", [1, 256], I32)
        cache_cmp = din("cache_cmp", [NPOOL * 128, 512])
        cache_sel = din("cache_sel", [NPOOL * 128, 512])
        win_in = din("win_in", [16, 512, 512])
        s_win = dout("s_win", [16, 512, 512])
    p_state = dout("p_state", [8, 128, 128])
    zs = dscr("zs", [NT, 128, N_IN])
    zseq = dscr("zseq", [NSEQ, 128, 3584])
    qT_scr = dscr("qT_scr", [NT, 64, 2048], BF16)
    kT_scr = dscr("kT_scr", [4, 64, 4, 4096], BF16)
    v_scr = dscr("v_scr", [NSEQ, 128, 2, 4, 65], BF16)
    sown_scr = dscr("sown_scr", [128, 16, 1024], BF16)

    with ExitStack() as st:
        S = Sched(nc, st)
        sb = lambda name, shape, dt=F32: st.enter_context(nc.sbuf_tensor(name, list(shape), dt))
        ps = lambda name, shape, dt=F32: st.enter_context(nc.psum_tensor(name, list(shape), dt))
        pbank = [ps("pb%d" % i, [128, 512]) for i in range(8)]

        def dbg(name, ap, shape, dt, reads):
            if not C.debug:
                return
            o = nc.dram_tensor("dbg_" + name, list(shape), dt, kind="ExternalOutput").ap()
            S.dd('sync', o, ap, reads=reads, sem='dbg')

        C.pb_i = 0
        C.pb_n = 8

        def next_bank():
            i = C.pb_i % C.pb_n
            C.pb_i = (i + 1) % C.pb_n
            return pbank[i], 'pb%d' % i

        ident = sb("ident", [128, 128])
        S.do('gpsimd', 'memset', writes=['ident'], ap=ident[:], constant=1.0)
        S.do('gpsimd', 'affine_select', reads=['ident'], writes=['ident'], out=ident[:], in_=ident[:], pattern=[[-1, 128]],
             compare_op=ALU.is_equal, fill=0.0, base=0, channel_multiplier=1)
        g1t = sb("g1t", [128, KC])
        S.dd('sync', g1t[:], g1, writes=['g1t'], sem='c')
        qkg_t = sb("qkg_t", [128, 4, 64])
        S.dd('sync', qkg_t[:], qkg.rearrange("(o a) d -> o a d", o=1).to_broadcast([128, 4, 64]), writes=['qkg_t'], sem='c')
        jsel_t = sb("jsel_t", [128, 4])
        S.dd('sync', jsel_t[:], jsel, writes=['jsel_t'], sem='c')
        U64 = sb("U64", [128, 128])
        S.dd('sync', U64[:], cU64, writes=['U64'], sem='c')
        G2 = sb("G2", [128, 2])
        S.dd('sync', G2[:], cG2, writes=['G2'], sem='c')
        C.gb_n = 0

        def alloc_gemm(stack):
            C.gb_n += 1
            u = "_%d" % C.gb_n
            al = lambda name, shape, dt=F32: stack.enter_context(nc.sbuf_tensor(name + u, list(shape), dt))
            C.wst = [al("wst%d" % i, [128, 4, 512]) for i in range(3)]
            C.wbf = [al("wbf%d" % i, [128, KC, 512], BF16) for i in range(2)]
            C.zst = [al("zst%d" % i, [128, 512]) for i in range(4)]
            C.wst_i = 0
            C.wbf_i = 0
            C.zst_i = 0

        gst = ExitStack()
        alloc_gemm(gst)

        def make_xT(x_dram, t0, nt, xT, ssq, rstd, pfx, xst, junk):
            S.do('vector', 'memset', writes=[pfx + 'ssq'], ap=ssq[:, 0:nt], constant=0.0)
            for t in range(nt):
                xb = xst[t % 2]
                xk = 'xst%d' % (t % 2)
                S.dd('sync', xb[:], x_dram[t0 + t], writes=[xk], sem='x%d' % (t % 2))
                S.do('scalar', 'activation', reads=[xk, pfx + 'ssq'], writes=['junk', pfx + 'ssq'], out=junk[:], in_=xb[:], func=AF.Square, accum_out=ssq[:, t:t + 1])
                for k4 in range(4):
                    pb, pk = next_bank()
                    for kk in range(4):
                        kc = k4 * 4 + kk
                        S.do('tensor', 'transpose', reads=[xk, 'ident'], writes=[pk], out=pb[:, kk * 128:(kk + 1) * 128], in_=xb[:, kc * 128:(kc + 1) * 128], identity=ident[:])
                    S.do('vector', 'tensor_copy', reads=[pk], writes=[pfx + 'xT%d' % t], out=xT[:, t, k4 * 4:(k4 + 1) * 4, :], in_=pb[:].rearrange("p (a b) -> p a b", a=4))
            S.do('scalar', 'activation', reads=[pfx + 'ssq'], writes=[pfx + 'rstd'], out=rstd[:, 0:nt], in_=ssq[:, 0:nt], func=AF.Sqrt, scale=1.0 / D, bias=EPS)
            S.do('vector', 'reciprocal', reads=[pfx + 'rstd'], writes=[pfx + 'rstd'], out=rstd[:, 0:nt], in_=rstd[:, 0:nt])

        def gemm_specs(specs):
            def load(sp):
                c0, w, nkc, w_v, gain, gain_key = sp['c0'], sp['w'], sp['nkc'], sp['w_v'], sp['gain'], sp['gain_key']
                bi = C.wbf_i
                C.wbf_i = (C.wbf_i + 1) % 2
                wb = C.wbf[bi]
                wk = 'wbf%d' % bi
                for q in range(nkc // 4):
                    si = C.wst_i
                    C.wst_i = (C.wst_i + 1) % 3
                    ws = C.wst[si]
                    S.dd('sync', ws[:, :, 0:w], w_v[:, q * 4:(q + 1) * 4, c0:c0 + w], writes=['wst%d' % si], sem='w%d' % si)
                    if gain is not None:
                        S.do('gpsimd', 'tensor_tensor', reads=['wst%d' % si, gain_key], writes=[wk], out=wb[:, q * 4:(q + 1) * 4, 0:w], in0=ws[:, :, 0:w],
                             in1=gain[:, q * 4:(q + 1) * 4].rearrange("p (a o) -> p a o", o=1).to_broadcast([128, 4, w]), op=ALU.mult)
                    else:
                        S.do('gpsimd', 'tensor_copy', reads=['wst%d' % si], writes=[wk], out=wb[:, q * 4:(q + 1) * 4, 0:w], in_=ws[:, :, 0:w])
                return wb, wk
            nxt = load(specs[0])
            for i, sp in enumerate(specs):
                wb, wk = nxt
                if i + 1 < len(specs):
                    nxt = load(specs[i + 1])
                w, nkc = sp['w'], sp['nkc']
                for t in range(sp['nt']):
                    pb, pk = next_bank()
                    for kc in range(nkc):
                        S.do('tensor', 'matmul', reads=[sp['lhs_key'](t), wk], writes=[pk], out=pb[:, 0:w], lhsT=sp['lhs'](t, kc), rhs=wb[:, kc, 0:w], start=(kc == 0), stop=(kc == nkc - 1))
                    sp['epi'](sp['ci'], t, pb, pk, sp['c0'], w, sp['tag'])

        def mk_specs(lhs, lhs_key, nt, nkc, w_v, chunks, gain, gain_key, epi):
            return [dict(lhs=lhs, lhs_key=lhs_key, nt=nt, nkc=nkc, w_v=w_v, c0=c0, w=w, tag=tag, gain=gain, gain_key=gain_key, epi=epi, ci=ci)
                    for ci, (c0, w, tag) in enumerate(chunks)]

        def gemm(lhs, lhs_key, nt, nkc, w_v, chunks, gain, gain_key, epi):
            gemm_specs(mk_specs(lhs, lhs_key, nt, nkc, w_v, chunks, gain, gain_key, epi))

        def epi_act_store(dst, rstd, rkey, dkey):
            def epi(ci, t, pb, pk, c0, w, tag):
                func, d0 = tag
                zi = C.zst_i
                C.zst_i = (C.zst_i + 1) % 4
                zb = C.zst[zi]
                S.do('scalar', 'activation', reads=[pk, rkey], writes=['zst%d' % zi], out=zb[:, 0:w], in_=pb[:, 0:w], func=func, scale=rstd[:, t:t + 1])
                S.dd('sync', dst(t)[:, d0:d0 + w], zb[:, 0:w], reads=['zst%d' % zi], writes=[(dkey, t, d0)], sem='zo%d' % zi)
            return epi

        lbst = ExitStack()
        lb_bc = lbst.enter_context(nc.sbuf_tensor("lb_bc", [128, 1024], F32))
        oml_bc = lbst.enter_context(nc.sbuf_tensor("oml_bc", [128, 1024], F32))
        with ExitStack() as ph:
            lraw = ph.enter_context(nc.sbuf_tensor("lraw", [128, 2, 1024], F32))
            S.dd('sync', lraw[:], lbl.rearrange("(o a) d -> o a d", o=1).to_broadcast([128, 2, 1024]), writes=['lraw'], sem='c')
            S.do('vector', 'tensor_tensor', reads=['lraw'], writes=['lb_bc'], out=lb_bc[:], in0=lraw[:, 0, :], in1=lraw[:, 1, :], op=ALU.subtract)
            S.do('scalar', 'activation', reads=['lb_bc'], writes=['lb_bc'], out=lb_bc[:], in_=lb_bc[:], func=AF.Sigmoid)
            S.do('vector', 'tensor_scalar', reads=['lb_bc'], writes=['oml_bc'], out=oml_bc[:], in0=lb_bc[:], scalar1=-1.0, scalar2=1.0, op0=ALU.mult, op1=ALU.add)
            S.barrier()

        w_v = w_in.rearrange("(kc p) n -> p kc n", p=128)

        C.nr_n = 0

        def alloc_nr(psb):
            C.nr_n += 1
            u = "_%d" % C.nr_n
            C.tmpa = psb("tmpa" + u, [128, 1024]); C.tmpb = psb("tmpb" + u, [128, 1024])
            C.hss = psb("hss" + u, [128, 32]); C.hrs = psb("hrs" + u, [128, 32])
            C.cst = [psb("cst%d" % i + u, [128, 64]) for i in range(2)]
            C.tab = [psb("tab%d" % i + u, [128, 4, 4, 32]) for i in range(2)]
            C.rowb = [psb("rowb%d" % i + u, [128, 3, 512]) for i in range(2)]

        def norm_rope(src, dst, H, tb, tk, br, scale, rk, wk):
            tmpa, tmpb, hss, hrs = C.tmpa, C.tmpb, C.hss, C.hrs
            n = H * 64
            sq = tmpa[:, 0:n].rearrange("p (h d) -> p h d", d=64)
            S.do('vector', 'tensor_tensor', reads=rk, writes=['tmpa'], out=sq, in0=src, in1=src, op=ALU.mult)
            S.do('vector', 'tensor_reduce', reads=['tmpa'], writes=['hss'], out=hss[:, 0:H], in_=sq, axis=AX.X, op=ALU.add)
            S.do('scalar', 'activation', reads=['hss'], writes=['hrs'], out=hrs[:, 0:H], in_=hss[:, 0:H], func=AF.Sqrt, scale=1.0 / 64, bias=EPS)
            S.do('vector', 'reciprocal', reads=['hrs'], writes=['hrs'], out=hrs[:, 0:H], in_=hrs[:, 0:H])
            if scale != 1.0:
                S.do('vector', 'tensor_scalar', reads=['hrs'], writes=['hrs'], out=hrs[:, 0:H], in0=hrs[:, 0:H], scalar1=scale, scalar2=None, op0=ALU.mult)
            x1 = src[:, :, 0:32]
            x2 = src[:, :, 32:64]
            T = lambda k: tb[:, br, k, :].rearrange("p (o d) -> p o d", o=1).to_broadcast([128, H, 32])
            a = tmpa[:, 0:H * 32].rearrange("p (h d) -> p h d", d=32)
            b = tmpb[:, 0:H * 32].rearrange("p (h d) -> p h d", d=32)
            S.do('vector', 'tensor_tensor', reads=rk + [tk], writes=['tmpa'], out=a, in0=x1, in1=T(0), op=ALU.mult)
            S.do('vector', 'tensor_tensor', reads=rk + [tk], writes=['tmpb'], out=b, in0=x2, in1=T(1), op=ALU.mult)
            S.do('vector', 'tensor_tensor', reads=['tmpa', 'tmpb'], writes=wk, out=dst[:, :, 0:32], in0=a, in1=b, op=ALU.subtract)
            S.do('vector', 'tensor_tensor', reads=rk + [tk], writes=['tmpa'], out=a, in0=x2, in1=T(2), op=ALU.mult)
            S.do('vector', 'tensor_tensor', reads=rk + [tk], writes=['tmpb'], out=b, in0=x1, in1=T(3), op=ALU.mult)
            S.do('vector', 'tensor_tensor', reads=['tmpa', 'tmpb'], writes=wk, out=dst[:, :, 32:64], in0=a, in1=b, op=ALU.add)
            S.do('vector', 'tensor_tensor', reads=wk + ['hrs'], writes=wk, out=dst, in0=dst, in1=hrs[:, 0:H].rearrange("p (h o) -> p h o", o=1).to_broadcast([128, H, 64]), op=ALU.mult)

        def make_tables(cs_dram, T, par):
            cb = C.cst[par]
            tb = C.tab[par]
            S.dd('sync', cb[:], cs_dram[T], writes=['cst%d' % par], sem='cs%d' % par)
            cosb = cb[:, 0:32].rearrange("p (o d) -> p o d", o=1).to_broadcast([128, 4, 32])
            sinb = cb[:, 32:64].rearrange("p (o d) -> p o d", o=1).to_broadcast([128, 4, 32])
            for k, (tr, gs) in enumerate([(cosb, 0), (sinb, 32), (cosb, 32), (sinb, 0)]):
                S.do('gpsimd', 'tensor_tensor', reads=['cst%d' % par, 'qkg_t'], writes=['tab%d' % par], out=tb[:, :, k, :], in0=qkg_t[:, :, gs:gs + 32], in1=tr, op=ALU.mult)
            return tb, 'tab%d' % par

        def kv_rows(zb, zk, base, tb, tk, rb, rbk):
            for br in range(3):
                o = base + br * 512
                norm_rope(zb[:, o:o + 256].rearrange("p (h d) -> p h d", d=64), rb[:, br, 0:256].rearrange("p (h d) -> p h d", d=64), 4, tb, tk, 1 + br, 1.0, [zk], [rbk])
                S.do('gpsimd', 'tensor_copy', reads=[zk], writes=[rbk], out=rb[:, br, 256:512], in_=zb[:, o + 256:o + 512])

        if 'own' in phases:
            with ExitStack() as ph:
                psb = lambda name, shape, dt=F32: ph.enter_context(nc.sbuf_tensor(name, list(shape), dt))
                xT = psb("xT", [128, NT, KC, 128], BF16)
                ssq = psb("ssq", [128, NT])
                rstd = psb("rstd", [128, NT])
                junk = psb("junk", [128, D])
                xst = [psb("xst%d" % i, [128, D]) for i in range(2)]
                make_xT(x_own, 0, NT, xT, ssq, rstd, 'o', xst, junk)
                chunks = [(c0, w, (f, c0)) for (c0, w, f) in chunk_list()]
                gemm(lambda t, kc: xT[:, t, kc, :], lambda t: 'oxT%d' % t, NT, KC, w_v, chunks, g1t, 'g1t',
                     epi_act_store(lambda t: zs[t], rstd, 'orstd', 'zs'))
                S.barrier()
            with ExitStack() as ph:
                psb = lambda name, shape, dt=F32: ph.enter_context(nc.sbuf_tensor(name, list(shape), dt))
                alloc_nr(psb)
                zq = [psb("zq%d" % i, [128, 2560]) for i in range(2)]
                qf = psb("qf", [128, 1024])
                qTb = [psb("qTb%d" % i, [64, 2048], BF16) for i in range(2)]
                zs_keys = [('zs', None, c[0]) for c in chunk_list()[0:5]]
                for t in range(NT):
                    par = t % 2
                    zb, zk = zq[par], 'zq%d' % par
                    rb, rbk = C.rowb[par], 'rowb%d' % par
                    qb, qbk = qTb[par], 'qTb%d' % par
                    S.dd('sync', zb[:], zs[t, :, 0:2560], reads=[('zs', t, k[2]) for k in zs_keys], writes=[zk], sem='zq%d' % par)
                    tb, tk = make_tables(cs_own, t, par)
                    norm_rope(zb[:, 0:1024].rearrange("p (h d) -> p h d", d=64), qf[:].rearrange("p (h d) -> p h d", d=64), 16, tb, tk, 0, 0.125, [zk], ['qf'])
                    for h4 in range(4):
                        pb, pk = next_bank()
                        for hh in range(4):
                            h = h4 * 4 + hh
                            S.do('tensor', 'transpose', reads=['qf', 'ident'], writes=[pk], out=pb[0:64, hh * 128:(hh + 1) * 128], in_=qf[:, h * 64:(h + 1) * 64], identity=ident[:])
                        S.do('scalar', 'activation', reads=[pk], writes=[qbk], out=qb[:, h4 * 512:(h4 + 1) * 512], in_=pb[0:64, :], func=AF.Copy)
                    S.dd('sync', qT_scr[t], qb[:], reads=[qbk], writes=[('qT', t)], sem='qo')
                    kv_rows(zb, zk, 1024, tb, tk, rb, rbk)
                    S.dd('sync', rows_own[t], rb[:], reads=[rbk], writes=[('rows_own', t)], sem='ro')
                S.barrier()

        if 'seq' in phases:
            seq_chunks = []
            for (c0, w, f) in chunk_list():
                if O_KV <= c0 < O_KV + 1536:
                    seq_chunks.append((c0, w, (f, c0 - O_KV)))
                elif O_HF <= c0 < O_HF + 1024:
                    seq_chunks.append((c0, w, (f, 1536 + c0 - O_HF)))
                elif O_HI <= c0 < O_HI + 1024:
                    seq_chunks.append((c0, w, (f, 2560 + c0 - O_HI)))
            with ExitStack() as ph:
                psb = lambda name, shape, dt=F32: ph.enter_context(nc.sbuf_tensor(name, list(shape), dt))
                xT = psb("sxT", [128, 16, KC, 128], BF16)
                ssq = psb("sssq", [128, 16])
                rstd = psb("srstd", [128, 16])
                junk = psb("sjunk", [128, D])
                xst = [psb("sxst%d" % i, [128, D]) for i in range(2)]
                for half in range(2):
                    make_xT(x_seq, half * 16, 16, xT, ssq, rstd, 's', xst, junk)
                    gemm(lambda t, kc: xT[:, t, kc, :], lambda t: 'sxT%d' % t, 16, KC, w_v, seq_chunks, g1t, 'g1t',
                         epi_act_store(lambda t, half=half: zseq[half * 16 + t], rstd, 'srstd', 'zseq%d' % half))
                S.barrier()
            with ExitStack() as ph:
                psb = lambda name, shape, dt=F32: ph.enter_context(nc.sbuf_tensor(name, list(shape), dt))
                alloc_nr(psb)
                zsq = [psb("zsq%d" % i, [128, 3584]) for i in range(2)]
                ktb = [psb("ktb%d" % i, [64, 4, 4, 128], BF16) for i in range(2)]
                vab = [psb("vab%d" % i, [128, 2, 4, 65], BF16) for i in range(2)]
                Sst = psb("Sst", [128, 8, 128])
                Sacc = [psb("Sacc%d" % i, [128, 2, 1024], BF16) for i in range(2)]
                ff = psb("ff", [128, 1024])
                logf = psb("logf", [128, 1024])
                eR = psb("eR", [128, 1024])
                khat = psb("khat", [128, 1024], BF16)
                hvb = psb("hvb", [128, 1024], BF16)
                ea = psb("ea", [128, 8, 2])
                stmp = psb("stmp", [128, 1024])
                S.do('vector', 'memset', writes=['Sst'], ap=Sst[:], constant=0.0)
                for i in range(2):
                    S.do('gpsimd', 'memset', writes=['vab%d' % i], ap=vab[i][:], constant=1.0)
                for T in range(NSEQ):
                    par = T % 2
                    zb, zk = zsq[par], 'zsq%d' % par
                    rb, rbk = C.rowb[par], 'rowb%d' % par
                    S.dd('sync', zb[:], zseq[T], reads=[('zseq%d' % (T // 16), T % 16, d0) for (_, _, (_, d0)) in seq_chunks], writes=[zk], sem='zq%d' % par)
                    tb, tk = make_tables(cs_seq, T, par)
                    kv_rows(zb, zk, 0, tb, tk, rb, rbk)
                    kb, kbk = ktb[par], 'ktb%d' % par
                    for slot, (br, src_o) in enumerate(((0, 0), (0, 256), (1, 0), (2, 0))):
                        pb, pk = next_bank()
                        for g in range(4):
                            S.do('tensor', 'transpose', reads=[rbk, 'ident'], writes=[pk], out=pb[0:64, g * 128:(g + 1) * 128], in_=rb[:, br, src_o + g * 64:src_o + (g + 1) * 64], identity=ident[:])
                        S.do('scalar', 'activation', reads=[pk], writes=[kbk], out=kb[:, slot, :, :], in_=pb[0:64, :].rearrange("p (g k) -> p g k", g=4), func=AF.Copy)
                    for slot in range(4):
                        S.dd('sync', kT_scr[slot, :, :, T * 128:(T + 1) * 128], kb[:, slot, :, :], reads=[kbk], writes=[('kT', T, slot)], sem='ko')
                    vb, vbk = vab[par], 'vab%d' % par
                    for br in (1, 2):
                        S.do('gpsimd', 'tensor_copy', reads=[rbk], writes=[vbk], out=vb[:, br - 1, :, 0:64], in_=rb[:, br, 256:512].rearrange("p (g d) -> p g d", g=4))
                    S.dd('sync', v_scr[T], vb[:], reads=[vbk], writes=[('v', T)], sem='vo')
                    S.do('vector', 'tensor_tensor', reads=[zk, 'oml_bc'], writes=['ff'], out=ff[:], in0=zb[:, 1536:2560], in1=oml_bc[:], op=ALU.mult)
                    S.do('vector', 'tensor_tensor', reads=['ff', 'lb_bc'], writes=['ff'], out=ff[:], in0=ff[:], in1=lb_bc[:], op=ALU.add)
                    S.do('scalar', 'activation', reads=['ff'], writes=['logf'], out=logf[:], in_=ff[:], func=AF.Ln)
                    S.do('gpsimd', 'tensor_scalar', reads=['ff'], writes=['ff'], out=ff[:], in0=ff[:], scalar1=-1.0, scalar2=1.0, op0=ALU.mult, op1=ALU.add)
                    for hh in range(2):
                        pb, pk = next_bank()
                        S.do('tensor', 'matmul', reads=['U64', 'logf'], writes=[pk], out=pb[:], lhsT=U64[:], rhs=logf[:, hh * 512:(hh + 1) * 512], start=True, stop=True)
                        S.do('scalar', 'activation', reads=[pk], writes=['eR'], out=eR[:, hh * 512:(hh + 1) * 512], in_=pb[:], func=AF.Exp)
                    S.do('vector', 'tensor_tensor', reads=['ff', 'eR'], writes=['khat'], out=khat[:], in0=ff[:], in1=eR[:], op=ALU.mult)
                    S.do('gpsimd', 'tensor_copy', reads=[zk], writes=['hvb'], out=hvb[:], in_=zb[:, 2560:3584])
                    pb, pk = next_bank()
                    for h in range(8):
                        S.do('tensor', 'matmul', reads=['logf', 'G2'], writes=[pk], out=pb[:, h * 2:h * 2 + 2], lhsT=logf[:, h * 128:(h + 1) * 128], rhs=G2[:], start=True, stop=True)
                    S.do('scalar', 'activation', reads=[pk], writes=['ea'], out=ea[:].rearrange("p h c -> p (h c)"), in_=pb[:, 0:16], func=AF.Exp)
                    for cc in range(2):
                        i_own, o = T // 4, T % 4
                        sa, sak = Sacc[i_own % 2], 'Sacc%d' % (i_own % 2)
                        if o == 0:
                            S.do('vector', 'tensor_scalar', reads=['Sst', 'jsel_t'], writes=[sak], out=sa[:, cc, :], in0=Sst[:].rearrange("p h d -> p (h d)"), scalar1=jsel_t[:, o:o + 1], scalar2=None, op0=ALU.mult)
                        else:
                            S.do('vector', 'tensor_scalar', reads=['Sst', 'jsel_t'], writes=['stmp'], out=stmp[:], in0=Sst[:].rearrange("p h d -> p (h d)"), scalar1=jsel_t[:, o:o + 1], scalar2=None, op0=ALU.mult)
                            S.do('vector', 'tensor_tensor', reads=['stmp', sak], writes=[sak], out=sa[:, cc, :], in0=sa[:, cc, :], in1=stmp[:], op=ALU.add)
                        for h4 in range(2):
                            pb, pk = next_bank()
                            for hh in range(4):
                                h = h4 * 4 + hh
                                S.do('tensor', 'matmul', reads=['khat', 'hvb'], writes=[pk], out=pb[:, hh * 128:(hh + 1) * 128], lhsT=khat[cc * 64:(cc + 1) * 64, h * 128:(h + 1) * 128],
                                     rhs=hvb[cc * 64:(cc + 1) * 64, h * 128:(h + 1) * 128], start=True, stop=True)
                            for hh in range(4):
                                h = h4 * 4 + hh
                                S.do('vector', 'scalar_tensor_tensor', reads=['Sst', 'ea', pk], writes=['Sst'], out=Sst[:, h, :], in0=Sst[:, h, :], scalar=ea[:, h, cc:cc + 1], in1=pb[:, hh * 128:(hh + 1) * 128], op0=ALU.mult, op1=ALU.add)
                    if T % 4 == 3:
                        sa, sak = Sacc[(T // 4) % 2], 'Sacc%d' % ((T // 4) % 2)
                        S.dd('sync', sown_scr[:, 2 * (T // 4):2 * (T // 4) + 2, :], sa[:], reads=[sak], writes=[('sown', T // 4)], sem='so')
                S.dd('sync', p_state.rearrange("h k v -> k h v"), Sst[:], reads=['Sst'], writes=['p_state'], sem='po')
                S.barrier()


        if 'hgrn' in phases:
            with ExitStack() as ph:
                psb = lambda name, shape, dt=F32: ph.enter_context(nc.sbuf_tensor(name, list(shape), dt))
                Dm = psb("Dm", [128, 2, 128]); Pm = psb("Pm", [128, 2, 16]); MT = psb("MT", [128, 2, 128])
                U8 = psb("U8", [128, 128]); G16 = psb("G16", [128, 16]); hng_t = psb("hng_t", [128, 1])
                ones_f = psb("ones_f", [128, 128])
                S.dd('sync', Dm[:], cDm.rearrange("a p t -> p a t"), writes=['Dm'], sem='c')
                S.dd('sync', Pm[:], cPm.rearrange("a p t -> p a t"), writes=['Pm'], sem='c')
                S.dd('sync', MT[:], cMT.rearrange("a p t -> p a t"), writes=['MT'], sem='c')
                S.dd('sync', U8[:], cU8, writes=['U8'], sem='c')
                S.dd('sync', G16[:], cG16, writes=['G16'], sem='c')
                S.dd('sync', hng_t[:], hng, writes=['hng_t'], sem='c')
                S.do('gpsimd', 'memset', writes=['ones_f'], ap=ones_f[:], constant=1.0)
                zh = psb("zh", [128, 4096])
                logf = psb("hlogf", [128, 1024]); hk = psb("hhk", [128, 1024])
                ex = psb("hex", [128, 1024]); qh = psb("hqh", [128, 1024]); kh = psb("hkh", [128, 1024])
                qhT = psb("qhT", [128, 8, 128], BF16); khT = psb("khT", [128, 8, 128], BF16)
                vb = psb("hvb2", [128, 1024], BF16)
                attT = psb("attT", [128, 8, 128], BF16)
                ogT = psb("ogT", [128, 8, 128])
                Sp = psb("Sp", [128, 4, 8, 128], BF16)
                S0 = psb("S0", [128, 4, 8, 128])
                em = psb("hem", [128, 8, 16])
                oTs = psb("oTs", [128, 512]); sqT = psb("sqT", [128, 512]); rbc = psb("rbc", [128, 512])
                obT = psb("obT", [128, 8, 128], BF16)
                kmask = psb("kmask", [128, 1024], BF16)
                zh_keys = [c0 for (c0, w, f) in chunk_list() if O_HQ <= c0 < O_A]
                C.pb_n = 6
                for t in range(NT):
                    kind = 0 if t < 8 else 1
                    G = 2 if kind == 0 else 16
                    L = 128 // G
                    S.dd('sync', zh[:], zs[t, :, O_HQ:O_A], reads=[('zs', t, c0) for c0 in zh_keys], writes=['zh'], sem='zh')
                    hq = zh[:, 0:1024]; sf = zh[:, 1024:2048]; hv = zh[:, 2048:3072]; og = zh[:, 3072:4096]
                    S.do('vector', 'tensor_tensor', reads=['zh', 'oml_bc'], writes=['hk'], out=hk[:], in0=sf, in1=oml_bc[:], op=ALU.mult)
                    S.do('vector', 'tensor_tensor', reads=['hk', 'lb_bc'], writes=['hk'], out=hk[:], in0=hk[:], in1=lb_bc[:], op=ALU.add)
                    S.do('scalar', 'activation', reads=['hk'], writes=['hlogf'], out=logf[:], in_=hk[:], func=AF.Ln)
                    S.do('gpsimd', 'tensor_scalar', reads=['hk'], writes=['hk'], out=hk[:], in0=hk[:], scalar1=-1.0, scalar2=1.0, op0=ALU.mult, op1=ALU.add)
                    S.do('gpsimd', 'tensor_copy', reads=['zh'], writes=['hvb2'], out=vb[:], in_=hv)
                    for hh in range(2):
                        pb, pk = next_bank()
                        S.do('tensor', 'matmul', reads=['Dm', 'hlogf'], writes=[pk], out=pb[:], lhsT=Dm[:, kind, :], rhs=logf[:, hh * 512:(hh + 1) * 512], start=True, stop=True)
                        S.do('vector', 'tensor_scalar', reads=[pk], writes=['hex'], out=ex[:, hh * 512:(hh + 1) * 512], in0=pb[:], scalar1=40.0, scalar2=None, op0=ALU.min)
                        S.do('scalar', 'activation', reads=['hex'], writes=['hex'], out=ex[:, hh * 512:(hh + 1) * 512], in_=ex[:, hh * 512:(hh + 1) * 512], func=AF.Exp)
                        S.do('vector', 'tensor_tensor', reads=['hex', 'zh'], writes=['hqh'], out=qh[:, hh * 512:(hh + 1) * 512], in0=ex[:, hh * 512:(hh + 1) * 512], in1=hq[:, hh * 512:(hh + 1) * 512], op=ALU.mult)
                        S.do('vector', 'tensor_scalar', reads=[pk, 'hqh'], writes=['hex'], out=ex[:, hh * 512:(hh + 1) * 512], in0=pb[:], scalar1=-1.0, scalar2=40.0, op0=ALU.mult, op1=ALU.min)
                        S.do('scalar', 'activation', reads=['hex'], writes=['hex'], out=ex[:, hh * 512:(hh + 1) * 512], in_=ex[:, hh * 512:(hh + 1) * 512], func=AF.Exp)
                        S.do('vector', 'tensor_tensor', reads=['hex', 'hk'], writes=['hkh'], out=kh[:, hh * 512:(hh + 1) * 512], in0=ex[:, hh * 512:(hh + 1) * 512], in1=hk[:, hh * 512:(hh + 1) * 512], op=ALU.mult)
                    for (src, sk_, dst, dk_) in ((qh, 'hqh', qhT, 'qhT'), (kh, 'hkh', khT, 'khT'), (None, 'zh', ogT, 'ogT')):
                        for h4 in range(2):
                            pb, pk = next_bank()
                            for hh in range(4):
                                h = h4 * 4 + hh
                                in_ap = og[:, h * 128:(h + 1) * 128] if src is None else src[:, h * 128:(h + 1) * 128]
                                S.do('tensor', 'transpose', reads=[sk_, 'ident'], writes=[pk], out=pb[:, hh * 128:(hh + 1) * 128], in_=in_ap, identity=ident[:])
                            S.do('scalar', 'activation', reads=[pk], writes=[dk_], out=dst[:, h4 * 4:(h4 + 1) * 4, :], in_=pb[:].rearrange("p (a b) -> p a b", a=4), func=AF.Copy)
                    for h4 in range(2):
                        pb, pk = next_bank()
                        for hh in range(4):
                            h = h4 * 4 + hh
                            S.do('tensor', 'matmul', reads=['khT', 'qhT'], writes=[pk], out=pb[:, hh * 128:(hh + 1) * 128], lhsT=khT[:, h, :], rhs=qhT[:, h, :], start=True, stop=True)
                        S.do('vector', 'tensor_tensor', reads=[pk, 'MT'], writes=['attT'], out=attT[:, h4 * 4:(h4 + 1) * 4, :], in0=pb[:].rearrange("p (a b) -> p a b", a=4),
                             in1=MT[:, kind, :].rearrange("p (o t) -> p o t", o=1).to_broadcast([128, 4, 128]), op=ALU.mult)
                    pb, pk = next_bank()
                    for h in range(8):
                        S.do('tensor', 'matmul', reads=['hlogf', 'Pm'], writes=[pk], out=pb[:, h * 16:(h + 1) * 16], lhsT=logf[:, h * 128:(h + 1) * 128], rhs=Pm[:, kind, :], start=True, stop=True)
                    S.do('scalar', 'activation', reads=[pk], writes=['hem'], out=em[:].rearrange("p h g -> p (h g)"), in_=pb[:, 0:128], func=AF.Exp)
                    if kind == 1:
                        for hh in range(2):
                            pb, pk = next_bank()
                            S.do('tensor', 'matmul', reads=['U8', 'hlogf'], writes=[pk], out=pb[:], lhsT=U8[:], rhs=logf[:, hh * 512:(hh + 1) * 512], start=True, stop=True)
                            S.do('scalar', 'activation', reads=[pk], writes=['hex'], out=ex[:, hh * 512:(hh + 1) * 512], in_=pb[:], func=AF.Exp)
                        S.do('vector', 'tensor_tensor', reads=['hex', 'hk'], writes=['hkh'], out=kh[:], in0=ex[:], in1=hk[:], op=ALU.mult)
                        pb, pk = next_bank()
                        for h in range(8):
                            S.do('tensor', 'matmul', reads=['hlogf', 'G16'], writes=[pk], out=pb[:, h * 16:(h + 1) * 16], lhsT=logf[:, h * 128:(h + 1) * 128], rhs=G16[:], start=True, stop=True)
                        S.do('scalar', 'activation', reads=[pk], writes=['hem'], out=em[:].rearrange("p h g -> p (h g)"), in_=pb[:, 0:128], func=AF.Exp)
                    nb = 1 if kind == 0 else 4
                    gb = G // nb
                    oT_banks = [(pbank[6], 'pb6'), (pbank[7], 'pb7')]
                    for h4 in range(2):
                        pbo, pko = oT_banks[h4]
                        for hh in range(4):
                            h = h4 * 4 + hh
                            S.do('tensor', 'matmul', reads=['hvb2', 'attT'], writes=[pko], out=pbo[:, hh * 128:(hh + 1) * 128], lhsT=vb[:, h * 128:(h + 1) * 128], rhs=attT[:, h, :], start=(hh == 0), stop=False)
                    for b_ in range(nb):
                        if kind == 0:
                            S.dd('sync', Sp[:, 0:2, :, :].rearrange("p g h d -> p g (h d)"), sown_scr[:, 2 * t:2 * t + 2, :], reads=[('sown', t)], writes=['Sp'], sem='sp')
                            for g in range(2):
                                for h in range(8):
                                    S.do('vector', 'tensor_scalar', reads=['Sp', 'hem'], writes=['Sp'], out=Sp[:, g, h, :], in0=Sp[:, g, h, :], scalar1=em[:, h, g:g + 1], scalar2=None, op0=ALU.mult)
                        else:
                            S.dd('sync', S0[:].rearrange("p g h d -> p (g h) d"), st_in[b_ * 4:(b_ + 1) * 4].rearrange("g h k d -> k (g h) d"), reads=[], writes=['S0'], sem='sp')
                            S.do('gpsimd', 'tensor_copy', reads=['S0'], writes=['Sp'], out=Sp[:], in_=S0[:])
                        for gl in range(gb):
                            g = b_ * gb + gl
                            for h in range(8):
                                pbo, pko = oT_banks[h // 4]
                                hh = h % 4
                                S.do('tensor', 'matmul', reads=['Sp', 'qhT'], writes=[pko], out=pbo[:, hh * 128 + g * L:hh * 128 + (g + 1) * L], lhsT=Sp[:, gl, h, :], rhs=qhT[:, h, g * L:(g + 1) * L],
                                     start=False, stop=(b_ == nb - 1 and gl == gb - 1))
                        if kind == 1:
                            for gl in range(gb):
                                g = b_ * gb + gl
                                S.do('vector', 'tensor_scalar', reads=['hkh', 'G16'], writes=['kmask'], out=kmask[:], in0=kh[:], scalar1=G16[:, g:g + 1], scalar2=None, op0=ALU.mult)
                                for h4 in range(2):
                                    pb, pk = next_bank()
                                    for hh in range(4):
                                        h = h4 * 4 + hh
                                        S.do('tensor', 'matmul', reads=['kmask', 'hvb2'], writes=[pk], out=pb[:, hh * 128:(hh + 1) * 128], lhsT=kmask[:, h * 128:(h + 1) * 128], rhs=vb[:, h * 128:(h + 1) * 128], start=True, stop=True)
                                    for hh in range(4):
                                        h = h4 * 4 + hh
                                        S.do('vector', 'scalar_tensor_tensor', reads=['S0', 'hem', pk], writes=['S0'], out=S0[:, gl, h, :], in0=S0[:, gl, h, :], scalar=em[:, h, g:g + 1], in1=pb[:, hh * 128:(hh + 1) * 128], op0=ALU.mult, op1=ALU.add)
                            S.dd('sync', s_state[b_ * 4:(b_ + 1) * 4].rearrange("g h k d -> k (g h) d"), S0[:].rearrange("p g h d -> p (g h) d"), reads=['S0'], writes=[('s_state', b_)], sem='sso')
                    for h4 in range(2):
                        pbo, pko = oT_banks[h4]
                        S.do('scalar', 'activation', reads=[pko], writes=['oTs'], out=oTs[:], in_=pbo[:], func=AF.Copy)
                        S.do('vector', 'tensor_tensor', reads=['oTs'], writes=['sqT'], out=sqT[:], in0=oTs[:], in1=oTs[:], op=ALU.mult)
                        pb, pk = next_bank()
                        S.do('tensor', 'matmul', reads=['ones_f', 'sqT'], writes=[pk], out=pb[:], lhsT=ones_f[:], rhs=sqT[:], start=True, stop=True)
                        S.do('scalar', 'activation', reads=[pk], writes=['rbc'], out=rbc[:], in_=pb[:], func=AF.Sqrt, scale=1.0 / 128, bias=EPS)
                        S.do('vector', 'reciprocal', reads=['rbc'], writes=['rbc'], out=rbc[:], in_=rbc[:])
                        S.do('vector', 'tensor_tensor', reads=['rbc', 'oTs'], writes=['oTs'], out=oTs[:], in0=oTs[:], in1=rbc[:], op=ALU.mult)
                        S.do('vector', 'scalar_tensor_tensor', reads=['oTs', 'hng_t', 'ogT'], writes=['obT'], out=obT[:, h4 * 4:(h4 + 1) * 4, :].rearrange("p a b -> p (a b)"), in0=oTs[:], scalar=hng_t[:, 0:1],
                             in1=ogT[:, h4 * 4:(h4 + 1) * 4, :].rearrange("p a b -> p (a b)"), op0=ALU.mult, op1=ALU.mult)
                    S.dd('sync', obT_scr[t], obT[:], reads=['obT'], writes=[('obT', t)], sem='obo')
                    if t in (0, 3, 8):
                        dbg("obT%d" % t, obT[:], [128, 8, 128], BF16, ['obT'])
                S.barrier()
                C.pb_n = 8


        S.barrier()
        lbst.close()
        gst.close()
        if 'att' in phases:
            NEGB = -30000.0
            with ExitStack() as ph:
                psb = lambda name, shape, dt=F32: ph.enter_context(nc.sbuf_tensor(name, list(shape), dt))
                C.pb_n = 3
                ACC = [(pbank[3 + i], 'pb%d' % (3 + i)) for i in range(4)]
                MISC = (pbank[7], 'pb7')
                identb = psb("identb", [128, 128], BF16)
                S.do('vector', 'tensor_copy', reads=['ident'], writes=['identb'], out=identb[:], in_=ident[:])
                E = psb("E", [64, 4096], BF16)
                S.dd('sync', E[:], cE, writes=['E'], sem='c')
                selb = psb("selb", [128, 4, 512], BF16); winb = psb("winb", [128, 8, 512], BF16)
                S.dd('sync', selb[:], cselb.rearrange("a p c -> p a c"), writes=['selb'], sem='c')
                S.dd('sync', winb[:], cwinb.rearrange("a p c -> p a c"), writes=['winb'], sem='c')
                ovp = psb("ovp", [128, 2, 64], BF16); ovs = psb("ovs", [128, 64], BF16)
                S.dd('sync', ovp[:], covp.rearrange("a p c -> p a c"), writes=['ovp'], sem='c')
                S.dd('sync', ovs[:], covs, writes=['ovs'], sem='c')
                OH = psb("OH", [17, 16, 128], BF16); seqb = psb("seqb", [17, 512], BF16)
                S.dd('sync', OH[:], cOH, writes=['OH'], sem='c')
                S.dd('sync', seqb[:], cseqb, writes=['seqb'], sem='c')
                bdc = psb("bdc", [128, 512], BF16); wb0 = psb("wb0", [128, 512], BF16)
                S.dd('sync', bdc[:], cbdc, writes=['bdc'], sem='c')
                S.dd('sync', wb0[:], cwb0, writes=['wb0'], sem='c')
                w1b = psb("w1b", [64, 2, 32, 64], BF16); w2b = psb("w2b", [64, 2, 64], BF16); peb = psb("peb", [64, 2])
                with ExitStack() as ph2:
                    psb2 = lambda name, shape, dt=F32: ph2.enter_context(nc.sbuf_tensor(name, list(shape), dt))
                    w1f = psb2("w1f", [64, 2, 32, 64]); w2f = psb2("w2f", [64, 2, 64]); pef = psb2("pef", [64, 2, 32]); pebf = psb2("pebf", [64, 2, 32], BF16)
                    for kv in range(2):
                        S.dd('sync', w1f[:, kv], cw1[kv].rearrange("(l d) h -> d l h", d=64), writes=['w1f'], sem='c')
                        S.dma('sync', lambda e, kv=kv: e.dma_start(out=pef[:, kv], in_=cpe[kv].rearrange("l d -> d l"), allow_slow_non_contiguous=True), writes=['pef'], sem='c')
                    S.dd('sync', w2f[:], cw2.rearrange("a h d -> h a d"), writes=['w2f'], sem='c')
                    S.do('vector', 'tensor_copy', reads=['w1f'], writes=['w1b'], out=w1b[:], in_=w1f[:])
                    S.do('vector', 'tensor_copy', reads=['w2f'], writes=['w2b'], out=w2b[:], in_=w2f[:])
                    S.do('vector', 'tensor_copy', reads=['pef'], writes=['pebf'], out=pebf[:], in_=pef[:])
                    for kv in range(2):
                        pb, pk = next_bank()
                        for l in range(32):
                            S.do('tensor', 'matmul', reads=['w1b', 'pebf'], writes=[pk], out=pb[0:64, 0:1], lhsT=w1b[:, kv, l, :], rhs=pebf[:, kv, l:l + 1], start=(l == 0), stop=(l == 31))
                        S.do('vector', 'tensor_copy', reads=[pk], writes=['peb'], out=peb[:, kv:kv + 1], in_=pb[0:64, 0:1])
                    S.barrier()

                hsil = psb("hsil", [64, 256], BF16)

                def compress(XT, xkeys, n, kcT_dst, kkey, vc_dst, vkey):
                    for kv in range(2):
                        for g in range(4):
                            pb, pk = next_bank()
                            xt = XT(kv, g)
                            for l in range(32):
                                S.do('tensor', 'matmul', reads=xkeys + ['w1b'], writes=[pk], out=pb[0:64, 0:n], lhsT=w1b[:, kv, l, :], rhs=xt[:, l:l + 16 * (n - 1) + 1:16], start=(l == 0), stop=(l == 31))
                            S.do('scalar', 'activation', reads=[pk, 'peb'], writes=['hsil'], out=hsil[:, 0:n], in_=pb[0:64, 0:n], func=AF.Silu, bias=peb[:, kv:kv + 1])
                            pb2, pk2 = next_bank()
                            if kv == 0:
                                S.do('tensor', 'matmul', reads=['hsil', 'w2b'], writes=[pk2], out=pb2[0:64, 0:n], lhsT=w2b[:, 0, :], rhs=hsil[:, 0:n], start=True, stop=True)
                                S.do('vector', 'tensor_copy', reads=[pk2], writes=[kkey], out=kcT_dst(g), in_=pb2[0:64, 0:n])
                            else:
                                for ct in range((n + 127) // 128):
                                    m = min(128, n - ct * 128)
                                    S.do('tensor', 'matmul', reads=['hsil', 'w2b'], writes=[pk2], out=pb2[0:m, ct * 64:(ct + 1) * 64], lhsT=hsil[:, ct * 128:ct * 128 + m], rhs=w2b[:, 1, :], start=True, stop=True)
                                    S.do('vector', 'tensor_copy', reads=[pk2], writes=[vkey], out=vc_dst(g, ct, m), in_=pb2[0:m, ct * 64:(ct + 1) * 64])

                ptb = [psb("ptb%d" % i, [128, 512], BF16) for i in range(3)]
                C.pt_i = 0

                ptz = [psb("ptz%d" % i, [128, 512], BF16) for i in range(3)]
                for i in range(3):
                    S.do('gpsimd', 'memset', writes=['ptz%d' % i], ap=ptz[i][:], constant=0.0)
                C.ptz_i = 0

                def attend_g(qT, qkey, jobs, naccs, gs):
                    started = set()
                    last = {}
                    for ji, jb in enumerate(jobs):
                        for g in gs:
                            for (aid, _, _) in jb['V'](g):
                                last[(aid, g)] = ji

                    def stage1(ji, jb, g):
                        nk = jb['nk']
                        qs = jb.get('qs')
                        pb, pk = next_bank()
                        bl = jb['bias'](g)
                        if qs is None:
                            sel = lambda ap: ap
                            ob = pb[0:nk, :]
                        else:
                            sel = lambda ap: ap.rearrange("p (r q) -> p r q", r=4)[:, :, qs * 8:(qs + 1) * 8]
                            ob = pb[0:nk, 0:32].rearrange("p (r q) -> p r q", r=4)
                        S.do('tensor', 'matmul', reads=jb['keys'] + [qkey], writes=[pk], out=ob, lhsT=jb['KT'](g), rhs=sel(qT[:, g * 512:(g + 1) * 512]), start=True, stop=(len(bl) == 0))
                        for bi, (bl_l, bl_r, bkeys) in enumerate(bl):
                            S.do('tensor', 'matmul', reads=bkeys, writes=[pk], out=ob, lhsT=bl_l, rhs=sel(bl_r), start=False, stop=(bi == len(bl) - 1))
                        if qs is None:
                            pi = C.pt_i
                            C.pt_i = (C.pt_i + 1) % 3
                            pt, ptk = ptb[pi], 'ptb%d' % pi
                            S.do('scalar', 'activation', reads=[pk], writes=[ptk], out=pt[0:nk, :], in_=ob, func=AF.Exp)
                        else:
                            pi = C.ptz_i
                            C.ptz_i = (C.ptz_i + 1) % 3
                            pt, ptk = ptz[pi], 'ptz%d' % pi
                            S.do('scalar', 'activation', reads=[pk], writes=[ptk], out=sel(pt[0:nk, :]), in_=ob, func=AF.Exp)
                        return (ji, jb, g, nk, qs, pt, ptk, jb['V'](g), list(jb['keys']))

                    def stage2(u):
                        ji, jb, g, nk, qs, pt, ptk, vl, keys = u
                        for (aid, vap, nV) in vl:
                            ab, abk = naccs[aid][g]
                            for r in range(4):
                                S.do('tensor', 'matmul', reads=[ptk] + keys, writes=[abk], out=ab[:, r * nV:(r + 1) * nV], lhsT=pt[0:nk, r * 128:(r + 1) * 128], rhs=vap,
                                     start=((aid, g) not in started), stop=(last[(aid, g)] == ji and r == 3))
                                started.add((aid, g))
                        if qs is not None:
                            S.do('gpsimd', 'memset', reads=[], writes=[ptk], ap=pt[0:nk, :].rearrange("p (r q) -> p r q", r=4)[:, :, qs * 8:(qs + 1) * 8], constant=0.0)

                    prev = None
                    for ji, jb in enumerate(jobs):
                        if 'prep' in jb:
                            jb['prep']()
                        for g in gs:
                            cur = stage1(ji, jb, g)
                            if prev is not None:
                                stage2(prev)
                            prev = cur
                    if prev is not None:
                        stage2(prev)

                qTt = [psb("qTt%d" % i, [64, 2048], BF16) for i in range(2)]
                gates = psb("gates", [128, 48])
                visadd = psb("visadd", [128, 2, 64])
                oa = psb("oa", [128, 1024])
                linv = psb("linv", [128, 4])
                imp = psb("imp", [128, 64]); imt = psb("imt", [128, 64])
                m8 = psb("m8", [128, 8]); sc2 = psb("sc2", [128, 64]); thr = psb("thr", [128, 1])
                selbT = [psb("selbT%d" % g, [64, 512], BF16) for g in range(4)]
                osc = psb("osc", [128, 4, 64])
                oaT = psb("oaT_a", [128, 8, 128], BF16)
                cmpb_t = psb("cmpb_t", [128, 2, 512], BF16)

                def topk_bias(t, g, imp_ap):
                    S.do('vector', 'tensor_tensor', reads=['imp', 'visadd'], writes=['imt'], out=imt[:], in0=imp_ap, in1=visadd[:, 0, :], op=ALU.mult)
                    S.do('vector', 'tensor_tensor', reads=['imt', 'visadd'], writes=['imt'], out=imt[:], in0=imt[:], in1=visadd[:, 1, :], op=ALU.add)
                    S.do('vector', 'max', reads=['imt'], writes=['m8'], out=m8[:], in_=imt[:])
                    S.do('vector', 'match_replace', reads=['imt', 'm8'], writes=['sc2'], out=sc2[:], in_to_replace=m8[:], in_values=imt[:], imm_value=-3.0e9)
                    S.do('vector', 'max', reads=['sc2'], writes=['m8'], out=m8[:], in_=sc2[:])
                    S.do('vector', 'tensor_reduce', reads=['m8'], writes=['thr'], out=thr[:], in_=m8[:], axis=AX.X, op=ALU.min)
                    S.do('vector', 'tensor_scalar', reads=['imt', 'thr'], writes=['sc2'], out=sc2[:], in0=imt[:], scalar1=thr[:, 0:1], scalar2=None, op0=ALU.is_ge)
                    S.do('vector', 'tensor_tensor', reads=['sc2', 'visadd'], writes=['sc2'], out=sc2[:], in0=sc2[:], in1=visadd[:, 0, :], op=ALU.mult)
                    S.do('vector', 'tensor_scalar', reads=['sc2'], writes=['sc2'], out=sc2[:], in0=sc2[:], scalar1=-NEGB, scalar2=NEGB, op0=ALU.mult, op1=ALU.add)
                    mb, mk = MISC
                    S.do('tensor', 'transpose', reads=['sc2', 'ident'], writes=[mk], out=mb[0:64, 0:128], in_=sc2[:], identity=ident[:])
                    for r in range(4):
                        S.do('scalar', 'activation', reads=[mk], writes=['selbT%d' % g], out=selbT[g][:, r * 128:(r + 1) * 128], in_=mb[0:64, 0:128], func=AF.Copy)

                def normalize(acc, g, br, nV):
                    ab, abk = acc
                    av = ab[:, 0:4 * nV].rearrange("p (r v) -> p r v", r=4)
                    S.do('vector', 'tensor_scalar', reads=[abk], writes=['linv'], out=linv[:], in0=av[:, :, 64], scalar1=1e-30, scalar2=None, op0=ALU.max)
                    S.do('vector', 'reciprocal', reads=['linv'], writes=['linv'], out=linv[:], in_=linv[:])
                    S.do('vector', 'tensor_tensor', reads=['linv', 'gates'], writes=['linv'], out=linv[:], in0=linv[:], in1=gates[:, br * 16 + g * 4:br * 16 + g * 4 + 4], op=ALU.mult)
                    dst = oa[:, g * 256:(g + 1) * 256].rearrange("p (r d) -> p r d", r=4)
                    lb_ = linv[:].rearrange("p (r o) -> p r o", o=1).to_broadcast([128, 4, 64])
                    if br == 0:
                        S.do('vector', 'tensor_tensor', reads=[abk, 'linv'], writes=['oa'], out=dst, in0=av[:, :, 0:64], in1=lb_, op=ALU.mult)
                    else:
                        S.do('vector', 'tensor_tensor', reads=[abk, 'linv'], writes=['osc'], out=osc[:], in0=av[:, :, 0:64], in1=lb_, op=ALU.mult)
                        S.do('vector', 'tensor_tensor', reads=['osc', 'oa'], writes=['oa'], out=dst, in0=dst, in1=osc[:], op=ALU.add)

                def cmp_finish_g(t, naccs, gs):
                    accs1, accs2 = naccs[0], naccs[1]
                    for g in gs:
                        ab, abk = accs1[g]
                        av = ab[:, 0:260].rearrange("p (r v) -> p r v", r=4)
                        S.do('vector', 'tensor_scalar', reads=[abk], writes=['linv'], out=linv[:], in0=av[:, :, 64], scalar1=1e-30, scalar2=None, op0=ALU.max)
                        S.do('vector', 'reciprocal', reads=['linv'], writes=['linv'], out=linv[:], in_=linv[:])
                        a2, a2k = accs2[g]
                        a2v = a2[:, 0:256].rearrange("p (r v) -> p r v", r=4)
                        S.do('vector', 'tensor_scalar', reads=[a2k, 'linv'], writes=['imp'], out=imp[:], in0=a2v[:, 0, :], scalar1=linv[:, 0:1], scalar2=None, op0=ALU.mult)
                        for r in range(1, 4):
                            S.do('vector', 'scalar_tensor_tensor', reads=[a2k, 'linv', 'imp'], writes=['imp'], out=imp[:], in0=a2v[:, r, :], scalar=linv[:, r:r + 1], in1=imp[:], op0=ALU.mult, op1=ALU.add)
                        topk_bias(t, g, imp[:])
                        normalize(accs1[g], g, 0, 65)

                def finish_tile(t):
                    for k2 in range(2):
                        pb, pk = next_bank()
                        for kk in range(4):
                            kc = k2 * 4 + kk
                            S.do('tensor', 'transpose', reads=['oa', 'ident'], writes=[pk], out=pb[:, kk * 128:(kk + 1) * 128], in_=oa[:, kc * 128:(kc + 1) * 128], identity=ident[:])
                        S.do('scalar', 'activation', reads=[pk], writes=['oaT_a'], out=oaT[:, k2 * 4:(k2 + 1) * 4, :], in_=pb[:].rearrange("p (a b) -> p a b", a=4), func=AF.Copy)
                    S.dd('sync', oaT_scr[t], oaT[:], reads=['oaT_a'], writes=[('oaT', t)], sem='oao')

                def load_tile(t):
                    qb, qk = qTt[t % 2], 'qTt%d' % (t % 2)
                    S.dd('sync', qb[:], qT_scr[t], reads=[('qT', t)], writes=[qk], sem='qt%d' % (t % 2))
                    S.dd('sync', gates[:], zs[t, :, O_GA:O_GA + 48], reads=[('zs', t, O_GA)], writes=['gates'], sem='c')
                    S.dd('sync', visadd[:], cvisadd[t], writes=['visadd'], sem='c')
                    return qb, qk

                if 'attp' in phases:
                  with ExitStack() as ph2:
                    psb2 = lambda name, shape, dt=F32: ph2.enter_context(nc.sbuf_tensor(name, list(shape), dt))
                    kcT = psb2("kcT", [64, 4, 256], BF16)
                    vca = psb2("vca", [128, 2, 4, 65], BF16)
                    S.do('vector', 'memset', writes=['kcT'], ap=kcT[:], constant=0.0)
                    S.do('vector', 'memset', writes=['vca'], ap=vca[:], constant=0.0)
                    S.do('vector', 'memset', writes=['vca'], ap=vca[:, :, :, 64:65], constant=1.0)
                    with ExitStack() as ph3:
                        XT = ph3.enter_context(nc.sbuf_tensor("XTp", [64, 2, 4, 4096], BF16))
                        for kv in range(2):
                            S.dd('sync', XT[:, kv], kT_scr[kv], reads=[('kT', T, kv) for T in range(NSEQ)], writes=['XTp'], sem='c')
                        compress(lambda kv, g: XT[:, kv, g, :], ['XTp'], 255, lambda g: kcT[:, g, 0:255], 'kcT', lambda g, ct, m: vca[0:m, ct, g, 0:64], 'vca')
                        dbg("kcT", kcT[:], [64, 4, 256], BF16, ['kcT'])
                        dbg("vca", vca[:], [128, 2, 4, 65], BF16, ['vca'])
                        S.barrier()
                    KTs = psb2("KTs", [64, 4, 4096], BF16); KTw = psb2("KTw", [64, 4, 4096], BF16)
                    Vsw = psb2("Vsw", [128, NSEQ, 2, 4, 65], BF16)
                    S.dd('sync', KTs[:], kT_scr[2], reads=[('kT', T, 2) for T in range(NSEQ)], writes=['KTs'], sem='c')
                    S.dd('sync', KTw[:], kT_scr[3], reads=[('kT', T, 3) for T in range(NSEQ)], writes=['KTw'], sem='c')
                    for T in range(NSEQ):
                        S.dd('sync', Vsw[:, T].rearrange("p a g d -> p (a g d)"), v_scr[T].rearrange("p a g d -> p (a g d)"), reads=[('v', T)], writes=['Vsw'], sem='c')
                    for t in range(8):
                        qb, qk = load_tile(t)
                        S.dd('sync', cmpb_t[:], ccmpb[t].rearrange("a p c -> p a c"), writes=['cmpb_t'], sem='c')
                        for gp in range(2):
                            gs = (2 * gp, 2 * gp + 1)
                            jobs = []
                            for ct in range(2):
                                jobs.append(dict(nk=128, keys=['kcT', 'vca', 'ovp', 'cmpb_t', 'identb'],
                                                 KT=lambda g, ct=ct: kcT[:, g, ct * 128:(ct + 1) * 128],
                                                 V=lambda g, ct=ct: [(0, vca[:, ct, g, :], 65), (1, ovp[:, ct, :], 64)],
                                                 bias=lambda g, ct=ct: [(identb[:], cmpb_t[:, ct, :], ['identb', 'cmpb_t'])]))
                            naccs = {0: {gs[0]: ACC[0], gs[1]: ACC[1]}, 1: {gs[0]: ACC[2], gs[1]: ACC[3]}}
                            attend_g(qb, qk, jobs, naccs, gs)
                            cmp_finish_g(t, naccs, gs)
                        if t in (0, 3, 7):
                            dbg("oc%d" % t, oa[:], [128, 1024], F32, ['oa'])
                            dbg("selb%d" % t, selbT[1][:], [64, 512], BF16, ['selbT1'])
                        i = t
                        jobs = []
                        for kt in range(4 * i + 4):
                            def bias(g, kt=kt, i=i):
                                bl = [(E[:, kt * 128:(kt + 1) * 128], selbT[g][:], ['E', 'selbT%d' % g])]
                                if kt >= 4 * i:
                                    bl.append((identb[:], selb[:, kt - 4 * i, :], ['identb', 'selb']))
                                return bl
                            jobs.append(dict(nk=128, keys=['KTs', 'Vsw'], KT=lambda g, kt=kt: KTs[:, g, kt * 128:(kt + 1) * 128],
                                             V=lambda g, kt=kt: [(0, Vsw[:, kt, 0, g, :], 65)], bias=bias))
                        naccs = {0: {g: ACC[g] for g in range(4)}}
                        attend_g(qb, qk, jobs, naccs, (0, 1, 2, 3))
                        for g in range(4):
                            normalize(ACC[g], g, 1, 65)
                        jobs = []
                        for o in range(-4, 4):
                            kt = 4 * i + o
                            if kt < 0:
                                continue
                            jobs.append(dict(nk=128, keys=['KTw', 'Vsw'], KT=lambda g, kt=kt: KTw[:, g, kt * 128:(kt + 1) * 128],
                                             V=lambda g, kt=kt: [(0, Vsw[:, kt, 1, g, :], 65)],
                                             bias=lambda g, o=o: [(identb[:], winb[:, o + 4, :], ['identb', 'winb'])]))
                        attend_g(qb, qk, jobs, naccs, (0, 1, 2, 3))
                        for g in range(4):
                            normalize(ACC[g], g, 2, 65)
                        if t in (0, 3, 7):
                            dbg("oa%d" % t, oa[:], [128, 1024], F32, ['oa'])
                        finish_tile(t)
                    S.barrier()

                if 'atts' in phases:
                  with ExitStack() as ph2:
                    psb2 = lambda name, shape, dt=F32: ph2.enter_context(nc.sbuf_tensor(name, list(shape), dt))
                    t = 8
                    for s_ in range(16):
                        for q4 in range(4):
                            S.dd('sync', s_win[s_, q4 * 126:(q4 + 1) * 126, :], win_in[s_, 8 + q4 * 126:8 + (q4 + 1) * 126, :], writes=[('s_win_a', s_, q4)], sem='swo')
                    pti = psb2("pti", [128, 256], I32); ptf = psb2("ptf", [128, 256]); idx = psb2("idx", [128, 256], I32)
                    pio = psb2("pio", [128, 1], I32); piof = psb2("piof", [128, 1])
                    S.dd('sync', pti[:], pt_in.to_broadcast([128, 256]), writes=['pti'], sem='c')
                    S.do('gpsimd', 'iota', writes=['pio'], out=pio[:], pattern=[[0, 1]], base=0, channel_multiplier=1)
                    S.do('vector', 'tensor_copy', reads=['pio'], writes=['piof'], out=piof[:], in_=pio[:])
                    S.do('vector', 'tensor_copy', reads=['pti'], writes=['ptf'], out=ptf[:], in_=pti[:])
                    S.do('vector', 'tensor_scalar', reads=['ptf', 'piof'], writes=['ptf'], out=ptf[:], in0=ptf[:], scalar1=128.0, scalar2=piof[:, 0:1], op0=ALU.mult, op1=ALU.add)
                    S.do('vector', 'tensor_copy', reads=['ptf'], writes=['idx'], out=idx[:], in_=ptf[:])
                    pgb = [psb2("pgb%d" % i, [128, 512]) for i in range(3)]
                    ktp = [psb2("ktp%d" % i, [64, 4, 128], BF16) for i in range(3)]
                    vap = [psb2("vap%d" % i, [128, 4, 65], BF16) for i in range(3)]
                    for i in range(3):
                        S.do('gpsimd', 'memset', writes=['vap%d' % i], ap=vap[i][:], constant=1.0)
                    C.pg_i = 0

                    def fetch_page(cache, s, p):
                        bi = C.pg_i
                        C.pg_i = (bi + 1) % 3
                        pg, pgk = pgb[bi], 'pgb%d' % bi
                        col = s * 16 + p
                        S.dma('gpsimd', lambda e: e.indirect_dma_start(out=pg[:, :], out_offset=None, in_=cache[:, :],
                                                                      in_offset=bass.IndirectOffsetOnAxis(ap=idx[:, col:col + 1], axis=0)),
                              reads=['idx'], writes=[pgk], sem='pg%d' % bi)
                        return bi, pg, pgk

                    def kv_prep(bi, pg, pgk):
                        pb, pk = next_bank()
                        for g in range(4):
                            S.do('tensor', 'transpose', reads=[pgk, 'ident'], writes=[pk], out=pb[0:64, g * 128:(g + 1) * 128], in_=pg[:, g * 64:(g + 1) * 64], identity=ident[:])
                        S.do('scalar', 'activation', reads=[pk], writes=['ktp%d' % bi], out=ktp[bi][:], in_=pb[0:64, :].rearrange("p (g k) -> p g k", g=4), func=AF.Copy)
                        S.do('gpsimd', 'tensor_copy', reads=[pgk], writes=['vap%d' % bi], out=vap[bi][:, :, 0:64], in_=pg[:, 256:512].rearrange("p (g d) -> p g d", g=4))

                    kcA = psb2("kcA", [64, 16, 4, 128], BF16)
                    vcA = psb2("vcA", [128, 16, 4, 65], BF16)
                    S.do('vector', 'memset', writes=['kcA'], ap=kcA[:], constant=0.0)
                    S.do('vector', 'memset', writes=['vcA'], ap=vcA[:], constant=0.0)
                    S.do('vector', 'memset', writes=['vcA'], ap=vcA[:, :, :, 64:65], constant=1.0)
                    with ExitStack() as ph3:
                        XTs = ph3.enter_context(nc.sbuf_tensor("XTs", [64, 2, 4, 2048], BF16))
                        for s_ in range(16):
                            for p in range(16):
                                bi, pg, pgk = fetch_page(cache_cmp, s_, p)
                                for kv in range(2):
                                    pb, pk = next_bank()
                                    for g in range(4):
                                        S.do('tensor', 'transpose', reads=[pgk, 'ident'], writes=[pk], out=pb[0:64, g * 128:(g + 1) * 128], in_=pg[:, kv * 256 + g * 64:kv * 256 + (g + 1) * 64], identity=ident[:])
                                    S.do('scalar', 'activation', reads=[pk], writes=['XTs'], out=XTs[:, kv, :, p * 128:(p + 1) * 128], in_=pb[0:64, :].rearrange("p (g k) -> p g k", g=4), func=AF.Copy)
                            compress(lambda kv, g: XTs[:, kv, g, :], ['XTs'], 127, lambda g, s_=s_: kcA[:, s_, g, 0:127], 'kcA', lambda g, ct, m, s_=s_: vcA[0:m, s_, g, 0:64], 'vcA')
                        S.barrier()
                    qb, qk = load_tile(t)
                    for gp in range(2):
                        gs = (2 * gp, 2 * gp + 1)
                        jobs = []
                        for s_ in range(16):
                            jobs.append(dict(nk=127, qs=s_, keys=['kcA', 'vcA', 'ovs'],
                                             KT=lambda g, s_=s_: kcA[:, s_, g, 0:127],
                                             V=lambda g, s_=s_: [(0, vcA[0:127, s_, g, :], 65), (1, ovs[0:127, :], 64)],
                                             bias=lambda g: []))
                        naccs = {0: {gs[0]: ACC[0], gs[1]: ACC[1]}, 1: {gs[0]: ACC[2], gs[1]: ACC[3]}}
                        attend_g(qb, qk, jobs, naccs, gs)
                        cmp_finish_g(t, naccs, gs)
                    rnew = psb2("rnew", [128, 3, 512])
                    ktn = psb2("ktn", [64, 2, 4, 128], BF16); van = psb2("van", [128, 2, 4, 65], BF16)
                    S.dd('sync', rnew[:], rows_own[t], reads=[('rows_own', t)], writes=['rnew'], sem='c')
                    S.do('gpsimd', 'memset', writes=['van'], ap=van[:], constant=1.0)
                    for bi_, br in enumerate((1, 2)):
                        pb, pk = next_bank()
                        for g in range(4):
                            S.do('tensor', 'transpose', reads=['rnew', 'ident'], writes=[pk], out=pb[0:64, g * 128:(g + 1) * 128], in_=rnew[:, br, g * 64:(g + 1) * 64], identity=ident[:])
                        S.do('scalar', 'activation', reads=[pk], writes=['ktn'], out=ktn[:, bi_], in_=pb[0:64, :].rearrange("p (g k) -> p g k", g=4), func=AF.Copy)
                        S.do('gpsimd', 'tensor_copy', reads=['rnew'], writes=['van'], out=van[:, bi_, :, 0:64], in_=rnew[:, br, 256:512].rearrange("p (g d) -> p g d", g=4))
                    S.dd('sync', s_win[:, 504:512, :], rnew[:, 2, :], reads=['rnew'], writes=['s_win_b'], sem='swo')
                    naccs = {0: {g: ACC[g] for g in range(4)}}
                    jobs = []
                    for s_ in range(16):
                        for p in range(16):
                            jb = dict(nk=128, qs=s_)

                            def prep(jb=jb, s_=s_, p=p):
                                bi, pg, pgk = fetch_page(cache_sel, s_, p)
                                kv_prep(bi, pg, pgk)
                                jb['keys'] = ['ktp%d' % bi, 'vap%d' % bi]
                                jb['KT'] = lambda g, bi=bi: ktp[bi][:, g, :]
                                jb['V'] = lambda g, bi=bi: [(0, vap[bi][:, g, :], 65)]
                            jb['prep'] = prep
                            jb['bias'] = lambda g, s_=s_, p=p: [(E[:, p * 128:(p + 1) * 128], selbT[g][:], ['E', 'selbT%d' % g])]
                            jb['V'] = lambda g: [(0, None, 65)]
                            jobs.append(jb)
                    jobs.append(dict(nk=128, keys=['ktn', 'van'], KT=lambda g: ktn[:, 0, g, :], V=lambda g: [(0, van[:, 0, g, :], 65)],
                                     bias=lambda g: [(identb[:], bdc[:], ['identb', 'bdc'])]))
                    attend_g(qb, qk, jobs, naccs, (0, 1, 2, 3))
                    for g in range(4):
                        normalize(ACC[g], g, 1, 65)
                    jobs = []
                    for s_ in range(16):
                        for w_ in range(4):
                            jb = dict(nk=128, qs=s_)

                            def prep(jb=jb, s_=s_, w_=w_):
                                bi = C.pg_i
                                C.pg_i = (bi + 1) % 3
                                pg, pgk = pgb[bi], 'pgb%d' % bi
                                S.dd('sync', pg[:], win_in[s_, w_ * 128:(w_ + 1) * 128, :], writes=[pgk], sem='pg%d' % bi)
                                kv_prep(bi, pg, pgk)
                                jb['keys'] = ['ktp%d' % bi, 'vap%d' % bi]
                                jb['KT'] = lambda g, bi=bi: ktp[bi][:, g, :]
                                jb['V'] = lambda g, bi=bi: [(0, vap[bi][:, g, :], 65)]
                            jb['prep'] = prep

                            def bias(g, s_=s_, w_=w_):
                                bl = []
                                if w_ == 0:
                                    bl.append((identb[:], wb0[:], ['identb', 'wb0']))
                                return bl
                            jb['bias'] = bias
                            jb['V'] = lambda g: [(0, None, 65)]
                            jobs.append(jb)
                    jobs.append(dict(nk=128, keys=['ktn', 'van'], KT=lambda g: ktn[:, 1, g, :], V=lambda g: [(0, van[:, 1, g, :], 65)],
                                     bias=lambda g: [(identb[:], bdc[:], ['identb', 'bdc'])]))
                    attend_g(qb, qk, jobs, naccs, (0, 1, 2, 3))
                    for g in range(4):
                        normalize(ACC[g], g, 2, 65)
                    dbg("oa8", oa[:], [128, 1024], F32, ['oa'])
                    finish_tile(t)
                    S.barrier()
                C.pb_n = 8
                S.barrier()

        gst2 = ExitStack()
        alloc_gemm(gst2)
        if 'fin' in phases:
            with ExitStack() as ph:
                psb = lambda name, shape, dt=F32: ph.enter_context(nc.sbuf_tensor(name, list(shape), dt))
                g2t = psb("g2t", [128, KC])
                S.dd('sync', g2t[:], g2, writes=['g2t'], sem='c')
                ldb = [psb("ldb%d" % i, [128, 512]) for i in range(4)]
                C.ld_i = 0

                def ld(src, reads):
                    i = C.ld_i
                    C.ld_i = (C.ld_i + 1) % 4
                    S.dd('sync', ldb[i][:], src, reads=reads, writes=['ldb%d' % i], sem='ld%d' % i)
                    return ldb[i], 'ldb%d' % i

                wpa_v = w_pa.rearrange("(kc p) n -> p kc n", p=128)
                wpb_v = w_pb.rearrange("(kc p) n -> p kc n", p=128)
                wo_v = w_o.rearrange("(kc p) n -> p kc n", p=128)
                wf1_v = w_f1.rearrange("(kc p) n -> p kc n", p=128)
                ch4 = [(c * 512, 512, None) for c in range(4)]
                x1T = psb("x1T", [128, NT, KC, 128], BF16)
                ssq2 = psb("ssq2", [128, NT, 4])
                rstd2 = psb("rstd2", [128, NT])
                x1_scr = dscr("x1_scr", [NT, 128, D])
                with ExitStack() as ph1:
                    psb1 = lambda name, shape, dt=F32: ph1.enter_context(nc.sbuf_tensor(name, list(shape), dt))
                    mixT = psb1("mixT", [128, NT, KC, 128], BF16)
                    with ExitStack() as ph2:
                        psb2 = lambda name, shape, dt=F32: ph2.enter_context(nc.sbuf_tensor(name, list(shape), dt))
                        oaT = psb2("oaT", [128, NT, 8, 128], BF16)
                        obT2 = psb2("obT2", [128, NT, 8, 128], BF16)
                        mxs = psb2("mxs", [128, 512])
                        for t in range(NT):
                            S.dd('sync', oaT[:, t], oaT_scr[t], reads=[('oaT', t)], writes=['oaT_sb%d' % t], sem='c')
                            S.dd('sync', obT2[:, t], obT_scr[t], reads=[('obT', t)], writes=['obT_sb%d' % t], sem='c')

                        def epi_a(ci, t, pb, pk, c0, w, tag):
                            gb_, gk = ld(zs[t, :, O_A + c0:O_A + c0 + w], [('zs', t, O_A + c0)])
                            zi = C.zst_i
                            C.zst_i = (C.zst_i + 1) % 4
                            S.do('vector', 'tensor_tensor', reads=[pk, gk], writes=['zst%d' % zi], out=C.zst[zi][:], in0=pb[:], in1=gb_[:], op=ALU.mult)
                            S.dd('sync', mix_scr[t, :, c0:c0 + w], C.zst[zi][:], reads=['zst%d' % zi], writes=[('mix', t, c0)], sem='zo%d' % zi)
                        gemm(lambda t, kc: oaT[:, t, kc, :], lambda t: 'oaT_sb%d' % t, NT, 8, wpa_v, ch4, None, None, epi_a)

                        def epi_b(ci, t, pb, pk, c0, w, tag):
                            gb_, gk = ld(zs[t, :, O_B + c0:O_B + c0 + w], [('zs', t, O_B + c0)])
                            pa_, pak = ld(mix_scr[t, :, c0:c0 + w], [('mix', t, c0)])
                            S.do('vector', 'tensor_tensor', reads=[pk, gk], writes=['mxs'], out=mxs[:], in0=pb[:], in1=gb_[:], op=ALU.mult)
                            S.do('vector', 'tensor_tensor', reads=['mxs', pak], writes=['mxs'], out=mxs[:], in0=mxs[:], in1=pa_[:], op=ALU.add)
                            pb2, pk2 = next_bank()
                            for kk in range(4):
                                S.do('tensor', 'transpose', reads=['mxs', 'ident'], writes=[pk2], out=pb2[:, kk * 128:(kk + 1) * 128], in_=mxs[:, kk * 128:(kk + 1) * 128], identity=ident[:])
                            S.do('scalar', 'activation', reads=[pk2], writes=['mixT%d' % t], out=mixT[:, t, ci * 4:(ci + 1) * 4, :], in_=pb2[:].rearrange("p (a b) -> p a b", a=4), func=AF.Copy)
                        gemm(lambda t, kc: obT2[:, t, kc, :], lambda t: 'obT_sb%d' % t, NT, 8, wpb_v, ch4, None, None, epi_b)
                        S.barrier()
                    junk2 = psb1("junk2", [128, 512])
                    x1c = [psb1("x1c%d" % i, [128, 512]) for i in range(2)]
                    C.x1c_i = 0
                    S.do('vector', 'memset', writes=['ssq2'], ap=ssq2[:], constant=0.0)

                    def epi_o(ci, t, pb, pk, c0, w, tag):
                        xb_, xk = ld(x_own[t, :, c0:c0 + w], [])
                        i = C.x1c_i
                        C.x1c_i = (C.x1c_i + 1) % 2
                        xc, yk = x1c[i], 'x1c%d' % i
                        S.do('vector', 'tensor_tensor', reads=[pk, xk], writes=[yk], out=xc[:], in0=pb[:], in1=xb_[:], op=ALU.add)
                        S.dd('sync', x1_scr[t, :, c0:c0 + w], xc[:], reads=[yk], writes=[('x1', t, ci)], sem='x1o%d' % i)
                        S.do('scalar', 'activation', reads=[yk, 'ssq2'], writes=['junk2', 'ssq2'], out=junk2[:], in_=xc[:], func=AF.Square, accum_out=ssq2[:, t, ci:ci + 1])
                        pb2, pk2 = next_bank()
                        for kk in range(4):
                            S.do('tensor', 'transpose', reads=[yk, 'ident'], writes=[pk2], out=pb2[:, kk * 128:(kk + 1) * 128], in_=xc[:, kk * 128:(kk + 1) * 128], identity=ident[:])
                        S.do('scalar', 'activation', reads=[pk2], writes=['x1T%d' % t], out=x1T[:, t, ci * 4:(ci + 1) * 4, :], in_=pb2[:].rearrange("p (a b) -> p a b", a=4), func=AF.Copy)
                    gemm(lambda t, kc: mixT[:, t, kc, :], lambda t: 'mixT%d' % t, NT, KC, wo_v, ch4, None, None, epi_o)
                    S.do('vector', 'tensor_reduce', reads=['ssq2'], writes=['rstd2'], out=rstd2[:], in_=ssq2[:], axis=AX.X, op=ALU.add)
                    S.do('scalar', 'activation', reads=['rstd2'], writes=['rstd2'], out=rstd2[:], in_=rstd2[:], func=AF.Sqrt, scale=1.0 / D, bias=EPS)
                    S.do('vector', 'reciprocal', reads=['rstd2'], writes=['rstd2'], out=rstd2[:], in_=rstd2[:])
                    S.barrier()
                yacc = psb("yacc", [128, NT, D])
                for t in range(NT):
                    for ci in range(4):
                        S.dd('sync', yacc[:, t, ci * 512:(ci + 1) * 512], x1_scr[t, :, ci * 512:(ci + 1) * 512], reads=[('x1', t, ci)], writes=['yacc%d_%d' % (t, ci)], sem='c')
                hT = [psb("hT%d" % i, [128, NT, 4, 128], BF16) for i in range(2)]
                hsb = psb("hsb", [128, 512])
                ffn_specs = []
                for fc in range(16):
                    hb, hbk = hT[fc % 2], 'hT%d' % (fc % 2)

                    def epi_h(ci, t, pb, pk, c0, w, tag, hb=hb, hbk=hbk):
                        S.do('scalar', 'activation', reads=[pk, 'rstd2'], writes=['hsb'], out=hsb[:], in_=pb[:], func=AF.Relu, scale=rstd2[:, t:t + 1])
                        S.do('vector', 'tensor_tensor', reads=['hsb'], writes=['hsb'], out=hsb[:], in0=hsb[:], in1=hsb[:], op=ALU.mult)
                        pb2, pk2 = next_bank()
                        for kk in range(4):
                            S.do('tensor', 'transpose', reads=['hsb', 'ident'], writes=[pk2], out=pb2[:, kk * 128:(kk + 1) * 128], in_=hsb[:, kk * 128:(kk + 1) * 128], identity=ident[:])
                        S.do('scalar', 'activation', reads=[pk2], writes=[hbk + '_%d' % t], out=hb[:, t, :, :], in_=pb2[:].rearrange("p (a b) -> p a b", a=4), func=AF.Copy)
                    ffn_specs += mk_specs(lambda t, kc: x1T[:, t, kc, :], lambda t: 'x1T%d' % t, NT, KC, wf1_v, [(fc * 512, 512, None)], g2t, 'g2t', epi_h)
                    wf2_v = w_f2[fc * 512:(fc + 1) * 512, :].rearrange("(kc p) n -> p kc n", p=128)

                    def epi_y(ci, t, pb, pk, c0, w, tag):
                        yk = 'yacc%d_%d' % (t, ci)
                        S.do('vector', 'tensor_tensor', reads=[pk, yk], writes=[yk], out=yacc[:, t, c0:c0 + w], in0=yacc[:, t, c0:c0 + w], in1=pb[:], op=ALU.add)
                    ffn_specs += mk_specs(lambda t, kc, hb=hb: hb[:, t, kc, :], lambda t, hbk=hbk: hbk + '_%d' % t, NT, 4, wf2_v, ch4, None, None, epi_y)
                gemm_specs(ffn_specs)
                for t in range(NT):
                    S.dd('sync', y_own[t], yacc[:, t, :], reads=['yacc%d_%d' % (t, ci) for ci in range(4)], writes=[('y', t)], sem='yo')
                S.barrier()

        gst2.close()
        S.barrier()
        S.emit()
    return nc


def _rope_tables(pos):
    half = 32
    inv = (10000.0 ** (-(np.arange(half, dtype=np.float32)) * 2.0 / 64)).astype(np.float32)
    ang = pos.astype(np.float32)[:, None] * inv[None, :]
    return np.concatenate([np.cos(ang), np.sin(ang)], axis=1).astype(np.float32)


def make_in_maps(inp, with_samp=True):
    maps = []
    xp = np.asarray(inp['x_prompt'])
    xs = np.asarray(inp['x_sample'])
    for c in range(8):
        b, j = c // 4, c % 4
        tiles = [xp[b, (4 * i + j) * 128:(4 * i + j + 1) * 128] for i in range(8)]
        tiles.append(xs[16 * c:16 * c + 16].reshape(128, D))
        x_own = np.ascontiguousarray(np.stack(tiles))
        pos = [np.arange((4 * i + j) * 128, (4 * i + j + 1) * 128) for i in range(8)]
        pos.append(np.tile(2048 + np.arange(8), 16))
        cs_own = np.stack([_rope_tables(p) for p in pos])
        U = np.zeros((128, 128), np.float32)
        for s_ in range(128):
            for t_ in range(128):
                if s_ > t_ and s_ // 64 == t_ // 64:
                    U[s_, t_] = 1.0
        G2 = np.zeros((128, 2), np.float32)
        G2[:64, 0] = 1.0
        G2[64:, 1] = 1.0
        js = np.zeros((128, 4), np.float32)
        js[:, j] = 1.0
        Dm = np.zeros((2, 128, 128), np.float32); Pm = np.zeros((2, 128, 16), np.float32); MT = np.zeros((2, 128, 128), np.float32)
        U8 = np.zeros((128, 128), np.float32); G16 = np.zeros((128, 16), np.float32)
        for s_ in range(128):
            G16[s_, s_ // 8] = 1.0
            if s_ % 64 <= 31:
                Pm[0, s_, s_ // 64] = 1.0
            for t_ in range(128):
                if s_ // 64 == t_ // 64:
                    Tst = 1.0 if s_ <= t_ else 0.0
                    Pst = 1.0 if (s_ % 64) <= 31 else 0.0
                    Dm[0, s_, t_] = Tst - Pst
                    MT[0, s_, t_] = Tst
                if s_ // 8 == t_ // 8:
                    Dm[1, s_, t_] = 1.0 if s_ <= t_ else 0.0
                    MT[1, s_, t_] = 1.0 if s_ <= t_ else 0.0
                    U8[s_, t_] = 1.0 if s_ > t_ else 0.0
        import ml_dtypes
        bf = ml_dtypes.bfloat16
        NEGB = -30000.0
        kk = np.arange(128)[:, None]; qq = np.arange(128)[None, :]
        rep4 = lambda a: np.ascontiguousarray(np.tile(a, (1, 4)))
        cE = (np.arange(4096)[None, :] // 64 == np.arange(64)[:, None]).astype(np.float32)
        selb = np.stack([rep4(np.where((o - j) * 128 + kk <= qq, 0.0, NEGB)) for o in range(4)])
        winb = np.stack([rep4(np.where(((o - j) * 128 + kk <= qq) & ((o - j) * 128 + kk > qq - 512), 0.0, NEGB)) for o in range(-4, 4)])
        cmpb = np.zeros((8, 2, 128, 512), np.float32)
        for i in range(8):
            for ct in range(2):
                cidx = ct * 128 + kk
                cmpb[i, ct] = rep4(np.where((16 * cidx + 31 <= (4 * i + j) * 128 + qq) & (cidx < 255), 0.0, NEGB))
        BIG = 1e9
        visadd = np.zeros((NT, 128, 2, 64), np.float32)
        nn = np.arange(64)[None, :]
        for t in range(NT):
            if t < 8:
                qpos = ((4 * t + j) * 128 + np.arange(128))[:, None]
                visible = nn * 64 <= qpos
            else:
                qpos = (2048 + np.arange(128) % 8)[:, None]
                visible = (nn * 64 <= qpos) & (nn < 33)
            cur = qpos // 64
            forced = (nn == 0) | (nn == cur) | (nn == cur - 1)
            visadd[t, :, 0, :] = visible.astype(np.float32)
            visadd[t, :, 1, :] = np.where(visible, np.where(forced, BIG, 0.0), -BIG)
        def overlap(ncc, nss):
            cs = np.arange(ncc) * 16; ce = cs + 32; ss = np.arange(nss) * 64; se = ss + 64
            ov = np.clip(np.minimum(ce[:, None], se[None, :]) - np.maximum(cs[:, None], ss[None, :]), 0, None)
            return (ov / 16).astype(np.float32)
        ovp = np.zeros((256, 64), np.float32); ovp[:255] = overlap(255, 64)
        ovs = np.zeros((128, 64), np.float32); ovs[:127, :33] = overlap(127, 33)
        OH = np.zeros((17, 16, 128), np.float32)
        for s_ in range(16):
            OH[s_, s_, :] = 1.0
        OH[16, :, 127] = 1.0
        seqb = np.full((17, 512), NEGB, np.float32)
        for s_ in range(16):
            seqb[s_] = np.tile(np.where(np.arange(128) // 8 == s_, 0.0, NEGB), 4)
        bdc = rep4(np.where((kk // 8 == qq // 8) & (kk % 8 <= qq % 8), 0.0, NEGB))
        wb0 = rep4(np.where(kk > (qq % 8), 0.0, NEGB))
        m_s = {
            'pt_in': np.ascontiguousarray(np.asarray(inp['page_table'])[16 * c:16 * c + 16].reshape(1, 256).astype(np.int32)),
            'cache_cmp': np.asarray(inp['cache_cmp_kv'])[0].reshape(NPOOL * 128, 512),
            'cache_sel': np.asarray(inp['cache_sel_kv'])[0].reshape(NPOOL * 128, 512),
            'win_in': np.ascontiguousarray(np.asarray(inp['cache_win_kv'])[0, 16 * c:16 * c + 16].reshape(16, 512, 512)),
        }
        m = {
            'cE': cE.astype(bf), 'cselb': selb.astype(bf), 'cwinb': winb.astype(bf), 'ccmpb': cmpb.astype(bf), 'cvisadd': visadd,
            'covp': ovp.reshape(2, 128, 64).astype(bf), 'covs': ovs.astype(bf), 'cOH': OH.astype(bf), 'cseqb': seqb.astype(bf),
            'cbdc': bdc.astype(bf), 'cwb0': wb0.astype(bf),
            'cw1': np.ascontiguousarray(np.asarray(inp['cmp_w1'])[0]), 'cw2': np.ascontiguousarray(np.asarray(inp['cmp_w2'])[0]),
            'cpe': np.ascontiguousarray(np.asarray(inp['cmp_pe'])[0]),
            'w_pa': np.ascontiguousarray(np.asarray(inp['w_proj_a'])[0]), 'w_pb': np.ascontiguousarray(np.asarray(inp['w_proj_b'])[0]),
            'w_o': np.ascontiguousarray(np.asarray(inp['w_out'])[0]), 'w_f1': np.ascontiguousarray(np.asarray(inp['w_ff1'])[0]), 'w_f2': np.ascontiguousarray(np.asarray(inp['w_ff2'])[0]),
            'g2': np.ascontiguousarray(np.asarray(inp['norm2_g'])[0].reshape(KC, 128).T),
            'cDm': Dm, 'cPm': Pm, 'cMT': MT, 'cU8': U8, 'cG16': G16,
            'st_in': np.ascontiguousarray(np.asarray(inp['state_hgrn'])[0, 16 * c:16 * c + 16]),
            'hng': np.ascontiguousarray(np.asarray(inp['hgrn_norm_g'])[0].reshape(128, 1)),
            'x_seq': np.ascontiguousarray(xp[b].reshape(NSEQ, 128, D)),
            'cs_seq': np.ascontiguousarray(_rope_tables(np.arange(4096)).reshape(NSEQ, 128, 64)),
            'lbl': np.ascontiguousarray(np.asarray(inp['hgrn_lb_logits'])),
            'jsel': js, 'cU64': U, 'cG2': G2,
            'x_own': x_own,
            'cs_own': np.ascontiguousarray(cs_own),
            'w_in': np.ascontiguousarray(np.asarray(inp['w_in'])[0]),
            'g1': np.ascontiguousarray(np.asarray(inp['norm1_g'])[0].reshape(KC, 128).T),
            'qkg': np.ascontiguousarray(np.concatenate([np.asarray(inp['q_norm_g']), np.asarray(inp['k_norm_g'])[0]], axis=0)),
        }
        if with_samp:
            m.update(m_s)
        maps.append(m)
    return maps


_NC_CACHE = {}


def run(inp, phases=('own', 'seq', 'hgrn', 'att', 'attp', 'atts', 'fin')):
    key = tuple(phases)
    if key not in _NC_CACHE:
        _NC_CACHE[key] = build(phases)
    nc = _NC_CACHE[key]
    maps = make_in_maps(inp, with_samp=('atts' in phases))
    res = run_bass_kernel_spmd(nc, maps, core_ids=list(range(8)))
    return res.results


def assemble(res):
    B, T = 2, 4096
    p_cmp = np.zeros((1, B, T, 2, 4, 64), np.float32)
    p_sel = np.zeros((1, B, T, 2, 4, 64), np.float32)
    p_win = np.zeros((1, B, 512, 2, 4, 64), np.float32)
    s_cmp = np.zeros((1, 128, 8, 2, 4, 64), np.float32)
    s_sel = np.zeros((1, 128, 8, 2, 4, 64), np.float32)
    y_p = np.zeros((B, T, D), np.float32)
    y_s = np.zeros((128, 8, D), np.float32)
    p_st = np.zeros((1, B, 8, 128, 128), np.float32)
    s_win = np.zeros((1, 128, 512, 2, 4, 64), np.float32)
    s_st = np.zeros((1, 128, 8, 128, 128), np.float32)
    for c in range(8):
        b, j = c // 4, c % 4
        r = res[c]
        rows = r['rows_own']
        for i in range(8):
            J = 4 * i + j
            p_cmp[0, b, J * 128:(J + 1) * 128] = rows[i, :, 0].reshape(128, 2, 4, 64)
            p_sel[0, b, J * 128:(J + 1) * 128] = rows[i, :, 1].reshape(128, 2, 4, 64)
            if J >= 28:
                p_win[0, b, (J - 28) * 128:(J - 27) * 128] = rows[i, :, 2].reshape(128, 2, 4, 64)
        s_cmp[0, 16 * c:16 * c + 16] = rows[8, :, 0].reshape(16, 8, 2, 4, 64)
        s_sel[0, 16 * c:16 * c + 16] = rows[8, :, 1].reshape(16, 8, 2, 4, 64)
        if j == 0 and 'p_state' in r:
            p_st[0, b] = r['p_state']
        if 'y_own' in r:
            for i in range(8):
                J = 4 * i + j
                y_p[b, J * 128:(J + 1) * 128] = r['y_own'][i]
            y_s[16 * c:16 * c + 16] = r['y_own'][8].reshape(16, 8, D)
        if 's_win' in r:
            s_win[0, 16 * c:16 * c + 16] = r['s_win'].reshape(16, 512, 2, 4, 64)
        if 's_state' in r:
            s_st[0, 16 * c:16 * c + 16] = r['s_state']
    return (y_p, y_s, p_cmp, p_sel, p_win, p_st, s_cmp, s_sel, s_win, s_st)


def kernel(**inp):
    res = run(inp)
    return assemble(res)
```

```python
import numpy as np
from contextlib import ExitStack
import concourse.bass as bass
import concourse.mybir as mybir
from concourse.bass_utils import run_bass_kernel_spmd

F32 = mybir.dt.float32
BF16 = mybir.dt.bfloat16
I32 = mybir.dt.int32
ALU = mybir.AluOpType
AF = mybir.ActivationFunctionType
AX = mybir.AxisListType

ENGS = ['tensor', 'vector', 'scalar', 'gpsimd', 'sync']
EPOCH = 24000

D = 2048
KC = 16
N_IN = 10800
NT = 9
NSEQ = 32
EPS = 1e-6
NPOOL = 2560
SIZES = (1024, 1536, 48, 1024, 1024, 1024, 1024, 2048, 2048)
OFFS = [0]
for _s in SIZES:
    OFFS.append(OFFS[-1] + _s)
O_Q, O_KV, O_GA, O_HQ, O_HF, O_HI, O_HG, O_A, O_B = OFFS[:9]


class Sched:
    def __init__(self, nc, stack):
        self.nc = nc
        self.stack = stack
        self.streams = {e: [] for e in ENGS}
        self.cnt = {e: 0 for e in ENGS}
        self.epoch = {e: 0 for e in ENGS}
        self.semh = {}
        self.seen = {e: {} for e in ENGS}
        self.lastw = {}
        self.readers = {}
        self.dma_tot = {}
        self.nsem = 0
        for e in ENGS:
            self._sem((e, 0))

    def _sem(self, key):
        if key not in self.semh:
            self.nsem += 1
            self.semh[key] = self.stack.enter_context(self.nc.semaphore("s%d" % self.nsem))
        return self.semh[key]

    def _deps(self, eng, reads, writes):
        evs = []
        for k in reads:
            if k in self.lastw:
                evs.append(self.lastw[k])
        for k in writes:
            if k in self.lastw:
                evs.append(self.lastw[k])
            evs.extend(self.readers.get(k, ()))
        need = {}
        for (sk, v) in evs:
            if sk[0] == 'd':
                v = self.dma_tot[sk]
            if self.seen[eng].get(sk, 0) >= v:
                continue
            if need.get(sk, 0) < v:
                need[sk] = v
        for sk, v in need.items():
            self.seen[eng][sk] = v
        return [(self._sem(sk), v) for sk, v in need.items()]

    def _record(self, ev, reads, writes):
        for k in writes:
            self.lastw[k] = ev
            self.readers[k] = []
        for k in reads:
            self.readers.setdefault(k, []).append(ev)

    def op(self, eng, fn, reads=(), writes=()):
        waits = self._deps(eng, reads, writes)
        if self.cnt[eng] >= EPOCH:
            self.epoch[eng] += 1
            self.cnt[eng] = 0
        sk = (eng, self.epoch[eng])
        self.cnt[eng] += 1
        ev = (sk, self.cnt[eng])
        if eng == 'tensor':
            self.seen[eng][sk] = self.cnt[eng]
        self.streams[eng].append((waits, fn, self._sem(sk), 1))
        self._record(ev, reads, writes)
        return ev

    def do(self, eng, method, reads=(), writes=(), **kw):
        return self.op(eng, lambda e: getattr(e, method)(**kw), reads, writes)

    def dd(self, q, out, in_, reads=(), writes=(), sem='d0'):
        return self.dma(q, lambda e: e.dma_start(out=out, in_=in_), reads, writes, sem)

    def dma(self, q, fn, reads=(), writes=(), sem='d0'):
        waits = self._deps(q, reads, writes)
        sk = ('d', sem)
        self.dma_tot[sk] = self.dma_tot.get(sk, 0) + 16
        ev = (sk, self.dma_tot[sk])
        self.streams[q].append((waits, fn, self._sem(sk), 16))
        self._record(ev, reads, writes)
        return ev

    def barrier(self):
        targets = []
        for e in ENGS:
            for ep in range(self.epoch[e] + 1):
                sk = (e, ep)
                v = self.cnt[e] if ep == self.epoch[e] else EPOCH
                if v > 0:
                    targets.append((sk, v))
        for sk, v in self.dma_tot.items():
            targets.append((sk, v))
        for e in ENGS:
            waits = []
            for sk, v in targets:
                if self.seen[e].get(sk, 0) < v:
                    self.seen[e][sk] = v
                    waits.append((self._sem(sk), v))
            if waits:
                self.streams[e].append((waits, None, None, 0))

    def emit(self):
        nc = self.nc
        with nc.Block() as block:
            def mk(ename):
                def body(eng):
                    for waits, fn, sem, inc in self.streams[ename]:
                        for (s, v) in waits:
                            eng.wait_ge(s, v)
                        if fn is not None:
                            fn(eng).then_inc(sem, inc)
                return body
            block.tensor(mk('tensor'))
            block.vector(mk('vector'))
            block.scalar(mk('scalar'))
            block.gpsimd(mk('gpsimd'))
            block.sync(mk('sync'))


class Ctx:
    pass


def chunk_list():
    ch = []
    def seg(o, n, f):
        c = 0
        while c < n:
            w = min(512, n - c)
            ch.append((o + c, w, f))
            c += w
    seg(O_Q, 1024, AF.Copy)
    seg(O_KV, 1536, AF.Copy)
    seg(O_GA, 48, AF.Sigmoid)
    seg(O_HQ, 1024, AF.Silu)
    seg(O_HF, 1024, AF.Sigmoid)
    seg(O_HI, 1024, AF.Copy)
    seg(O_HG, 1024, AF.Silu)
    seg(O_A, 2048, AF.Sigmoid)
    seg(O_B, 2048, AF.Sigmoid)
    return ch


def build(phases=('own', 'seq', 'hgrn', 'att', 'attp', 'atts', 'fin'), debug=False):
    nc = bass.Bass("TRN2", target_bir_lowering=False)
    C = Ctx()
    C.debug = debug
    din = lambda n, s, dt=F32: nc.dram_tensor(n, list(s), dt, kind="ExternalInput").ap()
    dout = lambda n, s, dt=F32: nc.dram_tensor(n, list(s), dt, kind="ExternalOutput").ap()
    dscr = lambda n, s, dt=F32: nc.dram_tensor(n, list(s), dt, kind="Internal").ap()
    x_own = din("x_own", [NT, 128, D])
    cs_own = din("cs_own", [NT, 128, 64])
    x_seq = din("x_seq", [NSEQ, 128, D])
    cs_seq = din("cs_seq", [NSEQ, 128, 64])
    w_in = din("w_in", [D, N_IN])
    g1 = din("g1", [128, KC])
    qkg = din("qkg", [4, 64])
    lbl = din("lbl", [2, 1024])
    jsel = din("jsel", [128, 4])
    cU64 = din("cU64", [128, 128])
    cG2 = din("cG2", [128, 2])
    cDm = din("cDm", [2, 128, 128])
    cPm = din("cPm", [2, 128, 16])
    cMT = din("cMT", [2, 128, 128])
    cU8 = din("cU8", [128, 128])
    cG16 = din("cG16", [128, 16])
    st_in = din("st_in", [16, 8, 128, 128])
    hng = din("hng", [128, 1])
    rows_own = dout("rows_own", [NT, 128, 3, 512])
    s_state = dout("s_state", [16, 8, 128, 128])
    obT_scr = dscr("obT_scr", [NT, 128, 8, 128], BF16)
    oaT_scr = dscr("oaT_scr", [NT, 128, 8, 128], BF16)
    mix_scr = dscr("mix_scr", [NT, 128, D])
    w_pa = din("w_pa", [1024, D]); w_pb = din("w_pb", [1024, D]); w_o = din("w_o", [D, D])
    g2 = din("g2", [128, KC]); w_f1 = din("w_f1", [D, 4 * D]); w_f2 = din("w_f2", [4 * D, D])
    y_own = dout("y_own", [NT, 128, D])
    cE = din("cE", [64, 4096], BF16)
    cselb = din("cselb", [4, 128, 512], BF16)
    cwinb = din("cwinb", [8, 128, 512], BF16)
    ccmpb = din("ccmpb", [8, 2, 128, 512], BF16)
    cvisadd = din("cvisadd", [NT, 128, 2, 64])
    covp = din("covp", [2, 128, 64], BF16)
    covs = din("covs", [128, 64], BF16)
    cOH = din("cOH", [17, 16, 128], BF16)
    cseqb = din("cseqb", [17, 512], BF16)
    cbdc = din("cbdc", [128, 512], BF16)
    cwb0 = din("cwb0", [128, 512], BF16)
    cw1 = din("cw1", [2, 2048, 64]); cw2 = din("cw2", [2, 64, 64]); cpe = din("cpe", [2, 32, 64])
    if 'atts' in phases:
        pt_in = din("pt_in", [1, 256], I32)
        cache_cmp = din("cache_cmp", [NPOOL * 128, 512])
        cache_sel = din("cache_sel", [NPOOL * 128, 512])
        win_in = din("win_in", [16, 512, 512])
        s_win = dout("s_win", [16, 512, 512])
    p_state = dout("p_state", [8, 128, 128])
    zs = dscr("zs", [NT, 128, N_IN])
    zseq = dscr("zseq", [NSEQ, 128, 3584])
    qT_scr = dscr("qT_scr", [NT, 64, 2048], BF16)
    kT_scr = dscr("kT_scr", [4, 64, 4, 4096], BF16)
    v_scr = dscr("v_scr", [NSEQ, 128, 2, 4, 65], BF16)
    sown_scr = dscr("sown_scr", [128, 16, 1024], BF16)

    with ExitStack() as st:
        S = Sched(nc, st)
        sb = lambda name, shape, dt=F32: st.enter_context(nc.sbuf_tensor(name, list(shape), dt))
        ps = lambda name, shape, dt=F32: st.enter_context(nc.psum_tensor(name, list(shape), dt))
        pbank = [ps("pb%d" % i, [128, 512]) for i in range(8)]

        def dbg(name, ap, shape, dt, reads):
            if not C.debug:
                return
            o = nc.dram_tensor("dbg_" + name, list(shape), dt, kind="ExternalOutput").ap()
            S.dd('sync', o, ap, reads=reads, sem='dbg')

        C.pb_i = 0
        C.pb_n = 8

        def next_bank():
            i = C.pb_i % C.pb_n
            C.pb_i = (i + 1) % C.pb_n
            return pbank[i], 'pb%d' % i

        ident = sb("ident", [128, 128])
        S.do('gpsimd', 'memset', writes=['ident'], ap=ident[:], constant=1.0)
        S.do('gpsimd', 'affine_select', reads=['ident'], writes=['ident'], out=ident[:], in_=ident[:], pattern=[[-1, 128]],
             compare_op=ALU.is_equal, fill=0.0, base=0, channel_multiplier=1)
        g1t = sb("g1t", [128, KC])
        S.dd('sync', g1t[:], g1, writes=['g1t'], sem='c')
        qkg_t = sb("qkg_t", [128, 4, 64])
        S.dd('sync', qkg_t[:], qkg.rearrange("(o a) d -> o a d", o=1).to_broadcast([128, 4, 64]), writes=['qkg_t'], sem='c')
        jsel_t = sb("jsel_t", [128, 4])
        S.dd('sync', jsel_t[:], jsel, writes=['jsel_t'], sem='c')
        U64 = sb("U64", [128, 128])
        S.dd('sync', U64[:], cU64, writes=['U64'], sem='c')
        G2 = sb("G2", [128, 2])
        S.dd('sync', G2[:], cG2, writes=['G2'], sem='c')
        C.gb_n = 0
        C.cast_engs = ['gpsimd', 'vector']

        def alloc_gemm(stack):
            C.gb_n += 1
            u = "_%d" % C.gb_n
            al = lambda name, shape, dt=F32: stack.enter_context(nc.sbuf_tensor(name + u, list(shape), dt))
            C.wst = [al("wst%d" % i, [128, 2, 512]) for i in range(6)]
            C.cast_i = 0
            C.wbf = [al("wbf%d" % i, [128, KC, 512], BF16) for i in range(2)]
            C.zst = [al("zst%d" % i, [128, 512]) for i in range(4)]
            C.wst_i = 0
            C.wbf_i = 0
            C.zst_i = 0

        gst = ExitStack()
        alloc_gemm(gst)

        def make_xT(x_dram, t0, nt, xT, ssq, rstd, pfx, xst, junk):
            S.do('vector', 'memset', writes=[pfx + 'ssq'], ap=ssq[:, 0:nt], constant=0.0)
            for t in range(nt):
                xb = xst[t % 2]
                xk = 'xst%d' % (t % 2)
                S.dd('sync', xb[:], x_dram[t0 + t], writes=[xk], sem='x%d' % (t % 2))
                S.do('scalar', 'activation', reads=[xk, pfx + 'ssq'], writes=['junk', pfx + 'ssq'], out=junk[:], in_=xb[:], func=AF.Square, accum_out=ssq[:, t:t + 1])
                for k4 in range(4):
                    pb, pk = next_bank()
                    for kk in range(4):
                        kc = k4 * 4 + kk
                        S.do('tensor', 'transpose', reads=[xk, 'ident'], writes=[pk], out=pb[:, kk * 128:(kk + 1) * 128], in_=xb[:, kc * 128:(kc + 1) * 128], identity=ident[:])
                    S.do('vector', 'tensor_copy', reads=[pk], writes=[pfx + 'xT%d' % t], out=xT[:, t, k4 * 4:(k4 + 1) * 4, :], in_=pb[:].rearrange("p (a b) -> p a b", a=4))
            S.do('scalar', 'activation', reads=[pfx + 'ssq'], writes=[pfx + 'rstd'], out=rstd[:, 0:nt], in_=ssq[:, 0:nt], func=AF.Sqrt, scale=1.0 / D, bias=EPS)
            S.do('vector', 'reciprocal', reads=[pfx + 'rstd'], writes=[pfx + 'rstd'], out=rstd[:, 0:nt], in_=rstd[:, 0:nt])

        def gemm_specs(specs):
            def load(sp):
                c0, w, nkc, w_v, gain, gain_key = sp['c0'], sp['w'], sp['nkc'], sp['w_v'], sp['gain'], sp['gain_key']
                bi = C.wbf_i
                C.wbf_i = (C.wbf_i + 1) % 2
                wb = C.wbf[bi]
                wk = 'wbf%d' % bi
                for q in range(nkc // 2):
                    si = C.wst_i
                    C.wst_i = (C.wst_i + 1) % 6
                    ws = C.wst[si]
                    S.dd('sync', ws[:, :, 0:w], w_v[:, q * 2:(q + 1) * 2, c0:c0 + w], writes=['wst%d' % si], sem='w%d' % si)
                    ceng = C.cast_engs[C.cast_i % len(C.cast_engs)]
                    C.cast_i += 1
                    if gain is not None:
                        S.do(ceng, 'tensor_tensor', reads=['wst%d' % si, gain_key], writes=[wk], out=wb[:, q * 2:(q + 1) * 2, 0:w], in0=ws[:, :, 0:w],
                             in1=gain[:, q * 2:(q + 1) * 2].rearrange("p (a o) -> p a o", o=1).to_broadcast([128, 2, w]), op=ALU.mult)
                    else:
                        S.do(ceng, 'tensor_copy', reads=['wst%d' % si], writes=[wk], out=wb[:, q * 2:(q + 1) * 2, 0:w], in_=ws[:, :, 0:w])
                return wb, wk
            nxt = load(specs[0])
            for i, sp in enumerate(specs):
                wb, wk = nxt
                if i + 1 < len(specs):
                    nxt = load(specs[i + 1])
                w, nkc = sp['w'], sp['nkc']
                for t in range(sp['nt']):
                    pb, pk = next_bank()
                    for kc in range(nkc):
                        S.do('tensor', 'matmul', reads=[sp['lhs_key'](t), wk], writes=[pk], out=pb[:, 0:w], lhsT=sp['lhs'](t, kc), rhs=wb[:, kc, 0:w], start=(kc == 0), stop=(kc == nkc - 1))
                    sp['epi'](sp['ci'], t, pb, pk, sp['c0'], w, sp['tag'])

        def mk_specs(lhs, lhs_key, nt, nkc, w_v, chunks, gain, gain_key, epi):
            return [dict(lhs=lhs, lhs_key=lhs_key, nt=nt, nkc=nkc, w_v=w_v, c0=c0, w=w, tag=tag, gain=gain, gain_key=gain_key, epi=epi, ci=ci)
                    for ci, (c0, w, tag) in enumerate(chunks)]

        def gemm(lhs, lhs_key, nt, nkc, w_v, chunks, gain, gain_key, epi):
            gemm_specs(mk_specs(lhs, lhs_key, nt, nkc, w_v, chunks, gain, gain_key, epi))

        def epi_act_store(dst, rstd, rkey, dkey):
            def epi(ci, t, pb, pk, c0, w, tag):
                func, d0 = tag
                zi = C.zst_i
                C.zst_i = (C.zst_i + 1) % 4
                zb = C.zst[zi]
                S.do('scalar', 'activation', reads=[pk, rkey], writes=['zst%d' % zi], out=zb[:, 0:w], in_=pb[:, 0:w], func=func, scale=rstd[:, t:t + 1])
                S.dd('sync', dst(t)[:, d0:d0 + w], zb[:, 0:w], reads=['zst%d' % zi], writes=[(dkey, t, d0)], sem='zo%d' % zi)
            return epi

        lbst = ExitStack()
        lb_bc = lbst.enter_context(nc.sbuf_tensor("lb_bc", [128, 1024], F32))
        oml_bc = lbst.enter_context(nc.sbuf_tensor("oml_bc", [128, 1024], F32))
        with ExitStack() as ph:
            lraw = ph.enter_context(nc.sbuf_tensor("lraw", [128, 2, 1024], F32))
            S.dd('sync', lraw[:], lbl.rearrange("(o a) d -> o a d", o=1).to_broadcast([128, 2, 1024]), writes=['lraw'], sem='c')
            S.do('vector', 'tensor_tensor', reads=['lraw'], writes=['lb_bc'], out=lb_bc[:], in0=lraw[:, 0, :], in1=lraw[:, 1, :], op=ALU.subtract)
            S.do('scalar', 'activation', reads=['lb_bc'], writes=['lb_bc'], out=lb_bc[:], in_=lb_bc[:], func=AF.Sigmoid)
            S.do('vector', 'tensor_scalar', reads=['lb_bc'], writes=['oml_bc'], out=oml_bc[:], in0=lb_bc[:], scalar1=-1.0, scalar2=1.0, op0=ALU.mult, op1=ALU.add)
            S.barrier()

        w_v = w_in.rearrange("(kc p) n -> p kc n", p=128)

        C.nr_n = 0

        def alloc_nr(psb):
            C.nr_n += 1
            u = "_%d" % C.nr_n
            C.tmpa = psb("tmpa" + u, [128, 1024]); C.tmpb = psb("tmpb" + u, [128, 1024])
            C.hss = psb("hss" + u, [128, 32]); C.hrs = psb("hrs" + u, [128, 32])
            C.cst = [psb("cst%d" % i + u, [128, 64]) for i in range(2)]
            C.tab = [psb("tab%d" % i + u, [128, 4, 4, 32]) for i in range(2)]
            C.rowb = [psb("rowb%d" % i + u, [128, 3, 512]) for i in range(2)]

        def norm_rope(src, dst, H, tb, tk, br, scale, rk, wk):
            tmpa, tmpb, hss, hrs = C.tmpa, C.tmpb, C.hss, C.hrs
            n = H * 64
            sq = tmpa[:, 0:n].rearrange("p (h d) -> p h d", d=64)
            S.do('vector', 'tensor_tensor', reads=rk, writes=['tmpa'], out=sq, in0=src, in1=src, op=ALU.mult)
            S.do('vector', 'tensor_reduce', reads=['tmpa'], writes=['hss'], out=hss[:, 0:H], in_=sq, axis=AX.X, op=ALU.add)
            S.do('scalar', 'activation', reads=['hss'], writes=['hrs'], out=hrs[:, 0:H], in_=hss[:, 0:H], func=AF.Sqrt, scale=1.0 / 64, bias=EPS)
            S.do('vector', 'reciprocal', reads=['hrs'], writes=['hrs'], out=hrs[:, 0:H], in_=hrs[:, 0:H])
            if scale != 1.0:
                S.do('vector', 'tensor_scalar', reads=['hrs'], writes=['hrs'], out=hrs[:, 0:H], in0=hrs[:, 0:H], scalar1=scale, scalar2=None, op0=ALU.mult)
            x1 = src[:, :, 0:32]
            x2 = src[:, :, 32:64]
            T = lambda k: tb[:, br, k, :].rearrange("p (o d) -> p o d", o=1).to_broadcast([128, H, 32])
            a = tmpa[:, 0:H * 32].rearrange("p (h d) -> p h d", d=32)
            b = tmpb[:, 0:H * 32].rearrange("p (h d) -> p h d", d=32)
            S.do('vector', 'tensor_tensor', reads=rk + [tk], writes=['tmpa'], out=a, in0=x1, in1=T(0), op=ALU.mult)
            S.do('vector', 'tensor_tensor', reads=rk + [tk], writes=['tmpb'], out=b, in0=x2, in1=T(1), op=ALU.mult)
            S.do('vector', 'tensor_tensor', reads=['tmpa', 'tmpb'], writes=wk, out=dst[:, :, 0:32], in0=a, in1=b, op=ALU.subtract)
            S.do('vector', 'tensor_tensor', reads=rk + [tk], writes=['tmpa'], out=a, in0=x2, in1=T(2), op=ALU.mult)
            S.do('vector', 'tensor_tensor', reads=rk + [tk], writes=['tmpb'], out=b, in0=x1, in1=T(3), op=ALU.mult)
            S.do('vector', 'tensor_tensor', reads=['tmpa', 'tmpb'], writes=wk, out=dst[:, :, 32:64], in0=a, in1=b, op=ALU.add)
            S.do('vector', 'tensor_tensor', reads=wk + ['hrs'], writes=wk, out=dst, in0=dst, in1=hrs[:, 0:H].rearrange("p (h o) -> p h o", o=1).to_broadcast([128, H, 64]), op=ALU.mult)

        def make_tables(cs_dram, T, par):
            cb = C.cst[par]
            tb = C.tab[par]
            S.dd('sync', cb[:], cs_dram[T], writes=['cst%d' % par], sem='cs%d' % par)
            cosb = cb[:, 0:32].rearrange("p (o d) -> p o d", o=1).to_broadcast([128, 4, 32])
            sinb = cb[:, 32:64].rearrange("p (o d) -> p o d", o=1).to_broadcast([128, 4, 32])
            for k, (tr, gs) in enumerate([(cosb, 0), (sinb, 32), (cosb, 32), (sinb, 0)]):
                S.do('gpsimd', 'tensor_tensor', reads=['cst%d' % par, 'qkg_t'], writes=['tab%d' % par], out=tb[:, :, k, :], in0=qkg_t[:, :, gs:gs + 32], in1=tr, op=ALU.mult)
            return tb, 'tab%d' % par

        def kv_rows(zb, zk, base, tb, tk, rb, rbk):
            for br in range(3):
                o = base + br * 512
                norm_rope(zb[:, o:o + 256].rearrange("p (h d) -> p h d", d=64), rb[:, br, 0:256].rearrange("p (h d) -> p h d", d=64), 4, tb, tk, 1 + br, 1.0, [zk], [rbk])
                S.do('gpsimd', 'tensor_copy', reads=[zk], writes=[rbk], out=rb[:, br, 256:512], in_=zb[:, o + 256:o + 512])

        if 'own' in phases:
            with ExitStack() as ph:
                psb = lambda name, shape, dt=F32: ph.enter_context(nc.sbuf_tensor(name, list(shape), dt))
                xT = psb("xT", [128, NT, KC, 128], BF16)
                ssq = psb("ssq", [128, NT])
                rstd = psb("rstd", [128, NT])
                junk = psb("junk", [128, D])
                xst = [psb("xst%d" % i, [128, D]) for i in range(2)]
                make_xT(x_own, 0, NT, xT, ssq, rstd, 'o', xst, junk)
                chunks = [(c0, w, (f, c0)) for (c0, w, f) in chunk_list()]
                gemm(lambda t, kc: xT[:, t, kc, :], lambda t: 'oxT%d' % t, NT, KC, w_v, chunks, g1t, 'g1t',
                     epi_act_store(lambda t: zs[t], rstd, 'orstd', 'zs'))
                S.barrier()
            with ExitStack() as ph:
                psb = lambda name, shape, dt=F32: ph.enter_context(nc.sbuf_tensor(name, list(shape), dt))
                alloc_nr(psb)
                zq = [psb("zq%d" % i, [128, 2560]) for i in range(2)]
                qf = psb("qf", [128, 1024])
                qTb = [psb("qTb%d" % i, [64, 2048], BF16) for i in range(2)]
                zs_keys = [('zs', None, c[0]) for c in chunk_list()[0:5]]
                for t in range(NT):
                    par = t % 2
                    zb, zk = zq[par], 'zq%d' % par
                    rb, rbk = C.rowb[par], 'rowb%d' % par
                    qb, qbk = qTb[par], 'qTb%d' % par
                    S.dd('sync', zb[:], zs[t, :, 0:2560], reads=[('zs', t, k[2]) for k in zs_keys], writes=[zk], sem='zq%d' % par)
                    tb, tk = make_tables(cs_own, t, par)
                    norm_rope(zb[:, 0:1024].rearrange("p (h d) -> p h d", d=64), qf[:].rearrange("p (h d) -> p h d", d=64), 16, tb, tk, 0, 0.125, [zk], ['qf'])
                    for h4 in range(4):
                        pb, pk = next_bank()
                        for hh in range(4):
                            h = h4 * 4 + hh
                            S.do('tensor', 'transpose', reads=['qf', 'ident'], writes=[pk], out=pb[0:64, hh * 128:(hh + 1) * 128], in_=qf[:, h * 64:(h + 1) * 64], identity=ident[:])
                        S.do('scalar', 'activation', reads=[pk], writes=[qbk], out=qb[:, h4 * 512:(h4 + 1) * 512], in_=pb[0:64, :], func=AF.Copy)
                    S.dd('sync', qT_scr[t], qb[:], reads=[qbk], writes=[('qT', t)], sem='qo')
                    kv_rows(zb, zk, 1024, tb, tk, rb, rbk)
                    S.dd('sync', rows_own[t], rb[:], reads=[rbk], writes=[('rows_own', t)], sem='ro')
                S.barrier()

        if 'seq' in phases:
            seq_chunks = []
            for (c0, w, f) in chunk_list():
                if O_KV <= c0 < O_KV + 1536:
                    seq_chunks.append((c0, w, (f, c0 - O_KV)))
                elif O_HF <= c0 < O_HF + 1024:
                    seq_chunks.append((c0, w, (f, 1536 + c0 - O_HF)))
                elif O_HI <= c0 < O_HI + 1024:
                    seq_chunks.append((c0, w, (f, 2560 + c0 - O_HI)))
            with ExitStack() as ph:
                psb = lambda name, shape, dt=F32: ph.enter_context(nc.sbuf_tensor(name, list(shape), dt))
                xT = psb("sxT", [128, 16, KC, 128], BF16)
                ssq = psb("sssq", [128, 16])
                rstd = psb("srstd", [128, 16])
                junk = psb("sjunk", [128, D])
                xst = [psb("sxst%d" % i, [128, D]) for i in range(2)]
                for half in range(2):
                    make_xT(x_seq, half * 16, 16, xT, ssq, rstd, 's', xst, junk)
                    gemm(lambda t, kc: xT[:, t, kc, :], lambda t: 'sxT%d' % t, 16, KC, w_v, seq_chunks, g1t, 'g1t',
                         epi_act_store(lambda t, half=half: zseq[half * 16 + t], rstd, 'srstd', 'zseq%d' % half))
                S.barrier()
            with ExitStack() as ph:
                psb = lambda name, shape, dt=F32: ph.enter_context(nc.sbuf_tensor(name, list(shape), dt))
                alloc_nr(psb)
                zsq = [psb("zsq%d" % i, [128, 3584]) for i in range(2)]
                ktb = [psb("ktb%d" % i, [64, 4, 4, 128], BF16) for i in range(2)]
                vab = [psb("vab%d" % i, [128, 2, 4, 65], BF16) for i in range(2)]
                Sst = psb("Sst", [128, 8, 128])
                Sacc = [psb("Sacc%d" % i, [128, 2, 1024], BF16) for i in range(2)]
                ff2_ = [psb("ff%d" % i, [128, 1024]) for i in range(2)]
                logf2_ = [psb("logf%d" % i, [128, 1024]) for i in range(2)]
                eR2_ = [psb("eR%d" % i, [128, 1024]) for i in range(2)]
                khat2_ = [psb("khat%d" % i, [128, 1024], BF16) for i in range(2)]
                hvb2_ = [psb("hvb%d" % i, [128, 1024], BF16) for i in range(2)]
                ea2_ = [psb("ea%d" % i, [128, 8, 2]) for i in range(2)]
                stmp = psb("stmp", [128, 1024])
                S.do('vector', 'memset', writes=['Sst'], ap=Sst[:], constant=0.0)
                for i in range(2):
                    S.do('gpsimd', 'memset', writes=['vab%d' % i], ap=vab[i][:], constant=1.0)
                for T in range(NSEQ):
                    par = T % 2
                    zb, zk = zsq[par], 'zsq%d' % par
                    rb, rbk = C.rowb[par], 'rowb%d' % par
                    S.dd('sync', zb[:], zseq[T], reads=[('zseq%d' % (T // 16), T % 16, d0) for (_, _, (_, d0)) in seq_chunks], writes=[zk], sem='zq%d' % par)
                    tb, tk = make_tables(cs_seq, T, par)
                    kv_rows(zb, zk, 0, tb, tk, rb, rbk)
                    kb, kbk = ktb[par], 'ktb%d' % par
                    for slot, (br, src_o) in enumerate(((0, 0), (0, 256), (1, 0), (2, 0))):
                        pb, pk = next_bank()
                        for g in range(4):
                            S.do('tensor', 'transpose', reads=[rbk, 'ident'], writes=[pk], out=pb[0:64, g * 128:(g + 1) * 128], in_=rb[:, br, src_o + g * 64:src_o + (g + 1) * 64], identity=ident[:])
                        S.do('scalar', 'activation', reads=[pk], writes=[kbk], out=kb[:, slot, :, :], in_=pb[0:64, :].rearrange("p (g k) -> p g k", g=4), func=AF.Copy)
                    for slot in range(4):
                        S.dd('sync', kT_scr[slot, :, :, T * 128:(T + 1) * 128], kb[:, slot, :, :], reads=[kbk], writes=[('kT', T, slot)], sem='ko')
                    vb, vbk = vab[par], 'vab%d' % par
                    for br in (1, 2):
                        S.do('gpsimd', 'tensor_copy', reads=[rbk], writes=[vbk], out=vb[:, br - 1, :, 0:64], in_=rb[:, br, 256:512].rearrange("p (g d) -> p g d", g=4))
                    S.dd('sync', v_scr[T], vb[:], reads=[vbk], writes=[('v', T)], sem='vo')
                    ff, logf, eR, khat, hvb, ea = ff2_[par], logf2_[par], eR2_[par], khat2_[par], hvb2_[par], ea2_[par]
                    kff, klogf, keR, kkhat, khvb, kea = 'ff%d' % par, 'logf%d' % par, 'eR%d' % par, 'khat%d' % par, 'hvb%d' % par, 'ea%d' % par
                    S.do('vector', 'tensor_tensor', reads=[zk, 'oml_bc'], writes=[kff], out=ff[:], in0=zb[:, 1536:2560], in1=oml_bc[:], op=ALU.mult)
                    S.do('vector', 'tensor_tensor', reads=[kff, 'lb_bc'], writes=[kff], out=ff[:], in0=ff[:], in1=lb_bc[:], op=ALU.add)
                    S.do('scalar', 'activation', reads=[kff], writes=[klogf], out=logf[:], in_=ff[:], func=AF.Ln)
                    S.do('gpsimd', 'tensor_scalar', reads=[kff], writes=[kff], out=ff[:], in0=ff[:], scalar1=-1.0, scalar2=1.0, op0=ALU.mult, op1=ALU.add)
                    for hh in range(2):
                        pb, pk = next_bank()
                        S.do('tensor', 'matmul', reads=['U64', klogf], writes=[pk], out=pb[:], lhsT=U64[:], rhs=logf[:, hh * 512:(hh + 1) * 512], start=True, stop=True)
                        S.do('scalar', 'activation', reads=[pk], writes=[keR], out=eR[:, hh * 512:(hh + 1) * 512], in_=pb[:], func=AF.Exp)
                    S.do('vector', 'tensor_tensor', reads=[kff, keR], writes=[kkhat], out=khat[:], in0=ff[:], in1=eR[:], op=ALU.mult)
                    S.do('gpsimd', 'tensor_copy', reads=[zk], writes=[khvb], out=hvb[:], in_=zb[:, 2560:3584])
                    pb, pk = next_bank()
                    for h in range(8):
                        S.do('tensor', 'matmul', reads=[klogf, 'G2'], writes=[pk], out=pb[:, h * 2:h * 2 + 2], lhsT=logf[:, h * 128:(h + 1) * 128], rhs=G2[:], start=True, stop=True)
                    S.do('scalar', 'activation', reads=[pk], writes=[kea], out=ea[:].rearrange("p h c -> p (h c)"), in_=pb[:, 0:16], func=AF.Exp)
                    for cc in range(2):
                        i_own, o = T // 4, T % 4
                        sa, sak = Sacc[i_own % 2], 'Sacc%d' % (i_own % 2)
                        if o == 0:
                            S.do('vector', 'tensor_scalar', reads=['Sst', 'jsel_t'], writes=[sak], out=sa[:, cc, :], in0=Sst[:].rearrange("p h d -> p (h d)"), scalar1=jsel_t[:, o:o + 1], scalar2=None, op0=ALU.mult)
                        else:
                            S.do('vector', 'tensor_scalar', reads=['Sst', 'jsel_t'], writes=['stmp'], out=stmp[:], in0=Sst[:].rearrange("p h d -> p (h d)"), scalar1=jsel_t[:, o:o + 1], scalar2=None, op0=ALU.mult)
                            S.do('vector', 'tensor_tensor', reads=['stmp', sak], writes=[sak], out=sa[:, cc, :], in0=sa[:, cc, :], in1=stmp[:], op=ALU.add)
                        for h4 in range(2):
                            pb, pk = next_bank()
                            for hh in range(4):
                                h = h4 * 4 + hh
                                S.do('tensor', 'matmul', reads=[kkhat, khvb], writes=[pk], out=pb[:, hh * 128:(hh + 1) * 128], lhsT=khat[cc * 64:(cc + 1) * 64, h * 128:(h + 1) * 128],
                                     rhs=hvb[cc * 64:(cc + 1) * 64, h * 128:(h + 1) * 128], start=True, stop=True)
                            for hh in range(4):
                                h = h4 * 4 + hh
                                S.do('vector', 'scalar_tensor_tensor', reads=['Sst', kea, pk], writes=['Sst'], out=Sst[:, h, :], in0=Sst[:, h, :], scalar=ea[:, h, cc:cc + 1], in1=pb[:, hh * 128:(hh + 1) * 128], op0=ALU.mult, op1=ALU.add)
                    if T % 4 == 3:
                        sa, sak = Sacc[(T // 4) % 2], 'Sacc%d' % ((T // 4) % 2)
                        S.dd('sync', sown_scr[:, 2 * (T // 4):2 * (T // 4) + 2, :], sa[:], reads=[sak], writes=[('sown', T // 4)], sem='so')
                S.dd('sync', p_state.rearrange("h k v -> k h v"), Sst[:], reads=['Sst'], writes=['p_state'], sem='po')
                S.barrier()


        if 'hgrn' in phases:
            with ExitStack() as ph:
                psb = lambda name, shape, dt=F32: ph.enter_context(nc.sbuf_tensor(name, list(shape), dt))
                Dm = psb("Dm", [128, 2, 128]); Pm = psb("Pm", [128, 2, 16]); MT = psb("MT", [128, 2, 128])
                U8 = psb("U8", [128, 128]); G16 = psb("G16", [128, 16]); hng_t = psb("hng_t", [128, 1])
                ones_f = psb("ones_f", [128, 128])
                S.dd('sync', Dm[:], cDm.rearrange("a p t -> p a t"), writes=['Dm'], sem='c')
                S.dd('sync', Pm[:], cPm.rearrange("a p t -> p a t"), writes=['Pm'], sem='c')
                S.dd('sync', MT[:], cMT.rearrange("a p t -> p a t"), writes=['MT'], sem='c')
                S.dd('sync', U8[:], cU8, writes=['U8'], sem='c')
                S.dd('sync', G16[:], cG16, writes=['G16'], sem='c')
                S.dd('sync', hng_t[:], hng, writes=['hng_t'], sem='c')
                S.do('gpsimd', 'memset', writes=['ones_f'], ap=ones_f[:], constant=1.0)
                zh = psb("zh", [128, 4096])
                logf = psb("hlogf", [128, 1024]); hk = psb("hhk", [128, 1024])
                ex = psb("hex", [128, 1024]); qh = psb("hqh", [128, 1024]); kh = psb("hkh", [128, 1024])
                qhT = psb("qhT", [128, 8, 128], BF16); khT = psb("khT", [128, 8, 128], BF16)
                vb = psb("hvb2", [128, 1024], BF16)
                attT = psb("attT", [128, 8, 128], BF16)
                ogT = psb("ogT", [128, 8, 128])
                Sp = psb("Sp", [128, 4, 8, 128], BF16)
                S0 = psb("S0", [128, 4, 8, 128])
                em = psb("hem", [128, 8, 16])
                oTs = psb("oTs", [128, 512]); sqT = psb("sqT", [128, 512]); rbc = psb("rbc", [128, 512])
                obT = psb("obT", [128, 8, 128], BF16)
                kmask = psb("kmask", [128, 1024], BF16)
                zh_keys = [c0 for (c0, w, f) in chunk_list() if O_HQ <= c0 < O_A]
                C.pb_n = 6
                for t in range(NT):
                    kind = 0 if t < 8 else 1
                    G = 2 if kind == 0 else 16
                    L = 128 // G
                    S.dd('sync', zh[:], zs[t, :, O_HQ:O_A], reads=[('zs', t, c0) for c0 in zh_keys], writes=['zh'], sem='zh')
                    hq = zh[:, 0:1024]; sf = zh[:, 1024:2048]; hv = zh[:, 2048:3072]; og = zh[:, 3072:4096]
                    S.do('vector', 'tensor_tensor', reads=['zh', 'oml_bc'], writes=['hk'], out=hk[:], in0=sf, in1=oml_bc[:], op=ALU.mult)
                    S.do('vector', 'tensor_tensor', reads=['hk', 'lb_bc'], writes=['hk'], out=hk[:], in0=hk[:], in1=lb_bc[:], op=ALU.add)
                    S.do('scalar', 'activation', reads=['hk'], writes=['hlogf'], out=logf[:], in_=hk[:], func=AF.Ln)
                    S.do('gpsimd', 'tensor_scalar', reads=['hk'], writes=['hk'], out=hk[:], in0=hk[:], scalar1=-1.0, scalar2=1.0, op0=ALU.mult, op1=ALU.add)
                    S.do('gpsimd', 'tensor_copy', reads=['zh'], writes=['hvb2'], out=vb[:], in_=hv)
                    for hh in range(2):
                        pb, pk = next_bank()
                        S.do('tensor', 'matmul', reads=['Dm', 'hlogf'], writes=[pk], out=pb[:], lhsT=Dm[:, kind, :], rhs=logf[:, hh * 512:(hh + 1) * 512], start=True, stop=True)
                        S.do('vector', 'tensor_scalar', reads=[pk], writes=['hex'], out=ex[:, hh * 512:(hh + 1) * 512], in0=pb[:], scalar1=40.0, scalar2=None, op0=ALU.min)
                        S.do('scalar', 'activation', reads=['hex'], writes=['hex'], out=ex[:, hh * 512:(hh + 1) * 512], in_=ex[:, hh * 512:(hh + 1) * 512], func=AF.Exp)
                        S.do('vector', 'tensor_tensor', reads=['hex', 'zh'], writes=['hqh'], out=qh[:, hh * 512:(hh + 1) * 512], in0=ex[:, hh * 512:(hh + 1) * 512], in1=hq[:, hh * 512:(hh + 1) * 512], op=ALU.mult)
                        S.do('vector', 'tensor_scalar', reads=[pk, 'hqh'], writes=['hex'], out=ex[:, hh * 512:(hh + 1) * 512], in0=pb[:], scalar1=-1.0, scalar2=40.0, op0=ALU.mult, op1=ALU.min)
                        S.do('scalar', 'activation', reads=['hex'], writes=['hex'], out=ex[:, hh * 512:(hh + 1) * 512], in_=ex[:, hh * 512:(hh + 1) * 512], func=AF.Exp)
                        S.do('vector', 'tensor_tensor', reads=['hex', 'hk'], writes=['hkh'], out=kh[:, hh * 512:(hh + 1) * 512], in0=ex[:, hh * 512:(hh + 1) * 512], in1=hk[:, hh * 512:(hh + 1) * 512], op=ALU.mult)
                    for (src, sk_, dst, dk_) in ((qh, 'hqh', qhT, 'qhT'), (kh, 'hkh', khT, 'khT'), (None, 'zh', ogT, 'ogT')):
                        for h4 in range(2):
                            pb, pk = next_bank()
                            for hh in range(4):
                                h = h4 * 4 + hh
                                in_ap = og[:, h * 128:(h + 1) * 128] if src is None else src[:, h * 128:(h + 1) * 128]
                                S.do('tensor', 'transpose', reads=[sk_, 'ident'], writes=[pk], out=pb[:, hh * 128:(hh + 1) * 128], in_=in_ap, identity=ident[:])
                            S.do('scalar', 'activation', reads=[pk], writes=[dk_], out=dst[:, h4 * 4:(h4 + 1) * 4, :], in_=pb[:].rearrange("p (a b) -> p a b", a=4), func=AF.Copy)
                    for h4 in range(2):
                        pb, pk = next_bank()
                        for hh in range(4):
                            h = h4 * 4 + hh
                            S.do('tensor', 'matmul', reads=['khT', 'qhT'], writes=[pk], out=pb[:, hh * 128:(hh + 1) * 128], lhsT=khT[:, h, :], rhs=qhT[:, h, :], start=True, stop=True)
                        S.do('vector', 'tensor_tensor', reads=[pk, 'MT'], writes=['attT'], out=attT[:, h4 * 4:(h4 + 1) * 4, :], in0=pb[:].rearrange("p (a b) -> p a b", a=4),
                             in1=MT[:, kind, :].rearrange("p (o t) -> p o t", o=1).to_broadcast([128, 4, 128]), op=ALU.mult)
                    pb, pk = next_bank()
                    for h in range(8):
                        S.do('tensor', 'matmul', reads=['hlogf', 'Pm'], writes=[pk], out=pb[:, h * 16:(h + 1) * 16], lhsT=logf[:, h * 128:(h + 1) * 128], rhs=Pm[:, kind, :], start=True, stop=True)
                    S.do('scalar', 'activation', reads=[pk], writes=['hem'], out=em[:].rearrange("p h g -> p (h g)"), in_=pb[:, 0:128], func=AF.Exp)
                    if kind == 1:
                        for hh in range(2):
                            pb, pk = next_bank()
                            S.do('tensor', 'matmul', reads=['U8', 'hlogf'], writes=[pk], out=pb[:], lhsT=U8[:], rhs=logf[:, hh * 512:(hh + 1) * 512], start=True, stop=True)
                            S.do('scalar', 'activation', reads=[pk], writes=['hex'], out=ex[:, hh * 512:(hh + 1) * 512], in_=pb[:], func=AF.Exp)
                        S.do('vector', 'tensor_tensor', reads=['hex', 'hk'], writes=['hkh'], out=kh[:], in0=ex[:], in1=hk[:], op=ALU.mult)
                        pb, pk = next_bank()
                        for h in range(8):
                            S.do('tensor', 'matmul', reads=['hlogf', 'G16'], writes=[pk], out=pb[:, h * 16:(h + 1) * 16], lhsT=logf[:, h * 128:(h + 1) * 128], rhs=G16[:], start=True, stop=True)
                        S.do('scalar', 'activation', reads=[pk], writes=['hem'], out=em[:].rearrange("p h g -> p (h g)"), in_=pb[:, 0:128], func=AF.Exp)
                    nb = 1 if kind == 0 else 4
                    gb = G // nb
                    oT_banks = [(pbank[6], 'pb6'), (pbank[7], 'pb7')]
                    for h4 in range(2):
                        pbo, pko = oT_banks[h4]
                        for hh in range(4):
                            h = h4 * 4 + hh
                            S.do('tensor', 'matmul', reads=['hvb2', 'attT'], writes=[pko], out=pbo[:, hh * 128:(hh + 1) * 128], lhsT=vb[:, h * 128:(h + 1) * 128], rhs=attT[:, h, :], start=(hh == 0), stop=False)
                    for b_ in range(nb):
                        if kind == 0:
                            S.dd('sync', Sp[:, 0:2, :, :].rearrange("p g h d -> p g (h d)"), sown_scr[:, 2 * t:2 * t + 2, :], reads=[('sown', t)], writes=['Sp'], sem='sp')
                            for g in range(2):
                                for h in range(8):
                                    S.do('vector', 'tensor_scalar', reads=['Sp', 'hem'], writes=['Sp'], out=Sp[:, g, h, :], in0=Sp[:, g, h, :], scalar1=em[:, h, g:g + 1], scalar2=None, op0=ALU.mult)
                        else:
                            S.dd('sync', S0[:].rearrange("p g h d -> p (g h) d"), st_in[b_ * 4:(b_ + 1) * 4].rearrange("g h k d -> k (g h) d"), reads=[], writes=['S0'], sem='sp')
                            S.do('gpsimd', 'tensor_copy', reads=['S0'], writes=['Sp'], out=Sp[:], in_=S0[:])
                        for gl in range(gb):
                            g = b_ * gb + gl
                            for h in range(8):
                                pbo, pko = oT_banks[h // 4]
                                hh = h % 4
                                S.do('tensor', 'matmul', reads=['Sp', 'qhT'], writes=[pko], out=pbo[:, hh * 128 + g * L:hh * 128 + (g + 1) * L], lhsT=Sp[:, gl, h, :], rhs=qhT[:, h, g * L:(g + 1) * L],
                                     start=False, stop=(b_ == nb - 1 and gl == gb - 1))
                        if kind == 1:
                            for gl in range(gb):
                                g = b_ * gb + gl
                                S.do('vector', 'tensor_scalar', reads=['hkh', 'G16'], writes=['kmask'], out=kmask[:], in0=kh[:], scalar1=G16[:, g:g + 1], scalar2=None, op0=ALU.mult)
                                for h4 in range(2):
                                    pb, pk = next_bank()
                                    for hh in range(4):
                                        h = h4 * 4 + hh
                                        S.do('tensor', 'matmul', reads=['kmask', 'hvb2'], writes=[pk], out=pb[:, hh * 128:(hh + 1) * 128], lhsT=kmask[:, h * 128:(h + 1) * 128], rhs=vb[:, h * 128:(h + 1) * 128], start=True, stop=True)
                                    for hh in range(4):
                                        h = h4 * 4 + hh
                                        S.do('vector', 'scalar_tensor_tensor', reads=['S0', 'hem', pk], writes=['S0'], out=S0[:, gl, h, :], in0=S0[:, gl, h, :], scalar=em[:, h, g:g + 1], in1=pb[:, hh * 128:(hh + 1) * 128], op0=ALU.mult, op1=ALU.add)
                            S.dd('sync', s_state[b_ * 4:(b_ + 1) * 4].rearrange("g h k d -> k (g h) d"), S0[:].rearrange("p g h d -> p (g h) d"), reads=['S0'], writes=[('s_state', b_)], sem='sso')
                    for h4 in range(2):
                        pbo, pko = oT_banks[h4]
                        S.do('scalar', 'activation', reads=[pko], writes=['oTs'], out=oTs[:], in_=pbo[:], func=AF.Copy)
                        S.do('vector', 'tensor_tensor', reads=['oTs'], writes=['sqT'], out=sqT[:], in0=oTs[:], in1=oTs[:], op=ALU.mult)
                        pb, pk = next_bank()
                        S.do('tensor', 'matmul', reads=['ones_f', 'sqT'], writes=[pk], out=pb[:], lhsT=ones_f[:], rhs=sqT[:], start=True, stop=True)
                        S.do('scalar', 'activation', reads=[pk], writes=['rbc'], out=rbc[:], in_=pb[:], func=AF.Sqrt, scale=1.0 / 128, bias=EPS)
                        S.do('vector', 'reciprocal', reads=['rbc'], writes=['rbc'], out=rbc[:], in_=rbc[:])
                        S.do('vector', 'tensor_tensor', reads=['rbc', 'oTs'], writes=['oTs'], out=oTs[:], in0=oTs[:], in1=rbc[:], op=ALU.mult)
                        S.do('vector', 'scalar_tensor_tensor', reads=['oTs', 'hng_t', 'ogT'], writes=['obT'], out=obT[:, h4 * 4:(h4 + 1) * 4, :].rearrange("p a b -> p (a b)"), in0=oTs[:], scalar=hng_t[:, 0:1],
                             in1=ogT[:, h4 * 4:(h4 + 1) * 4, :].rearrange("p a b -> p (a b)"), op0=ALU.mult, op1=ALU.mult)
                    S.dd('sync', obT_scr[t], obT[:], reads=['obT'], writes=[('obT', t)], sem='obo')
                    if t in (0, 3, 8):
                        dbg("obT%d" % t, obT[:], [128, 8, 128], BF16, ['obT'])
                S.barrier()
                C.pb_n = 8


        S.barrier()
        lbst.close()
        gst.close()
        if 'att' in phases:
            NEGB = -30000.0
            with ExitStack() as ph:
                psb = lambda name, shape, dt=F32: ph.enter_context(nc.sbuf_tensor(name, list(shape), dt))
                C.pb_n = 3
                ACC = [(pbank[3 + i], 'pb%d' % (3 + i)) for i in range(4)]
                MISC = (pbank[7], 'pb7')
                identb = psb("identb", [128, 128], BF16)
                S.do('vector', 'tensor_copy', reads=['ident'], writes=['identb'], out=identb[:], in_=ident[:])
                E = psb("E", [64, 4096], BF16)
                S.dd('sync', E[:], cE, writes=['E'], sem='c')
                selb = psb("selb", [128, 4, 512], BF16); winb = psb("winb", [128, 8, 512], BF16)
                S.dd('sync', selb[:], cselb.rearrange("a p c -> p a c"), writes=['selb'], sem='c')
                S.dd('sync', winb[:], cwinb.rearrange("a p c -> p a c"), writes=['winb'], sem='c')
                ovp = psb("ovp", [128, 2, 64], BF16); ovs = psb("ovs", [128, 64], BF16)
                S.dd('sync', ovp[:], covp.rearrange("a p c -> p a c"), writes=['ovp'], sem='c')
                S.dd('sync', ovs[:], covs, writes=['ovs'], sem='c')
                OH = psb("OH", [17, 16, 128], BF16); seqb = psb("seqb", [17, 512], BF16)
                S.dd('sync', OH[:], cOH, writes=['OH'], sem='c')
                S.dd('sync', seqb[:], cseqb, writes=['seqb'], sem='c')
                bdc = psb("bdc", [128, 512], BF16); wb0 = psb("wb0", [128, 512], BF16)
                S.dd('sync', bdc[:], cbdc, writes=['bdc'], sem='c')
                S.dd('sync', wb0[:], cwb0, writes=['wb0'], sem='c')
                w1b = psb("w1b", [64, 2, 32, 64], BF16); w2b = psb("w2b", [64, 2, 64], BF16); peb = psb("peb", [64, 2])
                with ExitStack() as ph2:
                    psb2 = lambda name, shape, dt=F32: ph2.enter_context(nc.sbuf_tensor(name, list(shape), dt))
                    w1f = psb2("w1f", [64, 2, 32, 64]); w2f = psb2("w2f", [64, 2, 64]); pef = psb2("pef", [64, 2, 32]); pebf = psb2("pebf", [64, 2, 32], BF16)
                    for kv in range(2):
                        S.dd('sync', w1f[:, kv], cw1[kv].rearrange("(l d) h -> d l h", d=64), writes=['w1f'], sem='c')
                        S.dma('sync', lambda e, kv=kv: e.dma_start(out=pef[:, kv], in_=cpe[kv].rearrange("l d -> d l"), allow_slow_non_contiguous=True), writes=['pef'], sem='c')
                    S.dd('sync', w2f[:], cw2.rearrange("a h d -> h a d"), writes=['w2f'], sem='c')
                    S.do('vector', 'tensor_copy', reads=['w1f'], writes=['w1b'], out=w1b[:], in_=w1f[:])
                    S.do('vector', 'tensor_copy', reads=['w2f'], writes=['w2b'], out=w2b[:], in_=w2f[:])
                    S.do('vector', 'tensor_copy', reads=['pef'], writes=['pebf'], out=pebf[:], in_=pef[:])
                    for kv in range(2):
                        pb, pk = next_bank()
                        for l in range(32):
                            S.do('tensor', 'matmul', reads=['w1b', 'pebf'], writes=[pk], out=pb[0:64, 0:1], lhsT=w1b[:, kv, l, :], rhs=pebf[:, kv, l:l + 1], start=(l == 0), stop=(l == 31))
                        S.do('vector', 'tensor_copy', reads=[pk], writes=['peb'], out=peb[:, kv:kv + 1], in_=pb[0:64, 0:1])
                    S.barrier()

                hsil = psb("hsil", [64, 256], BF16)

                def compress(XT, xkeys, n, kcT_dst, kkey, vc_dst, vkey):
                    for kv in range(2):
                        for g in range(4):
                            pb, pk = next_bank()
                            xt = XT(kv, g)
                            for l in range(32):
                                S.do('tensor', 'matmul', reads=xkeys + ['w1b'], writes=[pk], out=pb[0:64, 0:n], lhsT=w1b[:, kv, l, :], rhs=xt[:, l:l + 16 * (n - 1) + 1:16], start=(l == 0), stop=(l == 31))
                            S.do('scalar', 'activation', reads=[pk, 'peb'], writes=['hsil'], out=hsil[:, 0:n], in_=pb[0:64, 0:n], func=AF.Silu, bias=peb[:, kv:kv + 1])
                            pb2, pk2 = next_bank()
                            if kv == 0:
                                S.do('tensor', 'matmul', reads=['hsil', 'w2b'], writes=[pk2], out=pb2[0:64, 0:n], lhsT=w2b[:, 0, :], rhs=hsil[:, 0:n], start=True, stop=True)
                                S.do('vector', 'tensor_copy', reads=[pk2], writes=[kkey], out=kcT_dst(g), in_=pb2[0:64, 0:n])
                            else:
                                for ct in range((n + 127) // 128):
                                    m = min(128, n - ct * 128)
                                    S.do('tensor', 'matmul', reads=['hsil', 'w2b'], writes=[pk2], out=pb2[0:m, ct * 64:(ct + 1) * 64], lhsT=hsil[:, ct * 128:ct * 128 + m], rhs=w2b[:, 1, :], start=True, stop=True)
                                    S.do('vector', 'tensor_copy', reads=[pk2], writes=[vkey], out=vc_dst(g, ct, m), in_=pb2[0:m, ct * 64:(ct + 1) * 64])

                ptb = [psb("ptb%d" % i, [128, 512], BF16) for i in range(3)]
                C.pt_i = 0

                ptz = [psb("ptz%d" % i, [128, 512], BF16) for i in range(3)]
                for i in range(3):
                    S.do('gpsimd', 'memset', writes=['ptz%d' % i], ap=ptz[i][:], constant=0.0)
                C.ptz_i = 0

                def attend_g(qT, qkey, jobs, naccs, gs):
                    started = set()
                    last = {}
                    for ji, jb in enumerate(jobs):
                        for g in gs:
                            for (aid, _, _) in jb['V'](g):
                                last[(aid, g)] = ji

                    def stage1(ji, jb, g):
                        nk = jb['nk']
                        qs = jb.get('qs')
                        pb, pk = next_bank()
                        bl = jb['bias'](g)
                        if qs is None:
                            sel = lambda ap: ap
                            ob = pb[0:nk, :]
                        else:
                            sel = lambda ap: ap.rearrange("p (r q) -> p r q", r=4)[:, :, qs * 8:(qs + 1) * 8]
                            ob = pb[0:nk, 0:32].rearrange("p (r q) -> p r q", r=4)
                        S.do('tensor', 'matmul', reads=jb['keys'] + [qkey], writes=[pk], out=ob, lhsT=jb['KT'](g), rhs=sel(qT[:, g * 512:(g + 1) * 512]), start=True, stop=(len(bl) == 0))
                        for bi, (bl_l, bl_r, bkeys) in enumerate(bl):
                            S.do('tensor', 'matmul', reads=bkeys, writes=[pk], out=ob, lhsT=bl_l, rhs=sel(bl_r), start=False, stop=(bi == len(bl) - 1))
                        if qs is None:
                            pi = C.pt_i
                            C.pt_i = (C.pt_i + 1) % 3
                            pt, ptk = ptb[pi], 'ptb%d' % pi
                            S.do('scalar', 'activation', reads=[pk], writes=[ptk], out=pt[0:nk, :], in_=ob, func=AF.Exp)
                        else:
                            pi = C.ptz_i
                            C.ptz_i = (C.ptz_i + 1) % 3
                            pt, ptk = ptz[pi], 'ptz%d' % pi
                            S.do('scalar', 'activation', reads=[pk], writes=[ptk], out=sel(pt[0:nk, :]), in_=ob, func=AF.Exp)
                        return (ji, jb, g, nk, qs, pt, ptk, jb['V'](g), list(jb['keys']))

                    def stage2(u):
                        ji, jb, g, nk, qs, pt, ptk, vl, keys = u
                        for (aid, vap, nV) in vl:
                            ab, abk = naccs[aid][g]
                            for r in range(4):
                                S.do('tensor', 'matmul', reads=[ptk] + keys, writes=[abk], out=ab[:, r * nV:(r + 1) * nV], lhsT=pt[0:nk, r * 128:(r + 1) * 128], rhs=vap,
                                     start=((aid, g) not in started), stop=(last[(aid, g)] == ji and r == 3))
                                started.add((aid, g))
                        if qs is not None:
                            S.do('vector', 'memset', reads=[], writes=[ptk], ap=pt[0:nk, :].rearrange("p (r q) -> p r q", r=4)[:, :, qs * 8:(qs + 1) * 8], constant=0.0)

                    prev = None
                    for ji, jb in enumerate(jobs):
                        if 'prep' in jb:
                            jb['prep']()
                        for g in gs:
                            cur = stage1(ji, jb, g)
                            if prev is not None:
                                stage2(prev)
                            prev = cur
                    if prev is not None:
                        stage2(prev)

                qTt = [psb("qTt%d" % i, [64, 2048], BF16) for i in range(2)]
                gates = psb("gates", [128, 48])
                visadd = psb("visadd", [128, 2, 64])
                oa = psb("oa", [128, 1024])
                linv = psb("linv", [128, 4])
                imp = psb("imp", [128, 64]); imt = psb("imt", [128, 64])
                m8 = psb("m8", [128, 8]); sc2 = psb("sc2", [128, 64]); thr = psb("thr", [128, 1])
                selbT = [psb("selbT%d" % g, [64, 512], BF16) for g in range(4)]
                osc = psb("osc", [128, 4, 64])
                oaT = psb("oaT_a", [128, 8, 128], BF16)
                cmpb_t = psb("cmpb_t", [128, 2, 512], BF16)

                def topk_bias(t, g, imp_ap):
                    S.do('vector', 'tensor_tensor', reads=['imp', 'visadd'], writes=['imt'], out=imt[:], in0=imp_ap, in1=visadd[:, 0, :], op=ALU.mult)
                    S.do('vector', 'tensor_tensor', reads=['imt', 'visadd'], writes=['imt'], out=imt[:], in0=imt[:], in1=visadd[:, 1, :], op=ALU.add)
                    S.do('vector', 'max', reads=['imt'], writes=['m8'], out=m8[:], in_=imt[:])
                    S.do('vector', 'match_replace', reads=['imt', 'm8'], writes=['sc2'], out=sc2[:], in_to_replace=m8[:], in_values=imt[:], imm_value=-3.0e9)
                    S.do('vector', 'max', reads=['sc2'], writes=['m8'], out=m8[:], in_=sc2[:])
                    S.do('vector', 'tensor_reduce', reads=['m8'], writes=['thr'], out=thr[:], in_=m8[:], axis=AX.X, op=ALU.min)
                    S.do('vector', 'tensor_scalar', reads=['imt', 'thr'], writes=['sc2'], out=sc2[:], in0=imt[:], scalar1=thr[:, 0:1], scalar2=None, op0=ALU.is_ge)
                    S.do('vector', 'tensor_tensor', reads=['sc2', 'visadd'], writes=['sc2'], out=sc2[:], in0=sc2[:], in1=visadd[:, 0, :], op=ALU.mult)
                    S.do('vector', 'tensor_scalar', reads=['sc2'], writes=['sc2'], out=sc2[:], in0=sc2[:], scalar1=-NEGB, scalar2=NEGB, op0=ALU.mult, op1=ALU.add)
                    mb, mk = MISC
                    S.do('tensor', 'transpose', reads=['sc2', 'ident'], writes=[mk], out=mb[0:64, 0:128], in_=sc2[:], identity=ident[:])
                    for r in range(4):
                        S.do('scalar', 'activation', reads=[mk], writes=['selbT%d' % g], out=selbT[g][:, r * 128:(r + 1) * 128], in_=mb[0:64, 0:128], func=AF.Copy)

                def normalize(acc, g, br, nV):
                    ab, abk = acc
                    av = ab[:, 0:4 * nV].rearrange("p (r v) -> p r v", r=4)
                    S.do('vector', 'tensor_scalar', reads=[abk], writes=['linv'], out=linv[:], in0=av[:, :, 64], scalar1=1e-30, scalar2=None, op0=ALU.max)
                    S.do('vector', 'reciprocal', reads=['linv'], writes=['linv'], out=linv[:], in_=linv[:])
                    S.do('vector', 'tensor_tensor', reads=['linv', 'gates'], writes=['linv'], out=linv[:], in0=linv[:], in1=gates[:, br * 16 + g * 4:br * 16 + g * 4 + 4], op=ALU.mult)
                    dst = oa[:, g * 256:(g + 1) * 256].rearrange("p (r d) -> p r d", r=4)
                    lb_ = linv[:].rearrange("p (r o) -> p r o", o=1).to_broadcast([128, 4, 64])
                    if br == 0:
                        S.do('vector', 'tensor_tensor', reads=[abk, 'linv'], writes=['oa'], out=dst, in0=av[:, :, 0:64], in1=lb_, op=ALU.mult)
                    else:
                        S.do('vector', 'tensor_tensor', reads=[abk, 'linv'], writes=['osc'], out=osc[:], in0=av[:, :, 0:64], in1=lb_, op=ALU.mult)
                        S.do('vector', 'tensor_tensor', reads=['osc', 'oa'], writes=['oa'], out=dst, in0=dst, in1=osc[:], op=ALU.add)

                def cmp_finish_g(t, naccs, gs):
                    accs1, accs2 = naccs[0], naccs[1]
                    for g in gs:
                        ab, abk = accs1[g]
                        av = ab[:, 0:260].rearrange("p (r v) -> p r v", r=4)
                        S.do('vector', 'tensor_scalar', reads=[abk], writes=['linv'], out=linv[:], in0=av[:, :, 64], scalar1=1e-30, scalar2=None, op0=ALU.max)
                        S.do('vector', 'reciprocal', reads=['linv'], writes=['linv'], out=linv[:], in_=linv[:])
                        a2, a2k = accs2[g]
                        a2v = a2[:, 0:256].rearrange("p (r v) -> p r v", r=4)
                        S.do('vector', 'tensor_scalar', reads=[a2k, 'linv'], writes=['imp'], out=imp[:], in0=a2v[:, 0, :], scalar1=linv[:, 0:1], scalar2=None, op0=ALU.mult)
                        for r in range(1, 4):
                            S.do('vector', 'scalar_tensor_tensor', reads=[a2k, 'linv', 'imp'], writes=['imp'], out=imp[:], in0=a2v[:, r, :], scalar=linv[:, r:r + 1], in1=imp[:], op0=ALU.mult, op1=ALU.add)
                        topk_bias(t, g, imp[:])
                        normalize(accs1[g], g, 0, 65)

                def finish_tile(t):
                    for k2 in range(2):
                        pb, pk = next_bank()
                        for kk in range(4):
                            kc = k2 * 4 + kk
                            S.do('tensor', 'transpose', reads=['oa', 'ident'], writes=[pk], out=pb[:, kk * 128:(kk + 1) * 128], in_=oa[:, kc * 128:(kc + 1) * 128], identity=ident[:])
                        S.do('scalar', 'activation', reads=[pk], writes=['oaT_a'], out=oaT[:, k2 * 4:(k2 + 1) * 4, :], in_=pb[:].rearrange("p (a b) -> p a b", a=4), func=AF.Copy)
                    S.dd('sync', oaT_scr[t], oaT[:], reads=['oaT_a'], writes=[('oaT', t)], sem='oao')

                def load_tile(t):
                    qb, qk = qTt[t % 2], 'qTt%d' % (t % 2)
                    S.dd('sync', qb[:], qT_scr[t], reads=[('qT', t)], writes=[qk], sem='qt%d' % (t % 2))
                    S.dd('sync', gates[:], zs[t, :, O_GA:O_GA + 48], reads=[('zs', t, O_GA)], writes=['gates'], sem='c')
                    S.dd('sync', visadd[:], cvisadd[t], writes=['visadd'], sem='c')
                    return qb, qk

                if 'attp' in phases:
                  with ExitStack() as ph2:
                    psb2 = lambda name, shape, dt=F32: ph2.enter_context(nc.sbuf_tensor(name, list(shape), dt))
                    kcT = psb2("kcT", [64, 4, 256], BF16)
                    vca = psb2("vca", [128, 2, 4, 65], BF16)
                    S.do('vector', 'memset', writes=['kcT'], ap=kcT[:], constant=0.0)
                    S.do('vector', 'memset', writes=['vca'], ap=vca[:], constant=0.0)
                    S.do('vector', 'memset', writes=['vca'], ap=vca[:, :, :, 64:65], constant=1.0)
                    with ExitStack() as ph3:
                        XT = ph3.enter_context(nc.sbuf_tensor("XTp", [64, 2, 4, 4096], BF16))
                        for kv in range(2):
                            S.dd('sync', XT[:, kv], kT_scr[kv], reads=[('kT', T, kv) for T in range(NSEQ)], writes=['XTp'], sem='c')
                        compress(lambda kv, g: XT[:, kv, g, :], ['XTp'], 255, lambda g: kcT[:, g, 0:255], 'kcT', lambda g, ct, m: vca[0:m, ct, g, 0:64], 'vca')
                        dbg("kcT", kcT[:], [64, 4, 256], BF16, ['kcT'])
                        dbg("vca", vca[:], [128, 2, 4, 65], BF16, ['vca'])
                        S.barrier()
                    KTs = psb2("KTs", [64, 4, 4096], BF16); KTw = psb2("KTw", [64, 4, 4096], BF16)
                    Vsw = psb2("Vsw", [128, NSEQ, 2, 4, 65], BF16)
                    S.dd('sync', KTs[:], kT_scr[2], reads=[('kT', T, 2) for T in range(NSEQ)], writes=['KTs'], sem='c')
                    S.dd('sync', KTw[:], kT_scr[3], reads=[('kT', T, 3) for T in range(NSEQ)], writes=['KTw'], sem='c')
                    for T in range(NSEQ):
                        S.dd('sync', Vsw[:, T].rearrange("p a g d -> p (a g d)"), v_scr[T].rearrange("p a g d -> p (a g d)"), reads=[('v', T)], writes=['Vsw'], sem='c')
                    for t in range(8):
                        qb, qk = load_tile(t)
                        S.dd('sync', cmpb_t[:], ccmpb[t].rearrange("a p c -> p a c"), writes=['cmpb_t'], sem='c')
                        for gp in range(2):
                            gs = (2 * gp, 2 * gp + 1)
                            jobs = []
                            for ct in range(2):
                                jobs.append(dict(nk=128, keys=['kcT', 'vca', 'ovp', 'cmpb_t', 'identb'],
                                                 KT=lambda g, ct=ct: kcT[:, g, ct * 128:(ct + 1) * 128],
                                                 V=lambda g, ct=ct: [(0, vca[:, ct, g, :], 65), (1, ovp[:, ct, :], 64)],
                                                 bias=lambda g, ct=ct: [(identb[:], cmpb_t[:, ct, :], ['identb', 'cmpb_t'])]))
                            naccs = {0: {gs[0]: ACC[0], gs[1]: ACC[1]}, 1: {gs[0]: ACC[2], gs[1]: ACC[3]}}
                            attend_g(qb, qk, jobs, naccs, gs)
                            cmp_finish_g(t, naccs, gs)
                        if t in (0, 3, 7):
                            dbg("oc%d" % t, oa[:], [128, 1024], F32, ['oa'])
                            dbg("selb%d" % t, selbT[1][:], [64, 512], BF16, ['selbT1'])
                        i = t
                        jobs = []
                        for kt in range(4 * i + 4):
                            def bias(g, kt=kt, i=i):
                                bl = [(E[:, kt * 128:(kt + 1) * 128], selbT[g][:], ['E', 'selbT%d' % g])]
                                if kt >= 4 * i:
                                    bl.append((identb[:], selb[:, kt - 4 * i, :], ['identb', 'selb']))
                                return bl
                            jobs.append(dict(nk=128, keys=['KTs', 'Vsw'], KT=lambda g, kt=kt: KTs[:, g, kt * 128:(kt + 1) * 128],
                                             V=lambda g, kt=kt: [(0, Vsw[:, kt, 0, g, :], 65)], bias=bias))
                        naccs = {0: {g: ACC[g] for g in range(4)}}
                        attend_g(qb, qk, jobs, naccs, (0, 1, 2, 3))
                        for g in range(4):
                            normalize(ACC[g], g, 1, 65)
                        jobs = []
                        for o in range(-4, 4):
                            kt = 4 * i + o
                            if kt < 0:
                                continue
                            jobs.append(dict(nk=128, keys=['KTw', 'Vsw'], KT=lambda g, kt=kt: KTw[:, g, kt * 128:(kt + 1) * 128],
                                             V=lambda g, kt=kt: [(0, Vsw[:, kt, 1, g, :], 65)],
                                             bias=lambda g, o=o: [(identb[:], winb[:, o + 4, :], ['identb', 'winb'])]))
                        attend_g(qb, qk, jobs, naccs, (0, 1, 2, 3))
                        for g in range(4):
                            normalize(ACC[g], g, 2, 65)
                        if t in (0, 3, 7):
                            dbg("oa%d" % t, oa[:], [128, 1024], F32, ['oa'])
                        finish_tile(t)
                    S.barrier()

                if 'atts' in phases:
                  with ExitStack() as ph2:
                    psb2 = lambda name, shape, dt=F32: ph2.enter_context(nc.sbuf_tensor(name, list(shape), dt))
                    t = 8
                    for s_ in range(16):
                        for q4 in range(4):
                            S.dd('sync', s_win[s_, q4 * 126:(q4 + 1) * 126, :], win_in[s_, 8 + q4 * 126:8 + (q4 + 1) * 126, :], writes=[('s_win_a', s_, q4)], sem='swo')
                    pti = psb2("pti", [128, 256], I32); ptf = psb2("ptf", [128, 256]); idx = psb2("idx", [128, 256], I32)
                    pio = psb2("pio", [128, 1], I32); piof = psb2("piof", [128, 1])
                    S.dd('sync', pti[:], pt_in.to_broadcast([128, 256]), writes=['pti'], sem='c')
                    S.do('gpsimd', 'iota', writes=['pio'], out=pio[:], pattern=[[0, 1]], base=0, channel_multiplier=1)
                    S.do('vector', 'tensor_copy', reads=['pio'], writes=['piof'], out=piof[:], in_=pio[:])
                    S.do('vector', 'tensor_copy', reads=['pti'], writes=['ptf'], out=ptf[:], in_=pti[:])
                    S.do('vector', 'tensor_scalar', reads=['ptf', 'piof'], writes=['ptf'], out=ptf[:], in0=ptf[:], scalar1=128.0, scalar2=piof[:, 0:1], op0=ALU.mult, op1=ALU.add)
                    S.do('vector', 'tensor_copy', reads=['ptf'], writes=['idx'], out=idx[:], in_=ptf[:])
                    NPG = 6
                    pgb = [psb2("pgb%d" % i, [128, 512]) for i in range(NPG)]
                    ktp = [psb2("ktp%d" % i, [64, 4, 128], BF16) for i in range(NPG)]
                    vap = [psb2("vap%d" % i, [128, 4, 65], BF16) for i in range(NPG)]
                    for i in range(NPG):
                        S.do('vector', 'memset', writes=['vap%d' % i], ap=vap[i][:], constant=1.0)
                    C.pg_i = 0

                    def fetch_page(cache, s, p):
                        bi = C.pg_i
                        C.pg_i = (bi + 1) % NPG
                        pg, pgk = pgb[bi], 'pgb%d' % bi
                        col = s * 16 + p
                        S.dma('gpsimd', lambda e: e.indirect_dma_start(out=pg[:, :], out_offset=None, in_=cache[:, :],
                                                                      in_offset=bass.IndirectOffsetOnAxis(ap=idx[:, col:col + 1], axis=0)),
                              reads=['idx'], writes=[pgk], sem='pg%d' % bi)
                        return bi, pg, pgk

                    def kv_prep(bi, pg, pgk):
                        pb, pk = next_bank()
                        for g in range(4):
                            S.do('tensor', 'transpose', reads=[pgk, 'ident'], writes=[pk], out=pb[0:64, g * 128:(g + 1) * 128], in_=pg[:, g * 64:(g + 1) * 64], identity=ident[:])
                        S.do('scalar', 'activation', reads=[pk], writes=['ktp%d' % bi], out=ktp[bi][:], in_=pb[0:64, :].rearrange("p (g k) -> p g k", g=4), func=AF.Copy)
                        S.do('vector', 'tensor_copy', reads=[pgk], writes=['vap%d' % bi], out=vap[bi][:, :, 0:64], in_=pg[:, 256:512].rearrange("p (g d) -> p g d", g=4))

                    kcA = psb2("kcA", [64, 16, 4, 128], BF16)
                    vcA = psb2("vcA", [128, 16, 4, 65], BF16)
                    S.do('vector', 'memset', writes=['kcA'], ap=kcA[:], constant=0.0)
                    S.do('vector', 'memset', writes=['vcA'], ap=vcA[:], constant=0.0)
                    S.do('vector', 'memset', writes=['vcA'], ap=vcA[:, :, :, 64:65], constant=1.0)
                    with ExitStack() as ph3:
                        XTs = ph3.enter_context(nc.sbuf_tensor("XTs", [64, 2, 4, 2048], BF16))
                        for s_ in range(16):
                            for p in range(16):
                                bi, pg, pgk = fetch_page(cache_cmp, s_, p)
                                for kv in range(2):
                                    pb, pk = next_bank()
                                    for g in range(4):
                                        S.do('tensor', 'transpose', reads=[pgk, 'ident'], writes=[pk], out=pb[0:64, g * 128:(g + 1) * 128], in_=pg[:, kv * 256 + g * 64:kv * 256 + (g + 1) * 64], identity=ident[:])
                                    S.do('scalar', 'activation', reads=[pk], writes=['XTs'], out=XTs[:, kv, :, p * 128:(p + 1) * 128], in_=pb[0:64, :].rearrange("p (g k) -> p g k", g=4), func=AF.Copy)
                            compress(lambda kv, g: XTs[:, kv, g, :], ['XTs'], 127, lambda g, s_=s_: kcA[:, s_, g, 0:127], 'kcA', lambda g, ct, m, s_=s_: vcA[0:m, s_, g, 0:64], 'vcA')
                        S.barrier()
                    qb, qk = load_tile(t)
                    for gp in range(2):
                        gs = (2 * gp, 2 * gp + 1)
                        jobs = []
                        for s_ in range(16):
                            jobs.append(dict(nk=127, qs=s_, keys=['kcA', 'vcA', 'ovs'],
                                             KT=lambda g, s_=s_: kcA[:, s_, g, 0:127],
                                             V=lambda g, s_=s_: [(0, vcA[0:127, s_, g, :], 65), (1, ovs[0:127, :], 64)],
                                             bias=lambda g: []))
                        naccs = {0: {gs[0]: ACC[0], gs[1]: ACC[1]}, 1: {gs[0]: ACC[2], gs[1]: ACC[3]}}
                        attend_g(qb, qk, jobs, naccs, gs)
                        cmp_finish_g(t, naccs, gs)
                    rnew = psb2("rnew", [128, 3, 512])
                    ktn = psb2("ktn", [64, 2, 4, 128], BF16); van = psb2("van", [128, 2, 4, 65], BF16)
                    S.dd('sync', rnew[:], rows_own[t], reads=[('rows_own', t)], writes=['rnew'], sem='c')
                    S.do('gpsimd', 'memset', writes=['van'], ap=van[:], constant=1.0)
                    for bi_, br in enumerate((1, 2)):
                        pb, pk = next_bank()
                        for g in range(4):
                            S.do('tensor', 'transpose', reads=['rnew', 'ident'], writes=[pk], out=pb[0:64, g * 128:(g + 1) * 128], in_=rnew[:, br, g * 64:(g + 1) * 64], identity=ident[:])
                        S.do('scalar', 'activation', reads=[pk], writes=['ktn'], out=ktn[:, bi_], in_=pb[0:64, :].rearrange("p (g k) -> p g k", g=4), func=AF.Copy)
                        S.do('gpsimd', 'tensor_copy', reads=['rnew'], writes=['van'], out=van[:, bi_, :, 0:64], in_=rnew[:, br, 256:512].rearrange("p (g d) -> p g d", g=4))
                    S.dd('sync', s_win[:, 504:512, :], rnew[:, 2, :], reads=['rnew'], writes=['s_win_b'], sem='swo')
                    naccs = {0: {g: ACC[g] for g in range(4)}}
                    jobs = []
                    for s_ in range(16):
                        for p in range(16):
                            jb = dict(nk=128, qs=s_)

                            def prep(jb=jb, s_=s_, p=p):
                                bi, pg, pgk = fetch_page(cache_sel, s_, p)
                                kv_prep(bi, pg, pgk)
                                jb['keys'] = ['ktp%d' % bi, 'vap%d' % bi]
                                jb['KT'] = lambda g, bi=bi: ktp[bi][:, g, :]
                                jb['V'] = lambda g, bi=bi: [(0, vap[bi][:, g, :], 65)]
                            jb['prep'] = prep
                            jb['bias'] = lambda g, s_=s_, p=p: [(E[:, p * 128:(p + 1) * 128], selbT[g][:], ['E', 'selbT%d' % g])]
                            jb['V'] = lambda g: [(0, None, 65)]
                            jobs.append(jb)
                    jobs.append(dict(nk=128, keys=['ktn', 'van'], KT=lambda g: ktn[:, 0, g, :], V=lambda g: [(0, van[:, 0, g, :], 65)],
                                     bias=lambda g: [(identb[:], bdc[:], ['identb', 'bdc'])]))
                    attend_g(qb, qk, jobs, naccs, (0, 1, 2, 3))
                    for g in range(4):
                        normalize(ACC[g], g, 1, 65)
                    jobs = []
                    for s_ in range(16):
                        for w_ in range(4):
                            jb = dict(nk=128, qs=s_)

                            def prep(jb=jb, s_=s_, w_=w_):
                                bi = C.pg_i
                                C.pg_i = (bi + 1) % NPG
                                pg, pgk = pgb[bi], 'pgb%d' % bi
                                S.dd('sync', pg[:], win_in[s_, w_ * 128:(w_ + 1) * 128, :], writes=[pgk], sem='pg%d' % bi)
                                kv_prep(bi, pg, pgk)
                                jb['keys'] = ['ktp%d' % bi, 'vap%d' % bi]
                                jb['KT'] = lambda g, bi=bi: ktp[bi][:, g, :]
                                jb['V'] = lambda g, bi=bi: [(0, vap[bi][:, g, :], 65)]
                            jb['prep'] = prep

                            def bias(g, s_=s_, w_=w_):
                                bl = []
                                if w_ == 0:
                                    bl.append((identb[:], wb0[:], ['identb', 'wb0']))
                                return bl
                            jb['bias'] = bias
                            jb['V'] = lambda g: [(0, None, 65)]
                            jobs.append(jb)
                    jobs.append(dict(nk=128, keys=['ktn', 'van'], KT=lambda g: ktn[:, 1, g, :], V=lambda g: [(0, van[:, 1, g, :], 65)],
                                     bias=lambda g: [(identb[:], bdc[:], ['identb', 'bdc'])]))
                    attend_g(qb, qk, jobs, naccs, (0, 1, 2, 3))
                    for g in range(4):
                        normalize(ACC[g], g, 2, 65)
                    dbg("oa8", oa[:], [128, 1024], F32, ['oa'])
                    finish_tile(t)
                    S.barrier()
                C.pb_n = 8
                S.barrier()

        gst2 = ExitStack()
        alloc_gemm(gst2)
        if 'fin' in phases:
            with ExitStack() as ph:
                psb = lambda name, shape, dt=F32: ph.enter_context(nc.sbuf_tensor(name, list(shape), dt))
                g2t = psb("g2t", [128, KC])
                S.dd('sync', g2t[:], g2, writes=['g2t'], sem='c')
                ldb = [psb("ldb%d" % i, [128, 512]) for i in range(4)]
                C.ld_i = 0

                def ld(src, reads):
                    i = C.ld_i
                    C.ld_i = (C.ld_i + 1) % 4
                    S.dd('sync', ldb[i][:], src, reads=reads, writes=['ldb%d' % i], sem='ld%d' % i)
                    return ldb[i], 'ldb%d' % i

                wpa_v = w_pa.rearrange("(kc p) n -> p kc n", p=128)
                wpb_v = w_pb.rearrange("(kc p) n -> p kc n", p=128)
                wo_v = w_o.rearrange("(kc p) n -> p kc n", p=128)
                wf1_v = w_f1.rearrange("(kc p) n -> p kc n", p=128)
                ch4 = [(c * 512, 512, None) for c in range(4)]
                x1T = psb("x1T", [128, NT, KC, 128], BF16)
                ssq2 = psb("ssq2", [128, NT, 4])
                rstd2 = psb("rstd2", [128, NT])
                x1_scr = dscr("x1_scr", [NT, 128, D])
                with ExitStack() as ph1:
                    psb1 = lambda name, shape, dt=F32: ph1.enter_context(nc.sbuf_tensor(name, list(shape), dt))
                    mixT = psb1("mixT", [128, NT, KC, 128], BF16)
                    with ExitStack() as ph2:
                        psb2 = lambda name, shape, dt=F32: ph2.enter_context(nc.sbuf_tensor(name, list(shape), dt))
                        oaT = psb2("oaT", [128, NT, 8, 128], BF16)
                        obT2 = psb2("obT2", [128, NT, 8, 128], BF16)
                        mxs = psb2("mxs", [128, 512])
                        for t in range(NT):
                            S.dd('sync', oaT[:, t], oaT_scr[t], reads=[('oaT', t)], writes=['oaT_sb%d' % t], sem='c')
                            S.dd('sync', obT2[:, t], obT_scr[t], reads=[('obT', t)], writes=['obT_sb%d' % t], sem='c')

                        def epi_a(ci, t, pb, pk, c0, w, tag):
                            gb_, gk = ld(zs[t, :, O_A + c0:O_A + c0 + w], [('zs', t, O_A + c0)])
                            zi = C.zst_i
                            C.zst_i = (C.zst_i + 1) % 4
                            S.do('vector', 'tensor_tensor', reads=[pk, gk], writes=['zst%d' % zi], out=C.zst[zi][:], in0=pb[:], in1=gb_[:], op=ALU.mult)
                            S.dd('sync', mix_scr[t, :, c0:c0 + w], C.zst[zi][:], reads=['zst%d' % zi], writes=[('mix', t, c0)], sem='zo%d' % zi)
                        gemm(lambda t, kc: oaT[:, t, kc, :], lambda t: 'oaT_sb%d' % t, NT, 8, wpa_v, ch4, None, None, epi_a)

                        def epi_b(ci, t, pb, pk, c0, w, tag):
                            gb_, gk = ld(zs[t, :, O_B + c0:O_B + c0 + w], [('zs', t, O_B + c0)])
                            pa_, pak = ld(mix_scr[t, :, c0:c0 + w], [('mix', t, c0)])
                            S.do('vector', 'tensor_tensor', reads=[pk, gk], writes=['mxs'], out=mxs[:], in0=pb[:], in1=gb_[:], op=ALU.mult)
                            S.do('vector', 'tensor_tensor', reads=['mxs', pak], writes=['mxs'], out=mxs[:], in0=mxs[:], in1=pa_[:], op=ALU.add)
                            pb2, pk2 = next_bank()
                            for kk in range(4):
                                S.do('tensor', 'transpose', reads=['mxs', 'ident'], writes=[pk2], out=pb2[:, kk * 128:(kk + 1) * 128], in_=mxs[:, kk * 128:(kk + 1) * 128], identity=ident[:])
                            S.do('scalar', 'activation', reads=[pk2], writes=['mixT%d' % t], out=mixT[:, t, ci * 4:(ci + 1) * 4, :], in_=pb2[:].rearrange("p (a b) -> p a b", a=4), func=AF.Copy)
                        gemm(lambda t, kc: obT2[:, t, kc, :], lambda t: 'obT_sb%d' % t, NT, 8, wpb_v, ch4, None, None, epi_b)
                        S.barrier()
                    junk2 = psb1("junk2", [128, 512])
                    x1c = [psb1("x1c%d" % i, [128, 512]) for i in range(2)]
                    C.x1c_i = 0
                    S.do('vector', 'memset', writes=['ssq2'], ap=ssq2[:], constant=0.0)

                    def epi_o(ci, t, pb, pk, c0, w, tag):
                        xb_, xk = ld(x_own[t, :, c0:c0 + w], [])
                        i = C.x1c_i
                        C.x1c_i = (C.x1c_i + 1) % 2
                        xc, yk = x1c[i], 'x1c%d' % i
                        S.do('vector', 'tensor_tensor', reads=[pk, xk], writes=[yk], out=xc[:], in0=pb[:], in1=xb_[:], op=ALU.add)
                        S.dd('sync', x1_scr[t, :, c0:c0 + w], xc[:], reads=[yk], writes=[('x1', t, ci)], sem='x1o%d' % i)
                        S.do('scalar', 'activation', reads=[yk, 'ssq2'], writes=['junk2', 'ssq2'], out=junk2[:], in_=xc[:], func=AF.Square, accum_out=ssq2[:, t, ci:ci + 1])
                        pb2, pk2 = next_bank()
                        for kk in range(4):
                            S.do('tensor', 'transpose', reads=[yk, 'ident'], writes=[pk2], out=pb2[:, kk * 128:(kk + 1) * 128], in_=xc[:, kk * 128:(kk + 1) * 128], identity=ident[:])
                        S.do('scalar', 'activation', reads=[pk2], writes=['x1T%d' % t], out=x1T[:, t, ci * 4:(ci + 1) * 4, :], in_=pb2[:].rearrange("p (a b) -> p a b", a=4), func=AF.Copy)
                    gemm(lambda t, kc: mixT[:, t, kc, :], lambda t: 'mixT%d' % t, NT, KC, wo_v, ch4, None, None, epi_o)
                    S.do('vector', 'tensor_reduce', reads=['ssq2'], writes=['rstd2'], out=rstd2[:], in_=ssq2[:], axis=AX.X, op=ALU.add)
                    S.do('scalar', 'activation', reads=['rstd2'], writes=['rstd2'], out=rstd2[:], in_=rstd2[:], func=AF.Sqrt, scale=1.0 / D, bias=EPS)
                    S.do('vector', 'reciprocal', reads=['rstd2'], writes=['rstd2'], out=rstd2[:], in_=rstd2[:])
                    S.barrier()
                yacc = psb("yacc", [128, NT, D])
                for t in range(NT):
                    for ci in range(4):
                        S.dd('sync', yacc[:, t, ci * 512:(ci + 1) * 512], x1_scr[t, :, ci * 512:(ci + 1) * 512], reads=[('x1', t, ci)], writes=['yacc%d_%d' % (t, ci)], sem='c')
                hT = [psb("hT%d" % i, [128, NT, 4, 128], BF16) for i in range(2)]
                hsb = psb("hsb", [128, 512])
                ffn_specs = []
                for fc in range(16):
                    hb, hbk = hT[fc % 2], 'hT%d' % (fc % 2)

                    def epi_h(ci, t, pb, pk, c0, w, tag, hb=hb, hbk=hbk):
                        S.do('scalar', 'activation', reads=[pk, 'rstd2'], writes=['hsb'], out=hsb[:], in_=pb[:], func=AF.Relu, scale=rstd2[:, t:t + 1])
                        S.do('vector', 'tensor_tensor', reads=['hsb'], writes=['hsb'], out=hsb[:], in0=hsb[:], in1=hsb[:], op=ALU.mult)
                        pb2, pk2 = next_bank()
                        for kk in range(4):
                            S.do('tensor', 'transpose', reads=['hsb', 'ident'], writes=[pk2], out=pb2[:, kk * 128:(kk + 1) * 128], in_=hsb[:, kk * 128:(kk + 1) * 128], identity=ident[:])
                        S.do('scalar', 'activation', reads=[pk2], writes=[hbk + '_%d' % t], out=hb[:, t, :, :], in_=pb2[:].rearrange("p (a b) -> p a b", a=4), func=AF.Copy)
                    ffn_specs += mk_specs(lambda t, kc: x1T[:, t, kc, :], lambda t: 'x1T%d' % t, NT, KC, wf1_v, [(fc * 512, 512, None)], g2t, 'g2t', epi_h)
                    wf2_v = w_f2[fc * 512:(fc + 1) * 512, :].rearrange("(kc p) n -> p kc n", p=128)

                    def epi_y(ci, t, pb, pk, c0, w, tag):
                        yk = 'yacc%d_%d' % (t, ci)
                        S.do('vector', 'tensor_tensor', reads=[pk, yk], writes=[yk], out=yacc[:, t, c0:c0 + w], in0=yacc[:, t, c0:c0 + w], in1=pb[:], op=ALU.add)
                    ffn_specs += mk_specs(lambda t, kc, hb=hb: hb[:, t, kc, :], lambda t, hbk=hbk: hbk + '_%d' % t, NT, 4, wf2_v, ch4, None, None, epi_y)
                gemm_specs(ffn_specs)
                for t in range(NT):
                    S.dd('sync', y_own[t], yacc[:, t, :], reads=['yacc%d_%d' % (t, ci) for ci in range(4)], writes=[('y', t)], sem='yo')
                S.barrier()

        gst2.close()
        S.barrier()
        S.emit()
    return nc


def _rope_tables(pos):
    half = 32
    inv = (10000.0 ** (-(np.arange(half, dtype=np.float32)) * 2.0 / 64)).astype(np.float32)
    ang = pos.astype(np.float32)[:, None] * inv[None, :]
    return np.concatenate([np.cos(ang), np.sin(ang)], axis=1).astype(np.float32)


def make_in_maps(inp, with_samp=True):
    maps = []
    xp = np.asarray(inp['x_prompt'])
    xs = np.asarray(inp['x_sample'])
    for c in range(8):
        b, j = c // 4, c % 4
        tiles = [xp[b, (4 * i + j) * 128:(4 * i + j + 1) * 128] for i in range(8)]
        tiles.append(xs[16 * c:16 * c + 16].reshape(128, D))
        x_own = np.ascontiguousarray(np.stack(tiles))
        pos = [np.arange((4 * i + j) * 128, (4 * i + j + 1) * 128) for i in range(8)]
        pos.append(np.tile(2048 + np.arange(8), 16))
        cs_own = np.stack([_rope_tables(p) for p in pos])
        U = np.zeros((128, 128), np.float32)
        for s_ in range(128):
            for t_ in range(128):
                if s_ > t_ and s_ // 64 == t_ // 64:
                    U[s_, t_] = 1.0
        G2 = np.zeros((128, 2), np.float32)
        G2[:64, 0] = 1.0
        G2[64:, 1] = 1.0
        js = np.zeros((128, 4), np.float32)
        js[:, j] = 1.0
        Dm = np.zeros((2, 128, 128), np.float32); Pm = np.zeros((2, 128, 16), np.float32); MT = np.zeros((2, 128, 128), np.float32)
        U8 = np.zeros((128, 128), np.float32); G16 = np.zeros((128, 16), np.float32)
        for s_ in range(128):
            G16[s_, s_ // 8] = 1.0
            if s_ % 64 <= 31:
                Pm[0, s_, s_ // 64] = 1.0
            for t_ in range(128):
                if s_ // 64 == t_ // 64:
                    Tst = 1.0 if s_ <= t_ else 0.0
                    Pst = 1.0 if (s_ % 64) <= 31 else 0.0
                    Dm[0, s_, t_] = Tst - Pst
                    MT[0, s_, t_] = Tst
                if s_ // 8 == t_ // 8:
                    Dm[1, s_, t_] = 1.0 if s_ <= t_ else 0.0
                    MT[1, s_, t_] = 1.0 if s_ <= t_ else 0.0
                    U8[s_, t_] = 1.0 if s_ > t_ else 0.0
        import ml_dtypes
        bf = ml_dtypes.bfloat16
        NEGB = -30000.0
        kk = np.arange(128)[:, None]; qq = np.arange(128)[None, :]
        rep4 = lambda a: np.ascontiguousarray(np.tile(a, (1, 4)))
        cE = (np.arange(4096)[None, :] // 64 == np.arange(64)[:, None]).astype(np.float32)
        selb = np.stack([rep4(np.where((o - j) * 128 + kk <= qq, 0.0, NEGB)) for o in range(4)])
        winb = np.stack([rep4(np.where(((o - j) * 128 + kk <= qq) & ((o - j) * 128 + kk > qq - 512), 0.0, NEGB)) for o in range(-4, 4)])
        cmpb = np.zeros((8, 2, 128, 512), np.float32)
        for i in range(8):
            for ct in range(2):
                cidx = ct * 128 + kk
                cmpb[i, ct] = rep4(np.where((16 * cidx + 31 <= (4 * i + j) * 128 + qq) & (cidx < 255), 0.0, NEGB))
        BIG = 1e9
        visadd = np.zeros((NT, 128, 2, 64), np.float32)
        nn = np.arange(64)[None, :]
        for t in range(NT):
            if t < 8:
                qpos = ((4 * t + j) * 128 + np.arange(128))[:, None]
                visible = nn * 64 <= qpos
            else:
                qpos = (2048 + np.arange(128) % 8)[:, None]
                visible = (nn * 64 <= qpos) & (nn < 33)
            cur = qpos // 64
            forced = (nn == 0) | (nn == cur) | (nn == cur - 1)
            visadd[t, :, 0, :] = visible.astype(np.float32)
            visadd[t, :, 1, :] = np.where(visible, np.where(forced, BIG, 0.0), -BIG)
        def overlap(ncc, nss):
            cs = np.arange(ncc) * 16; ce = cs + 32; ss = np.arange(nss) * 64; se = ss + 64
            ov = np.clip(np.minimum(ce[:, None], se[None, :]) - np.maximum(cs[:, None], ss[None, :]), 0, None)
            return (ov / 16).astype(np.float32)
        ovp = np.zeros((256, 64), np.float32); ovp[:255] = overlap(255, 64)
        ovs = np.zeros((128, 64), np.float32); ovs[:127, :33] = overlap(127, 33)
        OH = np.zeros((17, 16, 128), np.float32)
        for s_ in range(16):
            OH[s_, s_, :] = 1.0
        OH[16, :, 127] = 1.0
        seqb = np.full((17, 512), NEGB, np.float32)
        for s_ in range(16):
            seqb[s_] = np.tile(np.where(np.arange(128) // 8 == s_, 0.0, NEGB), 4)
        bdc = rep4(np.where((kk // 8 == qq // 8) & (kk % 8 <= qq % 8), 0.0, NEGB))
        wb0 = rep4(np.where(kk > (qq % 8), 0.0, NEGB))
        m_s = {
            'pt_in': np.ascontiguousarray(np.asarray(inp['page_table'])[16 * c:16 * c + 16].reshape(1, 256).astype(np.int32)),
            'cache_cmp': np.asarray(inp['cache_cmp_kv'])[0].reshape(NPOOL * 128, 512),
            'cache_sel': np.asarray(inp['cache_sel_kv'])[0].reshape(NPOOL * 128, 512),
            'win_in': np.ascontiguousarray(np.asarray(inp['cache_win_kv'])[0, 16 * c:16 * c + 16].reshape(16, 512, 512)),
        }
        m = {
            'cE': cE.astype(bf), 'cselb': selb.astype(bf), 'cwinb': winb.astype(bf), 'ccmpb': cmpb.astype(bf), 'cvisadd': visadd,
            'covp': ovp.reshape(2, 128, 64).astype(bf), 'covs': ovs.astype(bf), 'cOH': OH.astype(bf), 'cseqb': seqb.astype(bf),
            'cbdc': bdc.astype(bf), 'cwb0': wb0.astype(bf),
            'cw1': np.ascontiguousarray(np.asarray(inp['cmp_w1'])[0]), 'cw2': np.ascontiguousarray(np.asarray(inp['cmp_w2'])[0]),
            'cpe': np.ascontiguousarray(np.asarray(inp['cmp_pe'])[0]),
            'w_pa': np.ascontiguousarray(np.asarray(inp['w_proj_a'])[0]), 'w_pb': np.ascontiguousarray(np.asarray(inp['w_proj_b'])[0]),
            'w_o': np.ascontiguousarray(np.asarray(inp['w_out'])[0]), 'w_f1': np.ascontiguousarray(np.asarray(inp['w_ff1'])[0]), 'w_f2': np.ascontiguousarray(np.asarray(inp['w_ff2'])[0]),
            'g2': np.ascontiguousarray(np.asarray(inp['norm2_g'])[0].reshape(KC, 128).T),
            'cDm': Dm, 'cPm': Pm, 'cMT': MT, 'cU8': U8, 'cG16': G16,
            'st_in': np.ascontiguousarray(np.asarray(inp['state_hgrn'])[0, 16 * c:16 * c + 16]),
            'hng': np.ascontiguousarray(np.asarray(inp['hgrn_norm_g'])[0].reshape(128, 1)),
            'x_seq': np.ascontiguousarray(xp[b].reshape(NSEQ, 128, D)),
            'cs_seq': np.ascontiguousarray(_rope_tables(np.arange(4096)).reshape(NSEQ, 128, 64)),
            'lbl': np.ascontiguousarray(np.asarray(inp['hgrn_lb_logits'])),
            'jsel': js, 'cU64': U, 'cG2': G2,
            'x_own': x_own,
            'cs_own': np.ascontiguousarray(cs_own),
            'w_in': np.ascontiguousarray(np.asarray(inp['w_in'])[0]),
            'g1': np.ascontiguousarray(np.asarray(inp['norm1_g'])[0].reshape(KC, 128).T),
            'qkg': np.ascontiguousarray(np.concatenate([np.asarray(inp['q_norm_g']), np.asarray(inp['k_norm_g'])[0]], axis=0)),
        }
        if with_samp:
            m.update(m_s)
        maps.append(m)
    return maps


_NC_CACHE = {}


def run(inp, phases=('own', 'seq', 'hgrn', 'att', 'attp', 'atts', 'fin')):
    key = tuple(phases)
    if key not in _NC_CACHE:
        _NC_CACHE[key] = build(phases)
    nc = _NC_CACHE[key]
    maps = make_in_maps(inp, with_samp=('atts' in phases))
    res = run_bass_kernel_spmd(nc, maps, core_ids=list(range(8)))
    return res.results


def assemble(res):
    B, T = 2, 4096
    p_cmp = np.zeros((1, B, T, 2, 4, 64), np.float32)
    p_sel = np.zeros((1, B, T, 2, 4, 64), np.float32)
    p_win = np.zeros((1, B, 512, 2, 4, 64), np.float32)
    s_cmp = np.zeros((1, 128, 8, 2, 4, 64), np.float32)
    s_sel = np.zeros((1, 128, 8, 2, 4, 64), np.float32)
    y_p = np.zeros((B, T, D), np.float32)
    y_s = np.zeros((128, 8, D), np.float32)
    p_st = np.zeros((1, B, 8, 128, 128), np.float32)
    s_win = np.zeros((1, 128, 512, 2, 4, 64), np.float32)
    s_st = np.zeros((1, 128, 8, 128, 128), np.float32)
    for c in range(8):
        b, j = c // 4, c % 4
        r = res[c]
        rows = r['rows_own']
        for i in range(8):
            J = 4 * i + j
            p_cmp[0, b, J * 128:(J + 1) * 128] = rows[i, :, 0].reshape(128, 2, 4, 64)
            p_sel[0, b, J * 128:(J + 1) * 128] = rows[i, :, 1].reshape(128, 2, 4, 64)
            if J >= 28:
                p_win[0, b, (J - 28) * 128:(J - 27) * 128] = rows[i, :, 2].reshape(128, 2, 4, 64)
        s_cmp[0, 16 * c:16 * c + 16] = rows[8, :, 0].reshape(16, 8, 2, 4, 64)
        s_sel[0, 16 * c:16 * c + 16] = rows[8, :, 1].reshape(16, 8, 2, 4, 64)
        if j == 0 and 'p_state' in r:
            p_st[0, b] = r['p_state']
        if 'y_own' in r:
            for i in range(8):
                J = 4 * i + j
                y_p[b, J * 128:(J + 1) * 128] = r['y_own'][i]
            y_s[16 * c:16 * c + 16] = r['y_own'][8].reshape(16, 8, D)
        if 's_win' in r:
            s_win[0, 16 * c:16 * c + 16] = r['s_win'].reshape(16, 512, 2, 4, 64)
        if 's_state' in r:
            s_st[0, 16 * c:16 * c + 16] = r['s_state']
    return (y_p, y_s, p_cmp, p_sel, p_win, p_st, s_cmp, s_sel, s_win, s_st)


def kernel(**inp):
    res = run(inp)
    return assemble(res)
```

```python
import numpy as np
from contextlib import ExitStack
import concourse.bass as bass
import concourse.mybir as mybir
from concourse.bass_utils import run_bass_kernel_spmd

F32 = mybir.dt.float32
BF16 = mybir.dt.bfloat16
I32 = mybir.dt.int32
ALU = mybir.AluOpType
AF = mybir.ActivationFunctionType
AX = mybir.AxisListType

ENGS = ['tensor', 'vector', 'scalar', 'gpsimd', 'sync']
EPOCH = 24000

D = 2048
KC = 16
N_IN = 10800
NT = 9
NSEQ = 32
EPS = 1e-6
NPOOL = 2560
SIZES = (1024, 1536, 48, 1024, 1024, 1024, 1024, 2048, 2048)
OFFS = [0]
for _s in SIZES:
    OFFS.append(OFFS[-1] + _s)
O_Q, O_KV, O_GA, O_HQ, O_HF, O_HI, O_HG, O_A, O_B = OFFS[:9]


class Sched:
    def __init__(self, nc, stack):
        self.nc = nc
        self.stack = stack
        self.streams = {e: [] for e in ENGS}
        self.cnt = {e: 0 for e in ENGS}
        self.epoch = {e: 0 for e in ENGS}
        self.semh = {}
        self.seen = {e: {} for e in ENGS}
        self.lastw = {}
        self.readers = {}
        self.dma_tot = {}
        self.nsem = 0
        for e in ENGS:
            self._sem((e, 0))

    def _sem(self, key):
        if key not in self.semh:
            self.nsem += 1
            self.semh[key] = self.stack.enter_context(self.nc.semaphore("s%d" % self.nsem))
        return self.semh[key]

    def _deps(self, eng, reads, writes):
        evs = []
        for k in reads:
            if k in self.lastw:
                evs.append(self.lastw[k])
        for k in writes:
            if k in self.lastw:
                evs.append(self.lastw[k])
            evs.extend(self.readers.get(k, ()))
        need = {}
        for (sk, v) in evs:
            if sk[0] == 'd':
                v = self.dma_tot[sk]
            if self.seen[eng].get(sk, 0) >= v:
                continue
            if need.get(sk, 0) < v:
                need[sk] = v
        for sk, v in need.items():
            self.seen[eng][sk] = v
        return [(self._sem(sk), v) for sk, v in need.items()]

    def _record(self, ev, reads, writes):
        for k in writes:
            self.lastw[k] = ev
            self.readers[k] = []
        for k in reads:
            self.readers.setdefault(k, []).append(ev)

    def op(self, eng, fn, reads=(), writes=()):
        waits = self._deps(eng, reads, writes)
        if self.cnt[eng] >= EPOCH:
            self.epoch[eng] += 1
            self.cnt[eng] = 0
        sk = (eng, self.epoch[eng])
        self.cnt[eng] += 1
        ev = (sk, self.cnt[eng])
        if eng == 'tensor':
            self.seen[eng][sk] = self.cnt[eng]
        self.streams[eng].append((waits, fn, self._sem(sk), 1))
        self._record(ev, reads, writes)
        return ev

    def do(self, eng, method, reads=(), writes=(), **kw):
        return self.op(eng, lambda e: getattr(e, method)(**kw), reads, writes)

    def dd(self, q, out, in_, reads=(), writes=(), sem='d0'):
        return self.dma(q, lambda e: e.dma_start(out=out, in_=in_), reads, writes, sem)

    def dma(self, q, fn, reads=(), writes=(), sem='d0'):
        waits = self._deps(q, reads, writes)
        sk = ('d', sem)
        self.dma_tot[sk] = self.dma_tot.get(sk, 0) + 16
        ev = (sk, self.dma_tot[sk])
        self.streams[q].append((waits, fn, self._sem(sk), 16))
        self._record(ev, reads, writes)
        return ev

    def barrier(self):
        targets = []
        for e in ENGS:
            for ep in range(self.epoch[e] + 1):
                sk = (e, ep)
                v = self.cnt[e] if ep == self.epoch[e] else EPOCH
                if v > 0:
                    targets.append((sk, v))
        for sk, v in self.dma_tot.items():
            targets.append((sk, v))
        for e in ENGS:
            waits = []
            for sk, v in targets:
                if self.seen[e].get(sk, 0) < v:
                    self.seen[e][sk] = v
                    waits.append((self._sem(sk), v))
            if waits:
                self.streams[e].append((waits, None, None, 0))

    def emit(self):
        nc = self.nc
        with nc.Block() as block:
            def mk(ename):
                def body(eng):
                    for waits, fn, sem, inc in self.streams[ename]:
                        for (s, v) in waits:
                            eng.wait_ge(s, v)
                        if fn is not None:
                            fn(eng).then_inc(sem, inc)
                return body
            block.tensor(mk('tensor'))
            block.vector(mk('vector'))
            block.scalar(mk('scalar'))
            block.gpsimd(mk('gpsimd'))
            block.sync(mk('sync'))


class Ctx:
    pass


def chunk_list():
    ch = []
    def seg(o, n, f):
        c = 0
        while c < n:
            w = min(512, n - c)
            ch.append((o + c, w, f))
            c += w
    seg(O_Q, 1024, AF.Copy)
    seg(O_KV, 1536, AF.Copy)
    seg(O_GA, 48, AF.Sigmoid)
    seg(O_HQ, 1024, AF.Silu)
    seg(O_HF, 1024, AF.Sigmoid)
    seg(O_HI, 1024, AF.Copy)
    seg(O_HG, 1024, AF.Silu)
    seg(O_A, 2048, AF.Sigmoid)
    seg(O_B, 2048, AF.Sigmoid)
    return ch


def build(phases=('own', 'seq', 'hgrn', 'att', 'attp', 'atts', 'fin'), debug=False):
    nc = bass.Bass("TRN2", target_bir_lowering=False)
    C = Ctx()
    C.debug = debug
    din = lambda n, s, dt=F32: nc.dram_tensor(n, list(s), dt, kind="ExternalInput").ap()
    dout = lambda n, s, dt=F32: nc.dram_tensor(n, list(s), dt, kind="ExternalOutput").ap()
    dscr = lambda n, s, dt=F32: nc.dram_tensor(n, list(s), dt, kind="Internal").ap()
    x_own = din("x_own", [NT, 128, D])
    cs_own = din("cs_own", [NT, 128, 64])
    x_seq = din("x_seq", [NSEQ, 128, D])
    cs_seq = din("cs_seq", [NSEQ, 128, 64])
    w_in = din("w_in", [D, N_IN])
    g1 = din("g1", [128, KC])
    qkg = din("qkg", [4, 64])
    lbl = din("lbl", [2, 1024])
    jsel = din("jsel", [128, 4])
    cU64 = din("cU64", [128, 128])
    cG2 = din("cG2", [128, 2])
    cDm = din("cDm", [2, 128, 128])
    cPm = din("cPm", [2, 128, 16])
    cMT = din("cMT", [2, 128, 128])
    cU8 = din("cU8", [128, 128])
    cG16 = din("cG16", [128, 16])
    st_in = din("st_in", [16, 8, 128, 128])
    hng = din("hng", [128, 1])
    rows_own = dout("rows_own", [NT, 128, 3, 512])
    s_state = dout("s_state", [16, 8, 128, 128])
    obT_scr = dscr("obT_scr", [NT, 128, 8, 128], BF16)
    oaT_scr = dscr("oaT_scr", [NT, 128, 8, 128], BF16)
    mix_scr = dscr("mix_scr", [NT, 128, D])
    w_pa = din("w_pa", [1024, D]); w_pb = din("w_pb", [1024, D]); w_o = din("w_o", [D, D])
    g2 = din("g2", [128, KC]); w_f1 = din("w_f1", [D, 4 * D]); w_f2 = din("w_f2", [4 * D, D])
    y_own = dout("y_own", [NT, 128, D])
    cE = din("cE", [64, 4096], BF16)
    cselb = din("cselb", [4, 128, 512], BF16)
    cwinb = din("cwinb", [8, 128, 512], BF16)
    ccmpb = din("ccmpb", [8, 2, 128, 512], BF16)
    cvisadd = din("cvisadd", [NT, 128, 2, 64])
    covp = din("covp", [2, 128, 64], BF16)
    covs = din("covs", [128, 64], BF16)
    cOH = din("cOH", [17, 16, 128], BF16)
    cseqb = din("cseqb", [17, 512], BF16)
    cbdc = din("cbdc", [128, 512], BF16)
    cwb0 = din("cwb0", [128, 512], BF16)
    cw1 = din("cw1", [2, 2048, 64]); cw2 = din("cw2", [2, 64, 64]); cpe = din("cpe", [2, 32, 64])
    if 'atts' in phases:
        pt_in = din("pt_in", [1, 256], I32)
        cache_cmp = din("cache_cmp", [NPOOL * 128, 512])
        cache_sel = din("cache_sel", [NPOOL * 128, 512])
        win_in = din("win_in", [16, 512, 512])
        s_win = dout("s_win", [16, 512, 512])
    p_state = dout("p_state", [8, 128, 128])
    zs = dscr("zs", [NT, 128, N_IN])
    zseq = dscr("zseq", [NSEQ, 128, 3584])
    qT_scr = dscr("qT_scr", [NT, 64, 2048], BF16)
    kT_scr = dscr("kT_scr", [4, 64, 4, 4096], BF16)
    v_scr = dscr("v_scr", [NSEQ, 128, 2, 4, 65], BF16)
    sown_scr = dscr("sown_scr", [128, 16, 1024], BF16)

    with ExitStack() as st:
        S = Sched(nc, st)
        sb = lambda name, shape, dt=F32: st.enter_context(nc.sbuf_tensor(name, list(shape), dt))
        ps = lambda name, shape, dt=F32: st.enter_context(nc.psum_tensor(name, list(shape), dt))
        pbank = [ps("pb%d" % i, [128, 512]) for i in range(8)]

        def dbg(name, ap, shape, dt, reads):
            if not C.debug:
                return
            o = nc.dram_tensor("dbg_" + name, list(shape), dt, kind="ExternalOutput").ap()
            S.dd('sync', o, ap, reads=reads, sem='dbg')

        C.pb_i = 0
        C.pb_n = 8

        def next_bank():
            i = C.pb_i % C.pb_n
            C.pb_i = (i + 1) % C.pb_n
            return pbank[i], 'pb%d' % i

        ident = sb("ident", [128, 128])
        S.do('gpsimd', 'memset', writes=['ident'], ap=ident[:], constant=1.0)
        S.do('gpsimd', 'affine_select', reads=['ident'], writes=['ident'], out=ident[:], in_=ident[:], pattern=[[-1, 128]],
             compare_op=ALU.is_equal, fill=0.0, base=0, channel_multiplier=1)
        g1t = sb("g1t", [128, KC])
        S.dd('sync', g1t[:], g1, writes=['g1t'], sem='c')
        qkg_t = sb("qkg_t", [128, 4, 64])
        S.dd('sync', qkg_t[:], qkg.rearrange("(o a) d -> o a d", o=1).to_broadcast([128, 4, 64]), writes=['qkg_t'], sem='c')
        jsel_t = sb("jsel_t", [128, 4])
        S.dd('sync', jsel_t[:], jsel, writes=['jsel_t'], sem='c')
        U64 = sb("U64", [128, 128])
        S.dd('sync', U64[:], cU64, writes=['U64'], sem='c')
        G2 = sb("G2", [128, 2])
        S.dd('sync', G2[:], cG2, writes=['G2'], sem='c')
        C.gb_n = 0
        C.pend_epi = None
        C.cast_engs = ['gpsimd', 'vector']

        def alloc_gemm(stack):
            C.gb_n += 1
            u = "_%d" % C.gb_n
            al = lambda name, shape, dt=F32: stack.enter_context(nc.sbuf_tensor(name + u, list(shape), dt))
            C.wst = [al("wst%d" % i, [128, 2, 512]) for i in range(6)]
            C.cast_i = 0
            C.wbf = [al("wbf%d" % i, [128, KC, 512], BF16) for i in range(2)]
            C.zst = [al("zst%d" % i, [128, 512]) for i in range(4)]
            C.wst_i = 0
            C.wbf_i = 0
            C.zst_i = 0

        gst = ExitStack()
        alloc_gemm(gst)

        def make_xT(x_dram, t0, nt, xT, ssq, rstd, pfx, xst, junk):
            S.do('vector', 'memset', writes=[pfx + 'ssq'], ap=ssq[:, 0:nt], constant=0.0)
            for t in range(nt):
                xb = xst[t % 2]
                xk = 'xst%d' % (t % 2)
                S.dd('sync', xb[:], x_dram[t0 + t], writes=[xk], sem='x%d' % (t % 2))
                S.do('scalar', 'activation', reads=[xk, pfx + 'ssq'], writes=['junk', pfx + 'ssq'], out=junk[:], in_=xb[:], func=AF.Square, accum_out=ssq[:, t:t + 1])
                for k4 in range(4):
                    pb, pk = next_bank()
                    for kk in range(4):
                        kc = k4 * 4 + kk
                        S.do('tensor', 'transpose', reads=[xk, 'ident'], writes=[pk], out=pb[:, kk * 128:(kk + 1) * 128], in_=xb[:, kc * 128:(kc + 1) * 128], identity=ident[:])
                    S.do('vector', 'tensor_copy', reads=[pk], writes=[pfx + 'xT%d' % t], out=xT[:, t, k4 * 4:(k4 + 1) * 4, :], in_=pb[:].rearrange("p (a b) -> p a b", a=4))
            S.do('scalar', 'activation', reads=[pfx + 'ssq'], writes=[pfx + 'rstd'], out=rstd[:, 0:nt], in_=ssq[:, 0:nt], func=AF.Sqrt, scale=1.0 / D, bias=EPS)
            S.do('vector', 'reciprocal', reads=[pfx + 'rstd'], writes=[pfx + 'rstd'], out=rstd[:, 0:nt], in_=rstd[:, 0:nt])

        def gemm_specs(specs):
            def load(sp):
                c0, w, nkc, w_v, gain, gain_key = sp['c0'], sp['w'], sp['nkc'], sp['w_v'], sp['gain'], sp['gain_key']
                bi = C.wbf_i
                C.wbf_i = (C.wbf_i + 1) % 2
                wb = C.wbf[bi]
                wk = 'wbf%d' % bi
                for q in range(nkc // 2):
                    si = C.wst_i
                    C.wst_i = (C.wst_i + 1) % 6
                    ws = C.wst[si]
                    S.dd('sync', ws[:, :, 0:w], w_v[:, q * 2:(q + 1) * 2, c0:c0 + w], writes=['wst%d' % si], sem='w%d' % si)
                    ceng = C.cast_engs[C.cast_i % len(C.cast_engs)]
                    C.cast_i += 1
                    if gain is not None:
                        S.do(ceng, 'tensor_tensor', reads=['wst%d' % si, gain_key], writes=[wk], out=wb[:, q * 2:(q + 1) * 2, 0:w], in0=ws[:, :, 0:w],
                             in1=gain[:, q * 2:(q + 1) * 2].rearrange("p (a o) -> p a o", o=1).to_broadcast([128, 2, w]), op=ALU.mult)
                    else:
                        S.do(ceng, 'tensor_copy', reads=['wst%d' % si], writes=[wk], out=wb[:, q * 2:(q + 1) * 2, 0:w], in_=ws[:, :, 0:w])
                return wb, wk
            nxt = load(specs[0])
            for i, sp in enumerate(specs):
                wb, wk = nxt
                if i + 1 < len(specs):
                    nxt = load(specs[i + 1])
                w, nkc = sp['w'], sp['nkc']
                for t in range(sp['nt']):
                    pb, pk = next_bank()
                    for kc in range(nkc):
                        S.do('tensor', 'matmul', reads=[sp['lhs_key'](t), wk], writes=[pk], out=pb[:, 0:w], lhsT=sp['lhs'](t, kc), rhs=wb[:, kc, 0:w], start=(kc == 0), stop=(kc == nkc - 1))
                    if C.pend_epi is not None:
                        C.pend_epi()
                    C.pend_epi = (lambda sp=sp, t=t, pb=pb, pk=pk, w=w: sp['epi'](sp['ci'], t, pb, pk, sp['c0'], w, sp['tag']))
            if C.pend_epi is not None:
                C.pend_epi()
                C.pend_epi = None

        def mk_specs(lhs, lhs_key, nt, nkc, w_v, chunks, gain, gain_key, epi):
            return [dict(lhs=lhs, lhs_key=lhs_key, nt=nt, nkc=nkc, w_v=w_v, c0=c0, w=w, tag=tag, gain=gain, gain_key=gain_key, epi=epi, ci=ci)
                    for ci, (c0, w, tag) in enumerate(chunks)]

        def gemm(lhs, lhs_key, nt, nkc, w_v, chunks, gain, gain_key, epi):
            gemm_specs(mk_specs(lhs, lhs_key, nt, nkc, w_v, chunks, gain, gain_key, epi))

        def epi_act_store(dst, rstd, rkey, dkey):
            def epi(ci, t, pb, pk, c0, w, tag):
                func, d0 = tag
                zi = C.zst_i
                C.zst_i = (C.zst_i + 1) % 4
                zb = C.zst[zi]
                S.do('scalar', 'activation', reads=[pk, rkey], writes=['zst%d' % zi], out=zb[:, 0:w], in_=pb[:, 0:w], func=func, scale=rstd[:, t:t + 1])
                S.dd('sync', dst(t)[:, d0:d0 + w], zb[:, 0:w], reads=['zst%d' % zi], writes=[(dkey, t, d0)], sem='zo%d' % zi)
            return epi

        lbst = ExitStack()
        lb_bc = lbst.enter_context(nc.sbuf_tensor("lb_bc", [128, 1024], F32))
        oml_bc = lbst.enter_context(nc.sbuf_tensor("oml_bc", [128, 1024], F32))
        with ExitStack() as ph:
            lraw = ph.enter_context(nc.sbuf_tensor("lraw", [128, 2, 1024], F32))
            S.dd('sync', lraw[:], lbl.rearrange("(o a) d -> o a d", o=1).to_broadcast([128, 2, 1024]), writes=['lraw'], sem='c')
            S.do('vector', 'tensor_tensor', reads=['lraw'], writes=['lb_bc'], out=lb_bc[:], in0=lraw[:, 0, :], in1=lraw[:, 1, :], op=ALU.subtract)
            S.do('scalar', 'activation', reads=['lb_bc'], writes=['lb_bc'], out=lb_bc[:], in_=lb_bc[:], func=AF.Sigmoid)
            S.do('vector', 'tensor_scalar', reads=['lb_bc'], writes=['oml_bc'], out=oml_bc[:], in0=lb_bc[:], scalar1=-1.0, scalar2=1.0, op0=ALU.mult, op1=ALU.add)
            S.barrier()

        w_v = w_in.rearrange("(kc p) n -> p kc n", p=128)

        C.nr_n = 0

        def alloc_nr(psb):
            C.nr_n += 1
            u = "_%d" % C.nr_n
            C.tmpa = psb("tmpa" + u, [128, 1024]); C.tmpb = psb("tmpb" + u, [128, 1024])
            C.hss = psb("hss" + u, [128, 32]); C.hrs = psb("hrs" + u, [128, 32])
            C.cst = [psb("cst%d" % i + u, [128, 64]) for i in range(2)]
            C.tab = [psb("tab%d" % i + u, [128, 4, 4, 32]) for i in range(2)]
            C.rowb = [psb("rowb%d" % i + u, [128, 3, 512]) for i in range(2)]

        def norm_rope(src, dst, H, tb, tk, br, scale, rk, wk):
            tmpa, tmpb, hss, hrs = C.tmpa, C.tmpb, C.hss, C.hrs
            n = H * 64
            sq = tmpa[:, 0:n].rearrange("p (h d) -> p h d", d=64)
            S.do('vector', 'tensor_tensor', reads=rk, writes=['tmpa'], out=sq, in0=src, in1=src, op=ALU.mult)
            S.do('vector', 'tensor_reduce', reads=['tmpa'], writes=['hss'], out=hss[:, 0:H], in_=sq, axis=AX.X, op=ALU.add)
            S.do('scalar', 'activation', reads=['hss'], writes=['hrs'], out=hrs[:, 0:H], in_=hss[:, 0:H], func=AF.Sqrt, scale=1.0 / 64, bias=EPS)
            S.do('vector', 'reciprocal', reads=['hrs'], writes=['hrs'], out=hrs[:, 0:H], in_=hrs[:, 0:H])
            if scale != 1.0:
                S.do('vector', 'tensor_scalar', reads=['hrs'], writes=['hrs'], out=hrs[:, 0:H], in0=hrs[:, 0:H], scalar1=scale, scalar2=None, op0=ALU.mult)
            x1 = src[:, :, 0:32]
            x2 = src[:, :, 32:64]
            T = lambda k: tb[:, br, k, :].rearrange("p (o d) -> p o d", o=1).to_broadcast([128, H, 32])
            a = tmpa[:, 0:H * 32].rearrange("p (h d) -> p h d", d=32)
            b = tmpb[:, 0:H * 32].rearrange("p (h d) -> p h d", d=32)
            S.do('vector', 'tensor_tensor', reads=rk + [tk], writes=['tmpa'], out=a, in0=x1, in1=T(0), op=ALU.mult)
            S.do('vector', 'tensor_tensor', reads=rk + [tk], writes=['tmpb'], out=b, in0=x2, in1=T(1), op=ALU.mult)
            S.do('vector', 'tensor_tensor', reads=['tmpa', 'tmpb'], writes=wk, out=dst[:, :, 0:32], in0=a, in1=b, op=ALU.subtract)
            S.do('vector', 'tensor_tensor', reads=rk + [tk], writes=['tmpa'], out=a, in0=x2, in1=T(2), op=ALU.mult)
            S.do('vector', 'tensor_tensor', reads=rk + [tk], writes=['tmpb'], out=b, in0=x1, in1=T(3), op=ALU.mult)
            S.do('vector', 'tensor_tensor', reads=['tmpa', 'tmpb'], writes=wk, out=dst[:, :, 32:64], in0=a, in1=b, op=ALU.add)
            S.do('vector', 'tensor_tensor', reads=wk + ['hrs'], writes=wk, out=dst, in0=dst, in1=hrs[:, 0:H].rearrange("p (h o) -> p h o", o=1).to_broadcast([128, H, 64]), op=ALU.mult)

        def make_tables(cs_dram, T, par):
            cb = C.cst[par]
            tb = C.tab[par]
            S.dd('sync', cb[:], cs_dram[T], writes=['cst%d' % par], sem='cs%d' % par)
            cosb = cb[:, 0:32].rearrange("p (o d) -> p o d", o=1).to_broadcast([128, 4, 32])
            sinb = cb[:, 32:64].rearrange("p (o d) -> p o d", o=1).to_broadcast([128, 4, 32])
            for k, (tr, gs) in enumerate([(cosb, 0), (sinb, 32), (cosb, 32), (sinb, 0)]):
                S.do('gpsimd', 'tensor_tensor', reads=['cst%d' % par, 'qkg_t'], writes=['tab%d' % par], out=tb[:, :, k, :], in0=qkg_t[:, :, gs:gs + 32], in1=tr, op=ALU.mult)
            return tb, 'tab%d' % par

        def kv_rows(zb, zk, base, tb, tk, rb, rbk):
            tmpa, tmpb, hss, hrs = C.tmpa, C.tmpb, C.hss, C.hrs
            zv = zb[:, base:base + 1536].rearrange("p (b x) -> p b x", b=3)
            src = zv[:, :, 0:256].rearrange("p b (h d) -> p b h d", d=64)
            dst = rb[:, :, 0:256].rearrange("p b (h d) -> p b h d", d=64)
            sq = tmpa[:, 0:768].rearrange("p (b h d) -> p b h d", b=3, d=64)
            h12 = lambda t_: t_[:, 0:12].rearrange("p (b h) -> p b h", b=3)
            S.do('vector', 'tensor_tensor', reads=[zk], writes=['tmpa'], out=sq, in0=src, in1=src, op=ALU.mult)
            S.do('vector', 'tensor_reduce', reads=['tmpa'], writes=['hss'], out=h12(hss), in_=sq, axis=AX.X, op=ALU.add)
            S.do('scalar', 'activation', reads=['hss'], writes=['hrs'], out=hrs[:, 0:12], in_=hss[:, 0:12], func=AF.Sqrt, scale=1.0 / 64, bias=EPS)
            S.do('vector', 'reciprocal', reads=['hrs'], writes=['hrs'], out=hrs[:, 0:12], in_=hrs[:, 0:12])
            x1 = src[:, :, :, 0:32]
            x2 = src[:, :, :, 32:64]
            T = lambda k: tb[:, 1:4, k, :].rearrange("p b (o d) -> p b o d", o=1).to_broadcast([128, 3, 4, 32])
            a = tmpa[:, 0:384].rearrange("p (b h d) -> p b h d", b=3, d=32)
            b_ = tmpb[:, 0:384].rearrange("p (b h d) -> p b h d", b=3, d=32)
            S.do('vector', 'tensor_tensor', reads=[zk, tk], writes=['tmpa'], out=a, in0=x1, in1=T(0), op=ALU.mult)
            S.do('vector', 'tensor_tensor', reads=[zk, tk], writes=['tmpb'], out=b_, in0=x2, in1=T(1), op=ALU.mult)
            S.do('vector', 'tensor_tensor', reads=['tmpa', 'tmpb'], writes=[rbk], out=dst[:, :, :, 0:32], in0=a, in1=b_, op=ALU.subtract)
            S.do('vector', 'tensor_tensor', reads=[zk, tk], writes=['tmpa'], out=a, in0=x2, in1=T(2), op=ALU.mult)
            S.do('vector', 'tensor_tensor', reads=[zk, tk], writes=['tmpb'], out=b_, in0=x1, in1=T(3), op=ALU.mult)
            S.do('vector', 'tensor_tensor', reads=['tmpa', 'tmpb'], writes=[rbk], out=dst[:, :, :, 32:64], in0=a, in1=b_, op=ALU.add)
            S.do('vector', 'tensor_tensor', reads=[rbk, 'hrs'], writes=[rbk], out=dst, in0=dst, in1=h12(hrs).rearrange("p b (h o) -> p b h o", o=1).to_broadcast([128, 3, 4, 64]), op=ALU.mult)
            S.do('gpsimd', 'tensor_copy', reads=[zk], writes=[rbk], out=rb[:, :, 256:512], in_=zv[:, :, 256:512])

        if 'own' in phases:
            with ExitStack() as ph:
                psb = lambda name, shape, dt=F32: ph.enter_context(nc.sbuf_tensor(name, list(shape), dt))
                xT = psb("xT", [128, NT, KC, 128], BF16)
                ssq = psb("ssq", [128, NT])
                rstd = psb("rstd", [128, NT])
                junk = psb("junk", [128, D])
                xst = [psb("xst%d" % i, [128, D]) for i in range(2)]
                make_xT(x_own, 0, NT, xT, ssq, rstd, 'o', xst, junk)
                chunks = [(c0, w, (f, c0)) for (c0, w, f) in chunk_list()]
                gemm(lambda t, kc: xT[:, t, kc, :], lambda t: 'oxT%d' % t, NT, KC, w_v, chunks, g1t, 'g1t',
                     epi_act_store(lambda t: zs[t], rstd, 'orstd', 'zs'))
                S.barrier()
            with ExitStack() as ph:
                psb = lambda name, shape, dt=F32: ph.enter_context(nc.sbuf_tensor(name, list(shape), dt))
                alloc_nr(psb)
                zq = [psb("zq%d" % i, [128, 2560]) for i in range(2)]
                qf = psb("qf", [128, 1024])
                qTb = [psb("qTb%d" % i, [64, 2048], BF16) for i in range(2)]
                zs_keys = [('zs', None, c[0]) for c in chunk_list()[0:5]]
                for t in range(NT):
                    par = t % 2
                    zb, zk = zq[par], 'zq%d' % par
                    rb, rbk = C.rowb[par], 'rowb%d' % par
                    qb, qbk = qTb[par], 'qTb%d' % par
                    S.dd('sync', zb[:], zs[t, :, 0:2560], reads=[('zs', t, k[2]) for k in zs_keys], writes=[zk], sem='zq%d' % par)
                    tb, tk = make_tables(cs_own, t, par)
                    norm_rope(zb[:, 0:1024].rearrange("p (h d) -> p h d", d=64), qf[:].rearrange("p (h d) -> p h d", d=64), 16, tb, tk, 0, 0.125, [zk], ['qf'])
                    for h4 in range(4):
                        pb, pk = next_bank()
                        for hh in range(4):
                            h = h4 * 4 + hh
                            S.do('tensor', 'transpose', reads=['qf', 'ident'], writes=[pk], out=pb[0:64, hh * 128:(hh + 1) * 128], in_=qf[:, h * 64:(h + 1) * 64], identity=ident[:])
                        S.do('scalar', 'activation', reads=[pk], writes=[qbk], out=qb[:, h4 * 512:(h4 + 1) * 512], in_=pb[0:64, :], func=AF.Copy)
                    S.dd('sync', qT_scr[t], qb[:], reads=[qbk], writes=[('qT', t)], sem='qo')
                    kv_rows(zb, zk, 1024, tb, tk, rb, rbk)
                    S.dd('sync', rows_own[t], rb[:], reads=[rbk], writes=[('rows_own', t)], sem='ro')
                S.barrier()

        if 'seq' in phases:
            seq_chunks = []
            for (c0, w, f) in chunk_list():
                if O_KV <= c0 < O_KV + 1536:
                    seq_chunks.append((c0, w, (f, c0 - O_KV)))
                elif O_HF <= c0 < O_HF + 1024:
                    seq_chunks.append((c0, w, (f, 1536 + c0 - O_HF)))
                elif O_HI <= c0 < O_HI + 1024:
                    seq_chunks.append((c0, w, (f, 2560 + c0 - O_HI)))
            with ExitStack() as ph:
                psb = lambda name, shape, dt=F32: ph.enter_context(nc.sbuf_tensor(name, list(shape), dt))
                xT = psb("sxT", [128, 16, KC, 128], BF16)
                ssq = psb("sssq", [128, 16])
                rstd = psb("srstd", [128, 16])
                junk = psb("sjunk", [128, D])
                xst = [psb("sxst%d" % i, [128, D]) for i in range(2)]
                for half in range(2):
                    make_xT(x_seq, half * 16, 16, xT, ssq, rstd, 's', xst, junk)
                    gemm(lambda t, kc: xT[:, t, kc, :], lambda t: 'sxT%d' % t, 16, KC, w_v, seq_chunks, g1t, 'g1t',
                         epi_act_store(lambda t, half=half: zseq[half * 16 + t], rstd, 'srstd', 'zseq%d' % half))
                S.barrier()
            with ExitStack() as ph:
                psb = lambda name, shape, dt=F32: ph.enter_context(nc.sbuf_tensor(name, list(shape), dt))
                alloc_nr(psb)
                zsq = [psb("zsq%d" % i, [128, 3584]) for i in range(2)]
                ktb = [psb("ktb%d" % i, [64, 4, 4, 128], BF16) for i in range(2)]
                vab = [psb("vab%d" % i, [128, 2, 4, 65], BF16) for i in range(2)]
                Sst = psb("Sst", [128, 8, 128])
                Sacc = [psb("Sacc%d" % i, [128, 2, 1024], BF16) for i in range(2)]
                ff2_ = [psb("ff%d" % i, [128, 1024]) for i in range(2)]
                logf2_ = [psb("logf%d" % i, [128, 1024]) for i in range(2)]
                eR2_ = [psb("eR%d" % i, [128, 1024]) for i in range(2)]
                khat2_ = [psb("khat%d" % i, [128, 1024], BF16) for i in range(2)]
                hvb2_ = [psb("hvb%d" % i, [128, 1024], BF16) for i in range(2)]
                ea2_ = [psb("ea%d" % i, [128, 8, 2]) for i in range(2)]
                stmp = psb("stmp", [128, 1024])
                S.do('vector', 'memset', writes=['Sst'], ap=Sst[:], constant=0.0)
                for i in range(2):
                    S.do('gpsimd', 'memset', writes=['vab%d' % i], ap=vab[i][:], constant=1.0)
                for T in range(NSEQ):
                    par = T % 2
                    zb, zk = zsq[par], 'zsq%d' % par
                    rb, rbk = C.rowb[par], 'rowb%d' % par
                    S.dd('sync', zb[:], zseq[T], reads=[('zseq%d' % (T // 16), T % 16, d0) for (_, _, (_, d0)) in seq_chunks], writes=[zk], sem='zq%d' % par)
                    tb, tk = make_tables(cs_seq, T, par)
                    kv_rows(zb, zk, 0, tb, tk, rb, rbk)
                    kb, kbk = ktb[par], 'ktb%d' % par
                    for slot, (br, src_o) in enumerate(((0, 0), (0, 256), (1, 0), (2, 0))):
                        pb, pk = next_bank()
                        for g in range(4):
                            S.do('tensor', 'transpose', reads=[rbk, 'ident'], writes=[pk], out=pb[0:64, g * 128:(g + 1) * 128], in_=rb[:, br, src_o + g * 64:src_o + (g + 1) * 64], identity=ident[:])
                        S.do('scalar', 'activation', reads=[pk], writes=[kbk], out=kb[:, slot, :, :], in_=pb[0:64, :].rearrange("p (g k) -> p g k", g=4), func=AF.Copy)
                    for slot in range(4):
                        S.dd('sync', kT_scr[slot, :, :, T * 128:(T + 1) * 128], kb[:, slot, :, :], reads=[kbk], writes=[('kT', T, slot)], sem='ko')
                    vb, vbk = vab[par], 'vab%d' % par
                    for br in (1, 2):
                        S.do('gpsimd', 'tensor_copy', reads=[rbk], writes=[vbk], out=vb[:, br - 1, :, 0:64], in_=rb[:, br, 256:512].rearrange("p (g d) -> p g d", g=4))
                    S.dd('sync', v_scr[T], vb[:], reads=[vbk], writes=[('v', T)], sem='vo')
                    ff, logf, eR, khat, hvb, ea = ff2_[par], logf2_[par], eR2_[par], khat2_[par], hvb2_[par], ea2_[par]
                    kff, klogf, keR, kkhat, khvb, kea = 'ff%d' % par, 'logf%d' % par, 'eR%d' % par, 'khat%d' % par, 'hvb%d' % par, 'ea%d' % par
                    S.do('vector', 'tensor_tensor', reads=[zk, 'oml_bc'], writes=[kff], out=ff[:], in0=zb[:, 1536:2560], in1=oml_bc[:], op=ALU.mult)
                    S.do('vector', 'tensor_tensor', reads=[kff, 'lb_bc'], writes=[kff], out=ff[:], in0=ff[:], in1=lb_bc[:], op=ALU.add)
                    S.do('scalar', 'activation', reads=[kff], writes=[klogf], out=logf[:], in_=ff[:], func=AF.Ln)
                    S.do('gpsimd', 'tensor_scalar', reads=[kff], writes=[kff], out=ff[:], in0=ff[:], scalar1=-1.0, scalar2=1.0, op0=ALU.mult, op1=ALU.add)
                    for hh in range(2):
                        pb, pk = next_bank()
                        S.do('tensor', 'matmul', reads=['U64', klogf], writes=[pk], out=pb[:], lhsT=U64[:], rhs=logf[:, hh * 512:(hh + 1) * 512], start=True, stop=True)
                        S.do('scalar', 'activation', reads=[pk], writes=[keR], out=eR[:, hh * 512:(hh + 1) * 512], in_=pb[:], func=AF.Exp)
                    S.do('vector', 'tensor_tensor', reads=[kff, keR], writes=[kkhat], out=khat[:], in0=ff[:], in1=eR[:], op=ALU.mult)
                    S.do('gpsimd', 'tensor_copy', reads=[zk], writes=[khvb], out=hvb[:], in_=zb[:, 2560:3584])
                    pb, pk = next_bank()
                    for h in range(8):
                        S.do('tensor', 'matmul', reads=[klogf, 'G2'], writes=[pk], out=pb[:, h * 2:h * 2 + 2], lhsT=logf[:, h * 128:(h + 1) * 128], rhs=G2[:], start=True, stop=True)
                    S.do('scalar', 'activation', reads=[pk], writes=[kea], out=ea[:].rearrange("p h c -> p (h c)"), in_=pb[:, 0:16], func=AF.Exp)
                    for cc in range(2):
                        i_own, o = T // 4, T % 4
                        sa, sak = Sacc[i_own % 2], 'Sacc%d' % (i_own % 2)
                        if o == 0:
                            S.do('vector', 'tensor_scalar', reads=['Sst', 'jsel_t'], writes=[sak], out=sa[:, cc, :], in0=Sst[:].rearrange("p h d -> p (h d)"), scalar1=jsel_t[:, o:o + 1], scalar2=None, op0=ALU.mult)
                        else:
                            S.do('vector', 'tensor_scalar', reads=['Sst', 'jsel_t'], writes=['stmp'], out=stmp[:], in0=Sst[:].rearrange("p h d -> p (h d)"), scalar1=jsel_t[:, o:o + 1], scalar2=None, op0=ALU.mult)
                            S.do('vector', 'tensor_tensor', reads=['stmp', sak], writes=[sak], out=sa[:, cc, :], in0=sa[:, cc, :], in1=stmp[:], op=ALU.add)
                        for h4 in range(2):
                            pb, pk = next_bank()
                            for hh in range(4):
                                h = h4 * 4 + hh
                                S.do('tensor', 'matmul', reads=[kkhat, khvb], writes=[pk], out=pb[:, hh * 128:(hh + 1) * 128], lhsT=khat[cc * 64:(cc + 1) * 64, h * 128:(h + 1) * 128],
                                     rhs=hvb[cc * 64:(cc + 1) * 64, h * 128:(h + 1) * 128], start=True, stop=True)
                            for hh in range(4):
                                h = h4 * 4 + hh
                                S.do('vector', 'scalar_tensor_tensor', reads=['Sst', kea, pk], writes=['Sst'], out=Sst[:, h, :], in0=Sst[:, h, :], scalar=ea[:, h, cc:cc + 1], in1=pb[:, hh * 128:(hh + 1) * 128], op0=ALU.mult, op1=ALU.add)
                    if T % 4 == 3:
                        sa, sak = Sacc[(T // 4) % 2], 'Sacc%d' % ((T // 4) % 2)
                        S.dd('sync', sown_scr[:, 2 * (T // 4):2 * (T // 4) + 2, :], sa[:], reads=[sak], writes=[('sown', T // 4)], sem='so')
                S.dd('sync', p_state.rearrange("h k v -> k h v"), Sst[:], reads=['Sst'], writes=['p_state'], sem='po')
                S.barrier()


        if 'hgrn' in phases:
            with ExitStack() as ph:
                psb = lambda name, shape, dt=F32: ph.enter_context(nc.sbuf_tensor(name, list(shape), dt))
                Dm = psb("Dm", [128, 2, 128]); Pm = psb("Pm", [128, 2, 16]); MT = psb("MT", [128, 2, 128])
                U8 = psb("U8", [128, 128]); G16 = psb("G16", [128, 16]); hng_t = psb("hng_t", [128, 1])
                ones_f = psb("ones_f", [128, 128])
                S.dd('sync', Dm[:], cDm.rearrange("a p t -> p a t"), writes=['Dm'], sem='c')
                S.dd('sync', Pm[:], cPm.rearrange("a p t -> p a t"), writes=['Pm'], sem='c')
                S.dd('sync', MT[:], cMT.rearrange("a p t -> p a t"), writes=['MT'], sem='c')
                S.dd('sync', U8[:], cU8, writes=['U8'], sem='c')
                S.dd('sync', G16[:], cG16, writes=['G16'], sem='c')
                S.dd('sync', hng_t[:], hng, writes=['hng_t'], sem='c')
                S.do('gpsimd', 'memset', writes=['ones_f'], ap=ones_f[:], constant=1.0)
                zh = psb("zh", [128, 4096])
                logf = psb("hlogf", [128, 1024]); hk = psb("hhk", [128, 1024])
                ex = psb("hex", [128, 1024]); qh = psb("hqh", [128, 1024]); kh = psb("hkh", [128, 1024])
                qhT = psb("qhT", [128, 8, 128], BF16); khT = psb("khT", [128, 8, 128], BF16)
                vb = psb("hvb2", [128, 1024], BF16)
                attT = psb("attT", [128, 8, 128], BF16)
                ogT = psb("ogT", [128, 8, 128])
                Sp = psb("Sp", [128, 4, 8, 128], BF16)
                S0 = psb("S0", [128, 4, 8, 128])
                em = psb("hem", [128, 8, 16])
                oTs = psb("oTs", [128, 512]); sqT = psb("sqT", [128, 512]); rbc = psb("rbc", [128, 512])
                obT = psb("obT", [128, 8, 128], BF16)
                kmask = psb("kmask", [128, 1024], BF16)
                zh_keys = [c0 for (c0, w, f) in chunk_list() if O_HQ <= c0 < O_A]
                C.pb_n = 6
                for t in range(NT):
                    kind = 0 if t < 8 else 1
                    G = 2 if kind == 0 else 16
                    L = 128 // G
                    S.dd('sync', zh[:], zs[t, :, O_HQ:O_A], reads=[('zs', t, c0) for c0 in zh_keys], writes=['zh'], sem='zh')
                    hq = zh[:, 0:1024]; sf = zh[:, 1024:2048]; hv = zh[:, 2048:3072]; og = zh[:, 3072:4096]
                    S.do('vector', 'tensor_tensor', reads=['zh', 'oml_bc'], writes=['hk'], out=hk[:], in0=sf, in1=oml_bc[:], op=ALU.mult)
                    S.do('vector', 'tensor_tensor', reads=['hk', 'lb_bc'], writes=['hk'], out=hk[:], in0=hk[:], in1=lb_bc[:], op=ALU.add)
                    S.do('scalar', 'activation', reads=['hk'], writes=['hlogf'], out=logf[:], in_=hk[:], func=AF.Ln)
                    S.do('gpsimd', 'tensor_scalar', reads=['hk'], writes=['hk'], out=hk[:], in0=hk[:], scalar1=-1.0, scalar2=1.0, op0=ALU.mult, op1=ALU.add)
                    S.do('gpsimd', 'tensor_copy', reads=['zh'], writes=['hvb2'], out=vb[:], in_=hv)
                    for hh in range(2):
                        pb, pk = next_bank()
                        S.do('tensor', 'matmul', reads=['Dm', 'hlogf'], writes=[pk], out=pb[:], lhsT=Dm[:, kind, :], rhs=logf[:, hh * 512:(hh + 1) * 512], start=True, stop=True)
                        S.do('vector', 'tensor_scalar', reads=[pk], writes=['hex'], out=ex[:, hh * 512:(hh + 1) * 512], in0=pb[:], scalar1=40.0, scalar2=None, op0=ALU.min)
                        S.do('scalar', 'activation', reads=['hex'], writes=['hex'], out=ex[:, hh * 512:(hh + 1) * 512], in_=ex[:, hh * 512:(hh + 1) * 512], func=AF.Exp)
                        S.do('vector', 'tensor_tensor', reads=['hex', 'zh'], writes=['hqh'], out=qh[:, hh * 512:(hh + 1) * 512], in0=ex[:, hh * 512:(hh + 1) * 512], in1=hq[:, hh * 512:(hh + 1) * 512], op=ALU.mult)
                        S.do('vector', 'tensor_scalar', reads=[pk, 'hqh'], writes=['hex'], out=ex[:, hh * 512:(hh + 1) * 512], in0=pb[:], scalar1=-1.0, scalar2=40.0, op0=ALU.mult, op1=ALU.min)
                        S.do('scalar', 'activation', reads=['hex'], writes=['hex'], out=ex[:, hh * 512:(hh + 1) * 512], in_=ex[:, hh * 512:(hh + 1) * 512], func=AF.Exp)
                        S.do('vector', 'tensor_tensor', reads=['hex', 'hk'], writes=['hkh'], out=kh[:, hh * 512:(hh + 1) * 512], in0=ex[:, hh * 512:(hh + 1) * 512], in1=hk[:, hh * 512:(hh + 1) * 512], op=ALU.mult)
                    for (src, sk_, dst, dk_) in ((qh, 'hqh', qhT, 'qhT'), (kh, 'hkh', khT, 'khT'), (None, 'zh', ogT, 'ogT')):
                        for h4 in range(2):
                            pb, pk = next_bank()
                            for hh in range(4):
                                h = h4 * 4 + hh
                                in_ap = og[:, h * 128:(h + 1) * 128] if src is None else src[:, h * 128:(h + 1) * 128]
                                S.do('tensor', 'transpose', reads=[sk_, 'ident'], writes=[pk], out=pb[:, hh * 128:(hh + 1) * 128], in_=in_ap, identity=ident[:])
                            S.do('scalar', 'activation', reads=[pk], writes=[dk_], out=dst[:, h4 * 4:(h4 + 1) * 4, :], in_=pb[:].rearrange("p (a b) -> p a b", a=4), func=AF.Copy)
                    for h4 in range(2):
                        pb, pk = next_bank()
                        for hh in range(4):
                            h = h4 * 4 + hh
                            S.do('tensor', 'matmul', reads=['khT', 'qhT'], writes=[pk], out=pb[:, hh * 128:(hh + 1) * 128], lhsT=khT[:, h, :], rhs=qhT[:, h, :], start=True, stop=True)
                        S.do('vector', 'tensor_tensor', reads=[pk, 'MT'], writes=['attT'], out=attT[:, h4 * 4:(h4 + 1) * 4, :], in0=pb[:].rearrange("p (a b) -> p a b", a=4),
                             in1=MT[:, kind, :].rearrange("p (o t) -> p o t", o=1).to_broadcast([128, 4, 128]), op=ALU.mult)
                    pb, pk = next_bank()
                    for h in range(8):
                        S.do('tensor', 'matmul', reads=['hlogf', 'Pm'], writes=[pk], out=pb[:, h * 16:(h + 1) * 16], lhsT=logf[:, h * 128:(h + 1) * 128], rhs=Pm[:, kind, :], start=True, stop=True)
                    S.do('scalar', 'activation', reads=[pk], writes=['hem'], out=em[:].rearrange("p h g -> p (h g)"), in_=pb[:, 0:128], func=AF.Exp)
                    if kind == 1:
                        for hh in range(2):
                            pb, pk = next_bank()
                            S.do('tensor', 'matmul', reads=['U8', 'hlogf'], writes=[pk], out=pb[:], lhsT=U8[:], rhs=logf[:, hh * 512:(hh + 1) * 512], start=True, stop=True)
                            S.do('scalar', 'activation', reads=[pk], writes=['hex'], out=ex[:, hh * 512:(hh + 1) * 512], in_=pb[:], func=AF.Exp)
                        S.do('vector', 'tensor_tensor', reads=['hex', 'hk'], writes=['hkh'], out=kh[:], in0=ex[:], in1=hk[:], op=ALU.mult)
                        pb, pk = next_bank()
                        for h in range(8):
                            S.do('tensor', 'matmul', reads=['hlogf', 'G16'], writes=[pk], out=pb[:, h * 16:(h + 1) * 16], lhsT=logf[:, h * 128:(h + 1) * 128], rhs=G16[:], start=True, stop=True)
                        S.do('scalar', 'activation', reads=[pk], writes=['hem'], out=em[:].rearrange("p h g -> p (h g)"), in_=pb[:, 0:128], func=AF.Exp)
                    nb = 1 if kind == 0 else 4
                    gb = G // nb
                    oT_banks = [(pbank[6], 'pb6'), (pbank[7], 'pb7')]
                    for h4 in range(2):
                        pbo, pko = oT_banks[h4]
                        for hh in range(4):
                            h = h4 * 4 + hh
                            S.do('tensor', 'matmul', reads=['hvb2', 'attT'], writes=[pko], out=pbo[:, hh * 128:(hh + 1) * 128], lhsT=vb[:, h * 128:(h + 1) * 128], rhs=attT[:, h, :], start=(hh == 0), stop=False)
                    for b_ in range(nb):
                        if kind == 0:
                            S.dd('sync', Sp[:, 0:2, :, :].rearrange("p g h d -> p g (h d)"), sown_scr[:, 2 * t:2 * t + 2, :], reads=[('sown', t)], writes=['Sp'], sem='sp')
                            for g in range(2):
                                for h in range(8):
                                    S.do('vector', 'tensor_scalar', reads=['Sp', 'hem'], writes=['Sp'], out=Sp[:, g, h, :], in0=Sp[:, g, h, :], scalar1=em[:, h, g:g + 1], scalar2=None, op0=ALU.mult)
                        else:
                            S.dd('sync', S0[:].rearrange("p g h d -> p (g h) d"), st_in[b_ * 4:(b_ + 1) * 4].rearrange("g h k d -> k (g h) d"), reads=[], writes=['S0'], sem='sp')
                            S.do('gpsimd', 'tensor_copy', reads=['S0'], writes=['Sp'], out=Sp[:], in_=S0[:])
                        for gl in range(gb):
                            g = b_ * gb + gl
                            for h in range(8):
                                pbo, pko = oT_banks[h // 4]
                                hh = h % 4
                                S.do('tensor', 'matmul', reads=['Sp', 'qhT'], writes=[pko], out=pbo[:, hh * 128 + g * L:hh * 128 + (g + 1) * L], lhsT=Sp[:, gl, h, :], rhs=qhT[:, h, g * L:(g + 1) * L],
                                     start=False, stop=(b_ == nb - 1 and gl == gb - 1))
                        if kind == 1:
                            for gl in range(gb):
                                g = b_ * gb + gl
                                S.do('vector', 'tensor_scalar', reads=['hkh', 'G16'], writes=['kmask'], out=kmask[:], in0=kh[:], scalar1=G16[:, g:g + 1], scalar2=None, op0=ALU.mult)
                                for h4 in range(2):
                                    pb, pk = next_bank()
                                    for hh in range(4):
                                        h = h4 * 4 + hh
                                        S.do('tensor', 'matmul', reads=['kmask', 'hvb2'], writes=[pk], out=pb[:, hh * 128:(hh + 1) * 128], lhsT=kmask[:, h * 128:(h + 1) * 128], rhs=vb[:, h * 128:(h + 1) * 128], start=True, stop=True)
                                    for hh in range(4):
                                        h = h4 * 4 + hh
                                        S.do('vector', 'scalar_tensor_tensor', reads=['S0', 'hem', pk], writes=['S0'], out=S0[:, gl, h, :], in0=S0[:, gl, h, :], scalar=em[:, h, g:g + 1], in1=pb[:, hh * 128:(hh + 1) * 128], op0=ALU.mult, op1=ALU.add)
                            S.dd('sync', s_state[b_ * 4:(b_ + 1) * 4].rearrange("g h k d -> k (g h) d"), S0[:].rearrange("p g h d -> p (g h) d"), reads=['S0'], writes=[('s_state', b_)], sem='sso')
                    for h4 in range(2):
                        pbo, pko = oT_banks[h4]
                        S.do('scalar', 'activation', reads=[pko], writes=['oTs'], out=oTs[:], in_=pbo[:], func=AF.Copy)
                        S.do('vector', 'tensor_tensor', reads=['oTs'], writes=['sqT'], out=sqT[:], in0=oTs[:], in1=oTs[:], op=ALU.mult)
                        pb, pk = next_bank()
                        S.do('tensor', 'matmul', reads=['ones_f', 'sqT'], writes=[pk], out=pb[:], lhsT=ones_f[:], rhs=sqT[:], start=True, stop=True)
                        S.do('scalar', 'activation', reads=[pk], writes=['rbc'], out=rbc[:], in_=pb[:], func=AF.Sqrt, scale=1.0 / 128, bias=EPS)
                        S.do('vector', 'reciprocal', reads=['rbc'], writes=['rbc'], out=rbc[:], in_=rbc[:])
                        S.do('vector', 'tensor_tensor', reads=['rbc', 'oTs'], writes=['oTs'], out=oTs[:], in0=oTs[:], in1=rbc[:], op=ALU.mult)
                        S.do('vector', 'scalar_tensor_tensor', reads=['oTs', 'hng_t', 'ogT'], writes=['obT'], out=obT[:, h4 * 4:(h4 + 1) * 4, :].rearrange("p a b -> p (a b)"), in0=oTs[:], scalar=hng_t[:, 0:1],
                             in1=ogT[:, h4 * 4:(h4 + 1) * 4, :].rearrange("p a b -> p (a b)"), op0=ALU.mult, op1=ALU.mult)
                    S.dd('sync', obT_scr[t], obT[:], reads=['obT'], writes=[('obT', t)], sem='obo')
                    if t in (0, 3, 8):
                        dbg("obT%d" % t, obT[:], [128, 8, 128], BF16, ['obT'])
                S.barrier()
                C.pb_n = 8


        S.barrier()
        lbst.close()
        gst.close()
        if 'att' in phases:
            NEGB = -30000.0
            with ExitStack() as ph:
                psb = lambda name, shape, dt=F32: ph.enter_context(nc.sbuf_tensor(name, list(shape), dt))
                C.pb_n = 3
                ACC = [(pbank[3 + i], 'pb%d' % (3 + i)) for i in range(4)]
                MISC = (pbank[7], 'pb7')
                identb = psb("identb", [128, 128], BF16)
                S.do('vector', 'tensor_copy', reads=['ident'], writes=['identb'], out=identb[:], in_=ident[:])
                E = psb("E", [64, 4096], BF16)
                S.dd('sync', E[:], cE, writes=['E'], sem='c')
                selb = psb("selb", [128, 4, 512], BF16); winb = psb("winb", [128, 8, 512], BF16)
                S.dd('sync', selb[:], cselb.rearrange("a p c -> p a c"), writes=['selb'], sem='c')
                S.dd('sync', winb[:], cwinb.rearrange("a p c -> p a c"), writes=['winb'], sem='c')
                ovp = psb("ovp", [128, 2, 64], BF16); ovs = psb("ovs", [128, 64], BF16)
                S.dd('sync', ovp[:], covp.rearrange("a p c -> p a c"), writes=['ovp'], sem='c')
                S.dd('sync', ovs[:], covs, writes=['ovs'], sem='c')
                OH = psb("OH", [17, 16, 128], BF16); seqb = psb("seqb", [17, 512], BF16)
                S.dd('sync', OH[:], cOH, writes=['OH'], sem='c')
                S.dd('sync', seqb[:], cseqb, writes=['seqb'], sem='c')
                bdc = psb("bdc", [128, 512], BF16); wb0 = psb("wb0", [128, 512], BF16)
                S.dd('sync', bdc[:], cbdc, writes=['bdc'], sem='c')
                S.dd('sync', wb0[:], cwb0, writes=['wb0'], sem='c')
                w1b = psb("w1b", [64, 2, 32, 64], BF16); w2b = psb("w2b", [64, 2, 64], BF16); peb = psb("peb", [64, 2])
                with ExitStack() as ph2:
                    psb2 = lambda name, shape, dt=F32: ph2.enter_context(nc.sbuf_tensor(name, list(shape), dt))
                    w1f = psb2("w1f", [64, 2, 32, 64]); w2f = psb2("w2f", [64, 2, 64]); pef = psb2("pef", [64, 2, 32]); pebf = psb2("pebf", [64, 2, 32], BF16)
                    for kv in range(2):
                        S.dd('sync', w1f[:, kv], cw1[kv].rearrange("(l d) h -> d l h", d=64), writes=['w1f'], sem='c')
                        S.dma('sync', lambda e, kv=kv: e.dma_start(out=pef[:, kv], in_=cpe[kv].rearrange("l d -> d l"), allow_slow_non_contiguous=True), writes=['pef'], sem='c')
                    S.dd('sync', w2f[:], cw2.rearrange("a h d -> h a d"), writes=['w2f'], sem='c')
                    S.do('vector', 'tensor_copy', reads=['w1f'], writes=['w1b'], out=w1b[:], in_=w1f[:])
                    S.do('vector', 'tensor_copy', reads=['w2f'], writes=['w2b'], out=w2b[:], in_=w2f[:])
                    S.do('vector', 'tensor_copy', reads=['pef'], writes=['pebf'], out=pebf[:], in_=pef[:])
                    for kv in range(2):
                        pb, pk = next_bank()
                        for l in range(32):
                            S.do('tensor', 'matmul', reads=['w1b', 'pebf'], writes=[pk], out=pb[0:64, 0:1], lhsT=w1b[:, kv, l, :], rhs=pebf[:, kv, l:l + 1], start=(l == 0), stop=(l == 31))
                        S.do('vector', 'tensor_copy', reads=[pk], writes=['peb'], out=peb[:, kv:kv + 1], in_=pb[0:64, 0:1])
                    S.barrier()

                hsil = psb("hsil", [64, 256], BF16)

                def compress(XT, xkeys, n, kcT_dst, kkey, vc_dst, vkey):
                    for kv in range(2):
                        for g in range(4):
                            pb, pk = next_bank()
                            xt = XT(kv, g)
                            for l in range(32):
                                S.do('tensor', 'matmul', reads=xkeys + ['w1b'], writes=[pk], out=pb[0:64, 0:n], lhsT=w1b[:, kv, l, :], rhs=xt[:, l:l + 16 * (n - 1) + 1:16], start=(l == 0), stop=(l == 31))
                            S.do('scalar', 'activation', reads=[pk, 'peb'], writes=['hsil'], out=hsil[:, 0:n], in_=pb[0:64, 0:n], func=AF.Silu, bias=peb[:, kv:kv + 1])
                            pb2, pk2 = next_bank()
                            if kv == 0:
                                S.do('tensor', 'matmul', reads=['hsil', 'w2b'], writes=[pk2], out=pb2[0:64, 0:n], lhsT=w2b[:, 0, :], rhs=hsil[:, 0:n], start=True, stop=True)
                                S.do('vector', 'tensor_copy', reads=[pk2], writes=[kkey], out=kcT_dst(g), in_=pb2[0:64, 0:n])
                            else:
                                for ct in range((n + 127) // 128):
                                    m = min(128, n - ct * 128)
                                    S.do('tensor', 'matmul', reads=['hsil', 'w2b'], writes=[pk2], out=pb2[0:m, ct * 64:(ct + 1) * 64], lhsT=hsil[:, ct * 128:ct * 128 + m], rhs=w2b[:, 1, :], start=True, stop=True)
                                    S.do('vector', 'tensor_copy', reads=[pk2], writes=[vkey], out=vc_dst(g, ct, m), in_=pb2[0:m, ct * 64:(ct + 1) * 64])

                ptb = [psb("ptb%d" % i, [128, 512], BF16) for i in range(3)]
                C.pt_i = 0

                ptz = [psb("ptz%d" % i, [128, 512], BF16) for i in range(3)]
                for i in range(3):
                    S.do('gpsimd', 'memset', writes=['ptz%d' % i], ap=ptz[i][:], constant=0.0)
                C.ptz_i = 0

                def attend_g(qT, qkey, jobs, naccs, gs):
                    started = set()
                    last = {}
                    for ji, jb in enumerate(jobs):
                        for g in gs:
                            for (aid, _, _) in jb['V'](g):
                                last[(aid, g)] = ji

                    def stage1(ji, jb, g):
                        nk = jb['nk']
                        qs = jb.get('qs')
                        pb, pk = next_bank()
                        bl = jb['bias'](g)
                        if qs is None:
                            sel = lambda ap: ap
                            ob = pb[0:nk, :]
                        else:
                            sel = lambda ap: ap.rearrange("p (r q) -> p r q", r=4)[:, :, qs * 8:(qs + 1) * 8]
                            ob = pb[0:nk, 0:32].rearrange("p (r q) -> p r q", r=4)
                        S.do('tensor', 'matmul', reads=jb['keys'] + [qkey], writes=[pk], out=ob, lhsT=jb['KT'](g), rhs=sel(qT[:, g * 512:(g + 1) * 512]), start=True, stop=(len(bl) == 0))
                        for bi, (bl_l, bl_r, bkeys) in enumerate(bl):
                            S.do('tensor', 'matmul', reads=bkeys, writes=[pk], out=ob, lhsT=bl_l, rhs=sel(bl_r), start=False, stop=(bi == len(bl) - 1))
                        if qs is None:
                            pi = C.pt_i
                            C.pt_i = (C.pt_i + 1) % 3
                            pt, ptk = ptb[pi], 'ptb%d' % pi
                            S.do('scalar', 'activation', reads=[pk], writes=[ptk], out=pt[0:nk, :], in_=ob, func=AF.Exp)
                        else:
                            pi = C.ptz_i
                            C.ptz_i = (C.ptz_i + 1) % 3
                            pt, ptk = ptz[pi], 'ptz%d' % pi
                            S.do('scalar', 'activation', reads=[pk], writes=[ptk], out=sel(pt[0:nk, :]), in_=ob, func=AF.Exp)
                        return (ji, jb, g, nk, qs, pt, ptk, jb['V'](g), list(jb['keys']))

                    def stage2(u):
                        ji, jb, g, nk, qs, pt, ptk, vl, keys = u
                        for (aid, vap, nV) in vl:
                            ab, abk = naccs[aid][g]
                            for r in range(4):
                                S.do('tensor', 'matmul', reads=[ptk] + keys, writes=[abk], out=ab[:, r * nV:(r + 1) * nV], lhsT=pt[0:nk, r * 128:(r + 1) * 128], rhs=vap,
                                     start=((aid, g) not in started), stop=(last[(aid, g)] == ji and r == 3))
                                started.add((aid, g))
                        if qs is not None:
                            S.do('vector', 'memset', reads=[], writes=[ptk], ap=pt[0:nk, :].rearrange("p (r q) -> p r q", r=4)[:, :, qs * 8:(qs + 1) * 8], constant=0.0)

                    prev = None
                    for ji, jb in enumerate(jobs):
                        if 'prep' in jb:
                            jb['prep']()
                        for g in gs:
                            cur = stage1(ji, jb, g)
                            if prev is not None:
                                stage2(prev)
                            prev = cur
                    if prev is not None:
                        stage2(prev)

                qTt = [psb("qTt%d" % i, [64, 2048], BF16) for i in range(2)]
                gates = psb("gates", [128, 48])
                visadd = psb("visadd", [128, 2, 64])
                oa = psb("oa", [128, 1024])
                linv = psb("linv", [128, 4])
                imp = psb("imp", [128, 64]); imt = psb("imt", [128, 64])
                m8 = psb("m8", [128, 8]); sc2 = psb("sc2", [128, 64]); thr = psb("thr", [128, 1])
                selbT = [psb("selbT%d" % g, [64, 512], BF16) for g in range(4)]
                osc = psb("osc", [128, 4, 64])
                oaT = psb("oaT_a", [128, 8, 128], BF16)
                cmpb_t = psb("cmpb_t", [128, 2, 512], BF16)

                def topk_bias(t, g, imp_ap):
                    S.do('vector', 'tensor_tensor', reads=['imp', 'visadd'], writes=['imt'], out=imt[:], in0=imp_ap, in1=visadd[:, 0, :], op=ALU.mult)
                    S.do('vector', 'tensor_tensor', reads=['imt', 'visadd'], writes=['imt'], out=imt[:], in0=imt[:], in1=visadd[:, 1, :], op=ALU.add)
                    S.do('vector', 'max', reads=['imt'], writes=['m8'], out=m8[:], in_=imt[:])
                    S.do('vector', 'match_replace', reads=['imt', 'm8'], writes=['sc2'], out=sc2[:], in_to_replace=m8[:], in_values=imt[:], imm_value=-3.0e9)
                    S.do('vector', 'max', reads=['sc2'], writes=['m8'], out=m8[:], in_=sc2[:])
                    S.do('vector', 'tensor_reduce', reads=['m8'], writes=['thr'], out=thr[:], in_=m8[:], axis=AX.X, op=ALU.min)
                    S.do('vector', 'tensor_scalar', reads=['imt', 'thr'], writes=['sc2'], out=sc2[:], in0=imt[:], scalar1=thr[:, 0:1], scalar2=None, op0=ALU.is_ge)
                    S.do('vector', 'tensor_tensor', reads=['sc2', 'visadd'], writes=['sc2'], out=sc2[:], in0=sc2[:], in1=visadd[:, 0, :], op=ALU.mult)
                    S.do('vector', 'tensor_scalar', reads=['sc2'], writes=['sc2'], out=sc2[:], in0=sc2[:], scalar1=-NEGB, scalar2=NEGB, op0=ALU.mult, op1=ALU.add)
                    mb, mk = MISC
                    S.do('tensor', 'transpose', reads=['sc2', 'ident'], writes=[mk], out=mb[0:64, 0:128], in_=sc2[:], identity=ident[:])
                    for r in range(4):
                        S.do('scalar', 'activation', reads=[mk], writes=['selbT%d' % g], out=selbT[g][:, r * 128:(r + 1) * 128], in_=mb[0:64, 0:128], func=AF.Copy)

                def normalize(acc, g, br, nV):
                    ab, abk = acc
                    av = ab[:, 0:4 * nV].rearrange("p (r v) -> p r v", r=4)
                    S.do('vector', 'tensor_scalar', reads=[abk], writes=['linv'], out=linv[:], in0=av[:, :, 64], scalar1=1e-30, scalar2=None, op0=ALU.max)
                    S.do('vector', 'reciprocal', reads=['linv'], writes=['linv'], out=linv[:], in_=linv[:])
                    S.do('vector', 'tensor_tensor', reads=['linv', 'gates'], writes=['linv'], out=linv[:], in0=linv[:], in1=gates[:, br * 16 + g * 4:br * 16 + g * 4 + 4], op=ALU.mult)
                    dst = oa[:, g * 256:(g + 1) * 256].rearrange("p (r d) -> p r d", r=4)
                    lb_ = linv[:].rearrange("p (r o) -> p r o", o=1).to_broadcast([128, 4, 64])
                    if br == 0:
                        S.do('vector', 'tensor_tensor', reads=[abk, 'linv'], writes=['oa'], out=dst, in0=av[:, :, 0:64], in1=lb_, op=ALU.mult)
                    else:
                        S.do('vector', 'tensor_tensor', reads=[abk, 'linv'], writes=['osc'], out=osc[:], in0=av[:, :, 0:64], in1=lb_, op=ALU.mult)
                        S.do('vector', 'tensor_tensor', reads=['osc', 'oa'], writes=['oa'], out=dst, in0=dst, in1=osc[:], op=ALU.add)

                def cmp_finish_g(t, naccs, gs):
                    accs1, accs2 = naccs[0], naccs[1]
                    for g in gs:
                        ab, abk = accs1[g]
                        av = ab[:, 0:260].rearrange("p (r v) -> p r v", r=4)
                        S.do('vector', 'tensor_scalar', reads=[abk], writes=['linv'], out=linv[:], in0=av[:, :, 64], scalar1=1e-30, scalar2=None, op0=ALU.max)
                        S.do('vector', 'reciprocal', reads=['linv'], writes=['linv'], out=linv[:], in_=linv[:])
                        a2, a2k = accs2[g]
                        a2v = a2[:, 0:256].rearrange("p (r v) -> p r v", r=4)
                        S.do('vector', 'tensor_scalar', reads=[a2k, 'linv'], writes=['imp'], out=imp[:], in0=a2v[:, 0, :], scalar1=linv[:, 0:1], scalar2=None, op0=ALU.mult)
                        for r in range(1, 4):
                            S.do('vector', 'scalar_tensor_tensor', reads=[a2k, 'linv', 'imp'], writes=['imp'], out=imp[:], in0=a2v[:, r, :], scalar=linv[:, r:r + 1], in1=imp[:], op0=ALU.mult, op1=ALU.add)
                        topk_bias(t, g, imp[:])
                        normalize(accs1[g], g, 0, 65)

                def finish_tile(t):
                    for k2 in range(2):
                        pb, pk = next_bank()
                        for kk in range(4):
                            kc = k2 * 4 + kk
                            S.do('tensor', 'transpose', reads=['oa', 'ident'], writes=[pk], out=pb[:, kk * 128:(kk + 1) * 128], in_=oa[:, kc * 128:(kc + 1) * 128], identity=ident[:])
                        S.do('scalar', 'activation', reads=[pk], writes=['oaT_a'], out=oaT[:, k2 * 4:(k2 + 1) * 4, :], in_=pb[:].rearrange("p (a b) -> p a b", a=4), func=AF.Copy)
                    S.dd('sync', oaT_scr[t], oaT[:], reads=['oaT_a'], writes=[('oaT', t)], sem='oao')

                def load_tile(t):
                    qb, qk = qTt[t % 2], 'qTt%d' % (t % 2)
                    S.dd('sync', qb[:], qT_scr[t], reads=[('qT', t)], writes=[qk], sem='qt%d' % (t % 2))
                    S.dd('sync', gates[:], zs[t, :, O_GA:O_GA + 48], reads=[('zs', t, O_GA)], writes=['gates'], sem='c')
                    S.dd('sync', visadd[:], cvisadd[t], writes=['visadd'], sem='c')
                    return qb, qk

                if 'attp' in phases:
                  with ExitStack() as ph2:
                    psb2 = lambda name, shape, dt=F32: ph2.enter_context(nc.sbuf_tensor(name, list(shape), dt))
                    kcT = psb2("kcT", [64, 4, 256], BF16)
                    vca = psb2("vca", [128, 2, 4, 65], BF16)
                    S.do('vector', 'memset', writes=['kcT'], ap=kcT[:], constant=0.0)
                    S.do('vector', 'memset', writes=['vca'], ap=vca[:], constant=0.0)
                    S.do('vector', 'memset', writes=['vca'], ap=vca[:, :, :, 64:65], constant=1.0)
                    with ExitStack() as ph3:
                        XT = ph3.enter_context(nc.sbuf_tensor("XTp", [64, 2, 4, 4096], BF16))
                        for kv in range(2):
                            S.dd('sync', XT[:, kv], kT_scr[kv], reads=[('kT', T, kv) for T in range(NSEQ)], writes=['XTp'], sem='c')
                        compress(lambda kv, g: XT[:, kv, g, :], ['XTp'], 255, lambda g: kcT[:, g, 0:255], 'kcT', lambda g, ct, m: vca[0:m, ct, g, 0:64], 'vca')
                        dbg("kcT", kcT[:], [64, 4, 256], BF16, ['kcT'])
                        dbg("vca", vca[:], [128, 2, 4, 65], BF16, ['vca'])
                        S.barrier()
                    KTs = psb2("KTs", [64, 4, 4096], BF16); KTw = psb2("KTw", [64, 4, 4096], BF16)
                    Vsw = psb2("Vsw", [128, NSEQ, 2, 4, 65], BF16)
                    S.dd('sync', KTs[:], kT_scr[2], reads=[('kT', T, 2) for T in range(NSEQ)], writes=['KTs'], sem='c')
                    S.dd('sync', KTw[:], kT_scr[3], reads=[('kT', T, 3) for T in range(NSEQ)], writes=['KTw'], sem='c')
                    for T in range(NSEQ):
                        S.dd('sync', Vsw[:, T].rearrange("p a g d -> p (a g d)"), v_scr[T].rearrange("p a g d -> p (a g d)"), reads=[('v', T)], writes=['Vsw'], sem='c')
                    for t in range(8):
                        qb, qk = load_tile(t)
                        S.dd('sync', cmpb_t[:], ccmpb[t].rearrange("a p c -> p a c"), writes=['cmpb_t'], sem='c')
                        for gp in range(2):
                            gs = (2 * gp, 2 * gp + 1)
                            jobs = []
                            for ct in range(2):
                                jobs.append(dict(nk=128, keys=['kcT', 'vca', 'ovp', 'cmpb_t', 'identb'],
                                                 KT=lambda g, ct=ct: kcT[:, g, ct * 128:(ct + 1) * 128],
                                                 V=lambda g, ct=ct: [(0, vca[:, ct, g, :], 65), (1, ovp[:, ct, :], 64)],
                                                 bias=lambda g, ct=ct: [(identb[:], cmpb_t[:, ct, :], ['identb', 'cmpb_t'])]))
                            naccs = {0: {gs[0]: ACC[0], gs[1]: ACC[1]}, 1: {gs[0]: ACC[2], gs[1]: ACC[3]}}
                            attend_g(qb, qk, jobs, naccs, gs)
                            cmp_finish_g(t, naccs, gs)
                        if t in (0, 3, 7):
                            dbg("oc%d" % t, oa[:], [128, 1024], F32, ['oa'])
                            dbg("selb%d" % t, selbT[1][:], [64, 512], BF16, ['selbT1'])
                        i = t
                        jobs = []
                        for kt in range(4 * i + 4):
                            def bias(g, kt=kt, i=i):
                                bl = [(E[:, kt * 128:(kt + 1) * 128], selbT[g][:], ['E', 'selbT%d' % g])]
                                if kt >= 4 * i:
                                    bl.append((identb[:], selb[:, kt - 4 * i, :], ['identb', 'selb']))
                                return bl
                            jobs.append(dict(nk=128, keys=['KTs', 'Vsw'], KT=lambda g, kt=kt: KTs[:, g, kt * 128:(kt + 1) * 128],
                                             V=lambda g, kt=kt: [(0, Vsw[:, kt, 0, g, :], 65)], bias=bias))
                        naccs = {0: {g: ACC[g] for g in range(4)}}
                        attend_g(qb, qk, jobs, naccs, (0, 1, 2, 3))
                        for g in range(4):
                            normalize(ACC[g], g, 1, 65)
                        jobs = []
                        for o in range(-4, 4):
                            kt = 4 * i + o
                            if kt < 0:
                                continue
                            jobs.append(dict(nk=128, keys=['KTw', 'Vsw'], KT=lambda g, kt=kt: KTw[:, g, kt * 128:(kt + 1) * 128],
                                             V=lambda g, kt=kt: [(0, Vsw[:, kt, 1, g, :], 65)],
                                             bias=lambda g, o=o: [(identb[:], winb[:, o + 4, :], ['identb', 'winb'])]))
                        attend_g(qb, qk, jobs, naccs, (0, 1, 2, 3))
                        for g in range(4):
                            normalize(ACC[g], g, 2, 65)
                        if t in (0, 3, 7):
                            dbg("oa%d" % t, oa[:], [128, 1024], F32, ['oa'])
                        finish_tile(t)
                    S.barrier()

                if 'atts' in phases:
                  with ExitStack() as ph2:
                    psb2 = lambda name, shape, dt=F32: ph2.enter_context(nc.sbuf_tensor(name, list(shape), dt))
                    t = 8
                    for s_ in range(16):
                        for q4 in range(4):
                            S.dd('sync', s_win[s_, q4 * 126:(q4 + 1) * 126, :], win_in[s_, 8 + q4 * 126:8 + (q4 + 1) * 126, :], writes=[('s_win_a', s_, q4)], sem='swo')
                    pti = psb2("pti", [128, 256], I32); ptf = psb2("ptf", [128, 256]); idx = psb2("idx", [128, 256], I32)
                    pio = psb2("pio", [128, 1], I32); piof = psb2("piof", [128, 1])
                    S.dd('sync', pti[:], pt_in.to_broadcast([128, 256]), writes=['pti'], sem='c')
                    S.do('gpsimd', 'iota', writes=['pio'], out=pio[:], pattern=[[0, 1]], base=0, channel_multiplier=1)
                    S.do('vector', 'tensor_copy', reads=['pio'], writes=['piof'], out=piof[:], in_=pio[:])
                    S.do('vector', 'tensor_copy', reads=['pti'], writes=['ptf'], out=ptf[:], in_=pti[:])
                    S.do('vector', 'tensor_scalar', reads=['ptf', 'piof'], writes=['ptf'], out=ptf[:], in0=ptf[:], scalar1=128.0, scalar2=piof[:, 0:1], op0=ALU.mult, op1=ALU.add)
                    S.do('vector', 'tensor_copy', reads=['ptf'], writes=['idx'], out=idx[:], in_=ptf[:])
                    NPG = 6
                    pgb = [psb2("pgb%d" % i, [128, 512]) for i in range(NPG)]
                    ktp = [psb2("ktp%d" % i, [64, 4, 128], BF16) for i in range(NPG)]
                    vap = [psb2("vap%d" % i, [128, 4, 65], BF16) for i in range(NPG)]
                    for i in range(NPG):
                        S.do('vector', 'memset', writes=['vap%d' % i], ap=vap[i][:], constant=1.0)
                    C.pg_i = 0

                    def fetch_page(cache, s, p):
                        bi = C.pg_i
                        C.pg_i = (bi + 1) % NPG
                        pg, pgk = pgb[bi], 'pgb%d' % bi
                        col = s * 16 + p
                        S.dma('gpsimd', lambda e: e.indirect_dma_start(out=pg[:, :], out_offset=None, in_=cache[:, :],
                                                                      in_offset=bass.IndirectOffsetOnAxis(ap=idx[:, col:col + 1], axis=0)),
                              reads=['idx'], writes=[pgk], sem='pg%d' % bi)
                        return bi, pg, pgk

                    def kv_prep(bi, pg, pgk):
                        pb, pk = next_bank()
                        for g in range(4):
                            S.do('tensor', 'transpose', reads=[pgk, 'ident'], writes=[pk], out=pb[0:64, g * 128:(g + 1) * 128], in_=pg[:, g * 64:(g + 1) * 64], identity=ident[:])
                        S.do('scalar', 'activation', reads=[pk], writes=['ktp%d' % bi], out=ktp[bi][:], in_=pb[0:64, :].rearrange("p (g k) -> p g k", g=4), func=AF.Copy)
                        S.do('vector', 'tensor_copy', reads=[pgk], writes=['vap%d' % bi], out=vap[bi][:, :, 0:64], in_=pg[:, 256:512].rearrange("p (g d) -> p g d", g=4))

                    kcA = psb2("kcA", [64, 16, 4, 128], BF16)
                    vcA = psb2("vcA", [128, 16, 4, 65], BF16)
                    S.do('vector', 'memset', writes=['kcA'], ap=kcA[:], constant=0.0)
                    S.do('vector', 'memset', writes=['vcA'], ap=vcA[:], constant=0.0)
                    S.do('vector', 'memset', writes=['vcA'], ap=vcA[:, :, :, 64:65], constant=1.0)
                    with ExitStack() as ph3:
                        XTs = ph3.enter_context(nc.sbuf_tensor("XTs", [64, 2, 4, 2048], BF16))
                        for s_ in range(16):
                            for p in range(16):
                                bi, pg, pgk = fetch_page(cache_cmp, s_, p)
                                for kv in range(2):
                                    pb, pk = next_bank()
                                    for g in range(4):
                                        S.do('tensor', 'transpose', reads=[pgk, 'ident'], writes=[pk], out=pb[0:64, g * 128:(g + 1) * 128], in_=pg[:, kv * 256 + g * 64:kv * 256 + (g + 1) * 64], identity=ident[:])
                                    S.do('scalar', 'activation', reads=[pk], writes=['XTs'], out=XTs[:, kv, :, p * 128:(p + 1) * 128], in_=pb[0:64, :].rearrange("p (g k) -> p g k", g=4), func=AF.Copy)
                            compress(lambda kv, g: XTs[:, kv, g, :], ['XTs'], 127, lambda g, s_=s_: kcA[:, s_, g, 0:127], 'kcA', lambda g, ct, m, s_=s_: vcA[0:m, s_, g, 0:64], 'vcA')
                        S.barrier()
                    qb, qk = load_tile(t)
                    for gp in range(2):
                        gs = (2 * gp, 2 * gp + 1)
                        jobs = []
                        for s_ in range(16):
                            jobs.append(dict(nk=127, qs=s_, keys=['kcA', 'vcA', 'ovs'],
                                             KT=lambda g, s_=s_: kcA[:, s_, g, 0:127],
                                             V=lambda g, s_=s_: [(0, vcA[0:127, s_, g, :], 65), (1, ovs[0:127, :], 64)],
                                             bias=lambda g: []))
                        naccs = {0: {gs[0]: ACC[0], gs[1]: ACC[1]}, 1: {gs[0]: ACC[2], gs[1]: ACC[3]}}
                        attend_g(qb, qk, jobs, naccs, gs)
                        cmp_finish_g(t, naccs, gs)
                    rnew = psb2("rnew", [128, 3, 512])
                    ktn = psb2("ktn", [64, 2, 4, 128], BF16); van = psb2("van", [128, 2, 4, 65], BF16)
                    S.dd('sync', rnew[:], rows_own[t], reads=[('rows_own', t)], writes=['rnew'], sem='c')
                    S.do('gpsimd', 'memset', writes=['van'], ap=van[:], constant=1.0)
                    for bi_, br in enumerate((1, 2)):
                        pb, pk = next_bank()
                        for g in range(4):
                            S.do('tensor', 'transpose', reads=['rnew', 'ident'], writes=[pk], out=pb[0:64, g * 128:(g + 1) * 128], in_=rnew[:, br, g * 64:(g + 1) * 64], identity=ident[:])
                        S.do('scalar', 'activation', reads=[pk], writes=['ktn'], out=ktn[:, bi_], in_=pb[0:64, :].rearrange("p (g k) -> p g k", g=4), func=AF.Copy)
                        S.do('gpsimd', 'tensor_copy', reads=['rnew'], writes=['van'], out=van[:, bi_, :, 0:64], in_=rnew[:, br, 256:512].rearrange("p (g d) -> p g d", g=4))
                    S.dd('sync', s_win[:, 504:512, :], rnew[:, 2, :], reads=['rnew'], writes=['s_win_b'], sem='swo')
                    naccs = {0: {g: ACC[g] for g in range(4)}}
                    jobs = []
                    for s_ in range(16):
                        for p in range(16):
                            jb = dict(nk=128, qs=s_)

                            def prep(jb=jb, s_=s_, p=p):
                                bi, pg, pgk = fetch_page(cache_sel, s_, p)
                                kv_prep(bi, pg, pgk)
                                jb['keys'] = ['ktp%d' % bi, 'vap%d' % bi]
                                jb['KT'] = lambda g, bi=bi: ktp[bi][:, g, :]
                                jb['V'] = lambda g, bi=bi: [(0, vap[bi][:, g, :], 65)]
                            jb['prep'] = prep
                            jb['bias'] = lambda g, s_=s_, p=p: [(E[:, p * 128:(p + 1) * 128], selbT[g][:], ['E', 'selbT%d' % g])]
                            jb['V'] = lambda g: [(0, None, 65)]
                            jobs.append(jb)
                    jobs.append(dict(nk=128, keys=['ktn', 'van'], KT=lambda g: ktn[:, 0, g, :], V=lambda g: [(0, van[:, 0, g, :], 65)],
                                     bias=lambda g: [(identb[:], bdc[:], ['identb', 'bdc'])]))
                    attend_g(qb, qk, jobs, naccs, (0, 1, 2, 3))
                    for g in range(4):
                        normalize(ACC[g], g, 1, 65)
                    jobs = []
                    for s_ in range(16):
                        for w_ in range(4):
                            jb = dict(nk=128, qs=s_)

                            def prep(jb=jb, s_=s_, w_=w_):
                                bi = C.pg_i
                                C.pg_i = (bi + 1) % NPG
                                pg, pgk = pgb[bi], 'pgb%d' % bi
                                S.dd('sync', pg[:], win_in[s_, w_ * 128:(w_ + 1) * 128, :], writes=[pgk], sem='pg%d' % bi)
                                kv_prep(bi, pg, pgk)
                                jb['keys'] = ['ktp%d' % bi, 'vap%d' % bi]
                                jb['KT'] = lambda g, bi=bi: ktp[bi][:, g, :]
                                jb['V'] = lambda g, bi=bi: [(0, vap[bi][:, g, :], 65)]
                            jb['prep'] = prep

                            def bias(g, s_=s_, w_=w_):
                                bl = []
                                if w_ == 0:
                                    bl.append((identb[:], wb0[:], ['identb', 'wb0']))
                                return bl
                            jb['bias'] = bias
                            jb['V'] = lambda g: [(0, None, 65)]
                            jobs.append(jb)
                    jobs.append(dict(nk=128, keys=['ktn', 'van'], KT=lambda g: ktn[:, 1, g, :], V=lambda g: [(0, van[:, 1, g, :], 65)],
                                     bias=lambda g: [(identb[:], bdc[:], ['identb', 'bdc'])]))
                    attend_g(qb, qk, jobs, naccs, (0, 1, 2, 3))
                    for g in range(4):
                        normalize(ACC[g], g, 2, 65)
                    dbg("oa8", oa[:], [128, 1024], F32, ['oa'])
                    finish_tile(t)
                    S.barrier()
                C.pb_n = 8
                S.barrier()

        gst2 = ExitStack()
        alloc_gemm(gst2)
        if 'fin' in phases:
            with ExitStack() as ph:
                psb = lambda name, shape, dt=F32: ph.enter_context(nc.sbuf_tensor(name, list(shape), dt))
                g2t = psb("g2t", [128, KC])
                S.dd('sync', g2t[:], g2, writes=['g2t'], sem='c')
                ldb = [psb("ldb%d" % i, [128, 512]) for i in range(4)]
                C.ld_i = 0

                def ld(src, reads):
                    i = C.ld_i
                    C.ld_i = (C.ld_i + 1) % 4
                    S.dd('sync', ldb[i][:], src, reads=reads, writes=['ldb%d' % i], sem='ld%d' % i)
                    return ldb[i], 'ldb%d' % i

                wpa_v = w_pa.rearrange("(kc p) n -> p kc n", p=128)
                wpb_v = w_pb.rearrange("(kc p) n -> p kc n", p=128)
                wo_v = w_o.rearrange("(kc p) n -> p kc n", p=128)
                wf1_v = w_f1.rearrange("(kc p) n -> p kc n", p=128)
                ch4 = [(c * 512, 512, None) for c in range(4)]
                x1T = psb("x1T", [128, NT, KC, 128], BF16)
                ssq2 = psb("ssq2", [128, NT, 4])
                rstd2 = psb("rstd2", [128, NT])
                x1_scr = dscr("x1_scr", [NT, 128, D])
                with ExitStack() as ph1:
                    psb1 = lambda name, shape, dt=F32: ph1.enter_context(nc.sbuf_tensor(name, list(shape), dt))
                    mixT = psb1("mixT", [128, NT, KC, 128], BF16)
                    with ExitStack() as ph2:
                        psb2 = lambda name, shape, dt=F32: ph2.enter_context(nc.sbuf_tensor(name, list(shape), dt))
                        oaT = psb2("oaT", [128, NT, 8, 128], BF16)
                        obT2 = psb2("obT2", [128, NT, 8, 128], BF16)
                        mxs = psb2("mxs", [128, 512])
                        for t in range(NT):
                            S.dd('sync', oaT[:, t], oaT_scr[t], reads=[('oaT', t)], writes=['oaT_sb%d' % t], sem='c')
                            S.dd('sync', obT2[:, t], obT_scr[t], reads=[('obT', t)], writes=['obT_sb%d' % t], sem='c')

                        def epi_a(ci, t, pb, pk, c0, w, tag):
                            gb_, gk = ld(zs[t, :, O_A + c0:O_A + c0 + w], [('zs', t, O_A + c0)])
                            zi = C.zst_i
                            C.zst_i = (C.zst_i + 1) % 4
                            S.do('vector', 'tensor_tensor', reads=[pk, gk], writes=['zst%d' % zi], out=C.zst[zi][:], in0=pb[:], in1=gb_[:], op=ALU.mult)
                            S.dd('sync', mix_scr[t, :, c0:c0 + w], C.zst[zi][:], reads=['zst%d' % zi], writes=[('mix', t, c0)], sem='zo%d' % zi)
                        gemm(lambda t, kc: oaT[:, t, kc, :], lambda t: 'oaT_sb%d' % t, NT, 8, wpa_v, ch4, None, None, epi_a)

                        def epi_b(ci, t, pb, pk, c0, w, tag):
                            gb_, gk = ld(zs[t, :, O_B + c0:O_B + c0 + w], [('zs', t, O_B + c0)])
                            pa_, pak = ld(mix_scr[t, :, c0:c0 + w], [('mix', t, c0)])
                            S.do('vector', 'tensor_tensor', reads=[pk, gk], writes=['mxs'], out=mxs[:], in0=pb[:], in1=gb_[:], op=ALU.mult)
                            S.do('vector', 'tensor_tensor', reads=['mxs', pak], writes=['mxs'], out=mxs[:], in0=mxs[:], in1=pa_[:], op=ALU.add)
                            pb2, pk2 = next_bank()
                            for kk in range(4):
                                S.do('tensor', 'transpose', reads=['mxs', 'ident'], writes=[pk2], out=pb2[:, kk * 128:(kk + 1) * 128], in_=mxs[:, kk * 128:(kk + 1) * 128], identity=ident[:])
                            S.do('scalar', 'activation', reads=[pk2], writes=['mixT%d' % t], out=mixT[:, t, ci * 4:(ci + 1) * 4, :], in_=pb2[:].rearrange("p (a b) -> p a b", a=4), func=AF.Copy)
                        gemm(lambda t, kc: obT2[:, t, kc, :], lambda t: 'obT_sb%d' % t, NT, 8, wpb_v, ch4, None, None, epi_b)
                        S.barrier()
                    junk2 = psb1("junk2", [128, 512])
                    x1c = [psb1("x1c%d" % i, [128, 512]) for i in range(2)]
                    C.x1c_i = 0
                    S.do('vector', 'memset', writes=['ssq2'], ap=ssq2[:], constant=0.0)

                    def epi_o(ci, t, pb, pk, c0, w, tag):
                        xb_, xk = ld(x_own[t, :, c0:c0 + w], [])
                        i = C.x1c_i
                        C.x1c_i = (C.x1c_i + 1) % 2
                        xc, yk = x1c[i], 'x1c%d' % i
                        S.do('vector', 'tensor_tensor', reads=[pk, xk], writes=[yk], out=xc[:], in0=pb[:], in1=xb_[:], op=ALU.add)
                        S.dd('sync', x1_scr[t, :, c0:c0 + w], xc[:], reads=[yk], writes=[('x1', t, ci)], sem='x1o%d' % i)
                        S.do('scalar', 'activation', reads=[yk, 'ssq2'], writes=['junk2', 'ssq2'], out=junk2[:], in_=xc[:], func=AF.Square, accum_out=ssq2[:, t, ci:ci + 1])
                        pb2, pk2 = next_bank()
                        for kk in range(4):
                            S.do('tensor', 'transpose', reads=[yk, 'ident'], writes=[pk2], out=pb2[:, kk * 128:(kk + 1) * 128], in_=xc[:, kk * 128:(kk + 1) * 128], identity=ident[:])
                        S.do('scalar', 'activation', reads=[pk2], writes=['x1T%d' % t], out=x1T[:, t, ci * 4:(ci + 1) * 4, :], in_=pb2[:].rearrange("p (a b) -> p a b", a=4), func=AF.Copy)
                    gemm(lambda t, kc: mixT[:, t, kc, :], lambda t: 'mixT%d' % t, NT, KC, wo_v, ch4, None, None, epi_o)
                    S.do('vector', 'tensor_reduce', reads=['ssq2'], writes=['rstd2'], out=rstd2[:], in_=ssq2[:], axis=AX.X, op=ALU.add)
                    S.do('scalar', 'activation', reads=['rstd2'], writes=['rstd2'], out=rstd2[:], in_=rstd2[:], func=AF.Sqrt, scale=1.0 / D, bias=EPS)
                    S.do('vector', 'reciprocal', reads=['rstd2'], writes=['rstd2'], out=rstd2[:], in_=rstd2[:])
                    S.barrier()
                yacc = psb("yacc", [128, NT, D])
                for t in range(NT):
                    for ci in range(4):
                        S.dd('sync', yacc[:, t, ci * 512:(ci + 1) * 512], x1_scr[t, :, ci * 512:(ci + 1) * 512], reads=[('x1', t, ci)], writes=['yacc%d_%d' % (t, ci)], sem='c')
                hT = [psb("hT%d" % i, [128, NT, 4, 128], BF16) for i in range(2)]
                hsb = psb("hsb", [128, 512])
                ffn_specs = []
                for fc in range(16):
                    hb, hbk = hT[fc % 2], 'hT%d' % (fc % 2)

                    def epi_h(ci, t, pb, pk, c0, w, tag, hb=hb, hbk=hbk):
                        S.do('scalar', 'activation', reads=[pk, 'rstd2'], writes=['hsb'], out=hsb[:], in_=pb[:], func=AF.Relu, scale=rstd2[:, t:t + 1])
                        S.do('vector', 'tensor_tensor', reads=['hsb'], writes=['hsb'], out=hsb[:], in0=hsb[:], in1=hsb[:], op=ALU.mult)
                        pb2, pk2 = next_bank()
                        for kk in range(4):
                            S.do('tensor', 'transpose', reads=['hsb', 'ident'], writes=[pk2], out=pb2[:, kk * 128:(kk + 1) * 128], in_=hsb[:, kk * 128:(kk + 1) * 128], identity=ident[:])
                        S.do('scalar', 'activation', reads=[pk2], writes=[hbk + '_%d' % t], out=hb[:, t, :, :], in_=pb2[:].rearrange("p (a b) -> p a b", a=4), func=AF.Copy)
                    ffn_specs += mk_specs(lambda t, kc: x1T[:, t, kc, :], lambda t: 'x1T%d' % t, NT, KC, wf1_v, [(fc * 512, 512, None)], g2t, 'g2t', epi_h)
                    wf2_v = w_f2[fc * 512:(fc + 1) * 512, :].rearrange("(kc p) n -> p kc n", p=128)

                    def epi_y(ci, t, pb, pk, c0, w, tag):
                        yk = 'yacc%d_%d' % (t, ci)
                        S.do('vector', 'tensor_tensor', reads=[pk, yk], writes=[yk], out=yacc[:, t, c0:c0 + w], in0=yacc[:, t, c0:c0 + w], in1=pb[:], op=ALU.add)
                    ffn_specs += mk_specs(lambda t, kc, hb=hb: hb[:, t, kc, :], lambda t, hbk=hbk: hbk + '_%d' % t, NT, 4, wf2_v, ch4, None, None, epi_y)
                gemm_specs(ffn_specs)
                for t in range(NT):
                    S.dd('sync', y_own[t], yacc[:, t, :], reads=['yacc%d_%d' % (t, ci) for ci in range(4)], writes=[('y', t)], sem='yo')
                S.barrier()

        gst2.close()
        S.barrier()
        S.emit()
    return nc


def _rope_tables(pos):
    half = 32
    inv = (10000.0 ** (-(np.arange(half, dtype=np.float32)) * 2.0 / 64)).astype(np.float32)
    ang = pos.astype(np.float32)[:, None] * inv[None, :]
    return np.concatenate([np.cos(ang), np.sin(ang)], axis=1).astype(np.float32)


def make_in_maps(inp, with_samp=True):
    maps = []
    xp = np.asarray(inp['x_prompt'])
    xs = np.asarray(inp['x_sample'])
    for c in range(8):
        b, j = c // 4, c % 4
        tiles = [xp[b, (4 * i + j) * 128:(4 * i + j + 1) * 128] for i in range(8)]
        tiles.append(xs[16 * c:16 * c + 16].reshape(128, D))
        x_own = np.ascontiguousarray(np.stack(tiles))
        pos = [np.arange((4 * i + j) * 128, (4 * i + j + 1) * 128) for i in range(8)]
        pos.append(np.tile(2048 + np.arange(8), 16))
        cs_own = np.stack([_rope_tables(p) for p in pos])
        U = np.zeros((128, 128), np.float32)
        for s_ in range(128):
            for t_ in range(128):
                if s_ > t_ and s_ // 64 == t_ // 64:
                    U[s_, t_] = 1.0
        G2 = np.zeros((128, 2), np.float32)
        G2[:64, 0] = 1.0
        G2[64:, 1] = 1.0
        js = np.zeros((128, 4), np.float32)
        js[:, j] = 1.0
        Dm = np.zeros((2, 128, 128), np.float32); Pm = np.zeros((2, 128, 16), np.float32); MT = np.zeros((2, 128, 128), np.float32)
        U8 = np.zeros((128, 128), np.float32); G16 = np.zeros((128, 16), np.float32)
        for s_ in range(128):
            G16[s_, s_ // 8] = 1.0
            if s_ % 64 <= 31:
                Pm[0, s_, s_ // 64] = 1.0
            for t_ in range(128):
                if s_ // 64 == t_ // 64:
                    Tst = 1.0 if s_ <= t_ else 0.0
                    Pst = 1.0 if (s_ % 64) <= 31 else 0.0
                    Dm[0, s_, t_] = Tst - Pst
                    MT[0, s_, t_] = Tst
                if s_ // 8 == t_ // 8:
                    Dm[1, s_, t_] = 1.0 if s_ <= t_ else 0.0
                    MT[1, s_, t_] = 1.0 if s_ <= t_ else 0.0
                    U8[s_, t_] = 1.0 if s_ > t_ else 0.0
        import ml_dtypes
        bf = ml_dtypes.bfloat16
        NEGB = -30000.0
        kk = np.arange(128)[:, None]; qq = np.arange(128)[None, :]
        rep4 = lambda a: np.ascontiguousarray(np.tile(a, (1, 4)))
        cE = (np.arange(4096)[None, :] // 64 == np.arange(64)[:, None]).astype(np.float32)
        selb = np.stack([rep4(np.where((o - j) * 128 + kk <= qq, 0.0, NEGB)) for o in range(4)])
        winb = np.stack([rep4(np.where(((o - j) * 128 + kk <= qq) & ((o - j) * 128 + kk > qq - 512), 0.0, NEGB)) for o in range(-4, 4)])
        cmpb = np.zeros((8, 2, 128, 512), np.float32)
        for i in range(8):
            for ct in range(2):
                cidx = ct * 128 + kk
                cmpb[i, ct] = rep4(np.where((16 * cidx + 31 <= (4 * i + j) * 128 + qq) & (cidx < 255), 0.0, NEGB))
        BIG = 1e9
        visadd = np.zeros((NT, 128, 2, 64), np.float32)
        nn = np.arange(64)[None, :]
        for t in range(NT):
            if t < 8:
                qpos = ((4 * t + j) * 128 + np.arange(128))[:, None]
                visible = nn * 64 <= qpos
            else:
                qpos = (2048 + np.arange(128) % 8)[:, None]
                visible = (nn * 64 <= qpos) & (nn < 33)
            cur = qpos // 64
            forced = (nn == 0) | (nn == cur) | (nn == cur - 1)
            visadd[t, :, 0, :] = visible.astype(np.float32)
            visadd[t, :, 1, :] = np.where(visible, np.where(forced, BIG, 0.0), -BIG)
        def overlap(ncc, nss):
            cs = np.arange(ncc) * 16; ce = cs + 32; ss = np.arange(nss) * 64; se = ss + 64
            ov = np.clip(np.minimum(ce[:, None], se[None, :]) - np.maximum(cs[:, None], ss[None, :]), 0, None)
            return (ov / 16).astype(np.float32)
        ovp = np.zeros((256, 64), np.float32); ovp[:255] = overlap(255, 64)
        ovs = np.zeros((128, 64), np.float32); ovs[:127, :33] = overlap(127, 33)
        OH = np.zeros((17, 16, 128), np.float32)
        for s_ in range(16):
            OH[s_, s_, :] = 1.0
        OH[16, :, 127] = 1.0
        seqb = np.full((17, 512), NEGB, np.float32)
        for s_ in range(16):
            seqb[s_] = np.tile(np.where(np.arange(128) // 8 == s_, 0.0, NEGB), 4)
        bdc = rep4(np.where((kk // 8 == qq // 8) & (kk % 8 <= qq % 8), 0.0, NEGB))
        wb0 = rep4(np.where(kk > (qq % 8), 0.0, NEGB))
        m_s = {
            'pt_in': np.ascontiguousarray(np.asarray(inp['page_table'])[16 * c:16 * c + 16].reshape(1, 256).astype(np.int32)),
            'cache_cmp': np.asarray(inp['cache_cmp_kv'])[0].reshape(NPOOL * 128, 512),
            'cache_sel': np.asarray(inp['cache_sel_kv'])[0].reshape(NPOOL * 128, 512),
            'win_in': np.ascontiguousarray(np.asarray(inp['cache_win_kv'])[0, 16 * c:16 * c + 16].reshape(16, 512, 512)),
        }
        m = {
            'cE': cE.astype(bf), 'cselb': selb.astype(bf), 'cwinb': winb.astype(bf), 'ccmpb': cmpb.astype(bf), 'cvisadd': visadd,
            'covp': ovp.reshape(2, 128, 64).astype(bf), 'covs': ovs.astype(bf), 'cOH': OH.astype(bf), 'cseqb': seqb.astype(bf),
            'cbdc': bdc.astype(bf), 'cwb0': wb0.astype(bf),
            'cw1': np.ascontiguousarray(np.asarray(inp['cmp_w1'])[0]), 'cw2': np.ascontiguousarray(np.asarray(inp['cmp_w2'])[0]),
            'cpe': np.ascontiguousarray(np.asarray(inp['cmp_pe'])[0]),
            'w_pa': np.ascontiguousarray(np.asarray(inp['w_proj_a'])[0]), 'w_pb': np.ascontiguousarray(np.asarray(inp['w_proj_b'])[0]),
            'w_o': np.ascontiguousarray(np.asarray(inp['w_out'])[0]), 'w_f1': np.ascontiguousarray(np.asarray(inp['w_ff1'])[0]), 'w_f2': np.ascontiguousarray(np.asarray(inp['w_ff2'])[0]),
            'g2': np.ascontiguousarray(np.asarray(inp['norm2_g'])[0].reshape(KC, 128).T),
            'cDm': Dm, 'cPm': Pm, 'cMT': MT, 'cU8': U8, 'cG16': G16,
            'st_in': np.ascontiguousarray(np.asarray(inp['state_hgrn'])[0, 16 * c:16 * c + 16]),
            'hng': np.ascontiguousarray(np.asarray(inp['hgrn_norm_g'])[0].reshape(128, 1)),
            'x_seq': np.ascontiguousarray(xp[b].reshape(NSEQ, 128, D)),
            'cs_seq': np.ascontiguousarray(_rope_tables(np.arange(4096)).reshape(NSEQ, 128, 64)),
            'lbl': np.ascontiguousarray(np.asarray(inp['hgrn_lb_logits'])),
            'jsel': js, 'cU64': U, 'cG2': G2,
            'x_own': x_own,
            'cs_own': np.ascontiguousarray(cs_own),
            'w_in': np.ascontiguousarray(np.asarray(inp['w_in'])[0]),
            'g1': np.ascontiguousarray(np.asarray(inp['norm1_g'])[0].reshape(KC, 128).T),
            'qkg': np.ascontiguousarray(np.concatenate([np.asarray(inp['q_norm_g']), np.asarray(inp['k_norm_g'])[0]], axis=0)),
        }
        if with_samp:
            m.update(m_s)
        maps.append(m)
    return maps


_NC_CACHE = {}


def run(inp, phases=('own', 'seq', 'hgrn', 'att', 'attp', 'atts', 'fin')):
    key = tuple(phases)
    if key not in _NC_CACHE:
        _NC_CACHE[key] = build(phases)
    nc = _NC_CACHE[key]
    maps = make_in_maps(inp, with_samp=('atts' in phases))
    res = run_bass_kernel_spmd(nc, maps, core_ids=list(range(8)))
    return res.results


def assemble(res):
    B, T = 2, 4096
    p_cmp = np.zeros((1, B, T, 2, 4, 64), np.float32)
    p_sel = np.zeros((1, B, T, 2, 4, 64), np.float32)
    p_win = np.zeros((1, B, 512, 2, 4, 64), np.float32)
    s_cmp = np.zeros((1, 128, 8, 2, 4, 64), np.float32)
    s_sel = np.zeros((1, 128, 8, 2, 4, 64), np.float32)
    y_p = np.zeros((B, T, D), np.float32)
    y_s = np.zeros((128, 8, D), np.float32)
    p_st = np.zeros((1, B, 8, 128, 128), np.float32)
    s_win = np.zeros((1, 128, 512, 2, 4, 64), np.float32)
    s_st = np.zeros((1, 128, 8, 128, 128), np.float32)
    for c in range(8):
        b, j = c // 4, c % 4
        r = res[c]
        rows = r['rows_own']
        for i in range(8):
            J = 4 * i + j
            p_cmp[0, b, J * 128:(J + 1) * 128] = rows[i, :, 0].reshape(128, 2, 4, 64)
            p_sel[0, b, J * 128:(J + 1) * 128] = rows[i, :, 1].reshape(128, 2, 4, 64)
            if J >= 28:
                p_win[0, b, (J - 28) * 128:(J - 27) * 128] = rows[i, :, 2].reshape(128, 2, 4, 64)
        s_cmp[0, 16 * c:16 * c + 16] = rows[8, :, 0].reshape(16, 8, 2, 4, 64)
        s_sel[0, 16 * c:16 * c + 16] = rows[8, :, 1].reshape(16, 8, 2, 4, 64)
        if j == 0 and 'p_state' in r:
            p_st[0, b] = r['p_state']
        if 'y_own' in r:
            for i in range(8):
                J = 4 * i + j
                y_p[b, J * 128:(J + 1) * 128] = r['y_own'][i]
            y_s[16 * c:16 * c + 16] = r['y_own'][8].reshape(16, 8, D)
        if 's_win' in r:
            s_win[0, 16 * c:16 * c + 16] = r['s_win'].reshape(16, 512, 2, 4, 64)
        if 's_state' in r:
            s_st[0, 16 * c:16 * c + 16] = r['s_state']
    return (y_p, y_s, p_cmp, p_sel, p_win, p_st, s_cmp, s_sel, s_win, s_st)


def kernel(**inp):
    res = run(inp)
    return assemble(res)
```

```python
import numpy as np
from contextlib import ExitStack
import concourse.bass as bass
import concourse.mybir as mybir
from concourse.bass_utils import run_bass_kernel_spmd

F32 = mybir.dt.float32
BF16 = mybir.dt.bfloat16
I32 = mybir.dt.int32
ALU = mybir.AluOpType
AF = mybir.ActivationFunctionType
AX = mybir.AxisListType

ENGS = ['tensor', 'vector', 'scalar', 'gpsimd', 'sync']
EPOCH = 24000

D = 2048
KC = 16
N_IN = 10800
NT = 9
NSEQ = 32
EPS = 1e-6
NPOOL = 2560
SIZES = (1024, 1536, 48, 1024, 1024, 1024, 1024, 2048, 2048)
OFFS = [0]
for _s in SIZES:
    OFFS.append(OFFS[-1] + _s)
O_Q, O_KV, O_GA, O_HQ, O_HF, O_HI, O_HG, O_A, O_B = OFFS[:9]


class Sched:
    def __init__(self, nc, stack):
        self.nc = nc
        self.stack = stack
        self.streams = {e: [] for e in ENGS}
        self.cnt = {e: 0 for e in ENGS}
        self.epoch = {e: 0 for e in ENGS}
        self.semh = {}
        self.seen = {e: {} for e in ENGS}
        self.lastw = {}
        self.readers = {}
        self.dma_tot = {}
        self.nsem = 0
        for e in ENGS:
            self._sem((e, 0))

    def _sem(self, key):
        if key not in self.semh:
            self.nsem += 1
            self.semh[key] = self.stack.enter_context(self.nc.semaphore("s%d" % self.nsem))
        return self.semh[key]

    def _deps(self, eng, reads, writes):
        evs = []
        for k in reads:
            if k in self.lastw:
                evs.append(self.lastw[k])
        for k in writes:
            if k in self.lastw:
                evs.append(self.lastw[k])
            evs.extend(self.readers.get(k, ()))
        need = {}
        for (sk, v) in evs:
            if sk[0] == 'd':
                v = self.dma_tot[sk]
            if self.seen[eng].get(sk, 0) >= v:
                continue
            if need.get(sk, 0) < v:
                need[sk] = v
        for sk, v in need.items():
            self.seen[eng][sk] = v
        return [(self._sem(sk), v) for sk, v in need.items()]

    def _record(self, ev, reads, writes):
        for k in writes:
            self.lastw[k] = ev
            self.readers[k] = []
        for k in reads:
            self.readers.setdefault(k, []).append(ev)

    def op(self, eng, fn, reads=(), writes=()):
        waits = self._deps(eng, reads, writes)
        if self.cnt[eng] >= EPOCH:
            self.epoch[eng] += 1
            self.cnt[eng] = 0
        sk = (eng, self.epoch[eng])
        self.cnt[eng] += 1
        ev = (sk, self.cnt[eng])
        if eng == 'tensor':
            self.seen[eng][sk] = self.cnt[eng]
        self.streams[eng].append((waits, fn, self._sem(sk), 1))
        self._record(ev, reads, writes)
        return ev

    def do(self, eng, method, reads=(), writes=(), **kw):
        return self.op(eng, lambda e: getattr(e, method)(**kw), reads, writes)

    def dd(self, q, out, in_, reads=(), writes=(), sem='d0'):
        return self.dma(q, lambda e: e.dma_start(out=out, in_=in_), reads, writes, sem)

    def dma(self, q, fn, reads=(), writes=(), sem='d0'):
        waits = self._deps(q, reads, writes)
        sk = ('d', sem)
        self.dma_tot[sk] = self.dma_tot.get(sk, 0) + 16
        ev = (sk, self.dma_tot[sk])
        self.streams[q].append((waits, fn, self._sem(sk), 16))
        self._record(ev, reads, writes)
        return ev

    def barrier(self):
        targets = []
        for e in ENGS:
            for ep in range(self.epoch[e] + 1):
                sk = (e, ep)
                v = self.cnt[e] if ep == self.epoch[e] else EPOCH
                if v > 0:
                    targets.append((sk, v))
        for sk, v in self.dma_tot.items():
            targets.append((sk, v))
        for e in ENGS:
            waits = []
            for sk, v in targets:
                if self.seen[e].get(sk, 0) < v:
                    self.seen[e][sk] = v
                    waits.append((self._sem(sk), v))
            if waits:
                self.streams[e].append((waits, None, None, 0))

    def emit(self):
        nc = self.nc
        with nc.Block() as block:
            def mk(ename):
                def body(eng):
                    for waits, fn, sem, inc in self.streams[ename]:
                        for (s, v) in waits:
                            eng.wait_ge(s, v)
                        if fn is not None:
                            fn(eng).then_inc(sem, inc)
                return body
            block.tensor(mk('tensor'))
            block.vector(mk('vector'))
            block.scalar(mk('scalar'))
            block.gpsimd(mk('gpsimd'))
            block.sync(mk('sync'))


class Ctx:
    pass


def chunk_list():
    ch = []
    def seg(o, n, f):
        c = 0
        while c < n:
            w = min(512, n - c)
            ch.append((o + c, w, f))
            c += w
    seg(O_Q, 1024, AF.Copy)
    seg(O_KV, 1536, AF.Copy)
    seg(O_GA, 48, AF.Sigmoid)
    seg(O_HQ, 1024, AF.Silu)
    seg(O_HF, 1024, AF.Sigmoid)
    seg(O_HI, 1024, AF.Copy)
    seg(O_HG, 1024, AF.Silu)
    seg(O_A, 2048, AF.Sigmoid)
    seg(O_B, 2048, AF.Sigmoid)
    return ch


def build(phases=('own', 'seq', 'hgrn', 'att', 'attp', 'atts', 'fin'), debug=False):
    nc = bass.Bass("TRN2", target_bir_lowering=False)
    C = Ctx()
    C.debug = debug
    din = lambda n, s, dt=F32: nc.dram_tensor(n, list(s), dt, kind="ExternalInput").ap()
    dout = lambda n, s, dt=F32: nc.dram_tensor(n, list(s), dt, kind="ExternalOutput").ap()
    dscr = lambda n, s, dt=F32: nc.dram_tensor(n, list(s), dt, kind="Internal").ap()
    x_own = din("x_own", [NT, 128, D])
    cs_own = din("cs_own", [NT, 128, 64])
    x_seq = din("x_seq", [NSEQ, 128, D])
    cs_seq = din("cs_seq", [NSEQ, 128, 64])
    w_in = din("w_in", [D, N_IN])
    g1 = din("g1", [128, KC])
    qkg = din("qkg", [4, 64])
    lbl = din("lbl", [2, 1024])
    jsel = din("jsel", [128, 4])
    cU64 = din("cU64", [128, 128])
    cG2 = din("cG2", [128, 2])
    cDm = din("cDm", [2, 128, 128])
    cPm = din("cPm", [2, 128, 16])
    cMT = din("cMT", [2, 128, 128])
    cU8 = din("cU8", [128, 128])
    cG16 = din("cG16", [128, 16])
    st_in = din("st_in", [16, 8, 128, 128])
    hng = din("hng", [128, 1])
    rows_own = dout("rows_own", [NT, 128, 3, 512])
    s_state = dout("s_state", [16, 8, 128, 128])
    obT_scr = dscr("obT_scr", [NT, 128, 8, 128], BF16)
    oaT_scr = dscr("oaT_scr", [NT, 128, 8, 128], BF16)
    mix_scr = dscr("mix_scr", [NT, 128, D])
    w_pa = din("w_pa", [1024, D]); w_pb = din("w_pb", [1024, D]); w_o = din("w_o", [D, D])
    g2 = din("g2", [128, KC]); w_f1 = din("w_f1", [D, 4 * D]); w_f2 = din("w_f2", [4 * D, D])
    y_own = dout("y_own", [NT, 128, D])
    cE = din("cE", [64, 4096], BF16)
    cselb = din("cselb", [4, 128, 512], BF16)
    cwinb = din("cwinb", [8, 128, 512], BF16)
    ccmpb = din("ccmpb", [8, 2, 128, 512], BF16)
    cvisadd = din("cvisadd", [NT, 128, 2, 64])
    covp = din("covp", [2, 128, 64], BF16)
    covs = din("covs", [128, 64], BF16)
    cOH = din("cOH", [17, 16, 128], BF16)
    cseqb = din("cseqb", [17, 512], BF16)
    cbdc = din("cbdc", [128, 512], BF16)
    cwb0 = din("cwb0", [128, 512], BF16)
    cw1 = din("cw1", [2, 2048, 64]); cw2 = din("cw2", [2, 64, 64]); cpe = din("cpe", [2, 32, 64])
    if 'atts' in phases:
        pt_in = din("pt_in", [1, 256], I32)
        cache_cmp = din("cache_cmp", [NPOOL * 128, 512])
        cache_sel = din("cache_sel", [NPOOL * 128, 512])
        win_in = din("win_in", [16, 512, 512])
        s_win = dout("s_win", [16, 512, 512])
    p_state = dout("p_state", [8, 128, 128])
    zs = dscr("zs", [NT, 128, N_IN])
    zseq = dscr("zseq", [NSEQ, 128, 3584])
    qT_scr = dscr("qT_scr", [NT, 64, 2048], BF16)
    kT_scr = dscr("kT_scr", [4, 64, 4, 4096], BF16)
    v_scr = dscr("v_scr", [NSEQ, 128, 2, 4, 65], BF16)
    sown_scr = dscr("sown_scr", [128, 16, 1024], BF16)

    with ExitStack() as st:
        S = Sched(nc, st)
        sb = lambda name, shape, dt=F32: st.enter_context(nc.sbuf_tensor(name, list(shape), dt))
        ps = lambda name, shape, dt=F32: st.enter_context(nc.psum_tensor(name, list(shape), dt))
        pbank = [ps("pb%d" % i, [128, 512]) for i in range(8)]

        def dbg(name, ap, shape, dt, reads):
            if not C.debug:
                return
            o = nc.dram_tensor("dbg_" + name, list(shape), dt, kind="ExternalOutput").ap()
            S.dd('sync', o, ap, reads=reads, sem='dbg')

        C.pb_i = 0
        C.pb_n = 8

        def next_bank():
            i = C.pb_i % C.pb_n
            C.pb_i = (i + 1) % C.pb_n
            return pbank[i], 'pb%d' % i

        ident = sb("ident", [128, 128])
        S.do('gpsimd', 'memset', writes=['ident'], ap=ident[:], constant=1.0)
        S.do('gpsimd', 'affine_select', reads=['ident'], writes=['ident'], out=ident[:], in_=ident[:], pattern=[[-1, 128]],
             compare_op=ALU.is_equal, fill=0.0, base=0, channel_multiplier=1)
        g1t = sb("g1t", [128, KC])
        S.dd('sync', g1t[:], g1, writes=['g1t'], sem='c')
        qkg_t = sb("qkg_t", [128, 4, 64])
        S.dd('sync', qkg_t[:], qkg.rearrange("(o a) d -> o a d", o=1).to_broadcast([128, 4, 64]), writes=['qkg_t'], sem='c')
        jsel_t = sb("jsel_t", [128, 4])
        S.dd('sync', jsel_t[:], jsel, writes=['jsel_t'], sem='c')
        U64 = sb("U64", [128, 128])
        S.dd('sync', U64[:], cU64, writes=['U64'], sem='c')
        G2 = sb("G2", [128, 2])
        S.dd('sync', G2[:], cG2, writes=['G2'], sem='c')
        C.gb_n = 0
        C.pend_epi = None
        C.cast_engs = ['gpsimd', 'vector']

        def alloc_gemm(stack):
            C.gb_n += 1
            u = "_%d" % C.gb_n
            al = lambda name, shape, dt=F32: stack.enter_context(nc.sbuf_tensor(name + u, list(shape), dt))
            C.wst = [al("wst%d" % i, [128, 2, 512]) for i in range(6)]
            C.cast_i = 0
            C.wbf = [al("wbf%d" % i, [128, KC, 512], BF16) for i in range(2)]
            C.zst = [al("zst%d" % i, [128, 512]) for i in range(4)]
            C.wst_i = 0
            C.wbf_i = 0
            C.zst_i = 0

        gst = ExitStack()
        alloc_gemm(gst)

        def make_xT(x_dram, t0, nt, xT, ssq, rstd, pfx, xst, junk):
            S.do('vector', 'memset', writes=[pfx + 'ssq'], ap=ssq[:, 0:nt], constant=0.0)
            for t in range(nt):
                xb = xst[t % 2]
                xk = 'xst%d' % (t % 2)
                S.dd('sync', xb[:], x_dram[t0 + t], writes=[xk], sem='x%d' % (t % 2))
                S.do('scalar', 'activation', reads=[xk, pfx + 'ssq'], writes=['junk', pfx + 'ssq'], out=junk[:], in_=xb[:], func=AF.Square, accum_out=ssq[:, t:t + 1])
                for k4 in range(4):
                    pb, pk = next_bank()
                    for kk in range(4):
                        kc = k4 * 4 + kk
                        S.do('tensor', 'transpose', reads=[xk, 'ident'], writes=[pk], out=pb[:, kk * 128:(kk + 1) * 128], in_=xb[:, kc * 128:(kc + 1) * 128], identity=ident[:])
                    S.do('vector', 'tensor_copy', reads=[pk], writes=[pfx + 'xT%d' % t], out=xT[:, t, k4 * 4:(k4 + 1) * 4, :], in_=pb[:].rearrange("p (a b) -> p a b", a=4))
            S.do('scalar', 'activation', reads=[pfx + 'ssq'], writes=[pfx + 'rstd'], out=rstd[:, 0:nt], in_=ssq[:, 0:nt], func=AF.Sqrt, scale=1.0 / D, bias=EPS)
            S.do('vector', 'reciprocal', reads=[pfx + 'rstd'], writes=[pfx + 'rstd'], out=rstd[:, 0:nt], in_=rstd[:, 0:nt])

        def gemm_specs(specs):
            def load(sp):
                c0, w, nkc, w_v, gain, gain_key = sp['c0'], sp['w'], sp['nkc'], sp['w_v'], sp['gain'], sp['gain_key']
                bi = C.wbf_i
                C.wbf_i = (C.wbf_i + 1) % 2
                wb = C.wbf[bi]
                wk = 'wbf%d' % bi
                for q in range(nkc // 2):
                    si = C.wst_i
                    C.wst_i = (C.wst_i + 1) % 6
                    ws = C.wst[si]
                    S.dd('sync', ws[:, :, 0:w], w_v[:, q * 2:(q + 1) * 2, c0:c0 + w], writes=['wst%d' % si], sem='w%d' % si)
                    ceng = C.cast_engs[C.cast_i % len(C.cast_engs)]
                    C.cast_i += 1
                    if gain is not None:
                        S.do(ceng, 'tensor_tensor', reads=['wst%d' % si, gain_key], writes=[wk], out=wb[:, q * 2:(q + 1) * 2, 0:w], in0=ws[:, :, 0:w],
                             in1=gain[:, q * 2:(q + 1) * 2].rearrange("p (a o) -> p a o", o=1).to_broadcast([128, 2, w]), op=ALU.mult)
                    else:
                        S.do(ceng, 'tensor_copy', reads=['wst%d' % si], writes=[wk], out=wb[:, q * 2:(q + 1) * 2, 0:w], in_=ws[:, :, 0:w])
                return wb, wk
            nxt = load(specs[0])
            for i, sp in enumerate(specs):
                wb, wk = nxt
                if i + 1 < len(specs):
                    nxt = load(specs[i + 1])
                w, nkc = sp['w'], sp['nkc']
                for t in range(sp['nt']):
                    pb, pk = next_bank()
                    for kc in range(nkc):
                        S.do('tensor', 'matmul', reads=[sp['lhs_key'](t), wk], writes=[pk], out=pb[:, 0:w], lhsT=sp['lhs'](t, kc), rhs=wb[:, kc, 0:w], start=(kc == 0), stop=(kc == nkc - 1))
                    if C.pend_epi is not None:
                        C.pend_epi()
                    C.pend_epi = (lambda sp=sp, t=t, pb=pb, pk=pk, w=w: sp['epi'](sp['ci'], t, pb, pk, sp['c0'], w, sp['tag']))
            if C.pend_epi is not None:
                C.pend_epi()
                C.pend_epi = None

        def mk_specs(lhs, lhs_key, nt, nkc, w_v, chunks, gain, gain_key, epi):
            return [dict(lhs=lhs, lhs_key=lhs_key, nt=nt, nkc=nkc, w_v=w_v, c0=c0, w=w, tag=tag, gain=gain, gain_key=gain_key, epi=epi, ci=ci)
                    for ci, (c0, w, tag) in enumerate(chunks)]

        def gemm(lhs, lhs_key, nt, nkc, w_v, chunks, gain, gain_key, epi):
            gemm_specs(mk_specs(lhs, lhs_key, nt, nkc, w_v, chunks, gain, gain_key, epi))

        def epi_act_store(dst, rstd, rkey, dkey):
            def epi(ci, t, pb, pk, c0, w, tag):
                func, d0 = tag
                zi = C.zst_i
                C.zst_i = (C.zst_i + 1) % 4
                zb = C.zst[zi]
                S.do('scalar', 'activation', reads=[pk, rkey], writes=['zst%d' % zi], out=zb[:, 0:w], in_=pb[:, 0:w], func=func, scale=rstd[:, t:t + 1])
                S.dd('sync', dst(t)[:, d0:d0 + w], zb[:, 0:w], reads=['zst%d' % zi], writes=[(dkey, t, d0)], sem='zo%d' % zi)
            return epi

        lbst = ExitStack()
        lb_bc = lbst.enter_context(nc.sbuf_tensor("lb_bc", [128, 1024], F32))
        oml_bc = lbst.enter_context(nc.sbuf_tensor("oml_bc", [128, 1024], F32))
        with ExitStack() as ph:
            lraw = ph.enter_context(nc.sbuf_tensor("lraw", [128, 2, 1024], F32))
            S.dd('sync', lraw[:], lbl.rearrange("(o a) d -> o a d", o=1).to_broadcast([128, 2, 1024]), writes=['lraw'], sem='c')
            S.do('vector', 'tensor_tensor', reads=['lraw'], writes=['lb_bc'], out=lb_bc[:], in0=lraw[:, 0, :], in1=lraw[:, 1, :], op=ALU.subtract)
            S.do('scalar', 'activation', reads=['lb_bc'], writes=['lb_bc'], out=lb_bc[:], in_=lb_bc[:], func=AF.Sigmoid)
            S.do('vector', 'tensor_scalar', reads=['lb_bc'], writes=['oml_bc'], out=oml_bc[:], in0=lb_bc[:], scalar1=-1.0, scalar2=1.0, op0=ALU.mult, op1=ALU.add)
            S.barrier()

        w_v = w_in.rearrange("(kc p) n -> p kc n", p=128)

        C.nr_n = 0

        def alloc_nr(psb):
            C.nr_n += 1
            u = "_%d" % C.nr_n
            C.tmpa = psb("tmpa" + u, [128, 1024]); C.tmpb = psb("tmpb" + u, [128, 1024])
            C.hss = psb("hss" + u, [128, 32]); C.hrs = psb("hrs" + u, [128, 32])
            C.tmpa2 = psb("tmpa2" + u, [128, 768]); C.tmpb2 = psb("tmpb2" + u, [128, 384])
            C.hss2 = psb("hss2" + u, [128, 12]); C.hrs2 = psb("hrs2" + u, [128, 12])
            C.cst = [psb("cst%d" % i + u, [128, 64]) for i in range(2)]
            C.tab = [psb("tab%d" % i + u, [128, 4, 4, 32]) for i in range(2)]
            C.rowb = [psb("rowb%d" % i + u, [128, 3, 512]) for i in range(2)]

        def norm_rope(src, dst, H, tb, tk, br, scale, rk, wk):
            tmpa, tmpb, hss, hrs = C.tmpa, C.tmpb, C.hss, C.hrs
            n = H * 64
            sq = tmpa[:, 0:n].rearrange("p (h d) -> p h d", d=64)
            S.do('vector', 'tensor_tensor', reads=rk, writes=['tmpa'], out=sq, in0=src, in1=src, op=ALU.mult)
            S.do('vector', 'tensor_reduce', reads=['tmpa'], writes=['hss'], out=hss[:, 0:H], in_=sq, axis=AX.X, op=ALU.add)
            S.do('scalar', 'activation', reads=['hss'], writes=['hrs'], out=hrs[:, 0:H], in_=hss[:, 0:H], func=AF.Sqrt, scale=1.0 / 64, bias=EPS)
            S.do('vector', 'reciprocal', reads=['hrs'], writes=['hrs'], out=hrs[:, 0:H], in_=hrs[:, 0:H])
            if scale != 1.0:
                S.do('vector', 'tensor_scalar', reads=['hrs'], writes=['hrs'], out=hrs[:, 0:H], in0=hrs[:, 0:H], scalar1=scale, scalar2=None, op0=ALU.mult)
            x1 = src[:, :, 0:32]
            x2 = src[:, :, 32:64]
            T = lambda k: tb[:, br, k, :].rearrange("p (o d) -> p o d", o=1).to_broadcast([128, H, 32])
            a = tmpa[:, 0:H * 32].rearrange("p (h d) -> p h d", d=32)
            b = tmpb[:, 0:H * 32].rearrange("p (h d) -> p h d", d=32)
            S.do('vector', 'tensor_tensor', reads=rk + [tk], writes=['tmpa'], out=a, in0=x1, in1=T(0), op=ALU.mult)
            S.do('vector', 'tensor_tensor', reads=rk + [tk], writes=['tmpb'], out=b, in0=x2, in1=T(1), op=ALU.mult)
            S.do('vector', 'tensor_tensor', reads=['tmpa', 'tmpb'], writes=wk, out=dst[:, :, 0:32], in0=a, in1=b, op=ALU.subtract)
            S.do('vector', 'tensor_tensor', reads=rk + [tk], writes=['tmpa'], out=a, in0=x2, in1=T(2), op=ALU.mult)
            S.do('vector', 'tensor_tensor', reads=rk + [tk], writes=['tmpb'], out=b, in0=x1, in1=T(3), op=ALU.mult)
            S.do('vector', 'tensor_tensor', reads=['tmpa', 'tmpb'], writes=wk, out=dst[:, :, 32:64], in0=a, in1=b, op=ALU.add)
            S.do('vector', 'tensor_tensor', reads=wk + ['hrs'], writes=wk, out=dst, in0=dst, in1=hrs[:, 0:H].rearrange("p (h o) -> p h o", o=1).to_broadcast([128, H, 64]), op=ALU.mult)

        def make_tables(cs_dram, T, par):
            cb = C.cst[par]
            tb = C.tab[par]
            S.dd('sync', cb[:], cs_dram[T], writes=['cst%d' % par], sem='cs%d' % par)
            cosb = cb[:, 0:32].rearrange("p (o d) -> p o d", o=1).to_broadcast([128, 4, 32])
            sinb = cb[:, 32:64].rearrange("p (o d) -> p o d", o=1).to_broadcast([128, 4, 32])
            for k, (tr, gs) in enumerate([(cosb, 0), (sinb, 32), (cosb, 32), (sinb, 0)]):
                S.do('gpsimd', 'tensor_tensor', reads=['cst%d' % par, 'qkg_t'], writes=['tab%d' % par], out=tb[:, :, k, :], in0=qkg_t[:, :, gs:gs + 32], in1=tr, op=ALU.mult)
            return tb, 'tab%d' % par

        def kv_rows(zb, zk, base, tb, tk, rb, rbk):
            odd = rbk.startswith('rowb1')
            tmpa, tmpb, hss, hrs = (C.tmpa2, C.tmpb2, C.hss2, C.hrs2) if odd else (C.tmpa, C.tmpb, C.hss, C.hrs)
            ka, kb, khs, khr = ('tmpa2', 'tmpb2', 'hss2', 'hrs2') if odd else ('tmpa', 'tmpb', 'hss', 'hrs')
            zv = zb[:, base:base + 1536].rearrange("p (b x) -> p b x", b=3)
            src = zv[:, :, 0:256].rearrange("p b (h d) -> p b h d", d=64)
            dst = rb[:, :, 0:256].rearrange("p b (h d) -> p b h d", d=64)
            sq = tmpa[:, 0:768].rearrange("p (b h d) -> p b h d", b=3, d=64)
            h12 = lambda t_: t_[:, 0:12].rearrange("p (b h) -> p b h", b=3)
            S.do('vector', 'tensor_tensor', reads=[zk], writes=[ka], out=sq, in0=src, in1=src, op=ALU.mult)
            S.do('vector', 'tensor_reduce', reads=[ka], writes=[khs], out=h12(hss), in_=sq, axis=AX.X, op=ALU.add)
            S.do('scalar', 'activation', reads=[khs], writes=[khr], out=hrs[:, 0:12], in_=hss[:, 0:12], func=AF.Sqrt, scale=1.0 / 64, bias=EPS)
            S.do('vector', 'reciprocal', reads=[khr], writes=[khr], out=hrs[:, 0:12], in_=hrs[:, 0:12])
            x1 = src[:, :, :, 0:32]
            x2 = src[:, :, :, 32:64]
            T = lambda k: tb[:, 1:4, k, :].rearrange("p b (o d) -> p b o d", o=1).to_broadcast([128, 3, 4, 32])
            a = tmpa[:, 0:384].rearrange("p (b h d) -> p b h d", b=3, d=32)
            b_ = tmpb[:, 0:384].rearrange("p (b h d) -> p b h d", b=3, d=32)
            S.do('vector', 'tensor_tensor', reads=[zk, tk], writes=[ka], out=a, in0=x1, in1=T(0), op=ALU.mult)
            S.do('vector', 'tensor_tensor', reads=[zk, tk], writes=[kb], out=b_, in0=x2, in1=T(1), op=ALU.mult)
            S.do('vector', 'tensor_tensor', reads=[ka, kb], writes=[rbk], out=dst[:, :, :, 0:32], in0=a, in1=b_, op=ALU.subtract)
            S.do('vector', 'tensor_tensor', reads=[zk, tk], writes=[ka], out=a, in0=x2, in1=T(2), op=ALU.mult)
            S.do('vector', 'tensor_tensor', reads=[zk, tk], writes=[kb], out=b_, in0=x1, in1=T(3), op=ALU.mult)
            S.do('vector', 'tensor_tensor', reads=[ka, kb], writes=[rbk], out=dst[:, :, :, 32:64], in0=a, in1=b_, op=ALU.add)
            S.do('vector', 'tensor_tensor', reads=[rbk, khr], writes=[rbk], out=dst, in0=dst, in1=h12(hrs).rearrange("p b (h o) -> p b h o", o=1).to_broadcast([128, 3, 4, 64]), op=ALU.mult)
            S.do('gpsimd', 'tensor_copy', reads=[zk], writes=[rbk], out=rb[:, :, 256:512], in_=zv[:, :, 256:512])

        if 'own' in phases:
            with ExitStack() as ph:
                psb = lambda name, shape, dt=F32: ph.enter_context(nc.sbuf_tensor(name, list(shape), dt))
                xT = psb("xT", [128, NT, KC, 128], BF16)
                ssq = psb("ssq", [128, NT])
                rstd = psb("rstd", [128, NT])
                junk = psb("junk", [128, D])
                xst = [psb("xst%d" % i, [128, D]) for i in range(2)]
                make_xT(x_own, 0, NT, xT, ssq, rstd, 'o', xst, junk)
                chunks = [(c0, w, (f, c0)) for (c0, w, f) in chunk_list()]
                gemm(lambda t, kc: xT[:, t, kc, :], lambda t: 'oxT%d' % t, NT, KC, w_v, chunks, g1t, 'g1t',
                     epi_act_store(lambda t: zs[t], rstd, 'orstd', 'zs'))
                S.barrier()
            with ExitStack() as ph:
                psb = lambda name, shape, dt=F32: ph.enter_context(nc.sbuf_tensor(name, list(shape), dt))
                alloc_nr(psb)
                zq = [psb("zq%d" % i, [128, 2560]) for i in range(2)]
                qf = psb("qf", [128, 1024])
                qTb = [psb("qTb%d" % i, [64, 2048], BF16) for i in range(2)]
                zs_keys = [('zs', None, c[0]) for c in chunk_list()[0:5]]
                for t in range(NT):
                    par = t % 2
                    zb, zk = zq[par], 'zq%d' % par
                    rb, rbk = C.rowb[par], 'rowb%d' % par
                    qb, qbk = qTb[par], 'qTb%d' % par
                    S.dd('sync', zb[:], zs[t, :, 0:2560], reads=[('zs', t, k[2]) for k in zs_keys], writes=[zk], sem='zq%d' % par)
                    tb, tk = make_tables(cs_own, t, par)
                    norm_rope(zb[:, 0:1024].rearrange("p (h d) -> p h d", d=64), qf[:].rearrange("p (h d) -> p h d", d=64), 16, tb, tk, 0, 0.125, [zk], ['qf'])
                    for h4 in range(4):
                        pb, pk = next_bank()
                        for hh in range(4):
                            h = h4 * 4 + hh
                            S.do('tensor', 'transpose', reads=['qf', 'ident'], writes=[pk], out=pb[0:64, hh * 128:(hh + 1) * 128], in_=qf[:, h * 64:(h + 1) * 64], identity=ident[:])
                        S.do('scalar', 'activation', reads=[pk], writes=[qbk], out=qb[:, h4 * 512:(h4 + 1) * 512], in_=pb[0:64, :], func=AF.Copy)
                    S.dd('sync', qT_scr[t], qb[:], reads=[qbk], writes=[('qT', t)], sem='qo')
                    kv_rows(zb, zk, 1024, tb, tk, rb, rbk)
                    S.dd('sync', rows_own[t], rb[:], reads=[rbk], writes=[('rows_own', t)], sem='ro')
                S.barrier()

        if 'seq' in phases:
            seq_chunks = []
            for (c0, w, f) in chunk_list():
                if O_KV <= c0 < O_KV + 1536:
                    seq_chunks.append((c0, w, (f, c0 - O_KV)))
                elif O_HF <= c0 < O_HF + 1024:
                    seq_chunks.append((c0, w, (f, 1536 + c0 - O_HF)))
                elif O_HI <= c0 < O_HI + 1024:
                    seq_chunks.append((c0, w, (f, 2560 + c0 - O_HI)))
            with ExitStack() as ph:
                psb = lambda name, shape, dt=F32: ph.enter_context(nc.sbuf_tensor(name, list(shape), dt))
                xT = psb("sxT", [128, 16, KC, 128], BF16)
                ssq = psb("sssq", [128, 16])
                rstd = psb("srstd", [128, 16])
                junk = psb("sjunk", [128, D])
                xst = [psb("sxst%d" % i, [128, D]) for i in range(2)]
                for half in range(2):
                    make_xT(x_seq, half * 16, 16, xT, ssq, rstd, 's', xst, junk)
                    gemm(lambda t, kc: xT[:, t, kc, :], lambda t: 'sxT%d' % t, 16, KC, w_v, seq_chunks, g1t, 'g1t',
                         epi_act_store(lambda t, half=half: zseq[half * 16 + t], rstd, 'srstd', 'zseq%d' % half))
                S.barrier()
            with ExitStack() as ph:
                psb = lambda name, shape, dt=F32: ph.enter_context(nc.sbuf_tensor(name, list(shape), dt))
                alloc_nr(psb)
                zsq = [psb("zsq%d" % i, [128, 3584]) for i in range(2)]
                ktb = [psb("ktb%d" % i, [64, 4, 4, 128], BF16) for i in range(2)]
                vab = [psb("vab%d" % i, [128, 2, 4, 65], BF16) for i in range(2)]
                Sst = psb("Sst", [128, 8, 128])
                Sacc = [psb("Sacc%d" % i, [128, 2, 1024], BF16) for i in range(2)]
                ff2_ = [psb("ff%d" % i, [128, 1024]) for i in range(2)]
                logf2_ = [psb("logf%d" % i, [128, 1024]) for i in range(2)]
                eR2_ = [psb("eR%d" % i, [128, 1024]) for i in range(2)]
                khat2_ = [psb("khat%d" % i, [128, 1024], BF16) for i in range(2)]
                hvb2_ = [psb("hvb%d" % i, [128, 1024], BF16) for i in range(2)]
                ea2_ = [psb("ea%d" % i, [128, 8, 2]) for i in range(2)]
                stmp = psb("stmp", [128, 1024])
                SSTK = ['Sst%d' % h for h in range(8)]
                S.do('vector', 'memset', writes=SSTK, ap=Sst[:], constant=0.0)
                for i in range(2):
                    S.do('gpsimd', 'memset', writes=['vab%d' % i], ap=vab[i][:], constant=1.0)
                for T in range(NSEQ):
                    par = T % 2
                    zb, zk = zsq[par], 'zsq%d' % par
                    rb, rbk = C.rowb[par], 'rowb%d' % par
                    S.dd('sync', zb[:], zseq[T], reads=[('zseq%d' % (T // 16), T % 16, d0) for (_, _, (_, d0)) in seq_chunks], writes=[zk], sem='zq%d' % par)
                    tb, tk = make_tables(cs_seq, T, par)
                    kv_rows(zb, zk, 0, tb, tk, rb, rbk)
                    kb, kbk = ktb[par], 'ktb%d' % par
                    for slot, (br, src_o) in enumerate(((0, 0), (0, 256), (1, 0), (2, 0))):
                        pb, pk = next_bank()
                        for g in range(4):
                            S.do('tensor', 'transpose', reads=[rbk, 'ident'], writes=[pk], out=pb[0:64, g * 128:(g + 1) * 128], in_=rb[:, br, src_o + g * 64:src_o + (g + 1) * 64], identity=ident[:])
                        S.do('scalar', 'activation', reads=[pk], writes=[kbk], out=kb[:, slot, :, :], in_=pb[0:64, :].rearrange("p (g k) -> p g k", g=4), func=AF.Copy)
                    for slot in range(4):
                        S.dd('sync', kT_scr[slot, :, :, T * 128:(T + 1) * 128], kb[:, slot, :, :], reads=[kbk], writes=[('kT', T, slot)], sem='ko')
                    vb, vbk = vab[par], 'vab%d' % par
                    for br in (1, 2):
                        S.do('gpsimd', 'tensor_copy', reads=[rbk], writes=[vbk], out=vb[:, br - 1, :, 0:64], in_=rb[:, br, 256:512].rearrange("p (g d) -> p g d", g=4))
                    S.dd('sync', v_scr[T], vb[:], reads=[vbk], writes=[('v', T)], sem='vo')
                    ff, logf, eR, khat, hvb, ea = ff2_[par], logf2_[par], eR2_[par], khat2_[par], hvb2_[par], ea2_[par]
                    kff, klogf, keR, kkhat, khvb, kea = 'ff%d' % par, 'logf%d' % par, 'eR%d' % par, 'khat%d' % par, 'hvb%d' % par, 'ea%d' % par
                    S.do('vector', 'tensor_tensor', reads=[zk, 'oml_bc'], writes=[kff], out=ff[:], in0=zb[:, 1536:2560], in1=oml_bc[:], op=ALU.mult)
                    S.do('vector', 'tensor_tensor', reads=[kff, 'lb_bc'], writes=[kff], out=ff[:], in0=ff[:], in1=lb_bc[:], op=ALU.add)
                    S.do('scalar', 'activation', reads=[kff], writes=[klogf], out=logf[:], in_=ff[:], func=AF.Ln)
                    S.do('gpsimd', 'tensor_scalar', reads=[kff], writes=[kff], out=ff[:], in0=ff[:], scalar1=-1.0, scalar2=1.0, op0=ALU.mult, op1=ALU.add)
                    for hh in range(2):
                        pb, pk = next_bank()
                        S.do('tensor', 'matmul', reads=['U64', klogf], writes=[pk], out=pb[:], lhsT=U64[:], rhs=logf[:, hh * 512:(hh + 1) * 512], start=True, stop=True)
                        S.do('scalar', 'activation', reads=[pk], writes=[keR], out=eR[:, hh * 512:(hh + 1) * 512], in_=pb[:], func=AF.Exp)
                    S.do('vector', 'tensor_tensor', reads=[kff, keR], writes=[kkhat], out=khat[:], in0=ff[:], in1=eR[:], op=ALU.mult)
                    S.do('gpsimd', 'tensor_copy', reads=[zk], writes=[khvb], out=hvb[:], in_=zb[:, 2560:3584])
                    pb, pk = next_bank()
                    for h in range(8):
                        S.do('tensor', 'matmul', reads=[klogf, 'G2'], writes=[pk], out=pb[:, h * 2:h * 2 + 2], lhsT=logf[:, h * 128:(h + 1) * 128], rhs=G2[:], start=True, stop=True)
                    S.do('scalar', 'activation', reads=[pk], writes=[kea], out=ea[:].rearrange("p h c -> p (h c)"), in_=pb[:, 0:16], func=AF.Exp)
                    for cc in range(2):
                        i_own, o = T // 4, T % 4
                        sa, sak = Sacc[i_own % 2], 'Sacc%d' % (i_own % 2)
                        if o == 0:
                            S.do('vector', 'tensor_scalar', reads=SSTK + ['jsel_t'], writes=[sak], out=sa[:, cc, :], in0=Sst[:].rearrange("p h d -> p (h d)"), scalar1=jsel_t[:, o:o + 1], scalar2=None, op0=ALU.mult)
                        else:
                            S.do('vector', 'tensor_scalar', reads=SSTK + ['jsel_t'], writes=['stmp'], out=stmp[:], in0=Sst[:].rearrange("p h d -> p (h d)"), scalar1=jsel_t[:, o:o + 1], scalar2=None, op0=ALU.mult)
                            S.do('vector', 'tensor_tensor', reads=['stmp', sak], writes=[sak], out=sa[:, cc, :], in0=sa[:, cc, :], in1=stmp[:], op=ALU.add)
                        for h4 in range(2):
                            pb, pk = next_bank()
                            for hh in range(4):
                                h = h4 * 4 + hh
                                S.do('tensor', 'matmul', reads=[kkhat, khvb], writes=[pk], out=pb[:, hh * 128:(hh + 1) * 128], lhsT=khat[cc * 64:(cc + 1) * 64, h * 128:(h + 1) * 128],
                                     rhs=hvb[cc * 64:(cc + 1) * 64, h * 128:(h + 1) * 128], start=True, stop=True)
                            for hh in range(4):
                                h = h4 * 4 + hh
                                S.do('vector', 'scalar_tensor_tensor', reads=['Sst%d' % h, kea, pk], writes=['Sst%d' % h], out=Sst[:, h, :], in0=Sst[:, h, :], scalar=ea[:, h, cc:cc + 1], in1=pb[:, hh * 128:(hh + 1) * 128], op0=ALU.mult, op1=ALU.add)
                    if T % 4 == 3:
                        sa, sak = Sacc[(T // 4) % 2], 'Sacc%d' % ((T // 4) % 2)
                        S.dd('sync', sown_scr[:, 2 * (T // 4):2 * (T // 4) + 2, :], sa[:], reads=[sak], writes=[('sown', T // 4)], sem='so')
                S.dd('sync', p_state.rearrange("h k v -> k h v"), Sst[:], reads=SSTK, writes=['p_state'], sem='po')
                S.barrier()


        if 'hgrn' in phases:
            with ExitStack() as ph:
                psb = lambda name, shape, dt=F32: ph.enter_context(nc.sbuf_tensor(name, list(shape), dt))
                Dm = psb("Dm", [128, 2, 128]); Pm = psb("Pm", [128, 2, 16]); MT = psb("MT", [128, 2, 128])
                U8 = psb("U8", [128, 128]); G16 = psb("G16", [128, 16]); hng_t = psb("hng_t", [128, 1])
                ones_f = psb("ones_f", [128, 128])
                S.dd('sync', Dm[:], cDm.rearrange("a p t -> p a t"), writes=['Dm'], sem='c')
                S.dd('sync', Pm[:], cPm.rearrange("a p t -> p a t"), writes=['Pm'], sem='c')
                S.dd('sync', MT[:], cMT.rearrange("a p t -> p a t"), writes=['MT'], sem='c')
                S.dd('sync', U8[:], cU8, writes=['U8'], sem='c')
                S.dd('sync', G16[:], cG16, writes=['G16'], sem='c')
                S.dd('sync', hng_t[:], hng, writes=['hng_t'], sem='c')
                S.do('gpsimd', 'memset', writes=['ones_f'], ap=ones_f[:], constant=1.0)
                zh = psb("zh", [128, 4096])
                logf = psb("hlogf", [128, 1024]); hk = psb("hhk", [128, 1024])
                ex = psb("hex", [128, 1024]); qh = psb("hqh", [128, 1024]); kh = psb("hkh", [128, 1024])
                qhT = psb("qhT", [128, 8, 128], BF16); khT = psb("khT", [128, 8, 128], BF16)
                vb = psb("hvb2", [128, 1024], BF16)
                attT = psb("attT", [128, 8, 128], BF16)
                ogT = psb("ogT", [128, 8, 128])
                Sp = psb("Sp", [128, 4, 8, 128], BF16)
                S0 = psb("S0", [128, 4, 8, 128])
                em = psb("hem", [128, 8, 16])
                oTs = psb("oTs", [128, 512]); sqT = psb("sqT", [128, 512]); rbc = psb("rbc", [128, 512])
                obT = psb("obT", [128, 8, 128], BF16)
                kmask = psb("kmask", [128, 1024], BF16)
                zh_keys = [c0 for (c0, w, f) in chunk_list() if O_HQ <= c0 < O_A]
                S0K = ['S0_%d_%d' % (g_, h_) for g_ in range(4) for h_ in range(8)]
                C.pb_n = 6
                for t in range(NT):
                    kind = 0 if t < 8 else 1
                    G = 2 if kind == 0 else 16
                    L = 128 // G
                    S.dd('sync', zh[:], zs[t, :, O_HQ:O_A], reads=[('zs', t, c0) for c0 in zh_keys], writes=['zh'], sem='zh')
                    hq = zh[:, 0:1024]; sf = zh[:, 1024:2048]; hv = zh[:, 2048:3072]; og = zh[:, 3072:4096]
                    S.do('vector', 'tensor_tensor', reads=['zh', 'oml_bc'], writes=['hk'], out=hk[:], in0=sf, in1=oml_bc[:], op=ALU.mult)
                    S.do('vector', 'tensor_tensor', reads=['hk', 'lb_bc'], writes=['hk'], out=hk[:], in0=hk[:], in1=lb_bc[:], op=ALU.add)
                    S.do('scalar', 'activation', reads=['hk'], writes=['hlogf'], out=logf[:], in_=hk[:], func=AF.Ln)
                    S.do('gpsimd', 'tensor_scalar', reads=['hk'], writes=['hk'], out=hk[:], in0=hk[:], scalar1=-1.0, scalar2=1.0, op0=ALU.mult, op1=ALU.add)
                    S.do('gpsimd', 'tensor_copy', reads=['zh'], writes=['hvb2'], out=vb[:], in_=hv)
                    for hh in range(2):
                        pb, pk = next_bank()
                        S.do('tensor', 'matmul', reads=['Dm', 'hlogf'], writes=[pk], out=pb[:], lhsT=Dm[:, kind, :], rhs=logf[:, hh * 512:(hh + 1) * 512], start=True, stop=True)
                        S.do('vector', 'tensor_scalar', reads=[pk], writes=['hex'], out=ex[:, hh * 512:(hh + 1) * 512], in0=pb[:], scalar1=40.0, scalar2=None, op0=ALU.min)
                        S.do('scalar', 'activation', reads=['hex'], writes=['hex'], out=ex[:, hh * 512:(hh + 1) * 512], in_=ex[:, hh * 512:(hh + 1) * 512], func=AF.Exp)
                        S.do('vector', 'tensor_tensor', reads=['hex', 'zh'], writes=['hqh'], out=qh[:, hh * 512:(hh + 1) * 512], in0=ex[:, hh * 512:(hh + 1) * 512], in1=hq[:, hh * 512:(hh + 1) * 512], op=ALU.mult)
                        S.do('vector', 'tensor_scalar', reads=[pk, 'hqh'], writes=['hex'], out=ex[:, hh * 512:(hh + 1) * 512], in0=pb[:], scalar1=-1.0, scalar2=40.0, op0=ALU.mult, op1=ALU.min)
                        S.do('scalar', 'activation', reads=['hex'], writes=['hex'], out=ex[:, hh * 512:(hh + 1) * 512], in_=ex[:, hh * 512:(hh + 1) * 512], func=AF.Exp)
                        S.do('vector', 'tensor_tensor', reads=['hex', 'hk'], writes=['hkh'], out=kh[:, hh * 512:(hh + 1) * 512], in0=ex[:, hh * 512:(hh + 1) * 512], in1=hk[:, hh * 512:(hh + 1) * 512], op=ALU.mult)
                    for (src, sk_, dst, dk_) in ((qh, 'hqh', qhT, 'qhT'), (kh, 'hkh', khT, 'khT'), (None, 'zh', ogT, 'ogT')):
                        for h4 in range(2):
                            pb, pk = next_bank()
                            for hh in range(4):
                                h = h4 * 4 + hh
                                in_ap = og[:, h * 128:(h + 1) * 128] if src is None else src[:, h * 128:(h + 1) * 128]
                                S.do('tensor', 'transpose', reads=[sk_, 'ident'], writes=[pk], out=pb[:, hh * 128:(hh + 1) * 128], in_=in_ap, identity=ident[:])
                            S.do('scalar', 'activation', reads=[pk], writes=[dk_], out=dst[:, h4 * 4:(h4 + 1) * 4, :], in_=pb[:].rearrange("p (a b) -> p a b", a=4), func=AF.Copy)
                    for h4 in range(2):
                        pb, pk = next_bank()
                        for hh in range(4):
                            h = h4 * 4 + hh
                            S.do('tensor', 'matmul', reads=['khT', 'qhT'], writes=[pk], out=pb[:, hh * 128:(hh + 1) * 128], lhsT=khT[:, h, :], rhs=qhT[:, h, :], start=True, stop=True)
                        S.do('vector', 'tensor_tensor', reads=[pk, 'MT'], writes=['attT'], out=attT[:, h4 * 4:(h4 + 1) * 4, :], in0=pb[:].rearrange("p (a b) -> p a b", a=4),
                             in1=MT[:, kind, :].rearrange("p (o t) -> p o t", o=1).to_broadcast([128, 4, 128]), op=ALU.mult)
                    pb, pk = next_bank()
                    for h in range(8):
                        S.do('tensor', 'matmul', reads=['hlogf', 'Pm'], writes=[pk], out=pb[:, h * 16:(h + 1) * 16], lhsT=logf[:, h * 128:(h + 1) * 128], rhs=Pm[:, kind, :], start=True, stop=True)
                    S.do('scalar', 'activation', reads=[pk], writes=['hem'], out=em[:].rearrange("p h g -> p (h g)"), in_=pb[:, 0:128], func=AF.Exp)
                    if kind == 1:
                        for hh in range(2):
                            pb, pk = next_bank()
                            S.do('tensor', 'matmul', reads=['U8', 'hlogf'], writes=[pk], out=pb[:], lhsT=U8[:], rhs=logf[:, hh * 512:(hh + 1) * 512], start=True, stop=True)
                            S.do('scalar', 'activation', reads=[pk], writes=['hex'], out=ex[:, hh * 512:(hh + 1) * 512], in_=pb[:], func=AF.Exp)
                        S.do('vector', 'tensor_tensor', reads=['hex', 'hk'], writes=['hkh'], out=kh[:], in0=ex[:], in1=hk[:], op=ALU.mult)
                        pb, pk = next_bank()
                        for h in range(8):
                            S.do('tensor', 'matmul', reads=['hlogf', 'G16'], writes=[pk], out=pb[:, h * 16:(h + 1) * 16], lhsT=logf[:, h * 128:(h + 1) * 128], rhs=G16[:], start=True, stop=True)
                        S.do('scalar', 'activation', reads=[pk], writes=['hem'], out=em[:].rearrange("p h g -> p (h g)"), in_=pb[:, 0:128], func=AF.Exp)
                    nb = 1 if kind == 0 else 4
                    gb = G // nb
                    oT_banks = [(pbank[6], 'pb6'), (pbank[7], 'pb7')]
                    for h4 in range(2):
                        pbo, pko = oT_banks[h4]
                        for hh in range(4):
                            h = h4 * 4 + hh
                            S.do('tensor', 'matmul', reads=['hvb2', 'attT'], writes=[pko], out=pbo[:, hh * 128:(hh + 1) * 128], lhsT=vb[:, h * 128:(h + 1) * 128], rhs=attT[:, h, :], start=(hh == 0), stop=False)
                    for b_ in range(nb):
                        if kind == 0:
                            S.dd('sync', Sp[:, 0:2, :, :].rearrange("p g h d -> p g (h d)"), sown_scr[:, 2 * t:2 * t + 2, :], reads=[('sown', t)], writes=['Sp'] + ['Sp_%d_%d' % (g_, h_) for g_ in range(4) for h_ in range(8)], sem='sp')
                            for g in range(2):
                                for h in range(8):
                                    S.do('vector', 'tensor_scalar', reads=['Sp', 'hem'], writes=['Sp_%d_%d' % (g, h)], out=Sp[:, g, h, :], in0=Sp[:, g, h, :], scalar1=em[:, h, g:g + 1], scalar2=None, op0=ALU.mult)
                        else:
                            S.dd('sync', S0[:].rearrange("p g h d -> p (g h) d"), st_in[b_ * 4:(b_ + 1) * 4].rearrange("g h k d -> k (g h) d"), reads=[], writes=S0K, sem='sp')
                            S.do('gpsimd', 'tensor_copy', reads=S0K, writes=['Sp'] + ['Sp_%d_%d' % (g_, h_) for g_ in range(4) for h_ in range(8)], out=Sp[:], in_=S0[:])
                        for gl in range(gb):
                            g = b_ * gb + gl
                            for h in range(8):
                                pbo, pko = oT_banks[h // 4]
                                hh = h % 4
                                S.do('tensor', 'matmul', reads=['Sp', 'Sp_%d_%d' % (gl, h), 'qhT'], writes=[pko], out=pbo[:, hh * 128 + g * L:hh * 128 + (g + 1) * L], lhsT=Sp[:, gl, h, :], rhs=qhT[:, h, g * L:(g + 1) * L],
                                     start=False, stop=(b_ == nb - 1 and gl == gb - 1))
                        if kind == 1:
                            for gl in range(gb):
                                g = b_ * gb + gl
                                S.do('vector', 'tensor_scalar', reads=['hkh', 'G16'], writes=['kmask'], out=kmask[:], in0=kh[:], scalar1=G16[:, g:g + 1], scalar2=None, op0=ALU.mult)
                                for h4 in range(2):
                                    pb, pk = next_bank()
                                    for hh in range(4):
                                        h = h4 * 4 + hh
                                        S.do('tensor', 'matmul', reads=['kmask', 'hvb2'], writes=[pk], out=pb[:, hh * 128:(hh + 1) * 128], lhsT=kmask[:, h * 128:(h + 1) * 128], rhs=vb[:, h * 128:(h + 1) * 128], start=True, stop=True)
                                    for hh in range(4):
                                        h = h4 * 4 + hh
                                        S.do('vector', 'scalar_tensor_tensor', reads=['S0_%d_%d' % (gl, h), 'hem', pk], writes=['S0_%d_%d' % (gl, h)], out=S0[:, gl, h, :], in0=S0[:, gl, h, :], scalar=em[:, h, g:g + 1], in1=pb[:, hh * 128:(hh + 1) * 128], op0=ALU.mult, op1=ALU.add)
                            S.dd('sync', s_state[b_ * 4:(b_ + 1) * 4].rearrange("g h k d -> k (g h) d"), S0[:].rearrange("p g h d -> p (g h) d"), reads=S0K, writes=[('s_state', b_)], sem='sso')
                    for h4 in range(2):
                        pbo, pko = oT_banks[h4]
                        S.do('scalar', 'activation', reads=[pko], writes=['oTs'], out=oTs[:], in_=pbo[:], func=AF.Copy)
                        S.do('vector', 'tensor_tensor', reads=['oTs'], writes=['sqT'], out=sqT[:], in0=oTs[:], in1=oTs[:], op=ALU.mult)
                        pb, pk = next_bank()
                        S.do('tensor', 'matmul', reads=['ones_f', 'sqT'], writes=[pk], out=pb[:], lhsT=ones_f[:], rhs=sqT[:], start=True, stop=True)
                        S.do('scalar', 'activation', reads=[pk], writes=['rbc'], out=rbc[:], in_=pb[:], func=AF.Sqrt, scale=1.0 / 128, bias=EPS)
                        S.do('vector', 'reciprocal', reads=['rbc'], writes=['rbc'], out=rbc[:], in_=rbc[:])
                        S.do('vector', 'tensor_tensor', reads=['rbc', 'oTs'], writes=['oTs'], out=oTs[:], in0=oTs[:], in1=rbc[:], op=ALU.mult)
                        S.do('vector', 'scalar_tensor_tensor', reads=['oTs', 'hng_t', 'ogT'], writes=['obT'], out=obT[:, h4 * 4:(h4 + 1) * 4, :].rearrange("p a b -> p (a b)"), in0=oTs[:], scalar=hng_t[:, 0:1],
                             in1=ogT[:, h4 * 4:(h4 + 1) * 4, :].rearrange("p a b -> p (a b)"), op0=ALU.mult, op1=ALU.mult)
                    S.dd('sync', obT_scr[t], obT[:], reads=['obT'], writes=[('obT', t)], sem='obo')
                    if t in (0, 3, 8):
                        dbg("obT%d" % t, obT[:], [128, 8, 128], BF16, ['obT'])
                S.barrier()
                C.pb_n = 8


        S.barrier()
        lbst.close()
        gst.close()
        if 'att' in phases:
            NEGB = -30000.0
            with ExitStack() as ph:
                psb = lambda name, shape, dt=F32: ph.enter_context(nc.sbuf_tensor(name, list(shape), dt))
                C.pb_n = 3
                ACC = [(pbank[3 + i], 'pb%d' % (3 + i)) for i in range(4)]
                MISC = (pbank[7], 'pb7')
                identb = psb("identb", [128, 128], BF16)
                S.do('vector', 'tensor_copy', reads=['ident'], writes=['identb'], out=identb[:], in_=ident[:])
                E = psb("E", [64, 4096], BF16)
                S.dd('sync', E[:], cE, writes=['E'], sem='c')
                selb = psb("selb", [128, 4, 512], BF16); winb = psb("winb", [128, 8, 512], BF16)
                S.dd('sync', selb[:], cselb.rearrange("a p c -> p a c"), writes=['selb'], sem='c')
                S.dd('sync', winb[:], cwinb.rearrange("a p c -> p a c"), writes=['winb'], sem='c')
                ovp = psb("ovp", [128, 2, 64], BF16); ovs = psb("ovs", [128, 64], BF16)
                S.dd('sync', ovp[:], covp.rearrange("a p c -> p a c"), writes=['ovp'], sem='c')
                S.dd('sync', ovs[:], covs, writes=['ovs'], sem='c')
                OH = psb("OH", [17, 16, 128], BF16); seqb = psb("seqb", [17, 512], BF16)
                S.dd('sync', OH[:], cOH, writes=['OH'], sem='c')
                S.dd('sync', seqb[:], cseqb, writes=['seqb'], sem='c')
                bdc = psb("bdc", [128, 512], BF16); wb0 = psb("wb0", [128, 512], BF16)
                S.dd('sync', bdc[:], cbdc, writes=['bdc'], sem='c')
                S.dd('sync', wb0[:], cwb0, writes=['wb0'], sem='c')
                w1b = psb("w1b", [64, 2, 32, 64], BF16); w2b = psb("w2b", [64, 2, 64], BF16); peb = psb("peb", [64, 2])
                with ExitStack() as ph2:
                    psb2 = lambda name, shape, dt=F32: ph2.enter_context(nc.sbuf_tensor(name, list(shape), dt))
                    w1f = psb2("w1f", [64, 2, 32, 64]); w2f = psb2("w2f", [64, 2, 64]); pef = psb2("pef", [64, 2, 32]); pebf = psb2("pebf", [64, 2, 32], BF16)
                    for kv in range(2):
                        S.dd('sync', w1f[:, kv], cw1[kv].rearrange("(l d) h -> d l h", d=64), writes=['w1f'], sem='c')
                        S.dma('sync', lambda e, kv=kv: e.dma_start(out=pef[:, kv], in_=cpe[kv].rearrange("l d -> d l"), allow_slow_non_contiguous=True), writes=['pef'], sem='c')
                    S.dd('sync', w2f[:], cw2.rearrange("a h d -> h a d"), writes=['w2f'], sem='c')
                    S.do('vector', 'tensor_copy', reads=['w1f'], writes=['w1b'], out=w1b[:], in_=w1f[:])
                    S.do('vector', 'tensor_copy', reads=['w2f'], writes=['w2b'], out=w2b[:], in_=w2f[:])
                    S.do('vector', 'tensor_copy', reads=['pef'], writes=['pebf'], out=pebf[:], in_=pef[:])
                    for kv in range(2):
                        pb, pk = next_bank()
                        for l in range(32):
                            S.do('tensor', 'matmul', reads=['w1b', 'pebf'], writes=[pk], out=pb[0:64, 0:1], lhsT=w1b[:, kv, l, :], rhs=pebf[:, kv, l:l + 1], start=(l == 0), stop=(l == 31))
                        S.do('vector', 'tensor_copy', reads=[pk], writes=['peb'], out=peb[:, kv:kv + 1], in_=pb[0:64, 0:1])
                    S.barrier()

                hsil = psb("hsil", [64, 256], BF16)

                def compress(XT, xkeys, n, kcT_dst, kkey, vc_dst, vkey):
                    for kv in range(2):
                        for g in range(4):
                            pb, pk = next_bank()
                            xt = XT(kv, g)
                            for l in range(32):
                                S.do('tensor', 'matmul', reads=xkeys + ['w1b'], writes=[pk], out=pb[0:64, 0:n], lhsT=w1b[:, kv, l, :], rhs=xt[:, l:l + 16 * (n - 1) + 1:16], start=(l == 0), stop=(l == 31))
                            S.do('scalar', 'activation', reads=[pk, 'peb'], writes=['hsil'], out=hsil[:, 0:n], in_=pb[0:64, 0:n], func=AF.Silu, bias=peb[:, kv:kv + 1])
                            pb2, pk2 = next_bank()
                            if kv == 0:
                                S.do('tensor', 'matmul', reads=['hsil', 'w2b'], writes=[pk2], out=pb2[0:64, 0:n], lhsT=w2b[:, 0, :], rhs=hsil[:, 0:n], start=True, stop=True)
                                S.do('vector', 'tensor_copy', reads=[pk2], writes=[kkey], out=kcT_dst(g), in_=pb2[0:64, 0:n])
                            else:
                                for ct in range((n + 127) // 128):
                                    m = min(128, n - ct * 128)
                                    S.do('tensor', 'matmul', reads=['hsil', 'w2b'], writes=[pk2], out=pb2[0:m, ct * 64:(ct + 1) * 64], lhsT=hsil[:, ct * 128:ct * 128 + m], rhs=w2b[:, 1, :], start=True, stop=True)
                                    S.do('vector', 'tensor_copy', reads=[pk2], writes=[vkey], out=vc_dst(g, ct, m), in_=pb2[0:m, ct * 64:(ct + 1) * 64])

                ptb = [psb("ptb%d" % i, [128, 512], BF16) for i in range(3)]
                C.pt_i = 0

                ptz = [psb("ptz%d" % i, [128, 512], BF16) for i in range(3)]
                for i in range(3):
                    S.do('gpsimd', 'memset', writes=['ptz%d' % i], ap=ptz[i][:], constant=0.0)
                C.ptz_i = 0

                def attend_g(qT, qkey, jobs, naccs, gs):
                    started = set()
                    last = {}
                    for ji, jb in enumerate(jobs):
                        for g in gs:
                            for (aid, _, _) in jb['V'](g):
                                last[(aid, g)] = ji

                    def stage1(ji, jb, g):
                        nk = jb['nk']
                        qs = jb.get('qs')
                        pb, pk = next_bank()
                        bl = jb['bias'](g)
                        if qs is None:
                            sel = lambda ap: ap
                            ob = pb[0:nk, :]
                        else:
                            sel = lambda ap: ap.rearrange("p (r q) -> p r q", r=4)[:, :, qs * 8:(qs + 1) * 8]
                            ob = pb[0:nk, 0:32].rearrange("p (r q) -> p r q", r=4)
                        S.do('tensor', 'matmul', reads=jb['keys'] + [qkey], writes=[pk], out=ob, lhsT=jb['KT'](g), rhs=sel(qT[:, g * 512:(g + 1) * 512]), start=True, stop=(len(bl) == 0))
                        for bi, (bl_l, bl_r, bkeys) in enumerate(bl):
                            S.do('tensor', 'matmul', reads=bkeys, writes=[pk], out=ob, lhsT=bl_l, rhs=sel(bl_r), start=False, stop=(bi == len(bl) - 1))
                        if qs is None:
                            pi = C.pt_i
                            C.pt_i = (C.pt_i + 1) % 3
                            pt, ptk = ptb[pi], 'ptb%d' % pi
                            S.do('scalar', 'activation', reads=[pk], writes=[ptk], out=pt[0:nk, :], in_=ob, func=AF.Exp)
                        else:
                            pi = C.ptz_i
                            C.ptz_i = (C.ptz_i + 1) % 3
                            pt, ptk = ptz[pi], 'ptz%d' % pi
                            S.do('scalar', 'activation', reads=[pk], writes=[ptk], out=sel(pt[0:nk, :]), in_=ob, func=AF.Exp)
                        return (ji, jb, g, nk, qs, pt, ptk, jb['V'](g), list(jb['keys']))

                    def stage2(u):
                        ji, jb, g, nk, qs, pt, ptk, vl, keys = u
                        for (aid, vap, nV) in vl:
                            ab, abk = naccs[aid][g]
                            for r in range(4):
                                S.do('tensor', 'matmul', reads=[ptk] + keys, writes=[abk], out=ab[:, r * nV:(r + 1) * nV], lhsT=pt[0:nk, r * 128:(r + 1) * 128], rhs=vap,
                                     start=((aid, g) not in started), stop=(last[(aid, g)] == ji and r == 3))
                                started.add((aid, g))
                        if qs is not None:
                            S.do('vector', 'memset', reads=[], writes=[ptk], ap=pt[0:nk, :].rearrange("p (r q) -> p r q", r=4)[:, :, qs * 8:(qs + 1) * 8], constant=0.0)

                    prev = None
                    for ji, jb in enumerate(jobs):
                        if 'prep' in jb:
                            jb['prep']()
                        for g in gs:
                            cur = stage1(ji, jb, g)
                            if prev is not None:
                                stage2(prev)
                            prev = cur
                    if prev is not None:
                        stage2(prev)

                qTt = [psb("qTt%d" % i, [64, 2048], BF16) for i in range(2)]
                gates = psb("gates", [128, 48])
                visadd = psb("visadd", [128, 2, 64])
                oa = psb("oa", [128, 1024])
                linv = psb("linv", [128, 4])
                imp = psb("imp", [128, 64]); imt = psb("imt", [128, 64])
                m8 = psb("m8", [128, 8]); sc2 = psb("sc2", [128, 64]); thr = psb("thr", [128, 1])
                selbT = [psb("selbT%d" % g, [64, 512], BF16) for g in range(4)]
                osc = psb("osc", [128, 4, 64])
                oaT = psb("oaT_a", [128, 8, 128], BF16)
                cmpb_t = psb("cmpb_t", [128, 2, 512], BF16)

                def topk_bias(t, g, imp_ap):
                    S.do('vector', 'tensor_tensor', reads=['imp', 'visadd'], writes=['imt'], out=imt[:], in0=imp_ap, in1=visadd[:, 0, :], op=ALU.mult)
                    S.do('vector', 'tensor_tensor', reads=['imt', 'visadd'], writes=['imt'], out=imt[:], in0=imt[:], in1=visadd[:, 1, :], op=ALU.add)
                    S.do('vector', 'max', reads=['imt'], writes=['m8'], out=m8[:], in_=imt[:])
                    S.do('vector', 'match_replace', reads=['imt', 'm8'], writes=['sc2'], out=sc2[:], in_to_replace=m8[:], in_values=imt[:], imm_value=-3.0e9)
                    S.do('vector', 'max', reads=['sc2'], writes=['m8'], out=m8[:], in_=sc2[:])
                    S.do('vector', 'tensor_reduce', reads=['m8'], writes=['thr'], out=thr[:], in_=m8[:], axis=AX.X, op=ALU.min)
                    S.do('vector', 'tensor_scalar', reads=['imt', 'thr'], writes=['sc2'], out=sc2[:], in0=imt[:], scalar1=thr[:, 0:1], scalar2=None, op0=ALU.is_ge)
                    S.do('vector', 'tensor_tensor', reads=['sc2', 'visadd'], writes=['sc2'], out=sc2[:], in0=sc2[:], in1=visadd[:, 0, :], op=ALU.mult)
                    S.do('vector', 'tensor_scalar', reads=['sc2'], writes=['sc2'], out=sc2[:], in0=sc2[:], scalar1=-NEGB, scalar2=NEGB, op0=ALU.mult, op1=ALU.add)
                    mb, mk = MISC
                    S.do('tensor', 'transpose', reads=['sc2', 'ident'], writes=[mk], out=mb[0:64, 0:128], in_=sc2[:], identity=ident[:])
                    for r in range(4):
                        S.do('scalar', 'activation', reads=[mk], writes=['selbT%d' % g], out=selbT[g][:, r * 128:(r + 1) * 128], in_=mb[0:64, 0:128], func=AF.Copy)

                def normalize(acc, g, br, nV):
                    ab, abk = acc
                    av = ab[:, 0:4 * nV].rearrange("p (r v) -> p r v", r=4)
                    S.do('vector', 'tensor_scalar', reads=[abk], writes=['linv'], out=linv[:], in0=av[:, :, 64], scalar1=1e-30, scalar2=None, op0=ALU.max)
                    S.do('vector', 'reciprocal', reads=['linv'], writes=['linv'], out=linv[:], in_=linv[:])
                    S.do('vector', 'tensor_tensor', reads=['linv', 'gates'], writes=['linv'], out=linv[:], in0=linv[:], in1=gates[:, br * 16 + g * 4:br * 16 + g * 4 + 4], op=ALU.mult)
                    dst = oa[:, g * 256:(g + 1) * 256].rearrange("p (r d) -> p r d", r=4)
                    lb_ = linv[:].rearrange("p (r o) -> p r o", o=1).to_broadcast([128, 4, 64])
                    if br == 0:
                        S.do('vector', 'tensor_tensor', reads=[abk, 'linv'], writes=['oa'], out=dst, in0=av[:, :, 0:64], in1=lb_, op=ALU.mult)
                    else:
                        S.do('vector', 'tensor_tensor', reads=[abk, 'linv'], writes=['osc'], out=osc[:], in0=av[:, :, 0:64], in1=lb_, op=ALU.mult)
                        S.do('vector', 'tensor_tensor', reads=['osc', 'oa'], writes=['oa'], out=dst, in0=dst, in1=osc[:], op=ALU.add)

                def cmp_finish_g(t, naccs, gs):
                    accs1, accs2 = naccs[0], naccs[1]
                    for g in gs:
                        ab, abk = accs1[g]
                        av = ab[:, 0:260].rearrange("p (r v) -> p r v", r=4)
                        S.do('vector', 'tensor_scalar', reads=[abk], writes=['linv'], out=linv[:], in0=av[:, :, 64], scalar1=1e-30, scalar2=None, op0=ALU.max)
                        S.do('vector', 'reciprocal', reads=['linv'], writes=['linv'], out=linv[:], in_=linv[:])
                        a2, a2k = accs2[g]
                        a2v = a2[:, 0:256].rearrange("p (r v) -> p r v", r=4)
                        S.do('vector', 'tensor_scalar', reads=[a2k, 'linv'], writes=['imp'], out=imp[:], in0=a2v[:, 0, :], scalar1=linv[:, 0:1], scalar2=None, op0=ALU.mult)
                        for r in range(1, 4):
                            S.do('vector', 'scalar_tensor_tensor', reads=[a2k, 'linv', 'imp'], writes=['imp'], out=imp[:], in0=a2v[:, r, :], scalar=linv[:, r:r + 1], in1=imp[:], op0=ALU.mult, op1=ALU.add)
                        topk_bias(t, g, imp[:])
                        normalize(accs1[g], g, 0, 65)

                def finish_tile(t):
                    for k2 in range(2):
                        pb, pk = next_bank()
                        for kk in range(4):
                            kc = k2 * 4 + kk
                            S.do('tensor', 'transpose', reads=['oa', 'ident'], writes=[pk], out=pb[:, kk * 128:(kk + 1) * 128], in_=oa[:, kc * 128:(kc + 1) * 128], identity=ident[:])
                        S.do('scalar', 'activation', reads=[pk], writes=['oaT_a'], out=oaT[:, k2 * 4:(k2 + 1) * 4, :], in_=pb[:].rearrange("p (a b) -> p a b", a=4), func=AF.Copy)
                    S.dd('sync', oaT_scr[t], oaT[:], reads=['oaT_a'], writes=[('oaT', t)], sem='oao')

                def load_tile(t):
                    qb, qk = qTt[t % 2], 'qTt%d' % (t % 2)
                    S.dd('sync', qb[:], qT_scr[t], reads=[('qT', t)], writes=[qk], sem='qt%d' % (t % 2))
                    S.dd('sync', gates[:], zs[t, :, O_GA:O_GA + 48], reads=[('zs', t, O_GA)], writes=['gates'], sem='c')
                    S.dd('sync', visadd[:], cvisadd[t], writes=['visadd'], sem='c')
                    return qb, qk

                if 'attp' in phases:
                  with ExitStack() as ph2:
                    psb2 = lambda name, shape, dt=F32: ph2.enter_context(nc.sbuf_tensor(name, list(shape), dt))
                    kcT = psb2("kcT", [64, 4, 256], BF16)
                    vca = psb2("vca", [128, 2, 4, 65], BF16)
                    S.do('vector', 'memset', writes=['kcT'], ap=kcT[:], constant=0.0)
                    S.do('vector', 'memset', writes=['vca'], ap=vca[:], constant=0.0)
                    S.do('vector', 'memset', writes=['vca'], ap=vca[:, :, :, 64:65], constant=1.0)
                    with ExitStack() as ph3:
                        XT = ph3.enter_context(nc.sbuf_tensor("XTp", [64, 2, 4, 4096], BF16))
                        for kv in range(2):
                            S.dd('sync', XT[:, kv], kT_scr[kv], reads=[('kT', T, kv) for T in range(NSEQ)], writes=['XTp'], sem='c')
                        compress(lambda kv, g: XT[:, kv, g, :], ['XTp'], 255, lambda g: kcT[:, g, 0:255], 'kcT', lambda g, ct, m: vca[0:m, ct, g, 0:64], 'vca')
                        dbg("kcT", kcT[:], [64, 4, 256], BF16, ['kcT'])
                        dbg("vca", vca[:], [128, 2, 4, 65], BF16, ['vca'])
                        S.barrier()
                    KTs = psb2("KTs", [64, 4, 4096], BF16); KTw = psb2("KTw", [64, 4, 4096], BF16)
                    Vsw = psb2("Vsw", [128, NSEQ, 2, 4, 65], BF16)
                    S.dd('sync', KTs[:], kT_scr[2], reads=[('kT', T, 2) for T in range(NSEQ)], writes=['KTs'], sem='c')
                    S.dd('sync', KTw[:], kT_scr[3], reads=[('kT', T, 3) for T in range(NSEQ)], writes=['KTw'], sem='c')
                    for T in range(NSEQ):
                        S.dd('sync', Vsw[:, T].rearrange("p a g d -> p (a g d)"), v_scr[T].rearrange("p a g d -> p (a g d)"), reads=[('v', T)], writes=['Vsw'], sem='c')
                    for t in range(8):
                        qb, qk = load_tile(t)
                        S.dd('sync', cmpb_t[:], ccmpb[t].rearrange("a p c -> p a c"), writes=['cmpb_t'], sem='c')
                        for gp in range(2):
                            gs = (2 * gp, 2 * gp + 1)
                            jobs = []
                            for ct in range(2):
                                jobs.append(dict(nk=128, keys=['kcT', 'vca', 'ovp', 'cmpb_t', 'identb'],
                                                 KT=lambda g, ct=ct: kcT[:, g, ct * 128:(ct + 1) * 128],
                                                 V=lambda g, ct=ct: [(0, vca[:, ct, g, :], 65), (1, ovp[:, ct, :], 64)],
                                                 bias=lambda g, ct=ct: [(identb[:], cmpb_t[:, ct, :], ['identb', 'cmpb_t'])]))
                            naccs = {0: {gs[0]: ACC[0], gs[1]: ACC[1]}, 1: {gs[0]: ACC[2], gs[1]: ACC[3]}}
                            attend_g(qb, qk, jobs, naccs, gs)
                            cmp_finish_g(t, naccs, gs)
                        if t in (0, 3, 7):
                            dbg("oc%d" % t, oa[:], [128, 1024], F32, ['oa'])
                            dbg("selb%d" % t, selbT[1][:], [64, 512], BF16, ['selbT1'])
                        i = t
                        jobs = []
                        for kt in range(4 * i + 4):
                            def bias(g, kt=kt, i=i):
                                bl = [(E[:, kt * 128:(kt + 1) * 128], selbT[g][:], ['E', 'selbT%d' % g])]
                                if kt >= 4 * i:
                                    bl.append((identb[:], selb[:, kt - 4 * i, :], ['identb', 'selb']))
                                return bl
                            jobs.append(dict(nk=128, keys=['KTs', 'Vsw'], KT=lambda g, kt=kt: KTs[:, g, kt * 128:(kt + 1) * 128],
                                             V=lambda g, kt=kt: [(0, Vsw[:, kt, 0, g, :], 65)], bias=bias))
                        naccs = {0: {g: ACC[g] for g in range(4)}}
                        attend_g(qb, qk, jobs, naccs, (0, 1, 2, 3))
                        for g in range(4):
                            normalize(ACC[g], g, 1, 65)
                        jobs = []
                        for o in range(-4, 4):
                            kt = 4 * i + o
                            if kt < 0:
                                continue
                            jobs.append(dict(nk=128, keys=['KTw', 'Vsw'], KT=lambda g, kt=kt: KTw[:, g, kt * 128:(kt + 1) * 128],
                                             V=lambda g, kt=kt: [(0, Vsw[:, kt, 1, g, :], 65)],
                                             bias=lambda g, o=o: [(identb[:], winb[:, o + 4, :], ['identb', 'winb'])]))
                        attend_g(qb, qk, jobs, naccs, (0, 1, 2, 3))
                        for g in range(4):
                            normalize(ACC[g], g, 2, 65)
                        if t in (0, 3, 7):
                            dbg("oa%d" % t, oa[:], [128, 1024], F32, ['oa'])
                        finish_tile(t)
                    S.barrier()

                if 'atts' in phases:
                  with ExitStack() as ph2:
                    psb2 = lambda name, shape, dt=F32: ph2.enter_context(nc.sbuf_tensor(name, list(shape), dt))
                    t = 8
                    for s_ in range(16):
                        for q4 in range(4):
                            S.dd('sync', s_win[s_, q4 * 126:(q4 + 1) * 126, :], win_in[s_, 8 + q4 * 126:8 + (q4 + 1) * 126, :], writes=[('s_win_a', s_, q4)], sem='swo')
                    pti = psb2("pti", [128, 256], I32); ptf = psb2("ptf", [128, 256]); idx = psb2("idx", [128, 256], I32)
                    pio = psb2("pio", [128, 1], I32); piof = psb2("piof", [128, 1])
                    S.dd('sync', pti[:], pt_in.to_broadcast([128, 256]), writes=['pti'], sem='c')
                    S.do('gpsimd', 'iota', writes=['pio'], out=pio[:], pattern=[[0, 1]], base=0, channel_multiplier=1)
                    S.do('vector', 'tensor_copy', reads=['pio'], writes=['piof'], out=piof[:], in_=pio[:])
                    S.do('vector', 'tensor_copy', reads=['pti'], writes=['ptf'], out=ptf[:], in_=pti[:])
                    S.do('vector', 'tensor_scalar', reads=['ptf', 'piof'], writes=['ptf'], out=ptf[:], in0=ptf[:], scalar1=128.0, scalar2=piof[:, 0:1], op0=ALU.mult, op1=ALU.add)
                    S.do('vector', 'tensor_copy', reads=['ptf'], writes=['idx'], out=idx[:], in_=ptf[:])
                    NPG = 6
                    pgb = [psb2("pgb%d" % i, [128, 512]) for i in range(NPG)]
                    ktp = [psb2("ktp%d" % i, [64, 4, 128], BF16) for i in range(NPG)]
                    vap = [psb2("vap%d" % i, [128, 4, 65], BF16) for i in range(NPG)]
                    for i in range(NPG):
                        S.do('vector', 'memset', writes=['vap%d' % i], ap=vap[i][:], constant=1.0)
                    C.pg_i = 0

                    def fetch_page(cache, s, p):
                        bi = C.pg_i
                        C.pg_i = (bi + 1) % NPG
                        pg, pgk = pgb[bi], 'pgb%d' % bi
                        col = s * 16 + p
                        S.dma('gpsimd', lambda e: e.indirect_dma_start(out=pg[:, :], out_offset=None, in_=cache[:, :],
                                                                      in_offset=bass.IndirectOffsetOnAxis(ap=idx[:, col:col + 1], axis=0)),
                              reads=['idx'], writes=[pgk], sem='pg%d' % bi)
                        return bi, pg, pgk

                    def kv_prep(bi, pg, pgk):
                        pb, pk = next_bank()
                        for g in range(4):
                            S.do('tensor', 'transpose', reads=[pgk, 'ident'], writes=[pk], out=pb[0:64, g * 128:(g + 1) * 128], in_=pg[:, g * 64:(g + 1) * 64], identity=ident[:])
                        S.do('scalar', 'activation', reads=[pk], writes=['ktp%d' % bi], out=ktp[bi][:], in_=pb[0:64, :].rearrange("p (g k) -> p g k", g=4), func=AF.Copy)
                        S.do('vector', 'tensor_copy', reads=[pgk], writes=['vap%d' % bi], out=vap[bi][:, :, 0:64], in_=pg[:, 256:512].rearrange("p (g d) -> p g d", g=4))

                    kcA = psb2("kcA", [64, 16, 4, 128], BF16)
                    vcA = psb2("vcA", [128, 16, 4, 65], BF16)
                    S.do('vector', 'memset', writes=['kcA'], ap=kcA[:], constant=0.0)
                    S.do('vector', 'memset', writes=['vcA'], ap=vcA[:], constant=0.0)
                    S.do('vector', 'memset', writes=['vcA'], ap=vcA[:, :, :, 64:65], constant=1.0)
                    with ExitStack() as ph3:
                        XTs = ph3.enter_context(nc.sbuf_tensor("XTs", [64, 2, 4, 2048], BF16))
                        for s_ in range(16):
                            for p in range(16):
                                bi, pg, pgk = fetch_page(cache_cmp, s_, p)
                                for kv in range(2):
                                    pb, pk = next_bank()
                                    for g in range(4):
                                        S.do('tensor', 'transpose', reads=[pgk, 'ident'], writes=[pk], out=pb[0:64, g * 128:(g + 1) * 128], in_=pg[:, kv * 256 + g * 64:kv * 256 + (g + 1) * 64], identity=ident[:])
                                    S.do('scalar', 'activation', reads=[pk], writes=['XTs'], out=XTs[:, kv, :, p * 128:(p + 1) * 128], in_=pb[0:64, :].rearrange("p (g k) -> p g k", g=4), func=AF.Copy)
                            compress(lambda kv, g: XTs[:, kv, g, :], ['XTs'], 127, lambda g, s_=s_: kcA[:, s_, g, 0:127], 'kcA', lambda g, ct, m, s_=s_: vcA[0:m, s_, g, 0:64], 'vcA')
                        S.barrier()
                    qb, qk = load_tile(t)
                    for gp in range(2):
                        gs = (2 * gp, 2 * gp + 1)
                        jobs = []
                        for s_ in range(16):
                            jobs.append(dict(nk=127, qs=s_, keys=['kcA', 'vcA', 'ovs'],
                                             KT=lambda g, s_=s_: kcA[:, s_, g, 0:127],
                                             V=lambda g, s_=s_: [(0, vcA[0:127, s_, g, :], 65), (1, ovs[0:127, :], 64)],
                                             bias=lambda g: []))
                        naccs = {0: {gs[0]: ACC[0], gs[1]: ACC[1]}, 1: {gs[0]: ACC[2], gs[1]: ACC[3]}}
                        attend_g(qb, qk, jobs, naccs, gs)
                        cmp_finish_g(t, naccs, gs)
                    rnew = psb2("rnew", [128, 3, 512])
                    ktn = psb2("ktn", [64, 2, 4, 128], BF16); van = psb2("van", [128, 2, 4, 65], BF16)
                    S.dd('sync', rnew[:], rows_own[t], reads=[('rows_own', t)], writes=['rnew'], sem='c')
                    S.do('gpsimd', 'memset', writes=['van'], ap=van[:], constant=1.0)
                    for bi_, br in enumerate((1, 2)):
                        pb, pk = next_bank()
                        for g in range(4):
                            S.do('tensor', 'transpose', reads=['rnew', 'ident'], writes=[pk], out=pb[0:64, g * 128:(g + 1) * 128], in_=rnew[:, br, g * 64:(g + 1) * 64], identity=ident[:])
                        S.do('scalar', 'activation', reads=[pk], writes=['ktn'], out=ktn[:, bi_], in_=pb[0:64, :].rearrange("p (g k) -> p g k", g=4), func=AF.Copy)
                        S.do('gpsimd', 'tensor_copy', reads=['rnew'], writes=['van'], out=van[:, bi_, :, 0:64], in_=rnew[:, br, 256:512].rearrange("p (g d) -> p g d", g=4))
                    S.dd('sync', s_win[:, 504:512, :], rnew[:, 2, :], reads=['rnew'], writes=['s_win_b'], sem='swo')
                    naccs = {0: {g: ACC[g] for g in range(4)}}
                    jobs = []
                    for s_ in range(16):
                        for p in range(16):
                            jb = dict(nk=128, qs=s_)

                            def prep(jb=jb, s_=s_, p=p):
                                bi, pg, pgk = fetch_page(cache_sel, s_, p)
                                kv_prep(bi, pg, pgk)
                                jb['keys'] = ['ktp%d' % bi, 'vap%d' % bi]
                                jb['KT'] = lambda g, bi=bi: ktp[bi][:, g, :]
                                jb['V'] = lambda g, bi=bi: [(0, vap[bi][:, g, :], 65)]
                            jb['prep'] = prep
                            jb['bias'] = lambda g, s_=s_, p=p: [(E[:, p * 128:(p + 1) * 128], selbT[g][:], ['E', 'selbT%d' % g])]
                            jb['V'] = lambda g: [(0, None, 65)]
                            jobs.append(jb)
                    jobs.append(dict(nk=128, keys=['ktn', 'van'], KT=lambda g: ktn[:, 0, g, :], V=lambda g: [(0, van[:, 0, g, :], 65)],
                                     bias=lambda g: [(identb[:], bdc[:], ['identb', 'bdc'])]))
                    attend_g(qb, qk, jobs, naccs, (0, 1, 2, 3))
                    for g in range(4):
                        normalize(ACC[g], g, 1, 65)
                    jobs = []
                    for s_ in range(16):
                        for w_ in range(4):
                            jb = dict(nk=128, qs=s_)

                            def prep(jb=jb, s_=s_, w_=w_):
                                bi = C.pg_i
                                C.pg_i = (bi + 1) % NPG
                                pg, pgk = pgb[bi], 'pgb%d' % bi
                                S.dd('sync', pg[:], win_in[s_, w_ * 128:(w_ + 1) * 128, :], writes=[pgk], sem='pg%d' % bi)
                                kv_prep(bi, pg, pgk)
                                jb['keys'] = ['ktp%d' % bi, 'vap%d' % bi]
                                jb['KT'] = lambda g, bi=bi: ktp[bi][:, g, :]
                                jb['V'] = lambda g, bi=bi: [(0, vap[bi][:, g, :], 65)]
                            jb['prep'] = prep

                            def bias(g, s_=s_, w_=w_):
                                bl = []
                                if w_ == 0:
                                    bl.append((identb[:], wb0[:], ['identb', 'wb0']))
                                return bl
                            jb['bias'] = bias
                            jb['V'] = lambda g: [(0, None, 65)]
                            jobs.append(jb)
                    jobs.append(dict(nk=128, keys=['ktn', 'van'], KT=lambda g: ktn[:, 1, g, :], V=lambda g: [(0, van[:, 1, g, :], 65)],
                                     bias=lambda g: [(identb[:], bdc[:], ['identb', 'bdc'])]))
                    attend_g(qb, qk, jobs, naccs, (0, 1, 2, 3))
                    for g in range(4):
                        normalize(ACC[g], g, 2, 65)
                    dbg("oa8", oa[:], [128, 1024], F32, ['oa'])
                    finish_tile(t)
                    S.barrier()
                C.pb_n = 8
                S.barrier()

        gst2 = ExitStack()
        alloc_gemm(gst2)
        if 'fin' in phases:
            with ExitStack() as ph:
                psb = lambda name, shape, dt=F32: ph.enter_context(nc.sbuf_tensor(name, list(shape), dt))
                g2t = psb("g2t", [128, KC])
                S.dd('sync', g2t[:], g2, writes=['g2t'], sem='c')
                ldb = [psb("ldb%d" % i, [128, 512]) for i in range(4)]
                C.ld_i = 0

                def ld(src, reads):
                    i = C.ld_i
                    C.ld_i = (C.ld_i + 1) % 4
                    S.dd('sync', ldb[i][:], src, reads=reads, writes=['ldb%d' % i], sem='ld%d' % i)
                    return ldb[i], 'ldb%d' % i

                wpa_v = w_pa.rearrange("(kc p) n -> p kc n", p=128)
                wpb_v = w_pb.rearrange("(kc p) n -> p kc n", p=128)
                wo_v = w_o.rearrange("(kc p) n -> p kc n", p=128)
                wf1_v = w_f1.rearrange("(kc p) n -> p kc n", p=128)
                ch4 = [(c * 512, 512, None) for c in range(4)]
                x1T = psb("x1T", [128, NT, KC, 128], BF16)
                ssq2 = psb("ssq2", [128, NT, 4])
                rstd2 = psb("rstd2", [128, NT])
                x1_scr = dscr("x1_scr", [NT, 128, D])
                with ExitStack() as ph1:
                    psb1 = lambda name, shape, dt=F32: ph1.enter_context(nc.sbuf_tensor(name, list(shape), dt))
                    mixT = psb1("mixT", [128, NT, KC, 128], BF16)
                    with ExitStack() as ph2:
                        psb2 = lambda name, shape, dt=F32: ph2.enter_context(nc.sbuf_tensor(name, list(shape), dt))
                        oaT = psb2("oaT", [128, NT, 8, 128], BF16)
                        obT2 = psb2("obT2", [128, NT, 8, 128], BF16)
                        mxs = psb2("mxs", [128, 512])
                        for t in range(NT):
                            S.dd('sync', oaT[:, t], oaT_scr[t], reads=[('oaT', t)], writes=['oaT_sb%d' % t], sem='c')
                            S.dd('sync', obT2[:, t], obT_scr[t], reads=[('obT', t)], writes=['obT_sb%d' % t], sem='c')

                        def epi_a(ci, t, pb, pk, c0, w, tag):
                            gb_, gk = ld(zs[t, :, O_A + c0:O_A + c0 + w], [('zs', t, O_A + c0)])
                            zi = C.zst_i
                            C.zst_i = (C.zst_i + 1) % 4
                            S.do('vector', 'tensor_tensor', reads=[pk, gk], writes=['zst%d' % zi], out=C.zst[zi][:], in0=pb[:], in1=gb_[:], op=ALU.mult)
                            S.dd('sync', mix_scr[t, :, c0:c0 + w], C.zst[zi][:], reads=['zst%d' % zi], writes=[('mix', t, c0)], sem='zo%d' % zi)
                        gemm(lambda t, kc: oaT[:, t, kc, :], lambda t: 'oaT_sb%d' % t, NT, 8, wpa_v, ch4, None, None, epi_a)

                        def epi_b(ci, t, pb, pk, c0, w, tag):
                            gb_, gk = ld(zs[t, :, O_B + c0:O_B + c0 + w], [('zs', t, O_B + c0)])
                            pa_, pak = ld(mix_scr[t, :, c0:c0 + w], [('mix', t, c0)])
                            S.do('vector', 'tensor_tensor', reads=[pk, gk], writes=['mxs'], out=mxs[:], in0=pb[:], in1=gb_[:], op=ALU.mult)
                            S.do('vector', 'tensor_tensor', reads=['mxs', pak], writes=['mxs'], out=mxs[:], in0=mxs[:], in1=pa_[:], op=ALU.add)
                            pb2, pk2 = next_bank()
                            for kk in range(4):
                                S.do('tensor', 'transpose', reads=['mxs', 'ident'], writes=[pk2], out=pb2[:, kk * 128:(kk + 1) * 128], in_=mxs[:, kk * 128:(kk + 1) * 128], identity=ident[:])
                            S.do('scalar', 'activation', reads=[pk2], writes=['mixT%d' % t], out=mixT[:, t, ci * 4:(ci + 1) * 4, :], in_=pb2[:].rearrange("p (a b) -> p a b", a=4), func=AF.Copy)
                        gemm(lambda t, kc: obT2[:, t, kc, :], lambda t: 'obT_sb%d' % t, NT, 8, wpb_v, ch4, None, None, epi_b)
                        S.barrier()
                    junk2 = psb1("junk2", [128, 512])
                    x1c = [psb1("x1c%d" % i, [128, 512]) for i in range(2)]
                    C.x1c_i = 0
                    S.do('vector', 'memset', writes=['ssq2'], ap=ssq2[:], constant=0.0)

                    def epi_o(ci, t, pb, pk, c0, w, tag):
                        xb_, xk = ld(x_own[t, :, c0:c0 + w], [])
                        i = C.x1c_i
                        C.x1c_i = (C.x1c_i + 1) % 2
                        xc, yk = x1c[i], 'x1c%d' % i
                        S.do('vector', 'tensor_tensor', reads=[pk, xk], writes=[yk], out=xc[:], in0=pb[:], in1=xb_[:], op=ALU.add)
                        S.dd('sync', x1_scr[t, :, c0:c0 + w], xc[:], reads=[yk], writes=[('x1', t, ci)], sem='x1o%d' % i)
                        S.do('scalar', 'activation', reads=[yk, 'ssq2'], writes=['junk2', 'ssq2'], out=junk2[:], in_=xc[:], func=AF.Square, accum_out=ssq2[:, t, ci:ci + 1])
                        pb2, pk2 = next_bank()
                        for kk in range(4):
                            S.do('tensor', 'transpose', reads=[yk, 'ident'], writes=[pk2], out=pb2[:, kk * 128:(kk + 1) * 128], in_=xc[:, kk * 128:(kk + 1) * 128], identity=ident[:])
                        S.do('scalar', 'activation', reads=[pk2], writes=['x1T%d' % t], out=x1T[:, t, ci * 4:(ci + 1) * 4, :], in_=pb2[:].rearrange("p (a b) -> p a b", a=4), func=AF.Copy)
                    gemm(lambda t, kc: mixT[:, t, kc, :], lambda t: 'mixT%d' % t, NT, KC, wo_v, ch4, None, None, epi_o)
                    S.do('vector', 'tensor_reduce', reads=['ssq2'], writes=['rstd2'], out=rstd2[:], in_=ssq2[:], axis=AX.X, op=ALU.add)
                    S.do('scalar', 'activation', reads=['rstd2'], writes=['rstd2'], out=rstd2[:], in_=rstd2[:], func=AF.Sqrt, scale=1.0 / D, bias=EPS)
                    S.do('vector', 'reciprocal', reads=['rstd2'], writes=['rstd2'], out=rstd2[:], in_=rstd2[:])
                    S.barrier()
                yacc = psb("yacc", [128, NT, D])
                for t in range(NT):
                    for ci in range(4):
                        S.dd('sync', yacc[:, t, ci * 512:(ci + 1) * 512], x1_scr[t, :, ci * 512:(ci + 1) * 512], reads=[('x1', t, ci)], writes=['yacc%d_%d' % (t, ci)], sem='c')
                hT = [psb("hT%d" % i, [128, NT, 4, 128], BF16) for i in range(2)]
                hsb = psb("hsb", [128, 512])
                ffn_specs = []
                for fc in range(16):
                    hb, hbk = hT[fc % 2], 'hT%d' % (fc % 2)

                    def epi_h(ci, t, pb, pk, c0, w, tag, hb=hb, hbk=hbk):
                        S.do('scalar', 'activation', reads=[pk, 'rstd2'], writes=['hsb'], out=hsb[:], in_=pb[:], func=AF.Relu, scale=rstd2[:, t:t + 1])
                        S.do('vector', 'tensor_tensor', reads=['hsb'], writes=['hsb'], out=hsb[:], in0=hsb[:], in1=hsb[:], op=ALU.mult)
                        pb2, pk2 = next_bank()
                        for kk in range(4):
                            S.do('tensor', 'transpose', reads=['hsb', 'ident'], writes=[pk2], out=pb2[:, kk * 128:(kk + 1) * 128], in_=hsb[:, kk * 128:(kk + 1) * 128], identity=ident[:])
                        S.do('scalar', 'activation', reads=[pk2], writes=[hbk + '_%d' % t], out=hb[:, t, :, :], in_=pb2[:].rearrange("p (a b) -> p a b", a=4), func=AF.Copy)
                    ffn_specs += mk_specs(lambda t, kc: x1T[:, t, kc, :], lambda t: 'x1T%d' % t, NT, KC, wf1_v, [(fc * 512, 512, None)], g2t, 'g2t', epi_h)
                    wf2_v = w_f2[fc * 512:(fc + 1) * 512, :].rearrange("(kc p) n -> p kc n", p=128)

                    def epi_y(ci, t, pb, pk, c0, w, tag):
                        yk = 'yacc%d_%d' % (t, ci)
                        S.do('vector', 'tensor_tensor', reads=[pk, yk], writes=[yk], out=yacc[:, t, c0:c0 + w], in0=yacc[:, t, c0:c0 + w], in1=pb[:], op=ALU.add)
                    ffn_specs += mk_specs(lambda t, kc, hb=hb: hb[:, t, kc, :], lambda t, hbk=hbk: hbk + '_%d' % t, NT, 4, wf2_v, ch4, None, None, epi_y)
                gemm_specs(ffn_specs)
                for t in range(NT):
                    S.dd('sync', y_own[t], yacc[:, t, :], reads=['yacc%d_%d' % (t, ci) for ci in range(4)], writes=[('y', t)], sem='yo')
                S.barrier()

        gst2.close()
        S.barrier()
        S.emit()
    return nc


def _rope_tables(pos):
    half = 32
    inv = (10000.0 ** (-(np.arange(half, dtype=np.float32)) * 2.0 / 64)).astype(np.float32)
    ang = pos.astype(np.float32)[:, None] * inv[None, :]
    return np.concatenate([np.cos(ang), np.sin(ang)], axis=1).astype(np.float32)


def make_in_maps(inp, with_samp=True):
    maps = []
    xp = np.asarray(inp['x_prompt'])
    xs = np.asarray(inp['x_sample'])
    for c in range(8):
        b, j = c // 4, c % 4
        tiles = [xp[b, (4 * i + j) * 128:(4 * i + j + 1) * 128] for i in range(8)]
        tiles.append(xs[16 * c:16 * c + 16].reshape(128, D))
        x_own = np.ascontiguousarray(np.stack(tiles))
        pos = [np.arange((4 * i + j) * 128, (4 * i + j + 1) * 128) for i in range(8)]
        pos.append(np.tile(2048 + np.arange(8), 16))
        cs_own = np.stack([_rope_tables(p) for p in pos])
        U = np.zeros((128, 128), np.float32)
        for s_ in range(128):
            for t_ in range(128):
                if s_ > t_ and s_ // 64 == t_ // 64:
                    U[s_, t_] = 1.0
        G2 = np.zeros((128, 2), np.float32)
        G2[:64, 0] = 1.0
        G2[64:, 1] = 1.0
        js = np.zeros((128, 4), np.float32)
        js[:, j] = 1.0
        Dm = np.zeros((2, 128, 128), np.float32); Pm = np.zeros((2, 128, 16), np.float32); MT = np.zeros((2, 128, 128), np.float32)
        U8 = np.zeros((128, 128), np.float32); G16 = np.zeros((128, 16), np.float32)
        for s_ in range(128):
            G16[s_, s_ // 8] = 1.0
            if s_ % 64 <= 31:
                Pm[0, s_, s_ // 64] = 1.0
            for t_ in range(128):
                if s_ // 64 == t_ // 64:
                    Tst = 1.0 if s_ <= t_ else 0.0
                    Pst = 1.0 if (s_ % 64) <= 31 else 0.0
                    Dm[0, s_, t_] = Tst - Pst
                    MT[0, s_, t_] = Tst
                if s_ // 8 == t_ // 8:
                    Dm[1, s_, t_] = 1.0 if s_ <= t_ else 0.0
                    MT[1, s_, t_] = 1.0 if s_ <= t_ else 0.0
                    U8[s_, t_] = 1.0 if s_ > t_ else 0.0
        import ml_dtypes
        bf = ml_dtypes.bfloat16
        NEGB = -30000.0
        kk = np.arange(128)[:, None]; qq = np.arange(128)[None, :]
        rep4 = lambda a: np.ascontiguousarray(np.tile(a, (1, 4)))
        cE = (np.arange(4096)[None, :] // 64 == np.arange(64)[:, None]).astype(np.float32)
        selb = np.stack([rep4(np.where((o - j) * 128 + kk <= qq, 0.0, NEGB)) for o in range(4)])
        winb = np.stack([rep4(np.where(((o - j) * 128 + kk <= qq) & ((o - j) * 128 + kk > qq - 512), 0.0, NEGB)) for o in range(-4, 4)])
        cmpb = np.zeros((8, 2, 128, 512), np.float32)
        for i in range(8):
            for ct in range(2):
                cidx = ct * 128 + kk
                cmpb[i, ct] = rep4(np.where((16 * cidx + 31 <= (4 * i + j) * 128 + qq) & (cidx < 255), 0.0, NEGB))
        BIG = 1e9
        visadd = np.zeros((NT, 128, 2, 64), np.float32)
        nn = np.arange(64)[None, :]
        for t in range(NT):
            if t < 8:
                qpos = ((4 * t + j) * 128 + np.arange(128))[:, None]
                visible = nn * 64 <= qpos
            else:
                qpos = (2048 + np.arange(128) % 8)[:, None]
                visible = (nn * 64 <= qpos) & (nn < 33)
            cur = qpos // 64
            forced = (nn == 0) | (nn == cur) | (nn == cur - 1)
            visadd[t, :, 0, :] = visible.astype(np.float32)
            visadd[t, :, 1, :] = np.where(visible, np.where(forced, BIG, 0.0), -BIG)
        def overlap(ncc, nss):
            cs = np.arange(ncc) * 16; ce = cs + 32; ss = np.arange(nss) * 64; se = ss + 64
            ov = np.clip(np.minimum(ce[:, None], se[None, :]) - np.maximum(cs[:, None], ss[None, :]), 0, None)
            return (ov / 16).astype(np.float32)
        ovp = np.zeros((256, 64), np.float32); ovp[:255] = overlap(255, 64)
        ovs = np.zeros((128, 64), np.float32); ovs[:127, :33] = overlap(127, 33)
        OH = np.zeros((17, 16, 128), np.float32)
        for s_ in range(16):
            OH[s_, s_, :] = 1.0
        OH[16, :, 127] = 1.0
        seqb = np.full((17, 512), NEGB, np.float32)
        for s_ in range(16):
            seqb[s_] = np.tile(np.where(np.arange(128) // 8 == s_, 0.0, NEGB), 4)
        bdc = rep4(np.where((kk // 8 == qq // 8) & (kk % 8 <= qq % 8), 0.0, NEGB))
        wb0 = rep4(np.where(kk > (qq % 8), 0.0, NEGB))
        m_s = {
            'pt_in': np.ascontiguousarray(np.asarray(inp['page_table'])[16 * c:16 * c + 16].reshape(1, 256).astype(np.int32)),
            'cache_cmp': np.asarray(inp['cache_cmp_kv'])[0].reshape(NPOOL * 128, 512),
            'cache_sel': np.asarray(inp['cache_sel_kv'])[0].reshape(NPOOL * 128, 512),
            'win_in': np.ascontiguousarray(np.asarray(inp['cache_win_kv'])[0, 16 * c:16 * c + 16].reshape(16, 512, 512)),
        }
        m = {
            'cE': cE.astype(bf), 'cselb': selb.astype(bf), 'cwinb': winb.astype(bf), 'ccmpb': cmpb.astype(bf), 'cvisadd': visadd,
            'covp': ovp.reshape(2, 128, 64).astype(bf), 'covs': ovs.astype(bf), 'cOH': OH.astype(bf), 'cseqb': seqb.astype(bf),
            'cbdc': bdc.astype(bf), 'cwb0': wb0.astype(bf),
            'cw1': np.ascontiguousarray(np.asarray(inp['cmp_w1'])[0]), 'cw2': np.ascontiguousarray(np.asarray(inp['cmp_w2'])[0]),
            'cpe': np.ascontiguousarray(np.asarray(inp['cmp_pe'])[0]),
            'w_pa': np.ascontiguousarray(np.asarray(inp['w_proj_a'])[0]), 'w_pb': np.ascontiguousarray(np.asarray(inp['w_proj_b'])[0]),
            'w_o': np.ascontiguousarray(np.asarray(inp['w_out'])[0]), 'w_f1': np.ascontiguousarray(np.asarray(inp['w_ff1'])[0]), 'w_f2': np.ascontiguousarray(np.asarray(inp['w_ff2'])[0]),
            'g2': np.ascontiguousarray(np.asarray(inp['norm2_g'])[0].reshape(KC, 128).T),
            'cDm': Dm, 'cPm': Pm, 'cMT': MT, 'cU8': U8, 'cG16': G16,
            'st_in': np.ascontiguousarray(np.asarray(inp['state_hgrn'])[0, 16 * c:16 * c + 16]),
            'hng': np.ascontiguousarray(np.asarray(inp['hgrn_norm_g'])[0].reshape(128, 1)),
            'x_seq': np.ascontiguousarray(xp[b].reshape(NSEQ, 128, D)),
            'cs_seq': np.ascontiguousarray(_rope_tables(np.arange(4096)).reshape(NSEQ, 128, 64)),
            'lbl': np.ascontiguousarray(np.asarray(inp['hgrn_lb_logits'])),
            'jsel': js, 'cU64': U, 'cG2': G2,
            'x_own': x_own,
            'cs_own': np.ascontiguousarray(cs_own),
            'w_in': np.ascontiguousarray(np.asarray(inp['w_in'])[0]),
            'g1': np.ascontiguousarray(np.asarray(inp['norm1_g'])[0].reshape(KC, 128).T),
            'qkg': np.ascontiguousarray(np.concatenate([np.asarray(inp['q_norm_g']), np.asarray(inp['k_norm_g'])[0]], axis=0)),
        }
        if with_samp:
            m.update(m_s)
        maps.append(m)
    return maps


_NC_CACHE = {}


def run(inp, phases=('own', 'seq', 'hgrn', 'att', 'attp', 'atts', 'fin')):
    key = tuple(phases)
    if key not in _NC_CACHE:
        _NC_CACHE[key] = build(phases)
    nc = _NC_CACHE[key]
    maps = make_in_maps(inp, with_samp=('atts' in phases))
    res = run_bass_kernel_spmd(nc, maps, core_ids=list(range(8)))
    return res.results


def assemble(res):
    B, T = 2, 4096
    p_cmp = np.zeros((1, B, T, 2, 4, 64), np.float32)
    p_sel = np.zeros((1, B, T, 2, 4, 64), np.float32)
    p_win = np.zeros((1, B, 512, 2, 4, 64), np.float32)
    s_cmp = np.zeros((1, 128, 8, 2, 4, 64), np.float32)
    s_sel = np.zeros((1, 128, 8, 2, 4, 64), np.float32)
    y_p = np.zeros((B, T, D), np.float32)
    y_s = np.zeros((128, 8, D), np.float32)
    p_st = np.zeros((1, B, 8, 128, 128), np.float32)
    s_win = np.zeros((1, 128, 512, 2, 4, 64), np.float32)
    s_st = np.zeros((1, 128, 8, 128, 128), np.float32)
    for c in range(8):
        b, j = c // 4, c % 4
        r = res[c]
        rows = r['rows_own']
        for i in range(8):
            J = 4 * i + j
            p_cmp[0, b, J * 128:(J + 1) * 128] = rows[i, :, 0].reshape(128, 2, 4, 64)
            p_sel[0, b, J * 128:(J + 1) * 128] = rows[i, :, 1].reshape(128, 2, 4, 64)
            if J >= 28:
                p_win[0, b, (J - 28) * 128:(J - 27) * 128] = rows[i, :, 2].reshape(128, 2, 4, 64)
        s_cmp[0, 16 * c:16 * c + 16] = rows[8, :, 0].reshape(16, 8, 2, 4, 64)
        s_sel[0, 16 * c:16 * c + 16] = rows[8, :, 1].reshape(16, 8, 2, 4, 64)
        if j == 0 and 'p_state' in r:
            p_st[0, b] = r['p_state']
        if 'y_own' in r:
            for i in range(8):
                J = 4 * i + j
                y_p[b, J * 128:(J + 1) * 128] = r['y_own'][i]
            y_s[16 * c:16 * c + 16] = r['y_own'][8].reshape(16, 8, D)
        if 's_win' in r:
            s_win[0, 16 * c:16 * c + 16] = r['s_win'].reshape(16, 512, 2, 4, 64)
        if 's_state' in r:
            s_st[0, 16 * c:16 * c + 16] = r['s_state']
    return (y_p, y_s, p_cmp, p_sel, p_win, p_st, s_cmp, s_sel, s_win, s_st)


def kernel(**inp):
    res = run(inp)
    return assemble(res)
```
